# Optimizing a Trainium2 kernel written in Bass

```python
import math
import jax, jax.numpy as jnp
from jax import lax
import numpy as np

D_MODEL = 1024
BATCH = 16
SEQ = 2048
DEPTH = 2

N_EVEN = (DEPTH + 1) // 2
N_ODD = DEPTH // 2
D_FF = 2816
RMS_EPS = 1e-6
LN_EPS = 1e-5
MIX_WIDTH = D_MODEL
CONV_CH = MIX_WIDTH // 2
CONV_TAPS = 31
SSM_WIDTH = MIX_WIDTH - CONV_CH
SSM_GROUP = 16
SSM_GROUPS = SSM_WIDTH // SSM_GROUP
SSM_STATE = 64
DT_MIN = 1e-3
DT_MAX = 1e-1
IN_WIDTH = 2 * CONV_CH + SSM_WIDTH
N_HEADS = 8
HEAD_DIM = D_MODEL // N_HEADS
MOBA_BLOCK = 256
MOBA_TOPK = 3
Q_CHUNK = 8

kernel_name = "hybrid_conv_s5_moba_macaron"


def rms_norm(x, g):
    xf = x.astype(jnp.float32)
    y = xf * lax.rsqrt(jnp.mean(xf * xf, axis=-1, keepdims=True) + RMS_EPS)
    return (y * g.astype(jnp.float32)).astype(x.dtype)


def swiglu(h, w1, w3, w2):
    return (jax.nn.silu(h @ w1) * (h @ w3)) @ w2


def conformer_conv(a, g, conv_w, conv_b, ln_g, ln_b):
    v = a * jax.nn.sigmoid(g)
    y = lax.conv_general_dilated(
        v, conv_w[:, None, :].astype(v.dtype), window_strides=(1,),
        padding=[(CONV_TAPS - 1, 0)], dimension_numbers=("NWC", "WIO", "NWC"),
        feature_group_count=CONV_CH) + conv_b
    yf = y.astype(jnp.float32)
    mu = jnp.mean(yf, axis=-1, keepdims=True)
    var = jnp.mean(jnp.square(yf - mu), axis=-1, keepdims=True)
    yn = (yf - mu) * lax.rsqrt(var + LN_EPS) * ln_g.astype(jnp.float32) + ln_b.astype(jnp.float32)
    return jax.nn.silu(yn).astype(a.dtype)


def _complex_linear_combine(e1, e2):
    a1r, a1i, b1r, b1i = e1
    a2r, a2i, b2r, b2i = e2
    ar = a1r * a2r - a1i * a2i
    ai = a1r * a2i + a1i * a2r
    br = a2r * b1r - a2i * b1i + b2r
    bi = a2r * b1i + a2i * b1r + b2i
    return (ar, ai, br, bi)


def s5_ssm(u, a_re, a_im, b_re, b_im, c_re, c_im, d, log_dt, glu_w, glu_b):
    f32 = jnp.float32
    bsz, L, _ = u.shape
    a_re, a_im = a_re.astype(f32), a_im.astype(f32)
    b_re, b_im = b_re.astype(f32), b_im.astype(f32)
    c_re, c_im = c_re.astype(f32), c_im.astype(f32)
    dt = jnp.exp(log_dt.astype(f32))[:, None]
    mag = jnp.exp(dt * a_re)
    ang = dt * a_im
    abar_re = mag * jnp.cos(ang)
    abar_im = mag * jnp.sin(ang)
    den = a_re * a_re + a_im * a_im
    nr = abar_re - 1.0
    ni = abar_im
    q_re = (nr * a_re + ni * a_im) / den
    q_im = (ni * a_re - nr * a_im) / den
    bbar_re = q_re[..., None] * b_re - q_im[..., None] * b_im
    bbar_im = q_re[..., None] * b_im + q_im[..., None] * b_re
    uf = u.astype(f32)
    ug = uf.reshape(bsz, L, SSM_GROUPS, SSM_GROUP)
    bu_re = jnp.einsum("blgh,gph->blgp", ug, bbar_re)
    bu_im = jnp.einsum("blgh,gph->blgp", ug, bbar_im)
    a_seq_re = jnp.broadcast_to(abar_re[None, None], (1, L, SSM_GROUPS, SSM_STATE))
    a_seq_im = jnp.broadcast_to(abar_im[None, None], (1, L, SSM_GROUPS, SSM_STATE))
    _, _, x_re, x_im = lax.associative_scan(
        _complex_linear_combine, (a_seq_re, a_seq_im, bu_re, bu_im), axis=1)
    y = jnp.einsum("blgp,ghp->blgh", x_re, c_re) - jnp.einsum("blgp,ghp->blgh", x_im, c_im)
    y = y.reshape(bsz, L, SSM_WIDTH) + d.astype(f32) * uf
    y = jax.nn.gelu(y)
    y = y * jax.nn.sigmoid(y @ glu_w.astype(f32) + glu_b.astype(f32))
    return y.astype(u.dtype)


def conv_ssm_mixer(h, w_in, conv_w, conv_b, ln_g, ln_b, a_re, a_im, b_re, b_im,
                   c_re, c_im, d, log_dt, glu_w, glu_b, w_out):
    p = h @ w_in
    a = p[..., :CONV_CH]
    g = p[..., CONV_CH:2 * CONV_CH]
    u = p[..., 2 * CONV_CH:]
    y_conv = conformer_conv(a, g, conv_w, conv_b, ln_g, ln_b)
    y_ssm = s5_ssm(u, a_re, a_im, b_re, b_im, c_re, c_im, d, log_dt, glu_w, glu_b)
    return jnp.concatenate([y_conv, y_ssm], axis=-1) @ w_out


def moba_attention(h, w_qkv, w_o):
    f32 = jnp.float32
    bsz, L, _ = h.shape
    qkv = (h @ w_qkv).reshape(bsz, L, 3, N_HEADS, HEAD_DIM)
    q = qkv[:, :, 0].transpose(0, 2, 1, 3)
    k = qkv[:, :, 1].transpose(0, 2, 1, 3)
    v = qkv[:, :, 2].transpose(0, 2, 1, 3)
    n_blk = -(-L // MOBA_BLOCK)
    pad = n_blk * MOBA_BLOCK - L
    kp = jnp.pad(k, ((0, 0), (0, 0), (0, pad), (0, 0)))
    vp = jnp.pad(v, ((0, 0), (0, 0), (0, pad), (0, 0)))
    kb = kp.reshape(bsz, N_HEADS, n_blk, MOBA_BLOCK, HEAD_DIM)
    vb = vp.reshape(bsz, N_HEADS, n_blk, MOBA_BLOCK, HEAD_DIM)
    k_mean = jnp.mean(kb.astype(f32), axis=3)
    gate = jnp.einsum("bhqd,bhnd->bhqn", q.astype(f32), k_mean)
    q_blk = jnp.arange(L) // MOBA_BLOCK
    fully_past = jnp.arange(n_blk)[None, :] < q_blk[:, None]
    gate = jnp.where(fully_past, gate, -jnp.inf)
    n_sel = min(MOBA_TOPK, n_blk)
    g_val, g_idx = lax.top_k(gate, n_sel)
    g_ok = jnp.isfinite(g_val)
    scale = HEAD_DIM ** -0.5
    b_ix = jnp.arange(bsz)[:, None, None, None]
    h_ix = jnp.arange(N_HEADS)[None, :, None, None]

    def chunk(c):
        start = c * Q_CHUNK
        qc = lax.dynamic_slice_in_dim(q, start, Q_CHUNK, axis=2)
        idx = lax.dynamic_slice_in_dim(g_idx, start, Q_CHUNK, axis=2)
        ok = lax.dynamic_slice_in_dim(g_ok, start, Q_CHUNK, axis=2)
        k_sel = kb[b_ix, h_ix, idx]
        v_sel = vb[b_ix, h_ix, idx]
        s_sel = jnp.einsum("bhqd,bhqskd->bhqsk", qc, k_sel).astype(f32) * scale
        s_sel = jnp.where(ok[..., None], s_sel, -jnp.inf)
        s_sel = s_sel.reshape(bsz, N_HEADS, Q_CHUNK, n_sel * MOBA_BLOCK)
        own_start = (start // MOBA_BLOCK) * MOBA_BLOCK
        k_own = lax.dynamic_slice_in_dim(kp, own_start, MOBA_BLOCK, axis=2)
        v_own = lax.dynamic_slice_in_dim(vp, own_start, MOBA_BLOCK, axis=2)
        s_own = jnp.einsum("bhqd,bhkd->bhqk", qc, k_own).astype(f32) * scale
        causal = (own_start + jnp.arange(MOBA_BLOCK))[None, :] <= (start + jnp.arange(Q_CHUNK))[:, None]
        s_own = jnp.where(causal, s_own, -jnp.inf)
        p = jax.nn.softmax(jnp.concatenate([s_sel, s_own], axis=-1), axis=-1).astype(v.dtype)
        p_sel = p[..., :n_sel * MOBA_BLOCK].reshape(bsz, N_HEADS, Q_CHUNK, n_sel, MOBA_BLOCK)
        p_own = p[..., n_sel * MOBA_BLOCK:]
        return (jnp.einsum("bhqsk,bhqskd->bhqd", p_sel, v_sel)
                + jnp.einsum("bhqk,bhkd->bhqd", p_own, v_own))

    out = lax.map(chunk, jnp.arange(L // Q_CHUNK))
    out = out.transpose(1, 0, 3, 2, 4).reshape(bsz, L, N_HEADS * HEAD_DIM)
    return out @ w_o


def setup_inputs(seed: int = 0) -> dict:
    key = jax.random.key(seed)
    ks = jax.random.split(key, 26)
    f32 = jnp.float32

    def nrm(k, shape, scale):
        return jax.random.normal(k, shape, f32) * scale

    x = nrm(ks[0], (BATCH, SEQ, D_MODEL), 1.0)
    ffn_norm = 1.0 + nrm(ks[1], (DEPTH, 2, D_MODEL), 0.02)
    ffn_w1 = nrm(ks[2], (DEPTH, 2, D_MODEL, D_FF), D_MODEL ** -0.5)
    ffn_w3 = nrm(ks[3], (DEPTH, 2, D_MODEL, D_FF), D_MODEL ** -0.5)
    ffn_w2 = nrm(ks[4], (DEPTH, 2, D_FF, D_MODEL), D_FF ** -0.5)
    mix_norm = 1.0 + nrm(ks[5], (DEPTH, D_MODEL), 0.02)
    ab_w_in = nrm(ks[6], (N_EVEN, D_MODEL, IN_WIDTH), D_MODEL ** -0.5)
    conv_w = nrm(ks[7], (N_EVEN, CONV_TAPS, CONV_CH), CONV_TAPS ** -0.5)
    conv_b = nrm(ks[8], (N_EVEN, CONV_CH), 0.02)
    conv_ln_g = 1.0 + nrm(ks[9], (N_EVEN, CONV_CH), 0.02)
    conv_ln_b = nrm(ks[10], (N_EVEN, CONV_CH), 0.02)
    n_idx = jnp.arange(SSM_STATE, dtype=f32)
    ssm_a_re = -0.5 + nrm(ks[11], (N_EVEN, SSM_GROUPS, SSM_STATE), 0.01)
    ssm_a_im = math.pi * n_idx + nrm(ks[12], (N_EVEN, SSM_GROUPS, SSM_STATE), 0.01)
    ssm_b_re = nrm(ks[13], (N_EVEN, SSM_GROUPS, SSM_STATE, SSM_GROUP), (2 * SSM_GROUP) ** -0.5)
    ssm_b_im = nrm(ks[14], (N_EVEN, SSM_GROUPS, SSM_STATE, SSM_GROUP), (2 * SSM_GROUP) ** -0.5)
    ssm_c_re = nrm(ks[15], (N_EVEN, SSM_GROUPS, SSM_GROUP, SSM_STATE), (2 * SSM_STATE) ** -0.5)
    ssm_c_im = nrm(ks[16], (N_EVEN, SSM_GROUPS, SSM_GROUP, SSM_STATE), (2 * SSM_STATE) ** -0.5)
    ssm_d = nrm(ks[17], (N_EVEN, SSM_WIDTH), 1.0)
    ssm_log_dt = jax.random.uniform(ks[18], (N_EVEN, SSM_GROUPS), f32,
                                    minval=math.log(DT_MIN), maxval=math.log(DT_MAX))
    ssm_glu_w = nrm(ks[19], (N_EVEN, SSM_WIDTH, SSM_WIDTH), SSM_WIDTH ** -0.5)
    ssm_glu_b = nrm(ks[20], (N_EVEN, SSM_WIDTH), 0.02)
    ab_w_out = nrm(ks[21], (N_EVEN, MIX_WIDTH, D_MODEL), MIX_WIDTH ** -0.5)
    attn_w_qkv = nrm(ks[22], (N_ODD, D_MODEL, 3 * N_HEADS * HEAD_DIM), D_MODEL ** -0.5)
    attn_w_o = nrm(ks[23], (N_ODD, N_HEADS * HEAD_DIM, D_MODEL), (N_HEADS * HEAD_DIM) ** -0.5)
    final_norm = 1.0 + nrm(ks[24], (D_MODEL,), 0.02)
    return {"x": x, "ffn_norm": ffn_norm, "ffn_w1": ffn_w1, "ffn_w3": ffn_w3, "ffn_w2": ffn_w2,
            "mix_norm": mix_norm, "ab_w_in": ab_w_in, "conv_w": conv_w, "conv_b": conv_b,
            "conv_ln_g": conv_ln_g, "conv_ln_b": conv_ln_b, "ssm_a_re": ssm_a_re,
            "ssm_a_im": ssm_a_im, "ssm_b_re": ssm_b_re, "ssm_b_im": ssm_b_im,
            "ssm_c_re": ssm_c_re, "ssm_c_im": ssm_c_im, "ssm_d": ssm_d,
            "ssm_log_dt": ssm_log_dt, "ssm_glu_w": ssm_glu_w, "ssm_glu_b": ssm_glu_b,
            "ab_w_out": ab_w_out, "attn_w_qkv": attn_w_qkv, "attn_w_o": attn_w_o,
            "final_norm": final_norm}


def reference(x, ffn_norm, ffn_w1, ffn_w3, ffn_w2, mix_norm, ab_w_in, conv_w, conv_b,
              conv_ln_g, conv_ln_b, ssm_a_re, ssm_a_im, ssm_b_re, ssm_b_im, ssm_c_re,
              ssm_c_im, ssm_d, ssm_log_dt, ssm_glu_w, ssm_glu_b, ab_w_out, attn_w_qkv,
              attn_w_o, final_norm):
    for l in range(DEPTH):
        h = rms_norm(x, ffn_norm[l, 0])
        x = x + 0.5 * swiglu(h, ffn_w1[l, 0], ffn_w3[l, 0], ffn_w2[l, 0])
        h = rms_norm(x, mix_norm[l])
        if l % 2 == 0:
            e = l // 2
            x = x + conv_ssm_mixer(h, ab_w_in[e], conv_w[e], conv_b[e], conv_ln_g[e], conv_ln_b[e],
                                   ssm_a_re[e], ssm_a_im[e], ssm_b_re[e], ssm_b_im[e],
                                   ssm_c_re[e], ssm_c_im[e], ssm_d[e], ssm_log_dt[e],
                                   ssm_glu_w[e], ssm_glu_b[e], ab_w_out[e])
        else:
            o = l // 2
            x = x + moba_attention(h, attn_w_qkv[o], attn_w_o[o])
        h = rms_norm(x, ffn_norm[l, 1])
        x = x + 0.5 * swiglu(h, ffn_w1[l, 1], ffn_w3[l, 1], ffn_w2[l, 1])
    return rms_norm(x, final_norm)
```

```python
import numpy as np
import concourse.bass as bass
import concourse.mybir as mybir
from concourse.bass_utils import run_bass_kernel_spmd

F32 = mybir.dt.float32
BF16 = mybir.dt.bfloat16
AF = mybir.ActivationFunctionType
ALU = mybir.AluOpType
AX = mybir.AxisListType

D = 1024
KC = 8
FF = 2816
FC = 22
L = 2048
NSEQ = 2
TT = 512
RMS_EPS = 1e-6
LN_EPS = 1e-5
NCORES = 8


class Buf:
    __slots__ = ("name", "w", "r", "dsem", "dcnt")

    def __init__(self, name):
        self.name = name
        self.w = None
        self.r = {}
        self.dsem = {}
        self.dcnt = {}


class Sched:
    def __init__(self, nc):
        self.nc = nc
        self.eng = {"pe": nc.tensor, "act": nc.scalar, "dve": nc.vector, "pool": nc.gpsimd, "sp": nc.sync}
        self.sem = {k: nc.alloc_semaphore("s_" + k) for k in self.eng}
        self.cnt = {k: 0 for k in self.eng}
        self.seen = {k: {} for k in self.eng}
        self.pr = {k: [] for k in self.eng}
        self.pw = {k: [] for k in self.eng}
        self.nds = 0
        self.ninst = 0
        self.dbufs = []

    def _deps(self, reads, writes):
        deps = {}

        def need(s, v):
            if deps.get(s, 0) < v:
                deps[s] = v

        for b in reads:
            if b.w is not None:
                need(*b.w)
        for b in writes:
            if b.w is not None:
                need(*b.w)
            for s, v in b.r.items():
                need(s, v)
        return deps

    def _wait(self, eng, deps):
        e = self.eng[eng]
        own = self.sem[eng]
        for s, v in deps.items():
            if s is own and eng == "pe":
                continue
            if self.seen[eng].get(s, 0) >= v:
                continue
            e.wait_ge(s, v)
            self.seen[eng][s] = v

    def op(self, eng, fn, reads=(), writes=(), signal=True):
        self._wait(eng, self._deps(reads, writes))
        ins = fn(self.eng[eng])
        self.ninst += 1
        self.pr[eng].extend(reads)
        self.pw[eng].extend(writes)
        if signal:
            own = self.sem[eng]
            self.cnt[eng] += 1
            ins.then_inc(own, 1)
            v = self.cnt[eng]
            for b in self.pr[eng]:
                b.r[own] = v
            for b in self.pw[eng]:
                b.w = (own, v)
                b.r = {}
            self.pr[eng] = []
            self.pw[eng] = []
        return ins

    def dma(self, q, out, in_, reads=(), writes=()):
        self._wait(q, self._deps(reads, writes))
        owner = writes[0] if writes else reads[0]
        kind = "sw" if q == "pool" else "hw"
        if kind not in owner.dsem:
            owner.dsem[kind] = self.nc.alloc_semaphore("d%d" % self.nds)
            owner.dcnt[kind] = 0
            self.nds += 1
            self.dbufs.append((owner, kind))
        owner.dcnt[kind] += 1
        sem = owner.dsem[kind]
        self.eng[q].dma_start(out=out, in_=in_).then_inc(sem, 16)
        self.ninst += 1
        tag = (sem, 16 * owner.dcnt[kind])
        for b in reads:
            b.r[tag[0]] = tag[1]
        for b in writes:
            b.w = tag
            b.r = {}

    def barrier(self):
        for k in self.eng:
            assert not self.pr[k] and not self.pw[k], "unsignalled ops pending on " + k
        for k in self.eng:
            deps = {self.sem[o]: self.cnt[o] for o in self.eng if o != k and self.cnt[o] > 0}
            for b, kind in self.dbufs:
                deps[b.dsem[kind]] = 16 * b.dcnt[kind]
            self._wait(k, deps)

    def wait_all(self, eng, bufs):
        deps = {}
        for b in bufs:
            if b.w is not None:
                deps[b.w[0]] = max(deps.get(b.w[0], 0), b.w[1])
            for s, v in b.r.items():
                deps[s] = max(deps.get(s, 0), v)
        self._wait(eng, deps)


class Arena:
    def __init__(self, nc, nbytes):
        self.t = nc.alloc_sbuf_tensor("arena", [128, nbytes // 2], BF16)
        self.nbytes = nbytes
        self.off = 0
        self.marks = []

    def alloc(self, shape, dtype, at=None):
        esz = 2 if dtype == BF16 else 4
        n = int(np.prod(shape[1:]))
        nb = n * esz
        off = self.off if at is None else at
        off = (off + 31) // 32 * 32
        assert off + nb <= self.nbytes, ("arena overflow", off, nb, self.nbytes)
        if at is None:
            self.off = off + nb
        ap = self.t[0:shape[0], off // 2:(off + nb) // 2]
        if dtype != BF16:
            ap = ap.bitcast(dtype)
        if len(shape) == 3:
            ap = ap.rearrange("p (a b) -> p a b", a=shape[1])
        elif len(shape) == 4:
            ap = ap.rearrange("p (a b c) -> p a b c", a=shape[1], b=shape[2])
        return ap


def build_program(stages=("ffn00", "mix0", "ffn01", "ffn10", "mix1", "ffn11", "final"), nseq=NSEQ, debug=False):
    nc = bass.Bass("TRN2", target_bir_lowering=False)
    S = Sched(nc)
    dbg_names = []

    def dump(name, ap2d, bufs):
        if not debug or name in dbg_names:
            return
        dbg_names.append(name)
        shp = list(ap2d.shape)
        dt = nc.dram_tensor("dbg_" + name, shp, F32, kind="ExternalOutput").ap()
        S.dma("pool", dt, ap2d, reads=bufs)

    def din(name, shape, dt=F32):
        return nc.dram_tensor(name, list(shape), dt, kind="ExternalInput").ap()

    xT_d = din("xT", [NSEQ, D, L])
    gains_d = din("gains", [128, 7 * KC])
    w1_d = din("w1", [4, D, FF])
    w3_d = din("w3", [4, D, FF])
    w2_d = din("w2", [4, FF, D])
    outT_d = nc.dram_tensor("outT", [NSEQ, D, L], F32, kind="ExternalOutput").ap()

    A = Arena(nc, 207 * 1024)
    xT = A.alloc([128, KC, L], F32)
    gains = A.alloc([128, 7, KC], F32)
    ones_bf = A.alloc([128, 128], BF16)
    epsc = A.alloc([128, 1], F32)
    cw = A.alloc([128, 4, 31], F32)
    chv = A.alloc([128, 5, 4], F32)
    lnepsc = A.alloc([128, 1], F32)
    xsq = [A.alloc([128, TT], BF16) for _ in range(2)]
    rstd = A.alloc([128, TT], F32)
    sil = [A.alloc([128, TT], BF16) for _ in range(2)]
    base_off = A.off
    hT = A.alloc([128, KC, 1024], BF16)
    G = A.alloc([128, FC, 1024], BF16)
    NS13 = 3
    w13 = [A.alloc([128, 2, KC, 256], BF16) for _ in range(NS13)]
    w2s = A.alloc([128, FC, D], BF16)
    ffn_end = A.off

    PS = [nc.alloc_psum_tensor("ps%d" % i, [128, TT], F32) for i in range(8)]
    PSB = [Buf("ps%d" % i) for i in range(8)]

    XB = [[Buf("x%d_%d" % (k, b)) for b in range(L // TT)] for k in range(KC)]
    HB = [[Buf("h%d_%d" % (k, t)) for t in range(2)] for k in range(KC)]
    GB = [[Buf("g%d_%d" % (f, t)) for t in range(2)] for f in range(FC)]
    W13B = [(Buf("w1_%d" % i), Buf("w3_%d" % i)) for i in range(NS13)]
    W2B = [Buf("w2s%d" % i) for i in range(FC // 2)]
    XSQB = [Buf("xsq%d" % i) for i in range(2)]
    RSTDB = Buf("rstd")
    SILB = [Buf("sil%d" % i) for i in range(2)]
    CONSTB = Buf("const")

    S.dma("sp", gains.rearrange("p a b -> p (a b)"), gains_d, writes=[CONSTB])
    S.op("dve", lambda e: e.memset(ones_bf, 1.0), writes=[CONSTB])
    S.op("dve", lambda e: e.memset(epsc, RMS_EPS), writes=[CONSTB])

    w13_state = {"n": 0}

    def load_w13(fi, fg):
        i = w13_state["n"] % NS13
        w13_state["n"] += 1
        slot = w13[i]
        b = W13B[i]
        src1 = w1_d[fi].rearrange("(kc p) f -> p kc f", p=128)[:, :, fg * 256:(fg + 1) * 256]
        src3 = w3_d[fi].rearrange("(kc p) f -> p kc f", p=128)[:, :, fg * 256:(fg + 1) * 256]
        S.dma("pool", slot[:, 0], src1, writes=[b[0]])
        S.dma("pool", slot[:, 1], src3, writes=[b[1]])
        return slot, b

    def load_w2(fi, j):
        src = w2_d[fi].rearrange("(fc p) d -> p fc d", p=128)
        S.dma("pool", w2s[:, 2 * j:2 * j + 2], src[:, 2 * j:2 * j + 2], writes=[W2B[j]])

    def rmsnorm_tile(gidx, t0, ntt, dstT, dstB, inplace=False):
        for tt in range(ntt):
            blk = (t0 + tt * TT) // TT
            ps = PS[6]
            psb = PSB[6]
            for kc in range(KC):
                q = dstT[:, kc, tt * TT:(tt + 1) * TT]
                xs = xT[:, kc, blk * TT:(blk + 1) * TT]
                S.op("act", lambda e, q=q, xs=xs: e.activation(out=q, in_=xs, func=AF.Square),
                     reads=[XB[kc][blk], CONSTB], writes=[dstB[kc][tt]])
            for kc in range(KC):
                q = dstT[:, kc, tt * TT:(tt + 1) * TT]
                S.op("pe", lambda e, q=q, kc=kc: e.matmul(ps[:], ones_bf, q, start=(kc == 0), stop=(kc == KC - 1)),
                     reads=[dstB[kc][tt], CONSTB], writes=[psb], signal=(kc == KC - 1))
            S.op("act", lambda e: e.activation(out=rstd, in_=ps[:], func=AF.Sqrt, scale=1.0 / D, bias=epsc),
                 reads=[psb, CONSTB], writes=[RSTDB])
            S.op("dve", lambda e: e.reciprocal(out=rstd, in_=rstd), reads=[RSTDB], writes=[RSTDB])
            for kc in range(KC):
                xs = xT[:, kc, blk * TT:(blk + 1) * TT]
                if inplace:
                    S.op("dve", lambda e, kc=kc, xs=xs: e.scalar_tensor_tensor(
                        out=xs, in0=xs, scalar=gains[:, gidx, kc:kc + 1], in1=rstd, op0=ALU.mult, op1=ALU.mult),
                        reads=[XB[kc][blk], RSTDB, CONSTB, dstB[kc][tt]], writes=[XB[kc][blk]])
                else:
                    S.op("dve", lambda e, kc=kc, xs=xs: e.scalar_tensor_tensor(
                        out=dstT[:, kc, tt * TT:(tt + 1) * TT], in0=xs,
                        scalar=gains[:, gidx, kc:kc + 1], in1=rstd, op0=ALU.mult, op1=ALU.mult),
                        reads=[XB[kc][blk], RSTDB, CONSTB], writes=[dstB[kc][tt]])

    def ffn_tile(fi, gidx, t0, do_pre=True, hook=None):
        if do_pre:
            rmsnorm_tile(gidx, t0, 2, hT, HB)
        pcount = 0
        slots = {}
        for fg in range(min(NS13, FC // 2)):
            slots[fg] = load_w13(fi, fg)
        for fg in range(FC // 2):
            slot, sb = slots.pop(fg)
            for fh in range(2):
                fc = fg * 2 + fh
                for tt in range(2):
                    pa = (pcount % 2) * 2
                    pcount += 1
                    for j, (pi, wi) in enumerate(((pa, 0), (pa + 1, 1))):
                        for kc in range(KC):
                            S.op("pe", lambda e, pi=pi, wi=wi, kc=kc: e.matmul(
                                PS[pi][:], slot[:, wi, kc, fh * 128:(fh + 1) * 128], hT[:, kc, tt * TT:(tt + 1) * TT],
                                start=(kc == 0), stop=(kc == KC - 1)),
                                reads=[sb[wi], HB[kc][tt]], writes=[PSB[pi]], signal=(kc == KC - 1))
                    sl = sil[pcount % 2]
                    slb = SILB[pcount % 2]
                    S.op("act", lambda e, sl=sl, pa=pa: e.activation(out=sl, in_=PS[pa][:], func=AF.Silu),
                         reads=[PSB[pa]], writes=[slb])
                    S.op("dve", lambda e, sl=sl, pa=pa, fc=fc, tt=tt: e.tensor_tensor(
                        out=G[:, fc, tt * TT:(tt + 1) * TT], in0=sl, in1=PS[pa + 1][:], op=ALU.mult),
                        reads=[slb, PSB[pa + 1]], writes=[GB[fc][tt]])
            if fg + NS13 < FC // 2:
                slots[fg + NS13] = load_w13(fi, fg + NS13)
            load_w2(fi, fg)
        pcount = 0
        for dc in range(KC):
            for tt in range(2):
                pi = 4 + (pcount % 2)
                pcount += 1
                blk = (t0 + tt * TT) // TT
                for fc in range(FC):
                    S.op("pe", lambda e, pi=pi, fc=fc, dc=dc, tt=tt: e.matmul(
                        PS[pi][:], w2s[:, fc, dc * 128:(dc + 1) * 128], G[:, fc, tt * TT:(tt + 1) * TT],
                        start=(fc == 0), stop=(fc == FC - 1)),
                        reads=[W2B[fc // 2], GB[fc][tt]], writes=[PSB[pi]], signal=(fc == FC - 1))
                xs = xT[:, dc, blk * TT:(blk + 1) * TT]
                S.op("dve", lambda e, pi=pi, xs=xs: e.scalar_tensor_tensor(
                    out=xs, in0=PS[pi][:], scalar=0.5, in1=xs, op0=ALU.mult, op1=ALU.add),
                    reads=[PSB[pi], XB[dc][blk]], writes=[XB[dc][blk]])
            if dc == 1 and hook is not None:
                hook()

    def ffn_chain(jobs):
        for j, (fi, gidx, t0) in enumerate(jobs):
            nxt = jobs[j + 1] if j + 1 < len(jobs) else None
            hook = None
            if nxt is not None:
                assert nxt[2] != t0
                hook = (lambda nxt=nxt: rmsnorm_tile(nxt[1], nxt[2], 2, hT, HB))
            ffn_tile(fi, gidx, t0, do_pre=(j == 0), hook=hook)

    RB = base_off
    KB = 1024

    def R(off_kb, shape, dt):
        return A.alloc(shape, dt, at=RB + int(off_kb * KB))

    I32 = mybir.dt.int32
    win_d = din("w_in", [D, 1536])
    wout_d = din("w_out", [D, D])
    gluw_d = din("glu_w", [512, 512])
    cw_d = din("cw", [128, 4 * 31])
    chv_d = din("chv", [128, 5 * 4])
    sp1_d = din("sp1", [64, 3 * 32])
    sp2_d = din("sp2", [64, 4 * 512])
    ident_d = din("ident", [128, 128])
    bmask_d = din("bmask", [128, 128])
    selin_d = din("selin", [128, 8 * 240])
    selout_d = din("selout", [128, 8 * 240])
    scr_toep = nc.dram_tensor("scr_toep", [128, 32 * 128], BF16, kind="Internal").ap()
    scr_wb = nc.dram_tensor("scr_wb", [128, 2 * 32 * 64], BF16, kind="Internal").ap()
    scr_cm = nc.dram_tensor("scr_cm", [64, 2 * 32 * 128], BF16, kind="Internal").ap()
    scr_at = nc.dram_tensor("scr_at", [64, 128], F32, kind="Internal").ap()

    M0CB = Buf("m0const")
    S.dma("sp", cw.rearrange("p a b -> p (a b)"), cw_d, writes=[M0CB])
    S.dma("sp", chv.rearrange("p a b -> p (a b)"), chv_d, writes=[M0CB])
    S.op("dve", lambda e: e.memset(lnepsc, LN_EPS), writes=[M0CB])

    uT = R(0, [128, 4, L], BF16)
    uT4 = R(0, [128, 4, 8, 256], BF16)
    mcat = R(16, [128, 8, L], BF16)
    Xbf = R(32, [128, 2, 16, 256], BF16)
    woutS = R(48, [128, 8, D], BF16)
    hT2 = R(64, [128, KC, L], BF16)
    winS = R(96, [128, KC, 1536], BF16)
    vpad = R(47.5, [128, 4, 30 + L], BF16)
    ycv = R(64, [128, 4, L], F32)
    lnt = [R(96 + 2 * i, [128, TT], F32) for i in range(5)]
    selin = R(64, [128, 8, 240], BF16)
    selout = R(68, [128, 8, 240], BF16)
    gluwS = R(72, [128, 4, 512], BF16)
    toepS = R(76, [128, 32, 128], BF16)
    wbS = R(84, [128, 2, 32, 64], BF16)
    cmS = R(92, [128, 2, 16, 128], BF16)
    atS = R(108, [128, 2, 2, 16], F32)
    Xh = R(109, [128, 32, 2, 16], F32)
    pq = R(113, [128, 2, 2, 16], F32)
    rtmp = R(114, [128, 2, 16], F32)
    Ut = R(116, [128, 32, 256], BF16)
    gt = [R(76 + 2 * i, [128, TT], F32) for i in range(6)]
    y1bS = R(92, [128, 4, L], BF16)
    sgt = [R(88 + i, [128, TT], BF16) for i in range(2)]

    diag = [R(o_, [128, 31, 128], BF16) for o_ in (32, 39.75, 114, 121.75)]
    identM = R(135.25, [128, 128], BF16)
    DIAGB = [Buf("diag%d" % i) for i in range(4)]
    IDMB = Buf("identM")
    UB = [[Buf("u%d_%d" % (c, t)) for t in range(4)] for c in range(4)]
    MCB = [[Buf("mc%d_%d" % (c, t)) for t in range(4)] for c in range(8)]
    H2B = [[Buf("h2_%d_%d" % (k, t)) for t in range(4)] for k in range(KC)]
    WINB = [Buf("win%d" % i) for i in range(3)]
    WOUTB = Buf("woutS")
    VB = [Buf("v%d" % c) for c in range(4)]
    YCB = [Buf("yc%d" % c) for c in range(4)]
    LNTB = [Buf("lnt%d" % i) for i in range(5)]
    PPB = Buf("ssmparams")
    SELIB = Buf("selin"); WBB = Buf("wbS"); TOEPB = Buf("toepS")
    CMB = [Buf("cm%d" % i) for i in range(4)]
    ATB = [Buf("at%d" % i) for i in range(8)]
    UTB = [Buf("ut%d" % g) for g in range(32)]
    XBFB = Buf("xbf")
    XHB = [Buf("xh%d" % i) for i in range(32)]
    PQB = Buf("pq")
    RTB = Buf("rtmp")
    GTB = [Buf("gt%d" % i) for i in range(6)]
    Y1B = [[Buf("y1_%d_%d" % (c, t)) for t in range(4)] for c in range(4)]
    SGTB = [Buf("sgt%d" % i) for i in range(2)]

    def ssm_param_prep():
        S.barrier()
        P = PPB
        cnt = {"o": 0}

        def T(shape, dt=F32):
            n = int(np.prod(shape[1:])) * (4 if dt != BF16 else 2)
            off = cnt["o"]
            cnt["o"] = (off + n + 31) // 32 * 32
            assert RB + cnt["o"] <= A.nbytes, cnt["o"]
            return A.alloc(shape, dt, at=RB + off)

        def v_tt(out, a, b, op):
            S.op("dve", lambda e: e.tensor_tensor(out=out, in0=a, in1=b, op=op), reads=[P], writes=[P])

        def v_ts(out, a, s1, s2, op0, op1=None):
            if op1 is None:
                S.op("dve", lambda e: e.tensor_scalar(out=out, in0=a, scalar1=s1, scalar2=None, op0=op0), reads=[P], writes=[P])
            else:
                S.op("dve", lambda e: e.tensor_scalar(out=out, in0=a, scalar1=s1, scalar2=s2, op0=op0, op1=op1), reads=[P], writes=[P])

        def a_act(out, a, func, scale=1.0):
            S.op("act", lambda e: e.activation(out=out, in_=a, func=func, scale=scale), reads=[P], writes=[P])

        def v_cp(out, a):
            S.op("dve", lambda e: e.tensor_copy(out=out, in_=a), reads=[P], writes=[P])

        sp1 = T([64, 3, 32]); sp2 = T([64, 4, 512])
        S.dma("act", sp1.rearrange("p a b -> p (a b)"), sp1_d, writes=[P])
        S.dma("act", sp2.rearrange("p a b -> p (a b)"), sp2_d, writes=[P])
        identS = T([128, 128], BF16); bmaskS = T([128, 128])
        S.dma("pool", identS, ident_d, writes=[P])
        S.dma("act", bmaskS, bmask_d, writes=[P])
        ldt, are, aim = sp1[:, 0], sp1[:, 1], sp1[:, 2]
        bre = sp2[:, 0].rearrange("p (g h) -> p g h", g=32)
        bim = sp2[:, 1].rearrange("p (g h) -> p g h", g=32)
        cre = sp2[:, 2].rearrange("p (g h) -> p g h", g=32)
        cim = sp2[:, 3].rearrange("p (g h) -> p g h", g=32)
        dt_ = T([64, 32]); mag = T([64, 32]); ang = T([64, 32]); t1 = T([64, 32]); t2 = T([64, 32])
        ti = T([64, 32], I32); cosv = T([64, 32]); sinv = T([64, 32])
        a_act(dt_, ldt, AF.Exp)
        v_tt(t1, dt_, are, ALU.mult)
        a_act(mag, t1, AF.Exp)
        v_tt(ang, dt_, aim, ALU.mult)
        TWO_PI = 2.0 * np.pi

        def sin_of(out, shift):
            v_ts(t1, ang, 1.0 / TWO_PI, float(shift), ALU.mult, ALU.add)
            v_cp(ti, t1)
            v_cp(t2, ti)
            v_tt(t1, t1, t2, ALU.subtract)
            v_ts(t2, t1, 0.5, None, ALU.is_gt)
            v_tt(t1, t1, t2, ALU.subtract)
            v_ts(t2, t1, -0.5, None, ALU.is_lt)
            v_tt(t1, t1, t2, ALU.add)
            a_act(out, t1, AF.Sin, scale=TWO_PI)

        dump("dt", dt_, [P]); dump("mag", mag, [P]); dump("ang", ang, [P])
        sin_of(sinv, 0.0)
        dump("frac_s", t1, [P]); dump("sinv", sinv, [P])
        sin_of(cosv, 0.25)
        dump("cosv", cosv, [P])
        pwr = T([64, 9, 32]); pwi = T([64, 9, 32])
        S.op("dve", lambda e: e.memset(pwr[:, 0], 1.0), reads=[P], writes=[P])
        S.op("dve", lambda e: e.memset(pwi[:, 0], 0.0), reads=[P], writes=[P])
        v_tt(pwr[:, 1], mag, cosv, ALU.mult)
        v_tt(pwi[:, 1], mag, sinv, ALU.mult)
        for k in range(1, 8):
            v_tt(t1, pwr[:, k], pwr[:, 1], ALU.mult)
            v_tt(t2, pwi[:, k], pwi[:, 1], ALU.mult)
            v_tt(pwr[:, k + 1], t1, t2, ALU.subtract)
            v_tt(t1, pwr[:, k], pwi[:, 1], ALU.mult)
            v_tt(t2, pwi[:, k], pwr[:, 1], ALU.mult)
            v_tt(pwi[:, k + 1], t1, t2, ALU.add)
        dump("pwr", pwr.rearrange("p k g -> p (k g)"), [P]); dump("pwi", pwi.rearrange("p k g -> p (k g)"), [P])
        ipr = T([64, 9, 32]); ipi = T([64, 9, 32]); n2 = T([64, 9, 32]); n3 = T([64, 9, 32])
        v_tt(n2, pwr, pwr, ALU.mult)
        v_tt(n3, pwi, pwi, ALU.mult)
        v_tt(n2, n2, n3, ALU.add)
        S.op("dve", lambda e: e.reciprocal(out=n2, in_=n2), reads=[P], writes=[P])
        v_tt(ipr, pwr, n2, ALU.mult)
        v_tt(ipi, pwi, n2, ALU.mult)
        v_ts(ipi, ipi, -1.0, None, ALU.mult)
        den = T([64, 32]); nr = T([64, 32]); qre = T([64, 32]); qim = T([64, 32])
        v_tt(den, are, are, ALU.mult)
        v_tt(t1, aim, aim, ALU.mult)
        v_tt(den, den, t1, ALU.add)
        S.op("dve", lambda e: e.reciprocal(out=den, in_=den), reads=[P], writes=[P])
        v_ts(nr, pwr[:, 1], -1.0, None, ALU.add)
        v_tt(t1, nr, are, ALU.mult)
        v_tt(t2, pwi[:, 1], aim, ALU.mult)
        v_tt(t1, t1, t2, ALU.add)
        v_tt(qre, t1, den, ALU.mult)
        v_tt(t1, pwi[:, 1], are, ALU.mult)
        v_tt(t2, nr, aim, ALU.mult)
        v_tt(t1, t1, t2, ALU.subtract)
        v_tt(qim, t1, den, ALU.mult)
        Bre = T([64, 32, 16]); Bim = T([64, 32, 16]); w1_ = T([64, 32, 16]); w2_ = T([64, 32, 16])
        qre_b = qre.unsqueeze(2).to_broadcast([64, 32, 16])
        qim_b = qim.unsqueeze(2).to_broadcast([64, 32, 16])
        v_tt(w1_, bre, qre_b, ALU.mult); v_tt(w2_, bim, qim_b, ALU.mult); v_tt(Bre, w1_, w2_, ALU.subtract)
        v_tt(w1_, bim, qre_b, ALU.mult); v_tt(w2_, bre, qim_b, ALU.mult); v_tt(Bim, w1_, w2_, ALU.add)
        big_off = cnt["o"]
        big1 = T([64, 32, 8, 16]); big2 = T([64, 32, 8, 16])
        Cmr = T([64, 32, 8, 16]); Cmi = T([64, 32, 8, 16])
        q0 = cnt["o"]
        Bsr = T([64, 32, 8, 16], BF16); Bsi = T([64, 32, 8, 16], BF16)
        cmr_bf = T([64, 32, 8, 16], BF16); cmi_bf = T([64, 32, 8, 16], BF16)
        Bmr = A.alloc([64, 32, 8, 16], F32, at=RB + q0); Bmi = A.alloc([64, 32, 8, 16], F32, at=RB + q0 + 16 * KB)

        def bc_x(x):
            return x.unsqueeze(2).to_broadcast([64, 32, 8, 16])

        def bc_p(p, lo, hi, rev=False):
            sl = p[:, lo:hi, :].rearrange("p k g -> p g k")
            return sl.unsqueeze(3).to_broadcast([64, 32, 8, 16])

        def cmul_big(o_re, o_im, xr, xi, pr, pi, neg_im=False):
            v_tt(big1, bc_x(xr), pr, ALU.mult); v_tt(big2, bc_x(xi), pi, ALU.mult)
            v_tt(o_re, big1, big2, ALU.subtract)
            v_tt(big1, bc_x(xr), pi, ALU.mult); v_tt(big2, bc_x(xi), pr, ALU.mult)
            if neg_im:
                v_tt(o_im, big1, big2, ALU.add)
                v_ts(o_im, o_im, -1.0, None, ALU.mult)
            else:
                v_tt(o_im, big1, big2, ALU.add)

        cmul_big(Cmr, Cmi, cre, cim, bc_p(pwr, 1, 9), bc_p(pwi, 1, 9), neg_im=True)
        v_cp(cmr_bf, Cmr); v_cp(cmi_bf, Cmi)
        S.dma("sp", scr_cm[:, 0:4096], cmr_bf.rearrange("p g j h -> p (g j h)"), reads=[P])
        S.dma("sp", scr_cm[:, 4096:8192], cmi_bf.rearrange("p g j h -> p (g j h)"), reads=[P])
        pwr_rev = T([64, 8, 32]); pwi_rev = T([64, 8, 32])
        for i in range(8):
            v_cp(pwr_rev[:, i], pwr[:, 7 - i]); v_cp(pwi_rev[:, i], pwi[:, 7 - i])
        prr = pwr_rev.rearrange("p k g -> p g k").unsqueeze(3).to_broadcast([64, 32, 8, 16])
        pri = pwi_rev.rearrange("p k g -> p g k").unsqueeze(3).to_broadcast([64, 32, 8, 16])
        cmul_big(Bsr, Bsi, Bre, Bim, prr, pri)
        wb_bf = T([128, 2, 32, 64], BF16)
        for o, src in ((0, Bsr), (1, Bsi)):
            for g8 in range(4):
                ps = PS[7]
                for gg in range(8):
                    g = g8 * 8 + gg
                    S.op("pe", lambda e, g=g, gg=gg, src=src: e.matmul(ps[:, gg * 64:(gg + 1) * 64], src[:, g].rearrange("p i h -> p (i h)"),
                                                                       identS[0:64, 0:64], start=True, stop=True),
                         reads=[P], writes=[PSB[7]], signal=(gg == 7))
                S.op("dve", lambda e, o=o, g8=g8: e.tensor_copy(out=wb_bf[:, o, g8 * 8:(g8 + 1) * 8, :],
                                                               in_=ps[:].rearrange("p (a b) -> p a b", a=8)), reads=[PSB[7], P], writes=[P])
        S.dma("sp", scr_wb, wb_bf.rearrange("p o g n -> p (o g n)"), reads=[P])
        at_ = T([64, 2, 2, 32])
        v_cp(at_[:, 0, 0], pwr[:, 8]); v_cp(at_[:, 1, 1], pwr[:, 8]); v_cp(at_[:, 1, 0], pwi[:, 8])
        v_ts(at_[:, 0, 1], pwi[:, 8], -1.0, None, ALU.mult)
        S.dma("sp", scr_at, at_.rearrange("p o k g -> p (o k g)"), reads=[P])
        cmul_big(Bmr, Bmi, Bre, Bim, bc_p(ipr, 1, 9), bc_p(ipi, 1, 9))
        toep_bf = A.alloc([128, 32, 128], BF16, at=RB + big_off)
        for g4 in range(8):
            ps = PS[7]
            for gg in range(4):
                g = g4 * 4 + gg
                S.op("pe", lambda e, g=g, gg=gg: e.matmul(ps[:, gg * 128:(gg + 1) * 128], Bmr[:, g].rearrange("p i h -> p (i h)"),
                                                          Cmr[:, g].rearrange("p j h -> p (j h)"), start=True, stop=False),
                     reads=[P], writes=[PSB[7]], signal=False)
                S.op("pe", lambda e, g=g, gg=gg: e.matmul(ps[:, gg * 128:(gg + 1) * 128], Bmi[:, g].rearrange("p i h -> p (i h)"),
                                                          Cmi[:, g].rearrange("p j h -> p (j h)"), start=False, stop=True),
                     reads=[P], writes=[PSB[7]], signal=(gg == 3))
            S.op("dve", lambda e, g4=g4: e.tensor_tensor(
                out=toep_bf[:, g4 * 4:(g4 + 1) * 4, :], in0=ps[:].rearrange("p (a b) -> p a b", a=4),
                in1=bmaskS.unsqueeze(1).to_broadcast([128, 4, 128]), op=ALU.mult), reads=[PSB[7], P], writes=[P])
        S.dma("sp", scr_toep, toep_bf.rearrange("p g m -> p (g m)"), reads=[P])
        S.barrier()

    def mixer0(s):
        S.barrier()
        for i in range(3):
            S.dma("pool", winS[:, :, i * 512:(i + 1) * 512],
                  win_d.rearrange("(kc p) f -> p kc f", p=128)[:, :, i * 512:(i + 1) * 512], writes=[WINB[i]])
        S.dma("pool", identM, ident_d, writes=[IDMB])
        for ct in range(2):
            for k in range(31):
                S.op("dve", lambda e, ct=ct, k=k: e.tensor_scalar(out=diag[ct][:, k, :], in0=identM, scalar1=cw[:, ct, k:k + 1], scalar2=None,
                                                              op0=ALU.mult), reads=[IDMB, M0CB], writes=[DIAGB[ct]])
        rmsnorm_tile(4, 0, 4, hT2, H2B)
        S.op("pool", lambda e: e.memset(vpad[:, :, 0:30], 0.0), writes=VB)
        pc = 0
        for ct in range(4):
            for tb in range(4):
                pa, pg = (pc % 2) * 2, (pc % 2) * 2 + 1
                pc += 1
                for pi, oc in ((pa, ct), (pg, ct + 4)):
                    for kc in range(KC):
                        S.op("pe", lambda e, pi=pi, oc=oc, kc=kc, tb=tb: e.matmul(
                            PS[pi][:], winS[:, kc, oc * 128:(oc + 1) * 128], hT2[:, kc, tb * TT:(tb + 1) * TT],
                            start=(kc == 0), stop=(kc == KC - 1)),
                            reads=[WINB[oc // 4], H2B[kc][tb]], writes=[PSB[pi]], signal=(kc == KC - 1))
                sg = sil[pc % 2]
                S.op("act", lambda e, sg=sg, pg=pg: e.activation(out=sg, in_=PS[pg][:], func=AF.Sigmoid),
                     reads=[PSB[pg]], writes=[SILB[pc % 2]])
                S.op("dve", lambda e, sg=sg, pa=pa, ct=ct, tb=tb: e.tensor_tensor(
                    out=vpad[:, ct, 30 + tb * TT:30 + (tb + 1) * TT], in0=sg, in1=PS[pa][:], op=ALU.mult),
                    reads=[SILB[pc % 2], PSB[pa]], writes=[VB[ct]])
        for ct in range(4):
            for tb in range(4):
                pi = 4 + (pc % 2)
                pc += 1
                oc = 8 + ct
                for kc in range(KC):
                    S.op("pe", lambda e, pi=pi, oc=oc, kc=kc, tb=tb: e.matmul(
                        PS[pi][:], winS[:, kc, oc * 128:(oc + 1) * 128], hT2[:, kc, tb * TT:(tb + 1) * TT],
                        start=(kc == 0), stop=(kc == KC - 1)),
                        reads=[WINB[2], H2B[kc][tb]], writes=[PSB[pi]], signal=(kc == KC - 1))
                S.op("act", lambda e, pi=pi, ct=ct, tb=tb: e.activation(out=uT4[:, ct, :, tb * 64:(tb + 1) * 64],
                                                                       in_=PS[pi][:].rearrange("p (c i) -> p i c", i=8), func=AF.Copy),
                     reads=[PSB[pi]], writes=[UB[ct][tb]])
        S.barrier()
        for ct in range(2, 4):
            for k in range(31):
                S.op("dve", lambda e, ct=ct, k=k: e.tensor_scalar(out=diag[ct][:, k, :], in0=identM, scalar1=cw[:, ct, k:k + 1], scalar2=None,
                                                              op0=ALU.mult), reads=[IDMB, M0CB], writes=[DIAGB[ct]])
        lnb = [R(96 + 2 * i, [128, TT], F32) for i in range(7)]
        ybf = [R(110 + i, [128, TT], BF16) for i in range(2)]
        ysq = [R(112 + i, [128, TT], BF16) for i in range(2)]
        LNB = [Buf("lnb%d" % i) for i in range(7)]
        YBB = [Buf("ybf%d" % i) for i in range(2)]
        YSB = [Buf("ysq%d" % i) for i in range(2)]
        YC2 = [[Buf("yc%d_%d" % (c, t)) for t in range(4)] for c in range(4)]
        tiles = [(tb, ct) for tb in range(4) for ct in range(4)]

        def conv_tile(n):
            tb, ct = tiles[n]
            pi = 2 + (n % 2)
            j = n % 2
            for k in range(31):
                S.op("pe", lambda e, pi=pi, ct=ct, k=k, tb=tb: e.matmul(
                    PS[pi][:], diag[ct][:, k, :], vpad[:, ct, k + tb * TT:k + (tb + 1) * TT], start=(k == 0), stop=(k == 30)),
                    reads=[DIAGB[ct], VB[ct]], writes=[PSB[pi]], signal=(k == 30))
            bia = chv[:, 0, ct:ct + 1]
            S.op("act", lambda e, pi=pi, ct=ct, tb=tb: e.activation(out=ycv[:, ct, tb * TT:(tb + 1) * TT], in_=PS[pi][:], func=AF.Identity, bias=bia),
                 reads=[PSB[pi], M0CB], writes=[YC2[ct][tb]])
            S.op("act", lambda e, pi=pi, j=j: e.activation(out=ybf[j], in_=PS[pi][:], func=AF.Identity, bias=bia),
                 reads=[PSB[pi], M0CB], writes=[YBB[j]])
            S.op("act", lambda e, pi=pi, j=j: e.activation(out=ysq[j], in_=PS[pi][:], func=AF.Square, bias=bia),
                 reads=[PSB[pi], M0CB], writes=[YSB[j]])

        def stat_mm(n):
            tb, ct = tiles[n]
            j = n % 2
            sb_, qb_ = ((0, 1), (4, 5))[tb % 2]
            S.op("pe", lambda e: e.matmul(PS[sb_][:], ones_bf, ybf[j], start=(ct == 0), stop=(ct == 3)),
                 reads=[YBB[j], CONSTB], writes=[PSB[sb_]], signal=True)
            S.op("pe", lambda e: e.matmul(PS[qb_][:], ones_bf, ysq[j], start=(ct == 0), stop=(ct == 3)),
                 reads=[YSB[j], CONSTB], writes=[PSB[qb_]], signal=True)
            if ct == 3:
                ln_finish(tb)

        def ln_finish(tb):
            sb_, qb_ = ((0, 1), (4, 5))[tb % 2]
            mean, var, msq = lnb[2 + tb % 2], lnb[4 + tb % 2], lnb[6]
            MB_, VB_, QB_ = LNB[2 + tb % 2], LNB[4 + tb % 2], LNB[6]
            sl = slice(tb * TT, (tb + 1) * TT)
            S.op("act", lambda e: e.activation(out=mean, in_=PS[sb_][:], func=AF.Copy, scale=1.0 / 512), reads=[PSB[sb_]], writes=[MB_])
            S.op("act", lambda e: e.activation(out=msq, in_=PS[sb_][:], func=AF.Square, scale=1.0 / 512), reads=[PSB[sb_]], writes=[QB_])
            S.op("dve", lambda e: e.scalar_tensor_tensor(out=var, in0=PS[qb_][:], scalar=1.0 / 512, in1=msq, op0=ALU.mult, op1=ALU.subtract),
                 reads=[PSB[qb_], QB_], writes=[VB_])
            S.op("act", lambda e: e.activation(out=var, in_=var, func=AF.Sqrt, bias=lnepsc), reads=[VB_, M0CB], writes=[VB_])
            S.op("dve", lambda e: e.reciprocal(out=var, in_=var), reads=[VB_], writes=[VB_])
            for ct in range(4):
                t_ = lnb[ct % 2]
                TB_ = LNB[ct % 2]
                S.op("dve", lambda e, ct=ct, t_=t_: e.tensor_tensor(out=t_, in0=ycv[:, ct, sl], in1=mean, op=ALU.subtract),
                     reads=[YC2[ct][tb], MB_], writes=[TB_])
                S.op("dve", lambda e, t_=t_: e.tensor_tensor(out=t_, in0=t_, in1=var, op=ALU.mult),
                     reads=[TB_, VB_], writes=[TB_])
                S.op("act", lambda e, ct=ct, t_=t_: e.activation(out=mcat[:, ct, sl], in_=t_, func=AF.Silu,
                                                                scale=chv[:, 1, ct:ct + 1], bias=chv[:, 2, ct:ct + 1]),
                     reads=[TB_, M0CB], writes=[MCB[ct][tb]])

        conv_tile(0)
        for n in range(len(tiles)):
            if n + 1 < len(tiles):
                conv_tile(n + 1)
            stat_mm(n)
        dump("mcA", mcat[:, 0:4, :].rearrange("p a b -> p (a b)"), [b for r in MCB[0:4] for b in r])
        dump("u", uT.rearrange("p a b -> p (a b)"), [b for r in UB for b in r])
        S.barrier()
        S.dma("pool", selin.rearrange("p a b -> p (a b)"), selin_d, writes=[SELIB])
        S.dma("sp", wbS.rearrange("p o g n -> p (o g n)"), scr_wb, writes=[WBB])
        S.dma("sp", toepS.rearrange("p g m -> p (g m)"), scr_toep, writes=[TOEPB])
        S.dma("pool", selout.rearrange("p a b -> p (a b)"), selout_d, writes=[PPB])
        S.dma("pool", gluwS, gluw_d.rearrange("(kc p) f -> p kc f", p=128), writes=[PPB])
        scr_cm4 = scr_cm.rearrange("p (o g m) -> p o g m", o=2, g=32)
        scr_at4 = scr_at.rearrange("p (o k g) -> p o k g", o=2, k=2)
        for gh in range(2):
            for o in range(2):
                S.dma("sp", cmS[64 * gh:64 * gh + 64, o, :, :], scr_cm4[:, o, gh * 16:(gh + 1) * 16, :], writes=[CMB[gh * 2 + o]])
                for k in range(2):
                    S.dma("sp", atS[64 * gh:64 * gh + 64, o, k, :], scr_at4[:, o, k, gh * 16:(gh + 1) * 16], writes=[ATB[gh * 4 + o * 2 + k]])
        for g2 in range(16):
            pi = g2 % 2
            for gg in range(2):
                g = g2 * 2 + gg
                ct, g8 = g // 8, g % 8
                for i in range(8):
                    S.op("pe", lambda e, pi=pi, gg=gg, ct=ct, g8=g8, i=i: e.matmul(
                        PS[pi][:, gg * 256:(gg + 1) * 256], selin[:, g8, 112 - 16 * i:240 - 16 * i], uT4[:, ct, i, :],
                        start=(i == 0), stop=(i == 7)),
                        reads=[SELIB] + UB[ct], writes=[PSB[pi]], signal=(gg == 1 and i == 7))
            S.op("act", lambda e, pi=pi, g2=g2: e.activation(out=Ut[:, g2 * 2:g2 * 2 + 2, :],
                                                            in_=PS[pi][:].rearrange("p (a b) -> p a b", a=2), func=AF.Copy),
                 reads=[PSB[pi]], writes=[UTB[g2 * 2], UTB[g2 * 2 + 1]])
        S.op("dve", lambda e: e.memset(Xh[:, 31], 0.0), writes=[XHB[31]])
        for cb in range(16):
            pi = 2 + (cb % 2)
            psv = PS[pi][:].rearrange("p (c o g) -> p c o g", c=16, o=2)
            for g in range(32):
                gh, gl = g // 16, g % 16
                for o in range(2):
                    S.op("pe", lambda e, psv=psv, g=g, gh=gh, gl=gl, o=o, cb=cb: e.matmul(
                        psv[64 * gh:64 * gh + 64, :, o, gl], wbS[:, o, g, :], Ut[:, g, cb * 16:(cb + 1) * 16], start=True, stop=True),
                        reads=[WBB, UTB[g]], writes=[PSB[pi]], signal=(g == 31 and o == 1))
            for cc in range(16):
                c = cb * 16 + cc
                prev = Xh[:, (c - 1) % 32]
                S.op("dve", lambda e, prev=prev: e.tensor_tensor(
                    out=pq, in0=prev.unsqueeze(1).to_broadcast([128, 2, 2, 16]), in1=atS, op=ALU.mult),
                    reads=[XHB[(c - 1) % 32]] + ATB, writes=[PQB])
                S.op("dve", lambda e: e.tensor_tensor(out=rtmp, in0=pq[:, :, 0, :], in1=pq[:, :, 1, :], op=ALU.add),
                     reads=[PQB], writes=[RTB])
                S.op("dve", lambda e, psv=psv, cc=cc, c=c: e.tensor_tensor(out=Xh[:, c % 32], in0=rtmp, in1=psv[:, cc], op=ALU.add),
                     reads=[RTB, PSB[pi]], writes=[XHB[c % 32]])
            half = (cb % 2) * 16
            S.op("act", lambda e, cb=cb, half=half: e.activation(
                out=Xbf[:, :, :, cb * 16:(cb + 1) * 16], in_=Xh[:, half:half + 16].rearrange("p c o g -> p o g c"), func=AF.Copy),
                reads=XHB[half:half + 16], writes=[XBFB])
        for g2 in range(16):
            pi = 4 + (g2 % 2)
            for gg in range(2):
                g = g2 * 2 + gg
                S.op("pe", lambda e, pi=pi, gg=gg, g=g: e.matmul(PS[pi][:, gg * 256:(gg + 1) * 256], toepS[:, g, :], Ut[:, g, :],
                                                                start=True, stop=False),
                     reads=[TOEPB, UTB[g]], writes=[PSB[pi]], signal=False)
                hs = slice(64 * (g // 16), 64 * (g // 16) + 64)
                gl = g % 16
                S.op("pe", lambda e, pi=pi, gg=gg, gl=gl, hs=hs: e.matmul(PS[pi][:, gg * 256 + 1:(gg + 1) * 256], cmS[hs, 0, gl, :], Xbf[hs, 0, gl, 0:255],
                                                                start=False, stop=False),
                     reads=CMB + [XBFB], writes=[PSB[pi]], signal=False)
                S.op("pe", lambda e, pi=pi, gg=gg, gl=gl, hs=hs: e.matmul(PS[pi][:, gg * 256 + 1:(gg + 1) * 256], cmS[hs, 1, gl, :], Xbf[hs, 1, gl, 0:255],
                                                                start=False, stop=True),
                     reads=CMB + [XBFB], writes=[PSB[pi]], signal=(gg == 1))
            S.op("act", lambda e, pi=pi, g2=g2: e.activation(out=Ut[:, g2 * 2:g2 * 2 + 2, :],
                                                            in_=PS[pi][:].rearrange("p (a b) -> p a b", a=2), func=AF.Copy),
                 reads=[PSB[pi]], writes=[UTB[g2 * 2], UTB[g2 * 2 + 1]])
        dump("toep", toepS.rearrange("p g m -> p (g m)"), [TOEPB])
        dump("wb", wbS.rearrange("p o g n -> p (o g n)"), [WBB])
        dump("Y", Ut.rearrange("p g c -> p (g c)"), UTB)
        S.barrier()
        S.dma("pool", woutS, wout_d.rearrange("(kc p) f -> p kc f", p=128), writes=[WOUTB])
        pc = 0
        for ct in range(4):
            for tb in range(4):
                pi = pc % 2
                pc += 1
                for j in range(8):
                    for g8 in range(8):
                        S.op("pe", lambda e, pi=pi, j=j, g8=g8, ct=ct, tb=tb: e.matmul(
                            PS[pi][:, j * 64:(j + 1) * 64], selout[:, j, 112 - 16 * g8:240 - 16 * g8],
                            Ut[:, ct * 8 + g8, tb * 64:(tb + 1) * 64], start=(g8 == 0), stop=(g8 == 7)),
                            reads=[PPB, UTB[ct * 8 + g8]], writes=[PSB[pi]], signal=(j == 7 and g8 == 7))
                ys, x2, x3, sg_ = gt[0], gt[1], gt[2], gt[3]
                sl = slice(tb * TT, (tb + 1) * TT)
                S.op("dve", lambda e, pi=pi, ct=ct, sl=sl: e.scalar_tensor_tensor(
                    out=ys.rearrange("p (c j) -> p c j", j=8), in0=uT4[:, ct, :, tb * 64:(tb + 1) * 64].rearrange("p j c -> p c j"),
                    scalar=chv[:, 3, ct:ct + 1], in1=PS[pi][:].rearrange("p (j c) -> p c j", j=8), op0=ALU.mult, op1=ALU.add),
                    reads=[PSB[pi], UB[ct][tb], M0CB], writes=[GTB[0]])
                S.op("act", lambda e: e.activation(out=x2, in_=ys, func=AF.Square), reads=[GTB[0]], writes=[GTB[1]])
                S.op("dve", lambda e: e.tensor_scalar(out=x2, in0=x2, scalar1=0.044715, scalar2=1.0, op0=ALU.mult, op1=ALU.add),
                     reads=[GTB[1]], writes=[GTB[1]])
                S.op("dve", lambda e: e.tensor_tensor(out=x3, in0=x2, in1=ys, op=ALU.mult), reads=[GTB[1], GTB[0]], writes=[GTB[2]])
                S.op("act", lambda e: e.activation(out=sg_, in_=x3, func=AF.Sigmoid, scale=1.5957691216057308), reads=[GTB[2]], writes=[GTB[3]])
                S.op("dve", lambda e, ct=ct, sl=sl: e.tensor_tensor(out=y1bS[:, ct, sl], in0=ys, in1=sg_, op=ALU.mult),
                     reads=[GTB[0], GTB[3]], writes=[Y1B[ct][tb]])
        for ot in range(4):
            for tb in range(4):
                pi = 2 + (pc % 2)
                pc += 1
                sl = slice(tb * TT, (tb + 1) * TT)
                for ct in range(4):
                    S.op("pe", lambda e, pi=pi, ct=ct, ot=ot, sl=sl: e.matmul(
                        PS[pi][:], gluwS[:, ct, ot * 128:(ot + 1) * 128], y1bS[:, ct, sl], start=(ct == 0), stop=(ct == 3)),
                        reads=[PPB, Y1B[ct][tb]], writes=[PSB[pi]], signal=(ct == 3))
                sg = sgt[pc % 2]
                S.op("act", lambda e, pi=pi, sg=sg, ot=ot: e.activation(out=sg, in_=PS[pi][:], func=AF.Sigmoid, bias=chv[:, 4, ot:ot + 1]),
                     reads=[PSB[pi], M0CB], writes=[SGTB[pc % 2]])
                S.op("dve", lambda e, sg=sg, ot=ot, sl=sl: e.tensor_tensor(out=mcat[:, 4 + ot, sl], in0=y1bS[:, ot, sl], in1=sg, op=ALU.mult),
                     reads=[SGTB[pc % 2], Y1B[ot][tb]], writes=[MCB[4 + ot][tb]])
        dump("mcB", mcat[:, 4:8, :].rearrange("p a b -> p (a b)"), [b for r in MCB[4:8] for b in r])
        dump("y1", y1bS.rearrange("p a b -> p (a b)"), [b for r in Y1B for b in r])
        for dc in range(KC):
            for tb in range(4):
                pi = 4 + (pc % 2)
                pc += 1
                sl = slice(tb * TT, (tb + 1) * TT)
                for mc in range(8):
                    S.op("pe", lambda e, pi=pi, mc=mc, dc=dc, sl=sl: e.matmul(
                        PS[pi][:], woutS[:, mc, dc * 128:(dc + 1) * 128], mcat[:, mc, sl], start=(mc == 0), stop=(mc == 7)),
                        reads=[WOUTB, MCB[mc][tb]], writes=[PSB[pi]], signal=(mc == 7))
                xs = xT[:, dc, sl]
                S.op("dve", lambda e, pi=pi, xs=xs: e.tensor_tensor(out=xs, in0=PS[pi][:], in1=xs, op=ALU.add),
                     reads=[PSB[pi], XB[dc][tb]], writes=[XB[dc][tb]])
        S.barrier()

    wqkv_d = din("w_qkv", [D, 3 * D])
    wo_d = din("w_o", [D, D])
    cbias_d = din("cbias", [128, 2 * 256])
    en_d = din("en", [8, 8 * 128])
    negm_d = din("negm", [128, 4 * 64])
    NEG = -30000.0
    ahT = R(0, [128, KC, L], BF16)
    qring = [R(32 + 8 * i, [128, KC, 512], BF16) for i in range(2)]
    qT = R(48, [128, 4, L], BF16)
    kT = R(64, [128, 4, L], BF16)
    Vh = R(80, [128, 16, 512], BF16)
    oT = R(96, [128, 4, L], BF16)
    woS = R(112, [128, 4, D], BF16)
    pT = [R(120 + i, [128, 2, 256], BF16) for i in range(2)]
    cbiasS = R(122, [128, 2, 256], BF16)
    EnS = R(123, [8, 8, 128], BF16)
    identA = R(125, [128, 128], BF16)
    negmS = R(125.5, [128, 4, 64], F32)
    rden = R(126.5, [128, 256], F32)
    mbT = [R(127.5 + 2 * i, [8, 4, 256], BF16) for i in range(2)]
    kmf = R(131.5, [128, 4, 8], F32)
    kmT = R(131.75, [128, 4, 8], BF16)
    gsb = R(132, [128, 32], F32)
    top8 = R(132.25, [128, 8], F32)
    mball = R(132.5, [128, 8, 32], BF16)
    rden2 = R(133, [128, 256], F32)
    rdens = [rden, rden2]
    RDBS = [Buf("rden0"), Buf("rden1")]
    mbTall = R(32, [8, 4, 4, 256], BF16)
    assert RB + int(134 * KB) <= A.nbytes

    AHB = [[Buf("ah%d_%d" % (k, t)) for t in range(4)] for k in range(KC)]
    QRB = [Buf("qring%d" % i) for i in range(2)]
    QTB = [Buf("qT%d" % h) for h in range(4)]
    KTB = [Buf("kT%d" % h) for h in range(4)]
    VHB = [Buf("vh%d" % k) for k in range(16)]
    OTB = [[Buf("oT%d_%d" % (h, q)) for q in range(8)] for h in range(4)]
    WOB = Buf("woS")
    PTB = [Buf("pT%d" % i) for i in range(2)]
    ACB = Buf("attnconst")
    RDB = Buf("rden")
    MBTB = [Buf("mbT%d" % i) for i in range(2)]
    KMB = Buf("km")
    GSB = Buf("gsb")
    T8B = Buf("top8")
    MBB = Buf("mb")
    SCALE = 128.0 ** -0.5

    def mixer1(s):
        S.barrier()
        S.dma("pool", cbiasS.rearrange("p a b -> p (a b)"), cbias_d, writes=[ACB])
        S.dma("pool", EnS.rearrange("p a b -> p (a b)"), en_d, writes=[ACB])
        S.dma("pool", identA, ident_d, writes=[ACB])
        S.dma("sp", negmS.rearrange("p a b -> p (a b)"), negm_d, writes=[ACB])
        rmsnorm_tile(5, 0, 4, ahT, AHB)
        nring = [0]
        pcs = [0]

        def load_cols(c0):
            i = nring[0] % 2
            nring[0] += 1
            S.dma("pool", qring[i], wqkv_d.rearrange("(kc p) f -> p kc f", p=128)[:, :, c0:c0 + 512], writes=[QRB[i]])
            return qring[i], QRB[i]

        for half in range(2):
            for which, dstT, dstB in ((0, qT, QTB), (1, kT, KTB)):
                if which == 0 and half == 1:
                    wsl, wb_ = pre_q
                else:
                    wsl, wb_ = load_cols(which * D + half * 512)
                for hl in range(4):
                    for tb in range(4):
                        pi = pcs[0] % 2
                        pcs[0] += 1
                        for kc in range(KC):
                            S.op("pe", lambda e, pi=pi, wsl=wsl, kc=kc, hl=hl, tb=tb: e.matmul(
                                PS[pi][:], wsl[:, kc, hl * 128:(hl + 1) * 128], ahT[:, kc, tb * TT:(tb + 1) * TT],
                                start=(kc == 0), stop=(kc == KC - 1)),
                                reads=[wb_, AHB[kc][tb]], writes=[PSB[pi]], signal=(kc == KC - 1))
                        S.op("act", lambda e, pi=pi, dstT=dstT, hl=hl, tb=tb: e.activation(
                            out=dstT[:, hl, tb * TT:(tb + 1) * TT], in_=PS[pi][:], func=AF.Copy),
                            reads=[PSB[pi]], writes=[dstB[hl]])
            wsl, wb_ = load_cols(2 * D + half * 512)
            for kt in range(16):
                pi = pcs[0] % 2
                pcs[0] += 1
                tb = kt // 4
                for kc in range(KC):
                    S.op("pe", lambda e, pi=pi, wsl=wsl, kc=kc, kt=kt: e.matmul(
                        PS[pi][:], ahT[:, kc, kt * 128:(kt + 1) * 128], wsl[:, kc, :], start=(kc == 0), stop=(kc == KC - 1)),
                        reads=[wb_, AHB[kc][tb]], writes=[PSB[pi]], signal=(kc == KC - 1))
                S.op("act", lambda e, pi=pi, kt=kt: e.activation(out=Vh[:, kt, :], in_=PS[pi][:], func=AF.Copy),
                     reads=[PSB[pi]], writes=[VHB[kt]])
            S.dma("pool", woS, wo_d.rearrange("(kc p) f -> p kc f", p=128)[:, half * 4:(half + 1) * 4, :], writes=[WOB])
            if half == 0:
                pre_q = load_cols(0 * D + 1 * 512)
            for hl in range(4):
                S.op("dve", lambda e, hl=hl: e.tensor_reduce(out=kmf[:, hl, :], in_=kT[:, hl, :].rearrange("p (n k) -> p n k", n=8),
                                                            axis=AX.X, op=ALU.add), reads=[KTB[hl]], writes=[KMB])
            S.op("dve", lambda e: e.tensor_copy(out=kmT, in_=kmf), reads=[KMB], writes=[KMB])
            for qb in range(4, 8):
                for q2 in range(2):
                    idx = (qb - 4) * 2 + q2
                    qsl = slice(qb * 256 + q2 * 128, qb * 256 + (q2 + 1) * 128)
                    for hl in range(4):
                        S.op("pe", lambda e, hl=hl, qsl=qsl, idx=idx: e.matmul(PS[6][:, idx * 32 + hl * 8:idx * 32 + (hl + 1) * 8], qT[:, hl, qsl],
                                                                              kmT[:, hl, :], start=True, stop=True),
                             reads=[QTB[hl], KMB], writes=[PSB[6]], signal=(hl == 3))
            for qb in range(4, 8):
                for q2 in range(2):
                    idx = (qb - 4) * 2 + q2
                    S.op("dve", lambda e, idx=idx, qb=qb: e.tensor_tensor(out=gsb, in0=PS[6][:, idx * 32:(idx + 1) * 32], in1=negmS[:, qb - 4, 0:32], op=ALU.add),
                         reads=[PSB[6], ACB], writes=[GSB])
                    for hl in range(4):
                        S.op("dve", lambda e, hl=hl: e.max(out=top8, in_=gsb[:, hl * 8:(hl + 1) * 8]), reads=[GSB], writes=[T8B])
                        S.op("dve", lambda e, hl=hl, idx=idx: e.tensor_scalar(out=mball[:, idx, hl * 8:(hl + 1) * 8], in0=gsb[:, hl * 8:(hl + 1) * 8],
                                                                    scalar1=top8[:, 2:3], scalar2=NEG, op0=ALU.is_lt, op1=ALU.mult),
                             reads=[GSB, T8B], writes=[MBB])

            def mask_transposes():
                for qb in range(4, 8):
                    for q2 in range(2):
                        idx = (qb - 4) * 2 + q2
                        tbk = 6 + (idx % 2)
                        for hl in range(4):
                            S.op("pe", lambda e, hl=hl, idx=idx, tbk=tbk: e.matmul(PS[tbk][0:8, hl * 128:(hl + 1) * 128], mball[:, idx, hl * 8:(hl + 1) * 8], identA,
                                                                         start=True, stop=True),
                                 reads=[MBB, ACB], writes=[PSB[tbk]], signal=(hl == 3))
                        S.op("act", lambda e, qb=qb, q2=q2, tbk=tbk: e.activation(out=mbTall[:, qb - 4, :, q2 * 128:(q2 + 1) * 128],
                                                                        in_=PS[tbk][0:8, :].rearrange("p (h q) -> p h q", h=4), func=AF.Copy),
                             reads=[PSB[tbk]], writes=[MBTB[0]])

            items = []
            for qb in range(8):
                for hl in range(4):
                    npair = qb + 1
                    for kp in range(npair):
                        items.append((qb, hl, kp, kp == 0, kp == npair - 1))

            def emit_S(i):
                qb, hl, kp, _, _ = items[i]
                qs = slice(qb * 256, (qb + 1) * 256)
                gated = qb >= 4
                pi = 2 + (i % 2)
                for k2 in range(2):
                    kt = kp * 2 + k2
                    n = kp
                    osl = PS[pi][:, k2 * 256:(k2 + 1) * 256]
                    extra = (n == qb) or gated
                    S.op("pe", lambda e, osl=osl, hl=hl, kt=kt, extra=extra, qs=qs: e.matmul(
                        osl, kT[:, hl, kt * 128:(kt + 1) * 128], qT[:, hl, qs], start=True, stop=(not extra)),
                        reads=[KTB[hl], QTB[hl]], writes=[PSB[pi]], signal=((not extra) and k2 == 1))
                    if n == qb:
                        S.op("pe", lambda e, osl=osl, k2=k2: e.matmul(osl, identA, cbiasS[:, k2, :], start=False, stop=True),
                             reads=[ACB], writes=[PSB[pi]], signal=(k2 == 1))
                    elif gated:
                        S.op("pe", lambda e, osl=osl, n=n, hl=hl, qb=qb: e.matmul(osl, EnS[:, n, :], mbTall[:, qb - 4, hl, :], start=False, stop=True),
                             reads=[ACB, MBTB[0]], writes=[PSB[pi]], signal=(k2 == 1))

            def emit_rest(i, gi):
                qb, hl, kp, first, last = items[i]
                qs = slice(qb * 256, (qb + 1) * 256)
                pi = 2 + (i % 2)
                ti = i % 2
                po, pd = ((4, 5), (0, 1))[gi % 2]
                S.op("act", lambda e, pi=pi, ti=ti: e.activation(out=pT[ti].rearrange("p a b -> p (a b)"), in_=PS[pi][:],
                                                                func=AF.Exp, scale=SCALE),
                     reads=[PSB[pi]], writes=[PTB[ti]])
                for k2 in range(2):
                    kt = kp * 2 + k2
                    f_ = first and k2 == 0
                    l_ = last and k2 == 1
                    S.op("pe", lambda e, po=po, kt=kt, hl=hl, ti=ti, k2=k2, f_=f_, l_=l_: e.matmul(
                        PS[po][:, 0:256], Vh[:, kt, hl * 128:(hl + 1) * 128], pT[ti][:, k2, :], start=f_, stop=l_),
                        reads=[VHB[kt], PTB[ti]], writes=[PSB[po]], signal=False)
                    S.op("pe", lambda e, pd=pd, ti=ti, k2=k2, f_=f_, l_=l_: e.matmul(
                        PS[pd][:, 0:256], ones_bf, pT[ti][:, k2, :], start=f_, stop=l_),
                        reads=[CONSTB, PTB[ti]], writes=[PSB[pd]], signal=(k2 == 1))
                if last:
                    rd = rdens[gi % 2]
                    S.op("dve", lambda e, pd=pd, rd=rd: e.reciprocal(out=rd, in_=PS[pd][:, 0:256]), reads=[PSB[pd]], writes=[RDBS[gi % 2]])
                    S.op("dve", lambda e, po=po, hl=hl, rd=rd, qs=qs: e.tensor_tensor(out=oT[:, hl, qs], in0=PS[po][:, 0:256], in1=rd, op=ALU.mult),
                         reads=[PSB[po], RDBS[gi % 2]], writes=[OTB[hl][qb]])

            n_items = len(items)
            first_gated = next(i for i, it in enumerate(items) if it[0] >= 4)
            gi = 0
            emit_S(0)
            for i in range(n_items):
                if i + 1 < n_items:
                    if i + 1 == first_gated:
                        mask_transposes()
                    emit_S(i + 1)
                emit_rest(i, gi)
                if items[i][4]:
                    gi += 1
            if half == 0:
                dump("oT0", oT.rearrange("p a b -> p (a b)"), [b for r_ in OTB for b in r_])
                dump("qT0", qT.rearrange("p a b -> p (a b)"), QTB)
                dump("kT0", kT.rearrange("p a b -> p (a b)"), KTB)
                dump("Vh0", Vh.rearrange("p a b -> p (a b)"), VHB)
            for dc in range(KC):
                for tb in range(4):
                    pi = pcs[0] % 2
                    pcs[0] += 1
                    sl = slice(tb * TT, (tb + 1) * TT)
                    for hl in range(4):
                        S.op("pe", lambda e, pi=pi, hl=hl, dc=dc, sl=sl: e.matmul(
                            PS[pi][:], woS[:, hl, dc * 128:(dc + 1) * 128], oT[:, hl, sl], start=(hl == 0), stop=(hl == 3)),
                            reads=[WOB, OTB[hl][2 * tb], OTB[hl][2 * tb + 1]], writes=[PSB[pi]], signal=(hl == 3))
                    xs = xT[:, dc, sl]
                    S.op("dve", lambda e, pi=pi, xs=xs: e.tensor_tensor(out=xs, in0=PS[pi][:], in1=xs, op=ALU.add),
                         reads=[PSB[pi], XB[dc][tb]], writes=[XB[dc][tb]])
            S.barrier()

    OUTB = [Buf("o%d" % i) for i in range(2)]
    for s in range(nseq):
        def load_blocks(sq, blks):
            for blk in blks:
                for kc in range(KC):
                    S.dma("sp", xT[:, kc, blk * TT:(blk + 1) * TT], xT_d[sq, kc * 128:(kc + 1) * 128, blk * TT:(blk + 1) * TT],
                          writes=[XB[kc][blk]])

        if s == 0 or "final" not in stages:
            load_blocks(s, range(L // TT))
        if s == 0 and "mix0" in stages:
            ssm_param_prep()
        if "ffn00" in stages:
            ffn_chain([(0, 0, 0), (0, 0, 1024)])
        if "mix0" in stages:
            mixer0(s)
        if "ffn01" in stages and "ffn10" in stages:
            ffn_chain([(1, 1, 0), (1, 1, 1024), (2, 2, 0), (2, 2, 1024)])
        else:
            if "ffn01" in stages:
                ffn_chain([(1, 1, 0), (1, 1, 1024)])
            if "ffn10" in stages:
                ffn_chain([(2, 2, 0), (2, 2, 1024)])
        if "mix1" in stages:
            mixer1(s)
        if "ffn11" in stages:
            ffn_chain([(3, 3, 0), (3, 3, 1024)])
        if "final" in stages:
            for t0 in (0, 1024):
                rmsnorm_tile(6, t0, 2, hT, HB, inplace=True)
                for blk in (t0 // TT, t0 // TT + 1):
                    for kc in range(KC):
                        S.dma("sp", outT_d[s, kc * 128:(kc + 1) * 128, blk * TT:(blk + 1) * TT], xT[:, kc, blk * TT:(blk + 1) * TT],
                              reads=[XB[kc][blk]])
                if s + 1 < nseq:
                    load_blocks(s + 1, (t0 // TT, t0 // TT + 1))
        if "final" not in stages:
            for blk in range(L // TT):
                for kc in range(KC):
                    S.dma("sp", outT_d[s, kc * 128:(kc + 1) * 128, blk * TT:(blk + 1) * TT], xT[:, kc, blk * TT:(blk + 1) * TT],
                          reads=[XB[kc][blk]])
    S.wait_all("sp", [b for row in XB for b in row])
    nc._sched_ninst = S.ninst
    nc._dbg_names = dbg_names
    return nc


def _prep_inputs(inputs):
    f = np.float32
    x = np.asarray(inputs["x"], f)
    ffn_norm = np.asarray(inputs["ffn_norm"], f)
    vecs = [ffn_norm[0, 0], ffn_norm[0, 1], ffn_norm[1, 0], ffn_norm[1, 1],
            np.asarray(inputs["mix_norm"], f)[0], np.asarray(inputs["mix_norm"], f)[1],
            np.asarray(inputs["final_norm"], f)]
    gains = np.stack([v.reshape(KC, 128).T for v in vecs], axis=1)
    gains = np.ascontiguousarray(gains.reshape(128, 7 * KC))
    shared = {
        "gains": gains,
        "w1": np.ascontiguousarray(np.asarray(inputs["ffn_w1"], f).reshape(4, D, FF)),
        "w3": np.ascontiguousarray(np.asarray(inputs["ffn_w3"], f).reshape(4, D, FF)),
        "w2": np.ascontiguousarray(np.asarray(inputs["ffn_w2"], f).reshape(4, FF, D)),
    }
    shared["w_in"] = np.ascontiguousarray(np.asarray(inputs["ab_w_in"], f)[0])
    shared["w_out"] = np.ascontiguousarray(np.asarray(inputs["ab_w_out"], f)[0])
    shared["glu_w"] = np.ascontiguousarray(np.asarray(inputs["ssm_glu_w"], f)[0])
    cwm = np.asarray(inputs["conv_w"], f)[0]
    shared["cw"] = np.ascontiguousarray(cwm.T.reshape(4, 128, 31).transpose(1, 0, 2).reshape(128, 4 * 31))
    chv = [np.asarray(inputs[k], f)[0] for k in ("conv_b", "conv_ln_g", "conv_ln_b", "ssm_d", "ssm_glu_b")]
    shared["chv"] = np.ascontiguousarray(np.stack([v.reshape(4, 128).T for v in chv], axis=1).reshape(128, 20))
    ldt = np.broadcast_to(np.asarray(inputs["ssm_log_dt"], f)[0][None, :], (64, 32))
    are = np.asarray(inputs["ssm_a_re"], f)[0].T
    aim = np.asarray(inputs["ssm_a_im"], f)[0].T
    shared["sp1"] = np.ascontiguousarray(np.stack([ldt, are, aim], axis=1).reshape(64, 96))
    bre = np.asarray(inputs["ssm_b_re"], f)[0].transpose(1, 0, 2).reshape(64, 512)
    bim = np.asarray(inputs["ssm_b_im"], f)[0].transpose(1, 0, 2).reshape(64, 512)
    cre = np.asarray(inputs["ssm_c_re"], f)[0].transpose(2, 0, 1).reshape(64, 512)
    cim = np.asarray(inputs["ssm_c_im"], f)[0].transpose(2, 0, 1).reshape(64, 512)
    shared["sp2"] = np.ascontiguousarray(np.stack([bre, bim, cre, cim], axis=1).reshape(64, 2048))
    shared["w_qkv"] = np.ascontiguousarray(np.asarray(inputs["attn_w_qkv"], f)[0])
    shared["w_o"] = np.ascontiguousarray(np.asarray(inputs["attn_w_o"], f)[0])
    shared.update(_const_tables())
    in_maps = []
    for c in range(NCORES):
        m = dict(shared)
        m["xT"] = np.ascontiguousarray(x[2 * c:2 * c + 2].transpose(0, 2, 1))
        in_maps.append(m)
    return in_maps


def _const_tables():
    f = np.float32
    ident = np.eye(128, dtype=f)
    p = np.arange(128)
    bmask = (p[None, :] // 16 >= p[:, None] // 16).astype(f)
    sel = np.zeros((128, 8, 240), f)
    for g8 in range(8):
        for h in range(16):
            sel[16 * g8 + h, g8, 112 + h] = 1.0
    cb = np.zeros((128, 2, 256), f)
    for par in range(2):
        cb[:, par, :] = np.where((par * 128 + p[:, None]) <= np.arange(256)[None, :], 0.0, -30000.0)
    en = np.zeros((8, 8, 128), f)
    for n in range(8):
        en[n, n, :] = 1.0
    negm = np.zeros((128, 4, 64), f)
    for qb in range(4, 8):
        for h in range(8):
            negm[:, qb - 4, h * 8 + qb:h * 8 + 8] = -1e30
    return {"cbias": np.ascontiguousarray(cb.reshape(128, 512)), "en": np.ascontiguousarray(en.reshape(8, 1024)),
            "negm": np.ascontiguousarray(negm.reshape(128, 256)),
            "ident": ident, "bmask": bmask, "selin": np.ascontiguousarray(sel.reshape(128, 1920)),
            "selout": np.ascontiguousarray(sel.reshape(128, 1920))}


_NC_CACHE = {}


def kernel(**inputs):
    in_maps = _prep_inputs(inputs)
    if "nc" not in _NC_CACHE:
        _NC_CACHE["nc"] = build_program()
    nc = _NC_CACHE["nc"]
    res = run_bass_kernel_spmd(nc, in_maps, core_ids=list(range(NCORES)))
    out = np.empty((2 * NCORES, L, D), np.float32)
    for c in range(NCORES):
        out[2 * c:2 * c + 2] = np.asarray(res.results[c]["outT"]).transpose(0, 2, 1)
    return out
```

```python
import numpy as np
import concourse.bass as bass
import concourse.mybir as mybir
from concourse.bass_utils import run_bass_kernel_spmd

F32 = mybir.dt.float32
BF16 = mybir.dt.bfloat16
AF = mybir.ActivationFunctionType
ALU = mybir.AluOpType
AX = mybir.AxisListType

D = 1024
KC = 8
FF = 2816
FC = 22
L = 2048
NSEQ = 2
TT = 512
RMS_EPS = 1e-6
LN_EPS = 1e-5
NCORES = 8


class Buf:
    __slots__ = ("name", "w", "r", "dsem", "dcnt")

    def __init__(self, name):
        self.name = name
        self.w = None
        self.r = {}
        self.dsem = {}
        self.dcnt = {}


class Sched:
    def __init__(self, nc):
        self.nc = nc
        self.eng = {"pe": nc.tensor, "act": nc.scalar, "dve": nc.vector, "pool": nc.gpsimd, "sp": nc.sync}
        self.sem = {k: nc.alloc_semaphore("s_" + k) for k in self.eng}
        self.cnt = {k: 0 for k in self.eng}
        self.seen = {k: {} for k in self.eng}
        self.pr = {k: [] for k in self.eng}
        self.pw = {k: [] for k in self.eng}
        self.nds = 0
        self.ninst = 0
        self.dbufs = []

    def _deps(self, reads, writes):
        deps = {}

        def need(s, v):
            if deps.get(s, 0) < v:
                deps[s] = v

        for b in reads:
            if b.w is not None:
                need(*b.w)
        for b in writes:
            if b.w is not None:
                need(*b.w)
            for s, v in b.r.items():
                need(s, v)
        return deps

    def _wait(self, eng, deps):
        e = self.eng[eng]
        own = self.sem[eng]
        for s, v in deps.items():
            if s is own and eng == "pe":
                continue
            if self.seen[eng].get(s, 0) >= v:
                continue
            e.wait_ge(s, v)
            self.seen[eng][s] = v

    def op(self, eng, fn, reads=(), writes=(), signal=True):
        self._wait(eng, self._deps(reads, writes))
        ins = fn(self.eng[eng])
        self.ninst += 1
        self.pr[eng].extend(reads)
        self.pw[eng].extend(writes)
        if signal:
            own = self.sem[eng]
            self.cnt[eng] += 1
            ins.then_inc(own, 1)
            v = self.cnt[eng]
            for b in self.pr[eng]:
                b.r[own] = v
            for b in self.pw[eng]:
                b.w = (own, v)
                b.r = {}
            self.pr[eng] = []
            self.pw[eng] = []
        return ins

    def dma(self, q, out, in_, reads=(), writes=()):
        self._wait(q, self._deps(reads, writes))
        owner = writes[0] if writes else reads[0]
        kind = "sw" if q == "pool" else "hw"
        if kind not in owner.dsem:
            owner.dsem[kind] = self.nc.alloc_semaphore("d%d" % self.nds)
            owner.dcnt[kind] = 0
            self.nds += 1
            self.dbufs.append((owner, kind))
        owner.dcnt[kind] += 1
        sem = owner.dsem[kind]
        self.eng[q].dma_start(out=out, in_=in_).then_inc(sem, 16)
        self.ninst += 1
        tag = (sem, 16 * owner.dcnt[kind])
        for b in reads:
            b.r[tag[0]] = tag[1]
        for b in writes:
            b.w = tag
            b.r = {}

    def barrier(self):
        for k in self.eng:
            assert not self.pr[k] and not self.pw[k], "unsignalled ops pending on " + k
        for k in self.eng:
            deps = {self.sem[o]: self.cnt[o] for o in self.eng if o != k and self.cnt[o] > 0}
            for b, kind in self.dbufs:
                deps[b.dsem[kind]] = 16 * b.dcnt[kind]
            self._wait(k, deps)

    def wait_all(self, eng, bufs):
        deps = {}
        for b in bufs:
            if b.w is not None:
                deps[b.w[0]] = max(deps.get(b.w[0], 0), b.w[1])
            for s, v in b.r.items():
                deps[s] = max(deps.get(s, 0), v)
        self._wait(eng, deps)


class Arena:
    def __init__(self, nc, nbytes):
        self.t = nc.alloc_sbuf_tensor("arena", [128, nbytes // 2], BF16)
        self.nbytes = nbytes
        self.off = 0
        self.marks = []

    def alloc(self, shape, dtype, at=None):
        esz = 2 if dtype == BF16 else 4
        n = int(np.prod(shape[1:]))
        nb = n * esz
        off = self.off if at is None else at
        off = (off + 31) // 32 * 32
        assert off + nb <= self.nbytes, ("arena overflow", off, nb, self.nbytes)
        if at is None:
            self.off = off + nb
        ap = self.t[0:shape[0], off // 2:(off + nb) // 2]
        if dtype != BF16:
            ap = ap.bitcast(dtype)
        if len(shape) == 3:
            ap = ap.rearrange("p (a b) -> p a b", a=shape[1])
        elif len(shape) == 4:
            ap = ap.rearrange("p (a b c) -> p a b c", a=shape[1], b=shape[2])
        return ap


def build_program(stages=("ffn00", "mix0", "ffn01", "ffn10", "mix1", "ffn11", "final"), nseq=NSEQ, debug=False):
    nc = bass.Bass("TRN2", target_bir_lowering=False)
    S = Sched(nc)
    dbg_names = []

    def dump(name, ap2d, bufs):
        if not debug or name in dbg_names:
            return
        dbg_names.append(name)
        shp = list(ap2d.shape)
        dt = nc.dram_tensor("dbg_" + name, shp, F32, kind="ExternalOutput").ap()
        S.dma("pool", dt, ap2d, reads=bufs)

    def din(name, shape, dt=F32):
        return nc.dram_tensor(name, list(shape), dt, kind="ExternalInput").ap()

    xT_d = din("xT", [NSEQ, D, L])
    gains_d = din("gains", [128, 7 * KC])
    w1_d = din("w1", [4, D, FF])
    w3_d = din("w3", [4, D, FF])
    w2_d = din("w2", [4, FF, D])
    outT_d = nc.dram_tensor("outT", [NSEQ, D, L], F32, kind="ExternalOutput").ap()

    A = Arena(nc, 207 * 1024)
    xT = A.alloc([128, KC, L], F32)
    gains = A.alloc([128, 7, KC], F32)
    ones_bf = A.alloc([128, 128], BF16)
    epsc = A.alloc([128, 1], F32)
    cw = A.alloc([128, 4, 31], F32)
    chv = A.alloc([128, 5, 4], F32)
    lnepsc = A.alloc([128, 1], F32)
    xsq = [A.alloc([128, TT], BF16) for _ in range(2)]
    rstd = A.alloc([128, TT], F32)
    sil = [A.alloc([128, TT], BF16) for _ in range(2)]
    base_off = A.off
    hT = A.alloc([128, KC, 1024], BF16)
    G = A.alloc([128, FC, 1024], BF16)
    NS13 = 3
    w13 = [A.alloc([128, 2, KC, 256], BF16) for _ in range(NS13)]
    w2s = A.alloc([128, FC, D], BF16)
    ffn_end = A.off

    PS = [nc.alloc_psum_tensor("ps%d" % i, [128, TT], F32) for i in range(8)]
    PSB = [Buf("ps%d" % i) for i in range(8)]

    XB = [[Buf("x%d_%d" % (k, b)) for b in range(L // TT)] for k in range(KC)]
    HB = [[Buf("h%d_%d" % (k, t)) for t in range(2)] for k in range(KC)]
    GB = [[Buf("g%d_%d" % (f, t)) for t in range(2)] for f in range(FC)]
    W13B = [(Buf("w1_%d" % i), Buf("w3_%d" % i)) for i in range(NS13)]
    W2B = [Buf("w2s%d" % i) for i in range(FC // 2)]
    XSQB = [Buf("xsq%d" % i) for i in range(2)]
    RSTDB = Buf("rstd")
    SILB = [Buf("sil%d" % i) for i in range(2)]
    CONSTB = Buf("const")

    S.dma("sp", gains.rearrange("p a b -> p (a b)"), gains_d, writes=[CONSTB])
    S.op("dve", lambda e: e.memset(ones_bf, 1.0), writes=[CONSTB])
    S.op("dve", lambda e: e.memset(epsc, RMS_EPS), writes=[CONSTB])

    w13_state = {"n": 0}

    def load_w13(fi, fg):
        i = w13_state["n"] % NS13
        w13_state["n"] += 1
        slot = w13[i]
        b = W13B[i]
        src1 = w1_d[fi].rearrange("(kc p) f -> p kc f", p=128)[:, :, fg * 256:(fg + 1) * 256]
        src3 = w3_d[fi].rearrange("(kc p) f -> p kc f", p=128)[:, :, fg * 256:(fg + 1) * 256]
        S.dma("pool", slot[:, 0], src1, writes=[b[0]])
        S.dma("pool", slot[:, 1], src3, writes=[b[1]])
        return slot, b

    def load_w2(fi, j):
        src = w2_d[fi].rearrange("(fc p) d -> p fc d", p=128)
        S.dma("pool", w2s[:, 2 * j:2 * j + 2], src[:, 2 * j:2 * j + 2], writes=[W2B[j]])

    def rmsnorm_tile(gidx, t0, ntt, dstT, dstB, inplace=False):
        for tt in range(ntt):
            blk = (t0 + tt * TT) // TT
            ps = PS[6]
            psb = PSB[6]
            for kc in range(KC):
                q = dstT[:, kc, tt * TT:(tt + 1) * TT]
                xs = xT[:, kc, blk * TT:(blk + 1) * TT]
                S.op("act", lambda e, q=q, xs=xs: e.activation(out=q, in_=xs, func=AF.Square),
                     reads=[XB[kc][blk], CONSTB], writes=[dstB[kc][tt]])
            for kc in range(KC):
                q = dstT[:, kc, tt * TT:(tt + 1) * TT]
                S.op("pe", lambda e, q=q, kc=kc: e.matmul(ps[:], ones_bf, q, start=(kc == 0), stop=(kc == KC - 1)),
                     reads=[dstB[kc][tt], CONSTB], writes=[psb], signal=(kc == KC - 1))
            S.op("act", lambda e: e.activation(out=rstd, in_=ps[:], func=AF.Sqrt, scale=1.0 / D, bias=epsc),
                 reads=[psb, CONSTB], writes=[RSTDB])
            S.op("dve", lambda e: e.reciprocal(out=rstd, in_=rstd), reads=[RSTDB], writes=[RSTDB])
            for kc in range(KC):
                xs = xT[:, kc, blk * TT:(blk + 1) * TT]
                if inplace:
                    S.op("dve", lambda e, kc=kc, xs=xs: e.scalar_tensor_tensor(
                        out=xs, in0=xs, scalar=gains[:, gidx, kc:kc + 1], in1=rstd, op0=ALU.mult, op1=ALU.mult),
                        reads=[XB[kc][blk], RSTDB, CONSTB, dstB[kc][tt]], writes=[XB[kc][blk]])
                else:
                    S.op("dve", lambda e, kc=kc, xs=xs: e.scalar_tensor_tensor(
                        out=dstT[:, kc, tt * TT:(tt + 1) * TT], in0=xs,
                        scalar=gains[:, gidx, kc:kc + 1], in1=rstd, op0=ALU.mult, op1=ALU.mult),
                        reads=[XB[kc][blk], RSTDB, CONSTB], writes=[dstB[kc][tt]])

    def ffn_tile(fi, gidx, t0, do_pre=True, hook=None):
        if do_pre:
            rmsnorm_tile(gidx, t0, 2, hT, HB)
        pcount = 0
        slots = {}
        for fg in range(min(NS13, FC // 2)):
            slots[fg] = load_w13(fi, fg)
        for fg in range(FC // 2):
            slot, sb = slots.pop(fg)
            for fh in range(2):
                fc = fg * 2 + fh
                for tt in range(2):
                    pa = (pcount % 2) * 2
                    pcount += 1
                    for j, (pi, wi) in enumerate(((pa, 0), (pa + 1, 1))):
                        for kc in range(KC):
                            S.op("pe", lambda e, pi=pi, wi=wi, kc=kc: e.matmul(
                                PS[pi][:], slot[:, wi, kc, fh * 128:(fh + 1) * 128], hT[:, kc, tt * TT:(tt + 1) * TT],
                                start=(kc == 0), stop=(kc == KC - 1)),
                                reads=[sb[wi], HB[kc][tt]], writes=[PSB[pi]], signal=(kc == KC - 1))
                    sl = sil[pcount % 2]
                    slb = SILB[pcount % 2]
                    S.op("act", lambda e, sl=sl, pa=pa: e.activation(out=sl, in_=PS[pa][:], func=AF.Silu),
                         reads=[PSB[pa]], writes=[slb])
                    S.op("dve", lambda e, sl=sl, pa=pa, fc=fc, tt=tt: e.tensor_tensor(
                        out=G[:, fc, tt * TT:(tt + 1) * TT], in0=sl, in1=PS[pa + 1][:], op=ALU.mult),
                        reads=[slb, PSB[pa + 1]], writes=[GB[fc][tt]])
            if fg + NS13 < FC // 2:
                slots[fg + NS13] = load_w13(fi, fg + NS13)
            load_w2(fi, fg)
        pcount = 0
        for dc in range(KC):
            for tt in range(2):
                pi = 4 + (pcount % 2)
                pcount += 1
                blk = (t0 + tt * TT) // TT
                for fc in range(FC):
                    S.op("pe", lambda e, pi=pi, fc=fc, dc=dc, tt=tt: e.matmul(
                        PS[pi][:], w2s[:, fc, dc * 128:(dc + 1) * 128], G[:, fc, tt * TT:(tt + 1) * TT],
                        start=(fc == 0), stop=(fc == FC - 1)),
                        reads=[W2B[fc // 2], GB[fc][tt]], writes=[PSB[pi]], signal=(fc == FC - 1))
                xs = xT[:, dc, blk * TT:(blk + 1) * TT]
                S.op("dve", lambda e, pi=pi, xs=xs: e.scalar_tensor_tensor(
                    out=xs, in0=PS[pi][:], scalar=0.5, in1=xs, op0=ALU.mult, op1=ALU.add),
                    reads=[PSB[pi], XB[dc][blk]], writes=[XB[dc][blk]])
            if dc == 1 and hook is not None:
                hook()

    def ffn_chain(jobs, last_hook=None):
        for j, (fi, gidx, t0) in enumerate(jobs):
            nxt = jobs[j + 1] if j + 1 < len(jobs) else None
            hook = last_hook
            if nxt is not None:
                assert nxt[2] != t0
                hook = (lambda nxt=nxt: rmsnorm_tile(nxt[1], nxt[2], 2, hT, HB))
            ffn_tile(fi, gidx, t0, do_pre=(j == 0), hook=hook)

    RB = base_off
    KB = 1024

    def R(off_kb, shape, dt):
        return A.alloc(shape, dt, at=RB + int(off_kb * KB))

    I32 = mybir.dt.int32
    win_d = din("w_in", [D, 1536])
    wout_d = din("w_out", [D, D])
    gluw_d = din("glu_w", [512, 512])
    cw_d = din("cw", [128, 4 * 31])
    chv_d = din("chv", [128, 5 * 4])
    sp1_d = din("sp1", [64, 3 * 32])
    sp2_d = din("sp2", [64, 4 * 512])
    ident_d = din("ident", [128, 128])
    bmask_d = din("bmask", [128, 128])
    selin_d = din("selin", [128, 8 * 240])
    selout_d = din("selout", [128, 8 * 240])
    scr_toep = nc.dram_tensor("scr_toep", [128, 32 * 128], BF16, kind="Internal").ap()
    scr_wb = nc.dram_tensor("scr_wb", [128, 2 * 32 * 64], BF16, kind="Internal").ap()
    scr_cm = nc.dram_tensor("scr_cm", [64, 2 * 32 * 128], BF16, kind="Internal").ap()
    scr_at = nc.dram_tensor("scr_at", [64, 128], F32, kind="Internal").ap()

    M0CB = Buf("m0const")
    S.dma("sp", cw.rearrange("p a b -> p (a b)"), cw_d, writes=[M0CB])
    S.dma("sp", chv.rearrange("p a b -> p (a b)"), chv_d, writes=[M0CB])
    S.op("dve", lambda e: e.memset(lnepsc, LN_EPS), writes=[M0CB])

    uT = R(0, [128, 4, L], BF16)
    uT4 = R(0, [128, 4, 8, 256], BF16)
    mcat = R(16, [128, 8, L], BF16)
    Xbf = R(32, [128, 2, 16, 256], BF16)
    woutS = R(48, [128, 8, D], BF16)
    hT2 = R(64, [128, KC, L], BF16)
    winS = R(96, [128, KC, 1536], BF16)
    vpad = R(47.5, [128, 4, 30 + L], BF16)
    ycv = R(64, [128, 4, L], F32)
    lnt = [R(96 + 2 * i, [128, TT], F32) for i in range(5)]
    selin = R(64, [128, 8, 240], BF16)
    selout = R(68, [128, 8, 240], BF16)
    gluwS = R(72, [128, 4, 512], BF16)
    toepS = R(76, [128, 32, 128], BF16)
    wbS = R(84, [128, 2, 32, 64], BF16)
    cmS = R(92, [128, 2, 16, 128], BF16)
    atS = R(108, [128, 2, 2, 16], F32)
    Xh = R(109, [128, 32, 2, 16], F32)
    pq = R(113, [128, 2, 2, 16], F32)
    rtmp = R(114, [128, 2, 16], F32)
    Ut = R(116, [128, 32, 256], BF16)
    gt = [R(76 + 2 * i, [128, TT], F32) for i in range(6)]
    y1bS = R(92, [128, 4, L], BF16)
    sgt = [R(88 + i, [128, TT], BF16) for i in range(2)]

    diag = [R(o_, [128, 31, 128], BF16) for o_ in (32, 39.75, 114, 121.75)]
    identM = R(135.25, [128, 128], BF16)
    DIAGB = [Buf("diag%d" % i) for i in range(4)]
    IDMB = Buf("identM")
    UB = [[Buf("u%d_%d" % (c, t)) for t in range(4)] for c in range(4)]
    MCB = [[Buf("mc%d_%d" % (c, t)) for t in range(4)] for c in range(8)]
    H2B = [[Buf("h2_%d_%d" % (k, t)) for t in range(4)] for k in range(KC)]
    WINB = [Buf("win%d" % i) for i in range(3)]
    WOUTB = Buf("woutS")
    VB = [Buf("v%d" % c) for c in range(4)]
    YCB = [Buf("yc%d" % c) for c in range(4)]
    LNTB = [Buf("lnt%d" % i) for i in range(5)]
    PPB = Buf("ssmparams")
    SELIB = Buf("selin"); WBB = Buf("wbS"); TOEPB = Buf("toepS")
    CMB = [Buf("cm%d" % i) for i in range(4)]
    ATB = [Buf("at%d" % i) for i in range(8)]
    UTB = [Buf("ut%d" % g) for g in range(32)]
    XBFB = Buf("xbf")
    XHB = [Buf("xh%d" % i) for i in range(32)]
    PQB = Buf("pq")
    RTB = Buf("rtmp")
    GTB = [Buf("gt%d" % i) for i in range(6)]
    Y1B = [[Buf("y1_%d_%d" % (c, t)) for t in range(4)] for c in range(4)]
    SGTB = [Buf("sgt%d" % i) for i in range(2)]

    def ssm_param_prep():
        S.barrier()
        P = PPB
        cnt = {"o": 0}

        def T(shape, dt=F32):
            n = int(np.prod(shape[1:])) * (4 if dt != BF16 else 2)
            off = cnt["o"]
            cnt["o"] = (off + n + 31) // 32 * 32
            assert RB + cnt["o"] <= A.nbytes, cnt["o"]
            return A.alloc(shape, dt, at=RB + off)

        def v_tt(out, a, b, op):
            S.op("dve", lambda e: e.tensor_tensor(out=out, in0=a, in1=b, op=op), reads=[P], writes=[P])

        def v_ts(out, a, s1, s2, op0, op1=None):
            if op1 is None:
                S.op("dve", lambda e: e.tensor_scalar(out=out, in0=a, scalar1=s1, scalar2=None, op0=op0), reads=[P], writes=[P])
            else:
                S.op("dve", lambda e: e.tensor_scalar(out=out, in0=a, scalar1=s1, scalar2=s2, op0=op0, op1=op1), reads=[P], writes=[P])

        def a_act(out, a, func, scale=1.0):
            S.op("act", lambda e: e.activation(out=out, in_=a, func=func, scale=scale), reads=[P], writes=[P])

        def v_cp(out, a):
            S.op("dve", lambda e: e.tensor_copy(out=out, in_=a), reads=[P], writes=[P])

        sp1 = T([64, 3, 32]); sp2 = T([64, 4, 512])
        S.dma("act", sp1.rearrange("p a b -> p (a b)"), sp1_d, writes=[P])
        S.dma("act", sp2.rearrange("p a b -> p (a b)"), sp2_d, writes=[P])
        identS = T([128, 128], BF16); bmaskS = T([128, 128])
        S.dma("pool", identS, ident_d, writes=[P])
        S.dma("act", bmaskS, bmask_d, writes=[P])
        ldt, are, aim = sp1[:, 0], sp1[:, 1], sp1[:, 2]
        bre = sp2[:, 0].rearrange("p (g h) -> p g h", g=32)
        bim = sp2[:, 1].rearrange("p (g h) -> p g h", g=32)
        cre = sp2[:, 2].rearrange("p (g h) -> p g h", g=32)
        cim = sp2[:, 3].rearrange("p (g h) -> p g h", g=32)
        dt_ = T([64, 32]); mag = T([64, 32]); ang = T([64, 32]); t1 = T([64, 32]); t2 = T([64, 32])
        ti = T([64, 32], I32); cosv = T([64, 32]); sinv = T([64, 32])
        a_act(dt_, ldt, AF.Exp)
        v_tt(t1, dt_, are, ALU.mult)
        a_act(mag, t1, AF.Exp)
        v_tt(ang, dt_, aim, ALU.mult)
        TWO_PI = 2.0 * np.pi

        def sin_of(out, shift):
            v_ts(t1, ang, 1.0 / TWO_PI, float(shift), ALU.mult, ALU.add)
            v_cp(ti, t1)
            v_cp(t2, ti)
            v_tt(t1, t1, t2, ALU.subtract)
            v_ts(t2, t1, 0.5, None, ALU.is_gt)
            v_tt(t1, t1, t2, ALU.subtract)
            v_ts(t2, t1, -0.5, None, ALU.is_lt)
            v_tt(t1, t1, t2, ALU.add)
            a_act(out, t1, AF.Sin, scale=TWO_PI)

        dump("dt", dt_, [P]); dump("mag", mag, [P]); dump("ang", ang, [P])
        sin_of(sinv, 0.0)
        dump("frac_s", t1, [P]); dump("sinv", sinv, [P])
        sin_of(cosv, 0.25)
        dump("cosv", cosv, [P])
        pwr = T([64, 9, 32]); pwi = T([64, 9, 32])
        S.op("dve", lambda e: e.memset(pwr[:, 0], 1.0), reads=[P], writes=[P])
        S.op("dve", lambda e: e.memset(pwi[:, 0], 0.0), reads=[P], writes=[P])
        v_tt(pwr[:, 1], mag, cosv, ALU.mult)
        v_tt(pwi[:, 1], mag, sinv, ALU.mult)
        for k in range(1, 8):
            v_tt(t1, pwr[:, k], pwr[:, 1], ALU.mult)
            v_tt(t2, pwi[:, k], pwi[:, 1], ALU.mult)
            v_tt(pwr[:, k + 1], t1, t2, ALU.subtract)
            v_tt(t1, pwr[:, k], pwi[:, 1], ALU.mult)
            v_tt(t2, pwi[:, k], pwr[:, 1], ALU.mult)
            v_tt(pwi[:, k + 1], t1, t2, ALU.add)
        dump("pwr", pwr.rearrange("p k g -> p (k g)"), [P]); dump("pwi", pwi.rearrange("p k g -> p (k g)"), [P])
        ipr = T([64, 9, 32]); ipi = T([64, 9, 32]); n2 = T([64, 9, 32]); n3 = T([64, 9, 32])
        v_tt(n2, pwr, pwr, ALU.mult)
        v_tt(n3, pwi, pwi, ALU.mult)
        v_tt(n2, n2, n3, ALU.add)
        S.op("dve", lambda e: e.reciprocal(out=n2, in_=n2), reads=[P], writes=[P])
        v_tt(ipr, pwr, n2, ALU.mult)
        v_tt(ipi, pwi, n2, ALU.mult)
        v_ts(ipi, ipi, -1.0, None, ALU.mult)
        den = T([64, 32]); nr = T([64, 32]); qre = T([64, 32]); qim = T([64, 32])
        v_tt(den, are, are, ALU.mult)
        v_tt(t1, aim, aim, ALU.mult)
        v_tt(den, den, t1, ALU.add)
        S.op("dve", lambda e: e.reciprocal(out=den, in_=den), reads=[P], writes=[P])
        v_ts(nr, pwr[:, 1], -1.0, None, ALU.add)
        v_tt(t1, nr, are, ALU.mult)
        v_tt(t2, pwi[:, 1], aim, ALU.mult)
        v_tt(t1, t1, t2, ALU.add)
        v_tt(qre, t1, den, ALU.mult)
        v_tt(t1, pwi[:, 1], are, ALU.mult)
        v_tt(t2, nr, aim, ALU.mult)
        v_tt(t1, t1, t2, ALU.subtract)
        v_tt(qim, t1, den, ALU.mult)
        Bre = T([64, 32, 16]); Bim = T([64, 32, 16]); w1_ = T([64, 32, 16]); w2_ = T([64, 32, 16])
        qre_b = qre.unsqueeze(2).to_broadcast([64, 32, 16])
        qim_b = qim.unsqueeze(2).to_broadcast([64, 32, 16])
        v_tt(w1_, bre, qre_b, ALU.mult); v_tt(w2_, bim, qim_b, ALU.mult); v_tt(Bre, w1_, w2_, ALU.subtract)
        v_tt(w1_, bim, qre_b, ALU.mult); v_tt(w2_, bre, qim_b, ALU.mult); v_tt(Bim, w1_, w2_, ALU.add)
        big_off = cnt["o"]
        big1 = T([64, 32, 8, 16]); big2 = T([64, 32, 8, 16])
        Cmr = T([64, 32, 8, 16]); Cmi = T([64, 32, 8, 16])
        q0 = cnt["o"]
        Bsr = T([64, 32, 8, 16], BF16); Bsi = T([64, 32, 8, 16], BF16)
        cmr_bf = T([64, 32, 8, 16], BF16); cmi_bf = T([64, 32, 8, 16], BF16)
        Bmr = A.alloc([64, 32, 8, 16], F32, at=RB + q0); Bmi = A.alloc([64, 32, 8, 16], F32, at=RB + q0 + 16 * KB)

        def bc_x(x):
            return x.unsqueeze(2).to_broadcast([64, 32, 8, 16])

        def bc_p(p, lo, hi, rev=False):
            sl = p[:, lo:hi, :].rearrange("p k g -> p g k")
            return sl.unsqueeze(3).to_broadcast([64, 32, 8, 16])

        def cmul_big(o_re, o_im, xr, xi, pr, pi, neg_im=False):
            v_tt(big1, bc_x(xr), pr, ALU.mult); v_tt(big2, bc_x(xi), pi, ALU.mult)
            v_tt(o_re, big1, big2, ALU.subtract)
            v_tt(big1, bc_x(xr), pi, ALU.mult); v_tt(big2, bc_x(xi), pr, ALU.mult)
            if neg_im:
                v_tt(o_im, big1, big2, ALU.add)
                v_ts(o_im, o_im, -1.0, None, ALU.mult)
            else:
                v_tt(o_im, big1, big2, ALU.add)

        cmul_big(Cmr, Cmi, cre, cim, bc_p(pwr, 1, 9), bc_p(pwi, 1, 9), neg_im=True)
        v_cp(cmr_bf, Cmr); v_cp(cmi_bf, Cmi)
        S.dma("sp", scr_cm[:, 0:4096], cmr_bf.rearrange("p g j h -> p (g j h)"), reads=[P])
        S.dma("sp", scr_cm[:, 4096:8192], cmi_bf.rearrange("p g j h -> p (g j h)"), reads=[P])
        pwr_rev = T([64, 8, 32]); pwi_rev = T([64, 8, 32])
        for i in range(8):
            v_cp(pwr_rev[:, i], pwr[:, 7 - i]); v_cp(pwi_rev[:, i], pwi[:, 7 - i])
        prr = pwr_rev.rearrange("p k g -> p g k").unsqueeze(3).to_broadcast([64, 32, 8, 16])
        pri = pwi_rev.rearrange("p k g -> p g k").unsqueeze(3).to_broadcast([64, 32, 8, 16])
        cmul_big(Bsr, Bsi, Bre, Bim, prr, pri)
        wb_bf = T([128, 2, 32, 64], BF16)
        for o, src in ((0, Bsr), (1, Bsi)):
            for g8 in range(4):
                ps = PS[7]
                for gg in range(8):
                    g = g8 * 8 + gg
                    S.op("pe", lambda e, g=g, gg=gg, src=src: e.matmul(ps[:, gg * 64:(gg + 1) * 64], src[:, g].rearrange("p i h -> p (i h)"),
                                                                       identS[0:64, 0:64], start=True, stop=True),
                         reads=[P], writes=[PSB[7]], signal=(gg == 7))
                S.op("dve", lambda e, o=o, g8=g8: e.tensor_copy(out=wb_bf[:, o, g8 * 8:(g8 + 1) * 8, :],
                                                               in_=ps[:].rearrange("p (a b) -> p a b", a=8)), reads=[PSB[7], P], writes=[P])
        S.dma("sp", scr_wb, wb_bf.rearrange("p o g n -> p (o g n)"), reads=[P])
        at_ = T([64, 2, 2, 32])
        v_cp(at_[:, 0, 0], pwr[:, 8]); v_cp(at_[:, 1, 1], pwr[:, 8]); v_cp(at_[:, 1, 0], pwi[:, 8])
        v_ts(at_[:, 0, 1], pwi[:, 8], -1.0, None, ALU.mult)
        S.dma("sp", scr_at, at_.rearrange("p o k g -> p (o k g)"), reads=[P])
        cmul_big(Bmr, Bmi, Bre, Bim, bc_p(ipr, 1, 9), bc_p(ipi, 1, 9))
        toep_bf = A.alloc([128, 32, 128], BF16, at=RB + big_off)
        for g4 in range(8):
            ps = PS[7]
            for gg in range(4):
                g = g4 * 4 + gg
                S.op("pe", lambda e, g=g, gg=gg: e.matmul(ps[:, gg * 128:(gg + 1) * 128], Bmr[:, g].rearrange("p i h -> p (i h)"),
                                                          Cmr[:, g].rearrange("p j h -> p (j h)"), start=True, stop=False),
                     reads=[P], writes=[PSB[7]], signal=False)
                S.op("pe", lambda e, g=g, gg=gg: e.matmul(ps[:, gg * 128:(gg + 1) * 128], Bmi[:, g].rearrange("p i h -> p (i h)"),
                                                          Cmi[:, g].rearrange("p j h -> p (j h)"), start=False, stop=True),
                     reads=[P], writes=[PSB[7]], signal=(gg == 3))
            S.op("dve", lambda e, g4=g4: e.tensor_tensor(
                out=toep_bf[:, g4 * 4:(g4 + 1) * 4, :], in0=ps[:].rearrange("p (a b) -> p a b", a=4),
                in1=bmaskS.unsqueeze(1).to_broadcast([128, 4, 128]), op=ALU.mult), reads=[PSB[7], P], writes=[P])
        S.dma("sp", scr_toep, toep_bf.rearrange("p g m -> p (g m)"), reads=[P])
        S.barrier()

    def mixer0(s):
        S.barrier()
        for i in range(3):
            S.dma("pool", winS[:, :, i * 512:(i + 1) * 512],
                  win_d.rearrange("(kc p) f -> p kc f", p=128)[:, :, i * 512:(i + 1) * 512], writes=[WINB[i]])
        S.dma("pool", identM, ident_d, writes=[IDMB])
        for ct in range(2):
            for k in range(31):
                S.op("dve", lambda e, ct=ct, k=k: e.tensor_scalar(out=diag[ct][:, k, :], in0=identM, scalar1=cw[:, ct, k:k + 1], scalar2=None,
                                                              op0=ALU.mult), reads=[IDMB, M0CB], writes=[DIAGB[ct]])
        rmsnorm_tile(4, 0, 4, hT2, H2B)
        S.op("pool", lambda e: e.memset(vpad[:, :, 0:30], 0.0), writes=VB)
        pc = 0
        for ct in range(4):
            for tb in range(4):
                pa, pg = (pc % 2) * 2, (pc % 2) * 2 + 1
                pc += 1
                for pi, oc in ((pa, ct), (pg, ct + 4)):
                    for kc in range(KC):
                        S.op("pe", lambda e, pi=pi, oc=oc, kc=kc, tb=tb: e.matmul(
                            PS[pi][:], winS[:, kc, oc * 128:(oc + 1) * 128], hT2[:, kc, tb * TT:(tb + 1) * TT],
                            start=(kc == 0), stop=(kc == KC - 1)),
                            reads=[WINB[oc // 4], H2B[kc][tb]], writes=[PSB[pi]], signal=(kc == KC - 1))
                sg = sil[pc % 2]
                S.op("act", lambda e, sg=sg, pg=pg: e.activation(out=sg, in_=PS[pg][:], func=AF.Sigmoid),
                     reads=[PSB[pg]], writes=[SILB[pc % 2]])
                S.op("dve", lambda e, sg=sg, pa=pa, ct=ct, tb=tb: e.tensor_tensor(
                    out=vpad[:, ct, 30 + tb * TT:30 + (tb + 1) * TT], in0=sg, in1=PS[pa][:], op=ALU.mult),
                    reads=[SILB[pc % 2], PSB[pa]], writes=[VB[ct]])
        for ct in range(4):
            for tb in range(4):
                pi = 4 + (pc % 2)
                pc += 1
                oc = 8 + ct
                for kc in range(KC):
                    S.op("pe", lambda e, pi=pi, oc=oc, kc=kc, tb=tb: e.matmul(
                        PS[pi][:], winS[:, kc, oc * 128:(oc + 1) * 128], hT2[:, kc, tb * TT:(tb + 1) * TT],
                        start=(kc == 0), stop=(kc == KC - 1)),
                        reads=[WINB[2], H2B[kc][tb]], writes=[PSB[pi]], signal=(kc == KC - 1))
                S.op("act", lambda e, pi=pi, ct=ct, tb=tb: e.activation(out=uT4[:, ct, :, tb * 64:(tb + 1) * 64],
                                                                       in_=PS[pi][:].rearrange("p (c i) -> p i c", i=8), func=AF.Copy),
                     reads=[PSB[pi]], writes=[UB[ct][tb]])
        S.barrier()
        for ct in range(2, 4):
            for k in range(31):
                S.op("dve", lambda e, ct=ct, k=k: e.tensor_scalar(out=diag[ct][:, k, :], in0=identM, scalar1=cw[:, ct, k:k + 1], scalar2=None,
                                                              op0=ALU.mult), reads=[IDMB, M0CB], writes=[DIAGB[ct]])
        lnb = [R(96 + 2 * i, [128, TT], F32) for i in range(7)]
        ybf = [R(110 + i, [128, TT], BF16) for i in range(2)]
        ysq = [R(112 + i, [128, TT], BF16) for i in range(2)]
        LNB = [Buf("lnb%d" % i) for i in range(7)]
        YBB = [Buf("ybf%d" % i) for i in range(2)]
        YSB = [Buf("ysq%d" % i) for i in range(2)]
        YC2 = [[Buf("yc%d_%d" % (c, t)) for t in range(4)] for c in range(4)]
        tiles = [(tb, ct) for tb in range(4) for ct in range(4)]

        def conv_tile(n):
            tb, ct = tiles[n]
            pi = 2 + (n % 2)
            j = n % 2
            for k in range(31):
                S.op("pe", lambda e, pi=pi, ct=ct, k=k, tb=tb: e.matmul(
                    PS[pi][:], diag[ct][:, k, :], vpad[:, ct, k + tb * TT:k + (tb + 1) * TT], start=(k == 0), stop=(k == 30)),
                    reads=[DIAGB[ct], VB[ct]], writes=[PSB[pi]], signal=(k == 30))
            bia = chv[:, 0, ct:ct + 1]
            S.op("act", lambda e, pi=pi, ct=ct, tb=tb: e.activation(out=ycv[:, ct, tb * TT:(tb + 1) * TT], in_=PS[pi][:], func=AF.Identity, bias=bia),
                 reads=[PSB[pi], M0CB], writes=[YC2[ct][tb]])
            S.op("act", lambda e, pi=pi, j=j: e.activation(out=ybf[j], in_=PS[pi][:], func=AF.Identity, bias=bia),
                 reads=[PSB[pi], M0CB], writes=[YBB[j]])
            S.op("act", lambda e, pi=pi, j=j: e.activation(out=ysq[j], in_=PS[pi][:], func=AF.Square, bias=bia),
                 reads=[PSB[pi], M0CB], writes=[YSB[j]])

        def stat_mm(n):
            tb, ct = tiles[n]
            j = n % 2
            sb_, qb_ = ((0, 1), (4, 5))[tb % 2]
            S.op("pe", lambda e: e.matmul(PS[sb_][:], ones_bf, ybf[j], start=(ct == 0), stop=(ct == 3)),
                 reads=[YBB[j], CONSTB], writes=[PSB[sb_]], signal=True)
            S.op("pe", lambda e: e.matmul(PS[qb_][:], ones_bf, ysq[j], start=(ct == 0), stop=(ct == 3)),
                 reads=[YSB[j], CONSTB], writes=[PSB[qb_]], signal=True)
            if ct == 3:
                ln_finish(tb)

        def ln_finish(tb):
            sb_, qb_ = ((0, 1), (4, 5))[tb % 2]
            mean, var, msq = lnb[2 + tb % 2], lnb[4 + tb % 2], lnb[6]
            MB_, VB_, QB_ = LNB[2 + tb % 2], LNB[4 + tb % 2], LNB[6]
            sl = slice(tb * TT, (tb + 1) * TT)
            S.op("act", lambda e: e.activation(out=mean, in_=PS[sb_][:], func=AF.Copy, scale=1.0 / 512), reads=[PSB[sb_]], writes=[MB_])
            S.op("act", lambda e: e.activation(out=msq, in_=PS[sb_][:], func=AF.Square, scale=1.0 / 512), reads=[PSB[sb_]], writes=[QB_])
            S.op("dve", lambda e: e.scalar_tensor_tensor(out=var, in0=PS[qb_][:], scalar=1.0 / 512, in1=msq, op0=ALU.mult, op1=ALU.subtract),
                 reads=[PSB[qb_], QB_], writes=[VB_])
            S.op("act", lambda e: e.activation(out=var, in_=var, func=AF.Sqrt, bias=lnepsc), reads=[VB_, M0CB], writes=[VB_])
            S.op("dve", lambda e: e.reciprocal(out=var, in_=var), reads=[VB_], writes=[VB_])
            for ct in range(4):
                t_ = lnb[ct % 2]
                TB_ = LNB[ct % 2]
                S.op("dve", lambda e, ct=ct, t_=t_: e.tensor_tensor(out=t_, in0=ycv[:, ct, sl], in1=mean, op=ALU.subtract),
                     reads=[YC2[ct][tb], MB_], writes=[TB_])
                S.op("dve", lambda e, t_=t_: e.tensor_tensor(out=t_, in0=t_, in1=var, op=ALU.mult),
                     reads=[TB_, VB_], writes=[TB_])
                S.op("act", lambda e, ct=ct, t_=t_: e.activation(out=mcat[:, ct, sl], in_=t_, func=AF.Silu,
                                                                scale=chv[:, 1, ct:ct + 1], bias=chv[:, 2, ct:ct + 1]),
                     reads=[TB_, M0CB], writes=[MCB[ct][tb]])

        conv_tile(0)
        for n in range(len(tiles)):
            if n + 1 < len(tiles):
                conv_tile(n + 1)
            stat_mm(n)
        dump("mcA", mcat[:, 0:4, :].rearrange("p a b -> p (a b)"), [b for r in MCB[0:4] for b in r])
        dump("u", uT.rearrange("p a b -> p (a b)"), [b for r in UB for b in r])
        S.barrier()
        S.dma("pool", selin.rearrange("p a b -> p (a b)"), selin_d, writes=[SELIB])
        S.dma("sp", wbS.rearrange("p o g n -> p (o g n)"), scr_wb, writes=[WBB])
        S.dma("sp", toepS.rearrange("p g m -> p (g m)"), scr_toep, writes=[TOEPB])
        S.dma("pool", selout.rearrange("p a b -> p (a b)"), selout_d, writes=[PPB])
        S.dma("pool", gluwS, gluw_d.rearrange("(kc p) f -> p kc f", p=128), writes=[PPB])
        scr_cm4 = scr_cm.rearrange("p (o g m) -> p o g m", o=2, g=32)
        scr_at4 = scr_at.rearrange("p (o k g) -> p o k g", o=2, k=2)
        for gh in range(2):
            for o in range(2):
                S.dma("sp", cmS[64 * gh:64 * gh + 64, o, :, :], scr_cm4[:, o, gh * 16:(gh + 1) * 16, :], writes=[CMB[gh * 2 + o]])
                for k in range(2):
                    S.dma("sp", atS[64 * gh:64 * gh + 64, o, k, :], scr_at4[:, o, k, gh * 16:(gh + 1) * 16], writes=[ATB[gh * 4 + o * 2 + k]])
        for g2 in range(16):
            pi = g2 % 2
            for gg in range(2):
                g = g2 * 2 + gg
                ct, g8 = g // 8, g % 8
                for i in range(8):
                    S.op("pe", lambda e, pi=pi, gg=gg, ct=ct, g8=g8, i=i: e.matmul(
                        PS[pi][:, gg * 256:(gg + 1) * 256], selin[:, g8, 112 - 16 * i:240 - 16 * i], uT4[:, ct, i, :],
                        start=(i == 0), stop=(i == 7)),
                        reads=[SELIB] + UB[ct], writes=[PSB[pi]], signal=(gg == 1 and i == 7))
            S.op("act", lambda e, pi=pi, g2=g2: e.activation(out=Ut[:, g2 * 2:g2 * 2 + 2, :],
                                                            in_=PS[pi][:].rearrange("p (a b) -> p a b", a=2), func=AF.Copy),
                 reads=[PSB[pi]], writes=[UTB[g2 * 2], UTB[g2 * 2 + 1]])
        S.op("dve", lambda e: e.memset(Xh[:, 31], 0.0), writes=[XHB[31]])
        for cb in range(16):
            pi = 2 + (cb % 2)
            psv = PS[pi][:].rearrange("p (c o g) -> p c o g", c=16, o=2)
            for g in range(32):
                gh, gl = g // 16, g % 16
                for o in range(2):
                    S.op("pe", lambda e, psv=psv, g=g, gh=gh, gl=gl, o=o, cb=cb: e.matmul(
                        psv[64 * gh:64 * gh + 64, :, o, gl], wbS[:, o, g, :], Ut[:, g, cb * 16:(cb + 1) * 16], start=True, stop=True),
                        reads=[WBB, UTB[g]], writes=[PSB[pi]], signal=(g == 31 and o == 1))
            for cc in range(16):
                c = cb * 16 + cc
                prev = Xh[:, (c - 1) % 32]
                S.op("dve", lambda e, prev=prev: e.tensor_tensor(
                    out=pq, in0=prev.unsqueeze(1).to_broadcast([128, 2, 2, 16]), in1=atS, op=ALU.mult),
                    reads=[XHB[(c - 1) % 32]] + ATB, writes=[PQB])
                S.op("dve", lambda e: e.tensor_tensor(out=rtmp, in0=pq[:, :, 0, :], in1=pq[:, :, 1, :], op=ALU.add),
                     reads=[PQB], writes=[RTB])
                S.op("dve", lambda e, psv=psv, cc=cc, c=c: e.tensor_tensor(out=Xh[:, c % 32], in0=rtmp, in1=psv[:, cc], op=ALU.add),
                     reads=[RTB, PSB[pi]], writes=[XHB[c % 32]])
            half = (cb % 2) * 16
            S.op("act", lambda e, cb=cb, half=half: e.activation(
                out=Xbf[:, :, :, cb * 16:(cb + 1) * 16], in_=Xh[:, half:half + 16].rearrange("p c o g -> p o g c"), func=AF.Copy),
                reads=XHB[half:half + 16], writes=[XBFB])
        for g2 in range(16):
            pi = 4 + (g2 % 2)
            for gg in range(2):
                g = g2 * 2 + gg
                S.op("pe", lambda e, pi=pi, gg=gg, g=g: e.matmul(PS[pi][:, gg * 256:(gg + 1) * 256], toepS[:, g, :], Ut[:, g, :],
                                                                start=True, stop=False),
                     reads=[TOEPB, UTB[g]], writes=[PSB[pi]], signal=False)
                hs = slice(64 * (g // 16), 64 * (g // 16) + 64)
                gl = g % 16
                S.op("pe", lambda e, pi=pi, gg=gg, gl=gl, hs=hs: e.matmul(PS[pi][:, gg * 256 + 1:(gg + 1) * 256], cmS[hs, 0, gl, :], Xbf[hs, 0, gl, 0:255],
                                                                start=False, stop=False),
                     reads=CMB + [XBFB], writes=[PSB[pi]], signal=False)
                S.op("pe", lambda e, pi=pi, gg=gg, gl=gl, hs=hs: e.matmul(PS[pi][:, gg * 256 + 1:(gg + 1) * 256], cmS[hs, 1, gl, :], Xbf[hs, 1, gl, 0:255],
                                                                start=False, stop=True),
                     reads=CMB + [XBFB], writes=[PSB[pi]], signal=(gg == 1))
            S.op("act", lambda e, pi=pi, g2=g2: e.activation(out=Ut[:, g2 * 2:g2 * 2 + 2, :],
                                                            in_=PS[pi][:].rearrange("p (a b) -> p a b", a=2), func=AF.Copy),
                 reads=[PSB[pi]], writes=[UTB[g2 * 2], UTB[g2 * 2 + 1]])
        dump("toep", toepS.rearrange("p g m -> p (g m)"), [TOEPB])
        dump("wb", wbS.rearrange("p o g n -> p (o g n)"), [WBB])
        dump("Y", Ut.rearrange("p g c -> p (g c)"), UTB)
        S.barrier()
        S.dma("pool", woutS, wout_d.rearrange("(kc p) f -> p kc f", p=128), writes=[WOUTB])
        pc = 0
        for ct in range(4):
            for tb in range(4):
                pi = pc % 2
                pc += 1
                for j in range(8):
                    for g8 in range(8):
                        S.op("pe", lambda e, pi=pi, j=j, g8=g8, ct=ct, tb=tb: e.matmul(
                            PS[pi][:, j * 64:(j + 1) * 64], selout[:, j, 112 - 16 * g8:240 - 16 * g8],
                            Ut[:, ct * 8 + g8, tb * 64:(tb + 1) * 64], start=(g8 == 0), stop=(g8 == 7)),
                            reads=[PPB, UTB[ct * 8 + g8]], writes=[PSB[pi]], signal=(j == 7 and g8 == 7))
                ys, x2, x3, sg_ = gt[0], gt[1], gt[2], gt[3]
                sl = slice(tb * TT, (tb + 1) * TT)
                S.op("dve", lambda e, pi=pi, ct=ct, sl=sl: e.scalar_tensor_tensor(
                    out=ys.rearrange("p (c j) -> p c j", j=8), in0=uT4[:, ct, :, tb * 64:(tb + 1) * 64].rearrange("p j c -> p c j"),
                    scalar=chv[:, 3, ct:ct + 1], in1=PS[pi][:].rearrange("p (j c) -> p c j", j=8), op0=ALU.mult, op1=ALU.add),
                    reads=[PSB[pi], UB[ct][tb], M0CB], writes=[GTB[0]])
                S.op("act", lambda e: e.activation(out=x2, in_=ys, func=AF.Square), reads=[GTB[0]], writes=[GTB[1]])
                S.op("dve", lambda e: e.tensor_scalar(out=x2, in0=x2, scalar1=0.044715, scalar2=1.0, op0=ALU.mult, op1=ALU.add),
                     reads=[GTB[1]], writes=[GTB[1]])
                S.op("dve", lambda e: e.tensor_tensor(out=x3, in0=x2, in1=ys, op=ALU.mult), reads=[GTB[1], GTB[0]], writes=[GTB[2]])
                S.op("act", lambda e: e.activation(out=sg_, in_=x3, func=AF.Sigmoid, scale=1.5957691216057308), reads=[GTB[2]], writes=[GTB[3]])
                S.op("dve", lambda e, ct=ct, sl=sl: e.tensor_tensor(out=y1bS[:, ct, sl], in0=ys, in1=sg_, op=ALU.mult),
                     reads=[GTB[0], GTB[3]], writes=[Y1B[ct][tb]])
        for ot in range(4):
            for tb in range(4):
                pi = 2 + (pc % 2)
                pc += 1
                sl = slice(tb * TT, (tb + 1) * TT)
                for ct in range(4):
                    S.op("pe", lambda e, pi=pi, ct=ct, ot=ot, sl=sl: e.matmul(
                        PS[pi][:], gluwS[:, ct, ot * 128:(ot + 1) * 128], y1bS[:, ct, sl], start=(ct == 0), stop=(ct == 3)),
                        reads=[PPB, Y1B[ct][tb]], writes=[PSB[pi]], signal=(ct == 3))
                sg = sgt[pc % 2]
                S.op("act", lambda e, pi=pi, sg=sg, ot=ot: e.activation(out=sg, in_=PS[pi][:], func=AF.Sigmoid, bias=chv[:, 4, ot:ot + 1]),
                     reads=[PSB[pi], M0CB], writes=[SGTB[pc % 2]])
                S.op("dve", lambda e, sg=sg, ot=ot, sl=sl: e.tensor_tensor(out=mcat[:, 4 + ot, sl], in0=y1bS[:, ot, sl], in1=sg, op=ALU.mult),
                     reads=[SGTB[pc % 2], Y1B[ot][tb]], writes=[MCB[4 + ot][tb]])
        dump("mcB", mcat[:, 4:8, :].rearrange("p a b -> p (a b)"), [b for r in MCB[4:8] for b in r])
        dump("y1", y1bS.rearrange("p a b -> p (a b)"), [b for r in Y1B for b in r])
        for dc in range(KC):
            for tb in range(4):
                pi = 4 + (pc % 2)
                pc += 1
                sl = slice(tb * TT, (tb + 1) * TT)
                for mc in range(8):
                    S.op("pe", lambda e, pi=pi, mc=mc, dc=dc, sl=sl: e.matmul(
                        PS[pi][:], woutS[:, mc, dc * 128:(dc + 1) * 128], mcat[:, mc, sl], start=(mc == 0), stop=(mc == 7)),
                        reads=[WOUTB, MCB[mc][tb]], writes=[PSB[pi]], signal=(mc == 7))
                xs = xT[:, dc, sl]
                S.op("dve", lambda e, pi=pi, xs=xs: e.tensor_tensor(out=xs, in0=PS[pi][:], in1=xs, op=ALU.add),
                     reads=[PSB[pi], XB[dc][tb]], writes=[XB[dc][tb]])
        S.barrier()

    wqkv_d = din("w_qkv", [D, 3 * D])
    wo_d = din("w_o", [D, D])
    cbias_d = din("cbias", [128, 2 * 256])
    en_d = din("en", [8, 8 * 128])
    negm_d = din("negm", [128, 4 * 64])
    NEG = -30000.0
    ahT = R(0, [128, KC, L], BF16)
    qring = [R(32 + 8 * i, [128, KC, 512], BF16) for i in range(2)]
    qT = R(48, [128, 4, L], BF16)
    kT = R(64, [128, 4, L], BF16)
    Vh = R(80, [128, 16, 512], BF16)
    oT = R(96, [128, 4, L], BF16)
    woS = R(112, [128, 4, D], BF16)
    pT = [R(120 + i, [128, 2, 256], BF16) for i in range(2)]
    cbiasS = R(122, [128, 2, 256], BF16)
    EnS = R(123, [8, 8, 128], BF16)
    identA = R(125, [128, 128], BF16)
    negmS = R(125.5, [128, 4, 64], F32)
    rden = R(126.5, [128, 256], F32)
    mbT = [R(127.5 + 2 * i, [8, 4, 256], BF16) for i in range(2)]
    kmf = R(131.5, [128, 4, 8], F32)
    kmT = R(131.75, [128, 4, 8], BF16)
    gsb = R(132, [128, 32], F32)
    top8 = R(132.25, [128, 8], F32)
    mball = R(132.5, [128, 8, 32], BF16)
    rden2 = R(133, [128, 256], F32)
    rdens = [rden, rden2]
    RDBS = [Buf("rden0"), Buf("rden1")]
    mbTall = R(32, [8, 4, 4, 256], BF16)
    assert RB + int(134 * KB) <= A.nbytes

    AHB = [[Buf("ah%d_%d" % (k, t)) for t in range(4)] for k in range(KC)]
    QRB = [Buf("qring%d" % i) for i in range(2)]
    QTB = [Buf("qT%d" % h) for h in range(4)]
    KTB = [Buf("kT%d" % h) for h in range(4)]
    VHB = [Buf("vh%d" % k) for k in range(16)]
    OTB = [[Buf("oT%d_%d" % (h, q)) for q in range(8)] for h in range(4)]
    WOB = Buf("woS")
    PTB = [Buf("pT%d" % i) for i in range(2)]
    ACB = Buf("attnconst")
    RDB = Buf("rden")
    MBTB = [Buf("mbT%d" % i) for i in range(2)]
    KMB = Buf("km")
    GSB = Buf("gsb")
    T8B = Buf("top8")
    MBB = Buf("mb")
    SCALE = 128.0 ** -0.5

    def mixer1(s):
        S.barrier()
        S.dma("pool", cbiasS.rearrange("p a b -> p (a b)"), cbias_d, writes=[ACB])
        S.dma("pool", EnS.rearrange("p a b -> p (a b)"), en_d, writes=[ACB])
        S.dma("pool", identA, ident_d, writes=[ACB])
        S.dma("sp", negmS.rearrange("p a b -> p (a b)"), negm_d, writes=[ACB])
        rmsnorm_tile(5, 0, 4, ahT, AHB)
        nring = [0]
        pcs = [0]

        def load_cols(c0):
            i = nring[0] % 2
            nring[0] += 1
            S.dma("pool", qring[i], wqkv_d.rearrange("(kc p) f -> p kc f", p=128)[:, :, c0:c0 + 512], writes=[QRB[i]])
            return qring[i], QRB[i]

        for half in range(2):
            for which, dstT, dstB in ((0, qT, QTB), (1, kT, KTB)):
                if which == 0 and half == 1:
                    wsl, wb_ = pre_q
                else:
                    wsl, wb_ = load_cols(which * D + half * 512)
                for hl in range(4):
                    for tb in range(4):
                        pi = pcs[0] % 2
                        pcs[0] += 1
                        for kc in range(KC):
                            S.op("pe", lambda e, pi=pi, wsl=wsl, kc=kc, hl=hl, tb=tb: e.matmul(
                                PS[pi][:], wsl[:, kc, hl * 128:(hl + 1) * 128], ahT[:, kc, tb * TT:(tb + 1) * TT],
                                start=(kc == 0), stop=(kc == KC - 1)),
                                reads=[wb_, AHB[kc][tb]], writes=[PSB[pi]], signal=(kc == KC - 1))
                        S.op("act", lambda e, pi=pi, dstT=dstT, hl=hl, tb=tb: e.activation(
                            out=dstT[:, hl, tb * TT:(tb + 1) * TT], in_=PS[pi][:], func=AF.Copy),
                            reads=[PSB[pi]], writes=[dstB[hl]])
            wsl, wb_ = load_cols(2 * D + half * 512)
            for kt in range(16):
                pi = pcs[0] % 2
                pcs[0] += 1
                tb = kt // 4
                for kc in range(KC):
                    S.op("pe", lambda e, pi=pi, wsl=wsl, kc=kc, kt=kt: e.matmul(
                        PS[pi][:], ahT[:, kc, kt * 128:(kt + 1) * 128], wsl[:, kc, :], start=(kc == 0), stop=(kc == KC - 1)),
                        reads=[wb_, AHB[kc][tb]], writes=[PSB[pi]], signal=(kc == KC - 1))
                S.op("act", lambda e, pi=pi, kt=kt: e.activation(out=Vh[:, kt, :], in_=PS[pi][:], func=AF.Copy),
                     reads=[PSB[pi]], writes=[VHB[kt]])
            S.dma("pool", woS, wo_d.rearrange("(kc p) f -> p kc f", p=128)[:, half * 4:(half + 1) * 4, :], writes=[WOB])
            if half == 0:
                pre_q = load_cols(0 * D + 1 * 512)
            for hl in range(4):
                S.op("dve", lambda e, hl=hl: e.tensor_reduce(out=kmf[:, hl, :], in_=kT[:, hl, :].rearrange("p (n k) -> p n k", n=8),
                                                            axis=AX.X, op=ALU.add), reads=[KTB[hl]], writes=[KMB])
            S.op("dve", lambda e: e.tensor_copy(out=kmT, in_=kmf), reads=[KMB], writes=[KMB])
            for qb in range(4, 8):
                for q2 in range(2):
                    idx = (qb - 4) * 2 + q2
                    qsl = slice(qb * 256 + q2 * 128, qb * 256 + (q2 + 1) * 128)
                    for hl in range(4):
                        S.op("pe", lambda e, hl=hl, qsl=qsl, idx=idx: e.matmul(PS[6][:, idx * 32 + hl * 8:idx * 32 + (hl + 1) * 8], qT[:, hl, qsl],
                                                                              kmT[:, hl, :], start=True, stop=True),
                             reads=[QTB[hl], KMB], writes=[PSB[6]], signal=(hl == 3))
            for qb in range(4, 8):
                for q2 in range(2):
                    idx = (qb - 4) * 2 + q2
                    S.op("dve", lambda e, idx=idx, qb=qb: e.tensor_tensor(out=gsb, in0=PS[6][:, idx * 32:(idx + 1) * 32], in1=negmS[:, qb - 4, 0:32], op=ALU.add),
                         reads=[PSB[6], ACB], writes=[GSB])
                    for hl in range(4):
                        S.op("dve", lambda e, hl=hl: e.max(out=top8, in_=gsb[:, hl * 8:(hl + 1) * 8]), reads=[GSB], writes=[T8B])
                        S.op("dve", lambda e, hl=hl, idx=idx: e.tensor_scalar(out=mball[:, idx, hl * 8:(hl + 1) * 8], in0=gsb[:, hl * 8:(hl + 1) * 8],
                                                                    scalar1=top8[:, 2:3], scalar2=NEG, op0=ALU.is_lt, op1=ALU.mult),
                             reads=[GSB, T8B], writes=[MBB])

            def mask_transposes():
                for qb in range(4, 8):
                    for q2 in range(2):
                        idx = (qb - 4) * 2 + q2
                        tbk = 6 + (idx % 2)
                        for hl in range(4):
                            S.op("pe", lambda e, hl=hl, idx=idx, tbk=tbk: e.matmul(PS[tbk][0:8, hl * 128:(hl + 1) * 128], mball[:, idx, hl * 8:(hl + 1) * 8], identA,
                                                                         start=True, stop=True),
                                 reads=[MBB, ACB], writes=[PSB[tbk]], signal=(hl == 3))
                        S.op("act", lambda e, qb=qb, q2=q2, tbk=tbk: e.activation(out=mbTall[:, qb - 4, :, q2 * 128:(q2 + 1) * 128],
                                                                        in_=PS[tbk][0:8, :].rearrange("p (h q) -> p h q", h=4), func=AF.Copy),
                             reads=[PSB[tbk]], writes=[MBTB[0]])

            items = []
            for qb in range(8):
                for hl in range(4):
                    npair = qb + 1
                    for kp in range(npair):
                        items.append((qb, hl, kp, kp == 0, kp == npair - 1))

            def emit_S(i):
                qb, hl, kp, _, _ = items[i]
                qs = slice(qb * 256, (qb + 1) * 256)
                gated = qb >= 4
                pi = 2 + (i % 2)
                for k2 in range(2):
                    kt = kp * 2 + k2
                    n = kp
                    osl = PS[pi][:, k2 * 256:(k2 + 1) * 256]
                    extra = (n == qb) or gated
                    S.op("pe", lambda e, osl=osl, hl=hl, kt=kt, extra=extra, qs=qs: e.matmul(
                        osl, kT[:, hl, kt * 128:(kt + 1) * 128], qT[:, hl, qs], start=True, stop=(not extra)),
                        reads=[KTB[hl], QTB[hl]], writes=[PSB[pi]], signal=((not extra) and k2 == 1))
                    if n == qb:
                        S.op("pe", lambda e, osl=osl, k2=k2: e.matmul(osl, identA, cbiasS[:, k2, :], start=False, stop=True),
                             reads=[ACB], writes=[PSB[pi]], signal=(k2 == 1))
                    elif gated:
                        S.op("pe", lambda e, osl=osl, n=n, hl=hl, qb=qb: e.matmul(osl, EnS[:, n, :], mbTall[:, qb - 4, hl, :], start=False, stop=True),
                             reads=[ACB, MBTB[0]], writes=[PSB[pi]], signal=(k2 == 1))

            def emit_rest(i, gi):
                qb, hl, kp, first, last = items[i]
                qs = slice(qb * 256, (qb + 1) * 256)
                pi = 2 + (i % 2)
                ti = i % 2
                po, pd = ((4, 5), (0, 1))[gi % 2]
                S.op("act", lambda e, pi=pi, ti=ti: e.activation(out=pT[ti].rearrange("p a b -> p (a b)"), in_=PS[pi][:],
                                                                func=AF.Exp, scale=SCALE),
                     reads=[PSB[pi]], writes=[PTB[ti]])
                for k2 in range(2):
                    kt = kp * 2 + k2
                    f_ = first and k2 == 0
                    l_ = last and k2 == 1
                    S.op("pe", lambda e, po=po, kt=kt, hl=hl, ti=ti, k2=k2, f_=f_, l_=l_: e.matmul(
                        PS[po][:, 0:256], Vh[:, kt, hl * 128:(hl + 1) * 128], pT[ti][:, k2, :], start=f_, stop=l_),
                        reads=[VHB[kt], PTB[ti]], writes=[PSB[po]], signal=False)
                    S.op("pe", lambda e, pd=pd, ti=ti, k2=k2, f_=f_, l_=l_: e.matmul(
                        PS[pd][:, 0:256], ones_bf, pT[ti][:, k2, :], start=f_, stop=l_),
                        reads=[CONSTB, PTB[ti]], writes=[PSB[pd]], signal=(k2 == 1))
                if last:
                    rd = rdens[gi % 2]
                    S.op("dve", lambda e, pd=pd, rd=rd: e.reciprocal(out=rd, in_=PS[pd][:, 0:256]), reads=[PSB[pd]], writes=[RDBS[gi % 2]])
                    S.op("dve", lambda e, po=po, hl=hl, rd=rd, qs=qs: e.tensor_tensor(out=oT[:, hl, qs], in0=PS[po][:, 0:256], in1=rd, op=ALU.mult),
                         reads=[PSB[po], RDBS[gi % 2]], writes=[OTB[hl][qb]])

            n_items = len(items)
            first_gated = next(i for i, it in enumerate(items) if it[0] >= 4)
            gi = 0
            emit_S(0)
            for i in range(n_items):
                if i + 1 < n_items:
                    if i + 1 == first_gated:
                        mask_transposes()
                    emit_S(i + 1)
                emit_rest(i, gi)
                if items[i][4]:
                    gi += 1
            if half == 0:
                dump("oT0", oT.rearrange("p a b -> p (a b)"), [b for r_ in OTB for b in r_])
                dump("qT0", qT.rearrange("p a b -> p (a b)"), QTB)
                dump("kT0", kT.rearrange("p a b -> p (a b)"), KTB)
                dump("Vh0", Vh.rearrange("p a b -> p (a b)"), VHB)
            for dc in range(KC):
                for tb in range(4):
                    pi = pcs[0] % 2
                    pcs[0] += 1
                    sl = slice(tb * TT, (tb + 1) * TT)
                    for hl in range(4):
                        S.op("pe", lambda e, pi=pi, hl=hl, dc=dc, sl=sl: e.matmul(
                            PS[pi][:], woS[:, hl, dc * 128:(dc + 1) * 128], oT[:, hl, sl], start=(hl == 0), stop=(hl == 3)),
                            reads=[WOB, OTB[hl][2 * tb], OTB[hl][2 * tb + 1]], writes=[PSB[pi]], signal=(hl == 3))
                    xs = xT[:, dc, sl]
                    S.op("dve", lambda e, pi=pi, xs=xs: e.tensor_tensor(out=xs, in0=PS[pi][:], in1=xs, op=ALU.add),
                         reads=[PSB[pi], XB[dc][tb]], writes=[XB[dc][tb]])
            S.barrier()

    OUTB = [Buf("o%d" % i) for i in range(2)]
    for s in range(nseq):
        def load_blocks(sq, blks):
            for blk in blks:
                for kc in range(KC):
                    S.dma("sp", xT[:, kc, blk * TT:(blk + 1) * TT], xT_d[sq, kc * 128:(kc + 1) * 128, blk * TT:(blk + 1) * TT],
                          writes=[XB[kc][blk]])

        if s == 0 or "final" not in stages:
            load_blocks(s, range(L // TT))
        if s == 0 and "mix0" in stages:
            ssm_param_prep()
        if "ffn00" in stages:
            ffn_chain([(0, 0, 0), (0, 0, 1024)])
        if "mix0" in stages:
            mixer0(s)
        if "ffn01" in stages and "ffn10" in stages:
            ffn_chain([(1, 1, 0), (1, 1, 1024), (2, 2, 0), (2, 2, 1024)])
        else:
            if "ffn01" in stages:
                ffn_chain([(1, 1, 0), (1, 1, 1024)])
            if "ffn10" in stages:
                ffn_chain([(2, 2, 0), (2, 2, 1024)])
        if "mix1" in stages:
            mixer1(s)
        def final_tile(t0, s=s):
            rmsnorm_tile(6, t0, 2, hT, HB, inplace=True)
            for blk in (t0 // TT, t0 // TT + 1):
                for kc in range(KC):
                    S.dma("sp", outT_d[s, kc * 128:(kc + 1) * 128, blk * TT:(blk + 1) * TT], xT[:, kc, blk * TT:(blk + 1) * TT],
                          reads=[XB[kc][blk]])
            if s + 1 < nseq:
                load_blocks(s + 1, (t0 // TT, t0 // TT + 1))

        fused_tail = ("ffn11" in stages) and ("final" in stages)
        if "ffn11" in stages:
            ffn_chain([(3, 3, 0), (3, 3, 1024)], last_hook=(lambda: final_tile(0)) if fused_tail else None)
        if "final" in stages:
            for t0 in ((1024,) if fused_tail else (0, 1024)):
                final_tile(t0)
        if "final" not in stages:
            for blk in range(L // TT):
                for kc in range(KC):
                    S.dma("sp", outT_d[s, kc * 128:(kc + 1) * 128, blk * TT:(blk + 1) * TT], xT[:, kc, blk * TT:(blk + 1) * TT],
                          reads=[XB[kc][blk]])
    S.wait_all("sp", [b for row in XB for b in row])
    nc._sched_ninst = S.ninst
    nc._dbg_names = dbg_names
    return nc


def _prep_inputs(inputs):
    f = np.float32
    x = np.asarray(inputs["x"], f)
    ffn_norm = np.asarray(inputs["ffn_norm"], f)
    vecs = [ffn_norm[0, 0], ffn_norm[0, 1], ffn_norm[1, 0], ffn_norm[1, 1],
            np.asarray(inputs["mix_norm"], f)[0], np.asarray(inputs["mix_norm"], f)[1],
            np.asarray(inputs["final_norm"], f)]
    gains = np.stack([v.reshape(KC, 128).T for v in vecs], axis=1)
    gains = np.ascontiguousarray(gains.reshape(128, 7 * KC))
    shared = {
        "gains": gains,
        "w1": np.ascontiguousarray(np.asarray(inputs["ffn_w1"], f).reshape(4, D, FF)),
        "w3": np.ascontiguousarray(np.asarray(inputs["ffn_w3"], f).reshape(4, D, FF)),
        "w2": np.ascontiguousarray(np.asarray(inputs["ffn_w2"], f).reshape(4, FF, D)),
    }
    shared["w_in"] = np.ascontiguousarray(np.asarray(inputs["ab_w_in"], f)[0])
    shared["w_out"] = np.ascontiguousarray(np.asarray(inputs["ab_w_out"], f)[0])
    shared["glu_w"] = np.ascontiguousarray(np.asarray(inputs["ssm_glu_w"], f)[0])
    cwm = np.asarray(inputs["conv_w"], f)[0]
    shared["cw"] = np.ascontiguousarray(cwm.T.reshape(4, 128, 31).transpose(1, 0, 2).reshape(128, 4 * 31))
    chv = [np.asarray(inputs[k], f)[0] for k in ("conv_b", "conv_ln_g", "conv_ln_b", "ssm_d", "ssm_glu_b")]
    shared["chv"] = np.ascontiguousarray(np.stack([v.reshape(4, 128).T for v in chv], axis=1).reshape(128, 20))
    ldt = np.broadcast_to(np.asarray(inputs["ssm_log_dt"], f)[0][None, :], (64, 32))
    are = np.asarray(inputs["ssm_a_re"], f)[0].T
    aim = np.asarray(inputs["ssm_a_im"], f)[0].T
    shared["sp1"] = np.ascontiguousarray(np.stack([ldt, are, aim], axis=1).reshape(64, 96))
    bre = np.asarray(inputs["ssm_b_re"], f)[0].transpose(1, 0, 2).reshape(64, 512)
    bim = np.asarray(inputs["ssm_b_im"], f)[0].transpose(1, 0, 2).reshape(64, 512)
    cre = np.asarray(inputs["ssm_c_re"], f)[0].transpose(2, 0, 1).reshape(64, 512)
    cim = np.asarray(inputs["ssm_c_im"], f)[0].transpose(2, 0, 1).reshape(64, 512)
    shared["sp2"] = np.ascontiguousarray(np.stack([bre, bim, cre, cim], axis=1).reshape(64, 2048))
    shared["w_qkv"] = np.ascontiguousarray(np.asarray(inputs["attn_w_qkv"], f)[0])
    shared["w_o"] = np.ascontiguousarray(np.asarray(inputs["attn_w_o"], f)[0])
    shared.update(_const_tables())
    in_maps = []
    for c in range(NCORES):
        m = dict(shared)
        m["xT"] = np.ascontiguousarray(x[2 * c:2 * c + 2].transpose(0, 2, 1))
        in_maps.append(m)
    return in_maps


def _const_tables():
    f = np.float32
    ident = np.eye(128, dtype=f)
    p = np.arange(128)
    bmask = (p[None, :] // 16 >= p[:, None] // 16).astype(f)
    sel = np.zeros((128, 8, 240), f)
    for g8 in range(8):
        for h in range(16):
            sel[16 * g8 + h, g8, 112 + h] = 1.0
    cb = np.zeros((128, 2, 256), f)
    for par in range(2):
        cb[:, par, :] = np.where((par * 128 + p[:, None]) <= np.arange(256)[None, :], 0.0, -30000.0)
    en = np.zeros((8, 8, 128), f)
    for n in range(8):
        en[n, n, :] = 1.0
    negm = np.zeros((128, 4, 64), f)
    for qb in range(4, 8):
        for h in range(8):
            negm[:, qb - 4, h * 8 + qb:h * 8 + 8] = -1e30
    return {"cbias": np.ascontiguousarray(cb.reshape(128, 512)), "en": np.ascontiguousarray(en.reshape(8, 1024)),
            "negm": np.ascontiguousarray(negm.reshape(128, 256)),
            "ident": ident, "bmask": bmask, "selin": np.ascontiguousarray(sel.reshape(128, 1920)),
            "selout": np.ascontiguousarray(sel.reshape(128, 1920))}


_NC_CACHE = {}


def kernel(**inputs):
    in_maps = _prep_inputs(inputs)
    if "nc" not in _NC_CACHE:
        _NC_CACHE["nc"] = build_program()
    nc = _NC_CACHE["nc"]
    res = run_bass_kernel_spmd(nc, in_maps, core_ids=list(range(NCORES)))
    out = np.empty((2 * NCORES, L, D), np.float32)
    for c in range(NCORES):
        out[2 * c:2 * c + 2] = np.asarray(res.results[c]["outT"]).transpose(0, 2, 1)
    return out
```

```python
import numpy as np
import concourse.bass as bass
import concourse.mybir as mybir
from concourse.bass_utils import run_bass_kernel_spmd

F32 = mybir.dt.float32
BF16 = mybir.dt.bfloat16
AF = mybir.ActivationFunctionType
ALU = mybir.AluOpType
AX = mybir.AxisListType

D = 1024
KC = 8
FF = 2816
FC = 22
L = 2048
NSEQ = 2
TT = 512
RMS_EPS = 1e-6
LN_EPS = 1e-5
NCORES = 8


class Buf:
    __slots__ = ("name", "w", "r", "dsem", "dcnt")

    def __init__(self, name):
        self.name = name
        self.w = None
        self.r = {}
        self.dsem = {}
        self.dcnt = {}


class Sched:
    def __init__(self, nc):
        self.nc = nc
        self.eng = {"pe": nc.tensor, "act": nc.scalar, "dve": nc.vector, "pool": nc.gpsimd, "sp": nc.sync}
        self.sem = {k: nc.alloc_semaphore("s_" + k) for k in self.eng}
        self.cnt = {k: 0 for k in self.eng}
        self.seen = {k: {} for k in self.eng}
        self.pr = {k: [] for k in self.eng}
        self.pw = {k: [] for k in self.eng}
        self.nds = 0
        self.ninst = 0
        self.dbufs = []

    def _deps(self, reads, writes):
        deps = {}

        def need(s, v):
            if deps.get(s, 0) < v:
                deps[s] = v

        for b in reads:
            if b.w is not None:
                need(*b.w)
        for b in writes:
            if b.w is not None:
                need(*b.w)
            for s, v in b.r.items():
                need(s, v)
        return deps

    def _wait(self, eng, deps):
        e = self.eng[eng]
        own = self.sem[eng]
        for s, v in deps.items():
            if s is own and eng == "pe":
                continue
            if self.seen[eng].get(s, 0) >= v:
                continue
            e.wait_ge(s, v)
            self.seen[eng][s] = v

    def op(self, eng, fn, reads=(), writes=(), signal=True):
        self._wait(eng, self._deps(reads, writes))
        ins = fn(self.eng[eng])
        self.ninst += 1
        self.pr[eng].extend(reads)
        self.pw[eng].extend(writes)
        if signal:
            own = self.sem[eng]
            self.cnt[eng] += 1
            ins.then_inc(own, 1)
            v = self.cnt[eng]
            for b in self.pr[eng]:
                b.r[own] = v
            for b in self.pw[eng]:
                b.w = (own, v)
                b.r = {}
            self.pr[eng] = []
            self.pw[eng] = []
        return ins

    def dma(self, q, out, in_, reads=(), writes=()):
        self._wait(q, self._deps(reads, writes))
        owner = writes[0] if writes else reads[0]
        kind = "sw" if q == "pool" else "hw"
        if kind not in owner.dsem:
            owner.dsem[kind] = self.nc.alloc_semaphore("d%d" % self.nds)
            owner.dcnt[kind] = 0
            self.nds += 1
            self.dbufs.append((owner, kind))
        owner.dcnt[kind] += 1
        sem = owner.dsem[kind]
        self.eng[q].dma_start(out=out, in_=in_).then_inc(sem, 16)
        self.ninst += 1
        tag = (sem, 16 * owner.dcnt[kind])
        for b in reads:
            b.r[tag[0]] = tag[1]
        for b in writes:
            b.w = tag
            b.r = {}

    def barrier(self):
        for k in self.eng:
            assert not self.pr[k] and not self.pw[k], "unsignalled ops pending on " + k
        for k in self.eng:
            deps = {self.sem[o]: self.cnt[o] for o in self.eng if o != k and self.cnt[o] > 0}
            for b, kind in self.dbufs:
                deps[b.dsem[kind]] = 16 * b.dcnt[kind]
            self._wait(k, deps)

    def wait_all(self, eng, bufs):
        deps = {}
        for b in bufs:
            if b.w is not None:
                deps[b.w[0]] = max(deps.get(b.w[0], 0), b.w[1])
            for s, v in b.r.items():
                deps[s] = max(deps.get(s, 0), v)
        self._wait(eng, deps)


class Arena:
    def __init__(self, nc, nbytes):
        self.t = nc.alloc_sbuf_tensor("arena", [128, nbytes // 2], BF16)
        self.nbytes = nbytes
        self.off = 0
        self.marks = []

    def alloc(self, shape, dtype, at=None):
        esz = 2 if dtype == BF16 else 4
        n = int(np.prod(shape[1:]))
        nb = n * esz
        off = self.off if at is None else at
        off = (off + 31) // 32 * 32
        assert off + nb <= self.nbytes, ("arena overflow", off, nb, self.nbytes)
        if at is None:
            self.off = off + nb
        ap = self.t[0:shape[0], off // 2:(off + nb) // 2]
        if dtype != BF16:
            ap = ap.bitcast(dtype)
        if len(shape) == 3:
            ap = ap.rearrange("p (a b) -> p a b", a=shape[1])
        elif len(shape) == 4:
            ap = ap.rearrange("p (a b c) -> p a b c", a=shape[1], b=shape[2])
        return ap


def build_program(stages=("ffn00", "mix0", "ffn01", "ffn10", "mix1", "ffn11", "final"), nseq=NSEQ, debug=False):
    nc = bass.Bass("TRN2", target_bir_lowering=False)
    S = Sched(nc)
    dbg_names = []

    def dump(name, ap2d, bufs):
        if not debug or name in dbg_names:
            return
        dbg_names.append(name)
        shp = list(ap2d.shape)
        dt = nc.dram_tensor("dbg_" + name, shp, F32, kind="ExternalOutput").ap()
        S.dma("pool", dt, ap2d, reads=bufs)

    def din(name, shape, dt=F32):
        return nc.dram_tensor(name, list(shape), dt, kind="ExternalInput").ap()

    xT_d = din("xT", [NSEQ, D, L])
    gains_d = din("gains", [128, 7 * KC])
    w1_d = din("w1", [4, D, FF])
    w3_d = din("w3", [4, D, FF])
    w2_d = din("w2", [4, FF, D])
    outT_d = nc.dram_tensor("outT", [NSEQ, D, L], F32, kind="ExternalOutput").ap()

    A = Arena(nc, 207 * 1024)
    xT = A.alloc([128, KC, L], F32)
    gains = A.alloc([128, 7, KC], F32)
    ones_bf = A.alloc([128, 128], BF16)
    epsc = A.alloc([128, 1], F32)
    cw = A.alloc([128, 4, 31], F32)
    chv = A.alloc([128, 5, 4], F32)
    lnepsc = A.alloc([128, 1], F32)
    xsq = [A.alloc([128, TT], BF16) for _ in range(2)]
    rstd = A.alloc([128, TT], F32)
    sil = [A.alloc([128, TT], BF16) for _ in range(2)]
    base_off = A.off
    hT = A.alloc([128, KC, 1024], BF16)
    G = A.alloc([128, FC, 1024], BF16)
    NS13 = 3
    w13 = [A.alloc([128, 2, KC, 256], BF16) for _ in range(NS13)]
    w2s = A.alloc([128, FC, D], BF16)
    ffn_end = A.off

    PS = [nc.alloc_psum_tensor("ps%d" % i, [128, TT], F32) for i in range(8)]
    PSB = [Buf("ps%d" % i) for i in range(8)]

    XB = [[Buf("x%d_%d" % (k, b)) for b in range(L // TT)] for k in range(KC)]
    HB = [[Buf("h%d_%d" % (k, t)) for t in range(2)] for k in range(KC)]
    GB = [[Buf("g%d_%d" % (f, t)) for t in range(2)] for f in range(FC)]
    W13B = [(Buf("w1_%d" % i), Buf("w3_%d" % i)) for i in range(NS13)]
    W2B = [Buf("w2s%d" % i) for i in range(FC // 2)]
    XSQB = [Buf("xsq%d" % i) for i in range(2)]
    RSTDB = Buf("rstd")
    SILB = [Buf("sil%d" % i) for i in range(2)]
    CONSTB = Buf("const")

    S.dma("sp", gains.rearrange("p a b -> p (a b)"), gains_d, writes=[CONSTB])
    S.op("dve", lambda e: e.memset(ones_bf, 1.0), writes=[CONSTB])
    S.op("dve", lambda e: e.memset(epsc, RMS_EPS), writes=[CONSTB])

    w13_state = {"n": 0}

    def load_w13(fi, fg):
        i = w13_state["n"] % NS13
        w13_state["n"] += 1
        slot = w13[i]
        b = W13B[i]
        src1 = w1_d[fi].rearrange("(kc p) f -> p kc f", p=128)[:, :, fg * 256:(fg + 1) * 256]
        src3 = w3_d[fi].rearrange("(kc p) f -> p kc f", p=128)[:, :, fg * 256:(fg + 1) * 256]
        S.dma("pool", slot[:, 0], src1, writes=[b[0]])
        S.dma("pool", slot[:, 1], src3, writes=[b[1]])
        return slot, b

    def load_w2(fi, j):
        src = w2_d[fi].rearrange("(fc p) d -> p fc d", p=128)
        S.dma("pool", w2s[:, 2 * j:2 * j + 2], src[:, 2 * j:2 * j + 2], writes=[W2B[j]])

    def rmsnorm_tile(gidx, t0, ntt, dstT, dstB, inplace=False):
        for tt in range(ntt):
            blk = (t0 + tt * TT) // TT
            ps = PS[6]
            psb = PSB[6]
            for kc in range(KC):
                q = dstT[:, kc, tt * TT:(tt + 1) * TT]
                xs = xT[:, kc, blk * TT:(blk + 1) * TT]
                S.op("act", lambda e, q=q, xs=xs: e.activation(out=q, in_=xs, func=AF.Square),
                     reads=[XB[kc][blk], CONSTB], writes=[dstB[kc][tt]])
            for kc in range(KC):
                q = dstT[:, kc, tt * TT:(tt + 1) * TT]
                S.op("pe", lambda e, q=q, kc=kc: e.matmul(ps[:], ones_bf, q, start=(kc == 0), stop=(kc == KC - 1)),
                     reads=[dstB[kc][tt], CONSTB], writes=[psb], signal=(kc == KC - 1))
            S.op("act", lambda e: e.activation(out=rstd, in_=ps[:], func=AF.Sqrt, scale=1.0 / D, bias=epsc),
                 reads=[psb, CONSTB], writes=[RSTDB])
            S.op("dve", lambda e: e.reciprocal(out=rstd, in_=rstd), reads=[RSTDB], writes=[RSTDB])
            for kc in range(KC):
                xs = xT[:, kc, blk * TT:(blk + 1) * TT]
                if inplace:
                    S.op("dve", lambda e, kc=kc, xs=xs: e.scalar_tensor_tensor(
                        out=xs, in0=xs, scalar=gains[:, gidx, kc:kc + 1], in1=rstd, op0=ALU.mult, op1=ALU.mult),
                        reads=[XB[kc][blk], RSTDB, CONSTB, dstB[kc][tt]], writes=[XB[kc][blk]])
                else:
                    S.op("dve", lambda e, kc=kc, xs=xs: e.scalar_tensor_tensor(
                        out=dstT[:, kc, tt * TT:(tt + 1) * TT], in0=xs,
                        scalar=gains[:, gidx, kc:kc + 1], in1=rstd, op0=ALU.mult, op1=ALU.mult),
                        reads=[XB[kc][blk], RSTDB, CONSTB], writes=[dstB[kc][tt]])

    def ffn_tile(fi, gidx, t0, do_pre=True, hook=None):
        if do_pre:
            rmsnorm_tile(gidx, t0, 2, hT, HB)
        pcount = 0
        slots = {}
        for fg in range(min(NS13, FC // 2)):
            slots[fg] = load_w13(fi, fg)
        for fg in range(FC // 2):
            slot, sb = slots.pop(fg)
            for fh in range(2):
                fc = fg * 2 + fh
                for tt in range(2):
                    pa = (pcount % 2) * 2
                    pcount += 1
                    for j, (pi, wi) in enumerate(((pa, 0), (pa + 1, 1))):
                        for kc in range(KC):
                            S.op("pe", lambda e, pi=pi, wi=wi, kc=kc: e.matmul(
                                PS[pi][:], slot[:, wi, kc, fh * 128:(fh + 1) * 128], hT[:, kc, tt * TT:(tt + 1) * TT],
                                start=(kc == 0), stop=(kc == KC - 1)),
                                reads=[sb[wi], HB[kc][tt]], writes=[PSB[pi]], signal=(kc == KC - 1))
                    sl = sil[pcount % 2]
                    slb = SILB[pcount % 2]
                    S.op("act", lambda e, sl=sl, pa=pa: e.activation(out=sl, in_=PS[pa][:], func=AF.Silu),
                         reads=[PSB[pa]], writes=[slb])
                    S.op("dve", lambda e, sl=sl, pa=pa, fc=fc, tt=tt: e.tensor_tensor(
                        out=G[:, fc, tt * TT:(tt + 1) * TT], in0=sl, in1=PS[pa + 1][:], op=ALU.mult),
                        reads=[slb, PSB[pa + 1]], writes=[GB[fc][tt]])
            if fg + NS13 < FC // 2:
                slots[fg + NS13] = load_w13(fi, fg + NS13)
            load_w2(fi, fg)
        pcount = 0
        for dc in range(KC):
            for tt in range(2):
                pi = 4 + (pcount % 2)
                pcount += 1
                blk = (t0 + tt * TT) // TT
                for fc in range(FC):
                    S.op("pe", lambda e, pi=pi, fc=fc, dc=dc, tt=tt: e.matmul(
                        PS[pi][:], w2s[:, fc, dc * 128:(dc + 1) * 128], G[:, fc, tt * TT:(tt + 1) * TT],
                        start=(fc == 0), stop=(fc == FC - 1)),
                        reads=[W2B[fc // 2], GB[fc][tt]], writes=[PSB[pi]], signal=(fc == FC - 1))
                xs = xT[:, dc, blk * TT:(blk + 1) * TT]
                S.op("dve", lambda e, pi=pi, xs=xs: e.scalar_tensor_tensor(
                    out=xs, in0=PS[pi][:], scalar=0.5, in1=xs, op0=ALU.mult, op1=ALU.add),
                    reads=[PSB[pi], XB[dc][blk]], writes=[XB[dc][blk]])
            if dc == 1 and hook is not None:
                hook()

    def ffn_chain(jobs, last_hook=None):
        for j, (fi, gidx, t0) in enumerate(jobs):
            nxt = jobs[j + 1] if j + 1 < len(jobs) else None
            hook = last_hook
            if nxt is not None:
                assert nxt[2] != t0
                hook = (lambda nxt=nxt: rmsnorm_tile(nxt[1], nxt[2], 2, hT, HB))
            ffn_tile(fi, gidx, t0, do_pre=(j == 0), hook=hook)

    RB = base_off
    KB = 1024

    def R(off_kb, shape, dt):
        return A.alloc(shape, dt, at=RB + int(off_kb * KB))

    I32 = mybir.dt.int32
    win_d = din("w_in", [D, 1536])
    wout_d = din("w_out", [D, D])
    gluw_d = din("glu_w", [512, 512])
    cw_d = din("cw", [128, 4 * 31])
    chv_d = din("chv", [128, 5 * 4])
    sp1_d = din("sp1", [64, 3 * 32])
    sp2_d = din("sp2", [64, 4 * 512])
    ident_d = din("ident", [128, 128])
    bmask_d = din("bmask", [128, 128])
    selin_d = din("selin", [128, 8 * 240])
    selout_d = din("selout", [128, 8 * 240])
    scr_toep = nc.dram_tensor("scr_toep", [128, 32 * 128], BF16, kind="Internal").ap()
    scr_wb = nc.dram_tensor("scr_wb", [128, 2 * 32 * 64], BF16, kind="Internal").ap()
    scr_cm = nc.dram_tensor("scr_cm", [64, 2 * 32 * 128], BF16, kind="Internal").ap()
    scr_at = nc.dram_tensor("scr_at", [64, 128], F32, kind="Internal").ap()

    M0CB = Buf("m0const")
    S.dma("sp", cw.rearrange("p a b -> p (a b)"), cw_d, writes=[M0CB])
    S.dma("sp", chv.rearrange("p a b -> p (a b)"), chv_d, writes=[M0CB])
    S.op("dve", lambda e: e.memset(lnepsc, LN_EPS), writes=[M0CB])

    uT = R(0, [128, 4, L], BF16)
    uT4 = R(0, [128, 4, 8, 256], BF16)
    mcat = R(16, [128, 8, L], BF16)
    Xbf = R(32, [128, 2, 16, 256], BF16)
    woutS = R(48, [128, 8, D], BF16)
    hT2 = R(64, [128, KC, L], BF16)
    winS = R(96, [128, KC, 1536], BF16)
    vpad = R(47.5, [128, 4, 30 + L], BF16)
    ycv = R(64, [128, 4, L], F32)
    lnt = [R(96 + 2 * i, [128, TT], F32) for i in range(5)]
    selin = R(64, [128, 8, 240], BF16)
    selout = R(68, [128, 8, 240], BF16)
    gluwS = R(72, [128, 4, 512], BF16)
    toepS = R(76, [128, 32, 128], BF16)
    wbS = R(84, [128, 2, 32, 64], BF16)
    cmS = R(92, [128, 2, 16, 128], BF16)
    atS = R(108, [128, 2, 2, 16], F32)
    Xh = R(109, [128, 32, 2, 16], F32)
    pq = R(113, [128, 2, 2, 16], F32)
    rtmp = R(114, [128, 2, 16], F32)
    Ut = R(116, [128, 32, 256], BF16)
    gt = [R(76 + 2 * i, [128, TT], F32) for i in range(6)]
    y1bS = R(92, [128, 4, L], BF16)
    sgt = [R(88 + i, [128, TT], BF16) for i in range(2)]

    diag = [R(o_, [128, 31, 128], BF16) for o_ in (32, 39.75, 114, 121.75)]
    identM = R(135.25, [128, 128], BF16)
    DIAGB = [Buf("diag%d" % i) for i in range(4)]
    IDMB = Buf("identM")
    UB = [[Buf("u%d_%d" % (c, t)) for t in range(4)] for c in range(4)]
    MCB = [[Buf("mc%d_%d" % (c, t)) for t in range(4)] for c in range(8)]
    H2B = [[Buf("h2_%d_%d" % (k, t)) for t in range(4)] for k in range(KC)]
    WINB = [Buf("win%d" % i) for i in range(3)]
    WOUTB = Buf("woutS")
    VB = [Buf("v%d" % c) for c in range(4)]
    YCB = [Buf("yc%d" % c) for c in range(4)]
    LNTB = [Buf("lnt%d" % i) for i in range(5)]
    PPB = Buf("ssmparams")
    SELIB = Buf("selin"); WBB = Buf("wbS"); TOEPB = Buf("toepS")
    CMB = [Buf("cm%d" % i) for i in range(4)]
    ATB = [Buf("at%d" % i) for i in range(8)]
    UTB = [Buf("ut%d" % g) for g in range(32)]
    XBFB = Buf("xbf")
    XHB = [Buf("xh%d" % i) for i in range(32)]
    PQB = Buf("pq")
    RTB = Buf("rtmp")
    GTB = [Buf("gt%d" % i) for i in range(6)]
    Y1B = [[Buf("y1_%d_%d" % (c, t)) for t in range(4)] for c in range(4)]
    SGTB = [Buf("sgt%d" % i) for i in range(2)]

    def ssm_param_prep():
        S.barrier()
        P = PPB
        cnt = {"o": 0}

        def T(shape, dt=F32):
            n = int(np.prod(shape[1:])) * (4 if dt != BF16 else 2)
            off = cnt["o"]
            cnt["o"] = (off + n + 31) // 32 * 32
            assert RB + cnt["o"] <= A.nbytes, cnt["o"]
            return A.alloc(shape, dt, at=RB + off)

        def v_tt(out, a, b, op):
            S.op("dve", lambda e: e.tensor_tensor(out=out, in0=a, in1=b, op=op), reads=[P], writes=[P])

        def v_ts(out, a, s1, s2, op0, op1=None):
            if op1 is None:
                S.op("dve", lambda e: e.tensor_scalar(out=out, in0=a, scalar1=s1, scalar2=None, op0=op0), reads=[P], writes=[P])
            else:
                S.op("dve", lambda e: e.tensor_scalar(out=out, in0=a, scalar1=s1, scalar2=s2, op0=op0, op1=op1), reads=[P], writes=[P])

        def a_act(out, a, func, scale=1.0):
            S.op("act", lambda e: e.activation(out=out, in_=a, func=func, scale=scale), reads=[P], writes=[P])

        def v_cp(out, a):
            S.op("dve", lambda e: e.tensor_copy(out=out, in_=a), reads=[P], writes=[P])

        sp1 = T([64, 3, 32]); sp2 = T([64, 4, 512])
        S.dma("act", sp1.rearrange("p a b -> p (a b)"), sp1_d, writes=[P])
        S.dma("act", sp2.rearrange("p a b -> p (a b)"), sp2_d, writes=[P])
        identS = T([128, 128], BF16); bmaskS = T([128, 128])
        S.dma("pool", identS, ident_d, writes=[P])
        S.dma("act", bmaskS, bmask_d, writes=[P])
        ldt, are, aim = sp1[:, 0], sp1[:, 1], sp1[:, 2]
        bre = sp2[:, 0].rearrange("p (g h) -> p g h", g=32)
        bim = sp2[:, 1].rearrange("p (g h) -> p g h", g=32)
        cre = sp2[:, 2].rearrange("p (g h) -> p g h", g=32)
        cim = sp2[:, 3].rearrange("p (g h) -> p g h", g=32)
        dt_ = T([64, 32]); mag = T([64, 32]); ang = T([64, 32]); t1 = T([64, 32]); t2 = T([64, 32])
        ti = T([64, 32], I32); cosv = T([64, 32]); sinv = T([64, 32])
        a_act(dt_, ldt, AF.Exp)
        v_tt(t1, dt_, are, ALU.mult)
        a_act(mag, t1, AF.Exp)
        v_tt(ang, dt_, aim, ALU.mult)
        TWO_PI = 2.0 * np.pi

        def sin_of(out, shift):
            v_ts(t1, ang, 1.0 / TWO_PI, float(shift), ALU.mult, ALU.add)
            v_cp(ti, t1)
            v_cp(t2, ti)
            v_tt(t1, t1, t2, ALU.subtract)
            v_ts(t2, t1, 0.5, None, ALU.is_gt)
            v_tt(t1, t1, t2, ALU.subtract)
            v_ts(t2, t1, -0.5, None, ALU.is_lt)
            v_tt(t1, t1, t2, ALU.add)
            a_act(out, t1, AF.Sin, scale=TWO_PI)

        dump("dt", dt_, [P]); dump("mag", mag, [P]); dump("ang", ang, [P])
        sin_of(sinv, 0.0)
        dump("frac_s", t1, [P]); dump("sinv", sinv, [P])
        sin_of(cosv, 0.25)
        dump("cosv", cosv, [P])
        pwr = T([64, 9, 32]); pwi = T([64, 9, 32])
        S.op("dve", lambda e: e.memset(pwr[:, 0], 1.0), reads=[P], writes=[P])
        S.op("dve", lambda e: e.memset(pwi[:, 0], 0.0), reads=[P], writes=[P])
        v_tt(pwr[:, 1], mag, cosv, ALU.mult)
        v_tt(pwi[:, 1], mag, sinv, ALU.mult)
        for k in range(1, 8):
            v_tt(t1, pwr[:, k], pwr[:, 1], ALU.mult)
            v_tt(t2, pwi[:, k], pwi[:, 1], ALU.mult)
            v_tt(pwr[:, k + 1], t1, t2, ALU.subtract)
            v_tt(t1, pwr[:, k], pwi[:, 1], ALU.mult)
            v_tt(t2, pwi[:, k], pwr[:, 1], ALU.mult)
            v_tt(pwi[:, k + 1], t1, t2, ALU.add)
        dump("pwr", pwr.rearrange("p k g -> p (k g)"), [P]); dump("pwi", pwi.rearrange("p k g -> p (k g)"), [P])
        ipr = T([64, 9, 32]); ipi = T([64, 9, 32]); n2 = T([64, 9, 32]); n3 = T([64, 9, 32])
        v_tt(n2, pwr, pwr, ALU.mult)
        v_tt(n3, pwi, pwi, ALU.mult)
        v_tt(n2, n2, n3, ALU.add)
        S.op("dve", lambda e: e.reciprocal(out=n2, in_=n2), reads=[P], writes=[P])
        v_tt(ipr, pwr, n2, ALU.mult)
        v_tt(ipi, pwi, n2, ALU.mult)
        v_ts(ipi, ipi, -1.0, None, ALU.mult)
        den = T([64, 32]); nr = T([64, 32]); qre = T([64, 32]); qim = T([64, 32])
        v_tt(den, are, are, ALU.mult)
        v_tt(t1, aim, aim, ALU.mult)
        v_tt(den, den, t1, ALU.add)
        S.op("dve", lambda e: e.reciprocal(out=den, in_=den), reads=[P], writes=[P])
        v_ts(nr, pwr[:, 1], -1.0, None, ALU.add)
        v_tt(t1, nr, are, ALU.mult)
        v_tt(t2, pwi[:, 1], aim, ALU.mult)
        v_tt(t1, t1, t2, ALU.add)
        v_tt(qre, t1, den, ALU.mult)
        v_tt(t1, pwi[:, 1], are, ALU.mult)
        v_tt(t2, nr, aim, ALU.mult)
        v_tt(t1, t1, t2, ALU.subtract)
        v_tt(qim, t1, den, ALU.mult)
        Bre = T([64, 32, 16]); Bim = T([64, 32, 16]); w1_ = T([64, 32, 16]); w2_ = T([64, 32, 16])
        qre_b = qre.unsqueeze(2).to_broadcast([64, 32, 16])
        qim_b = qim.unsqueeze(2).to_broadcast([64, 32, 16])
        v_tt(w1_, bre, qre_b, ALU.mult); v_tt(w2_, bim, qim_b, ALU.mult); v_tt(Bre, w1_, w2_, ALU.subtract)
        v_tt(w1_, bim, qre_b, ALU.mult); v_tt(w2_, bre, qim_b, ALU.mult); v_tt(Bim, w1_, w2_, ALU.add)
        big_off = cnt["o"]
        big1 = T([64, 32, 8, 16]); big2 = T([64, 32, 8, 16])
        Cmr = T([64, 32, 8, 16]); Cmi = T([64, 32, 8, 16])
        q0 = cnt["o"]
        Bsr = T([64, 32, 8, 16], BF16); Bsi = T([64, 32, 8, 16], BF16)
        cmr_bf = T([64, 32, 8, 16], BF16); cmi_bf = T([64, 32, 8, 16], BF16)
        Bmr = A.alloc([64, 32, 8, 16], F32, at=RB + q0); Bmi = A.alloc([64, 32, 8, 16], F32, at=RB + q0 + 16 * KB)

        def bc_x(x):
            return x.unsqueeze(2).to_broadcast([64, 32, 8, 16])

        def bc_p(p, lo, hi, rev=False):
            sl = p[:, lo:hi, :].rearrange("p k g -> p g k")
            return sl.unsqueeze(3).to_broadcast([64, 32, 8, 16])

        def cmul_big(o_re, o_im, xr, xi, pr, pi, neg_im=False):
            v_tt(big1, bc_x(xr), pr, ALU.mult); v_tt(big2, bc_x(xi), pi, ALU.mult)
            v_tt(o_re, big1, big2, ALU.subtract)
            v_tt(big1, bc_x(xr), pi, ALU.mult); v_tt(big2, bc_x(xi), pr, ALU.mult)
            if neg_im:
                v_tt(o_im, big1, big2, ALU.add)
                v_ts(o_im, o_im, -1.0, None, ALU.mult)
            else:
                v_tt(o_im, big1, big2, ALU.add)

        cmul_big(Cmr, Cmi, cre, cim, bc_p(pwr, 1, 9), bc_p(pwi, 1, 9), neg_im=True)
        v_cp(cmr_bf, Cmr); v_cp(cmi_bf, Cmi)
        S.dma("sp", scr_cm[:, 0:4096], cmr_bf.rearrange("p g j h -> p (g j h)"), reads=[P])
        S.dma("sp", scr_cm[:, 4096:8192], cmi_bf.rearrange("p g j h -> p (g j h)"), reads=[P])
        pwr_rev = T([64, 8, 32]); pwi_rev = T([64, 8, 32])
        for i in range(8):
            v_cp(pwr_rev[:, i], pwr[:, 7 - i]); v_cp(pwi_rev[:, i], pwi[:, 7 - i])
        prr = pwr_rev.rearrange("p k g -> p g k").unsqueeze(3).to_broadcast([64, 32, 8, 16])
        pri = pwi_rev.rearrange("p k g -> p g k").unsqueeze(3).to_broadcast([64, 32, 8, 16])
        cmul_big(Bsr, Bsi, Bre, Bim, prr, pri)
        wb_bf = T([128, 2, 32, 64], BF16)
        for o, src in ((0, Bsr), (1, Bsi)):
            for g8 in range(4):
                ps = PS[7]
                for gg in range(8):
                    g = g8 * 8 + gg
                    S.op("pe", lambda e, g=g, gg=gg, src=src: e.matmul(ps[:, gg * 64:(gg + 1) * 64], src[:, g].rearrange("p i h -> p (i h)"),
                                                                       identS[0:64, 0:64], start=True, stop=True),
                         reads=[P], writes=[PSB[7]], signal=(gg == 7))
                S.op("dve", lambda e, o=o, g8=g8: e.tensor_copy(out=wb_bf[:, o, g8 * 8:(g8 + 1) * 8, :],
                                                               in_=ps[:].rearrange("p (a b) -> p a b", a=8)), reads=[PSB[7], P], writes=[P])
        S.dma("sp", scr_wb, wb_bf.rearrange("p o g n -> p (o g n)"), reads=[P])
        at_ = T([64, 2, 2, 32])
        v_cp(at_[:, 0, 0], pwr[:, 8]); v_cp(at_[:, 1, 1], pwr[:, 8]); v_cp(at_[:, 1, 0], pwi[:, 8])
        v_ts(at_[:, 0, 1], pwi[:, 8], -1.0, None, ALU.mult)
        S.dma("sp", scr_at, at_.rearrange("p o k g -> p (o k g)"), reads=[P])
        cmul_big(Bmr, Bmi, Bre, Bim, bc_p(ipr, 1, 9), bc_p(ipi, 1, 9))
        toep_bf = A.alloc([128, 32, 128], BF16, at=RB + big_off)
        for g4 in range(8):
            ps = PS[7]
            for gg in range(4):
                g = g4 * 4 + gg
                S.op("pe", lambda e, g=g, gg=gg: e.matmul(ps[:, gg * 128:(gg + 1) * 128], Bmr[:, g].rearrange("p i h -> p (i h)"),
                                                          Cmr[:, g].rearrange("p j h -> p (j h)"), start=True, stop=False),
                     reads=[P], writes=[PSB[7]], signal=False)
                S.op("pe", lambda e, g=g, gg=gg: e.matmul(ps[:, gg * 128:(gg + 1) * 128], Bmi[:, g].rearrange("p i h -> p (i h)"),
                                                          Cmi[:, g].rearrange("p j h -> p (j h)"), start=False, stop=True),
                     reads=[P], writes=[PSB[7]], signal=(gg == 3))
            S.op("dve", lambda e, g4=g4: e.tensor_tensor(
                out=toep_bf[:, g4 * 4:(g4 + 1) * 4, :], in0=ps[:].rearrange("p (a b) -> p a b", a=4),
                in1=bmaskS.unsqueeze(1).to_broadcast([128, 4, 128]), op=ALU.mult), reads=[PSB[7], P], writes=[P])
        S.dma("sp", scr_toep, toep_bf.rearrange("p g m -> p (g m)"), reads=[P])
        S.barrier()

    def mixer0(s):
        S.barrier()
        for i in range(3):
            S.dma("pool", winS[:, :, i * 512:(i + 1) * 512],
                  win_d.rearrange("(kc p) f -> p kc f", p=128)[:, :, i * 512:(i + 1) * 512], writes=[WINB[i]])
        S.dma("pool", identM, ident_d, writes=[IDMB])
        for ct in range(2):
            for k in range(31):
                S.op("dve", lambda e, ct=ct, k=k: e.tensor_scalar(out=diag[ct][:, k, :], in0=identM, scalar1=cw[:, ct, k:k + 1], scalar2=None,
                                                              op0=ALU.mult), reads=[IDMB, M0CB], writes=[DIAGB[ct]])
        rmsnorm_tile(4, 0, 4, hT2, H2B)
        S.op("pool", lambda e: e.memset(vpad[:, :, 0:30], 0.0), writes=VB)
        pc = 0
        for ct in range(4):
            for tb in range(4):
                pa, pg = (pc % 2) * 2, (pc % 2) * 2 + 1
                pc += 1
                for pi, oc in ((pa, ct), (pg, ct + 4)):
                    for kc in range(KC):
                        S.op("pe", lambda e, pi=pi, oc=oc, kc=kc, tb=tb: e.matmul(
                            PS[pi][:], winS[:, kc, oc * 128:(oc + 1) * 128], hT2[:, kc, tb * TT:(tb + 1) * TT],
                            start=(kc == 0), stop=(kc == KC - 1)),
                            reads=[WINB[oc // 4], H2B[kc][tb]], writes=[PSB[pi]], signal=(kc == KC - 1))
                sg = sil[pc % 2]
                S.op("act", lambda e, sg=sg, pg=pg: e.activation(out=sg, in_=PS[pg][:], func=AF.Sigmoid),
                     reads=[PSB[pg]], writes=[SILB[pc % 2]])
                S.op("dve", lambda e, sg=sg, pa=pa, ct=ct, tb=tb: e.tensor_tensor(
                    out=vpad[:, ct, 30 + tb * TT:30 + (tb + 1) * TT], in0=sg, in1=PS[pa][:], op=ALU.mult),
                    reads=[SILB[pc % 2], PSB[pa]], writes=[VB[ct]])
        for ct in range(4):
            for tb in range(4):
                pi = 4 + (pc % 2)
                pc += 1
                oc = 8 + ct
                for kc in range(KC):
                    S.op("pe", lambda e, pi=pi, oc=oc, kc=kc, tb=tb: e.matmul(
                        PS[pi][:], winS[:, kc, oc * 128:(oc + 1) * 128], hT2[:, kc, tb * TT:(tb + 1) * TT],
                        start=(kc == 0), stop=(kc == KC - 1)),
                        reads=[WINB[2], H2B[kc][tb]], writes=[PSB[pi]], signal=(kc == KC - 1))
                S.op("act", lambda e, pi=pi, ct=ct, tb=tb: e.activation(out=uT4[:, ct, :, tb * 64:(tb + 1) * 64],
                                                                       in_=PS[pi][:].rearrange("p (c i) -> p i c", i=8), func=AF.Copy),
                     reads=[PSB[pi]], writes=[UB[ct][tb]])
        S.barrier()
        for ct in range(2, 4):
            for k in range(31):
                S.op("dve", lambda e, ct=ct, k=k: e.tensor_scalar(out=diag[ct][:, k, :], in0=identM, scalar1=cw[:, ct, k:k + 1], scalar2=None,
                                                              op0=ALU.mult), reads=[IDMB, M0CB], writes=[DIAGB[ct]])
        lnb = [R(96 + 2 * i, [128, TT], F32) for i in range(7)]
        ybf = [R(110 + i, [128, TT], BF16) for i in range(2)]
        ysq = [R(112 + i, [128, TT], BF16) for i in range(2)]
        LNB = [Buf("lnb%d" % i) for i in range(7)]
        YBB = [Buf("ybf%d" % i) for i in range(2)]
        YSB = [Buf("ysq%d" % i) for i in range(2)]
        YC2 = [[Buf("yc%d_%d" % (c, t)) for t in range(4)] for c in range(4)]
        tiles = [(tb, ct) for tb in range(4) for ct in range(4)]

        def conv_tile(n):
            tb, ct = tiles[n]
            pi = 2 + (n % 2)
            j = n % 2
            for k in range(31):
                S.op("pe", lambda e, pi=pi, ct=ct, k=k, tb=tb: e.matmul(
                    PS[pi][:], diag[ct][:, k, :], vpad[:, ct, k + tb * TT:k + (tb + 1) * TT], start=(k == 0), stop=(k == 30)),
                    reads=[DIAGB[ct], VB[ct]], writes=[PSB[pi]], signal=(k == 30))
            bia = chv[:, 0, ct:ct + 1]
            S.op("act", lambda e, pi=pi, ct=ct, tb=tb: e.activation(out=ycv[:, ct, tb * TT:(tb + 1) * TT], in_=PS[pi][:], func=AF.Identity, bias=bia),
                 reads=[PSB[pi], M0CB], writes=[YC2[ct][tb]])
            S.op("act", lambda e, pi=pi, j=j: e.activation(out=ybf[j], in_=PS[pi][:], func=AF.Identity, bias=bia),
                 reads=[PSB[pi], M0CB], writes=[YBB[j]])
            S.op("act", lambda e, pi=pi, j=j: e.activation(out=ysq[j], in_=PS[pi][:], func=AF.Square, bias=bia),
                 reads=[PSB[pi], M0CB], writes=[YSB[j]])

        def stat_mm(n):
            tb, ct = tiles[n]
            j = n % 2
            sb_, qb_ = ((0, 1), (4, 5))[tb % 2]
            S.op("pe", lambda e: e.matmul(PS[sb_][:], ones_bf, ybf[j], start=(ct == 0), stop=(ct == 3)),
                 reads=[YBB[j], CONSTB], writes=[PSB[sb_]], signal=True)
            S.op("pe", lambda e: e.matmul(PS[qb_][:], ones_bf, ysq[j], start=(ct == 0), stop=(ct == 3)),
                 reads=[YSB[j], CONSTB], writes=[PSB[qb_]], signal=True)

        def ln_stages(tb):
            sb_, qb_ = ((0, 1), (4, 5))[tb % 2]
            mean, var, msq = lnb[2 + tb % 2], lnb[4 + tb % 2], lnb[6]
            MB_, VB_, QB_ = LNB[2 + tb % 2], LNB[4 + tb % 2], LNB[6]
            sl = slice(tb * TT, (tb + 1) * TT)

            def st_a():
                S.op("act", lambda e: e.activation(out=mean, in_=PS[sb_][:], func=AF.Copy, scale=1.0 / 512), reads=[PSB[sb_]], writes=[MB_])
                S.op("act", lambda e: e.activation(out=msq, in_=PS[sb_][:], func=AF.Square, scale=1.0 / 512), reads=[PSB[sb_]], writes=[QB_])
                S.op("dve", lambda e: e.scalar_tensor_tensor(out=var, in0=PS[qb_][:], scalar=1.0 / 512, in1=msq, op0=ALU.mult, op1=ALU.subtract),
                     reads=[PSB[qb_], QB_], writes=[VB_])

            def st_b():
                S.op("act", lambda e: e.activation(out=var, in_=var, func=AF.Sqrt, bias=lnepsc), reads=[VB_, M0CB], writes=[VB_])
                S.op("dve", lambda e: e.reciprocal(out=var, in_=var), reads=[VB_], writes=[VB_])

            def t_ops(ct):
                t_ = lnb[ct % 2]
                TB_ = LNB[ct % 2]
                S.op("dve", lambda e: e.tensor_tensor(out=t_, in0=ycv[:, ct, sl], in1=mean, op=ALU.subtract),
                     reads=[YC2[ct][tb], MB_], writes=[TB_])
                S.op("dve", lambda e: e.tensor_tensor(out=t_, in0=t_, in1=var, op=ALU.mult),
                     reads=[TB_, VB_], writes=[TB_])

            def silu(ct):
                t_ = lnb[ct % 2]
                S.op("act", lambda e: e.activation(out=mcat[:, ct, sl], in_=t_, func=AF.Silu,
                                                   scale=chv[:, 1, ct:ct + 1], bias=chv[:, 2, ct:ct + 1]),
                     reads=[LNB[ct % 2], M0CB], writes=[MCB[ct][tb]])

            return [st_a, st_b, lambda: (t_ops(0), t_ops(1)), lambda: (silu(0), silu(1), t_ops(2), t_ops(3)), lambda: (silu(2), silu(3))]

        pending = []
        conv_tile(0)
        for n in range(len(tiles)):
            if n + 1 < len(tiles):
                conv_tile(n + 1)
            stat_mm(n)
            for st in pending:
                if st:
                    st.pop(0)()
            if tiles[n][1] == 3:
                pending.append(ln_stages(tiles[n][0]))
        while any(pending):
            for st in pending:
                if st:
                    st.pop(0)()
        dump("mcA", mcat[:, 0:4, :].rearrange("p a b -> p (a b)"), [b for r in MCB[0:4] for b in r])
        dump("u", uT.rearrange("p a b -> p (a b)"), [b for r in UB for b in r])
        S.barrier()
        S.dma("pool", selin.rearrange("p a b -> p (a b)"), selin_d, writes=[SELIB])
        S.dma("sp", wbS.rearrange("p o g n -> p (o g n)"), scr_wb, writes=[WBB])
        S.dma("sp", toepS.rearrange("p g m -> p (g m)"), scr_toep, writes=[TOEPB])
        S.dma("pool", selout.rearrange("p a b -> p (a b)"), selout_d, writes=[PPB])
        S.dma("pool", gluwS, gluw_d.rearrange("(kc p) f -> p kc f", p=128), writes=[PPB])
        scr_cm4 = scr_cm.rearrange("p (o g m) -> p o g m", o=2, g=32)
        scr_at4 = scr_at.rearrange("p (o k g) -> p o k g", o=2, k=2)
        for gh in range(2):
            for o in range(2):
                S.dma("sp", cmS[64 * gh:64 * gh + 64, o, :, :], scr_cm4[:, o, gh * 16:(gh + 1) * 16, :], writes=[CMB[gh * 2 + o]])
                for k in range(2):
                    S.dma("sp", atS[64 * gh:64 * gh + 64, o, k, :], scr_at4[:, o, k, gh * 16:(gh + 1) * 16], writes=[ATB[gh * 4 + o * 2 + k]])
        for g2 in range(16):
            pi = g2 % 2
            for gg in range(2):
                g = g2 * 2 + gg
                ct, g8 = g // 8, g % 8
                for i in range(8):
                    S.op("pe", lambda e, pi=pi, gg=gg, ct=ct, g8=g8, i=i: e.matmul(
                        PS[pi][:, gg * 256:(gg + 1) * 256], selin[:, g8, 112 - 16 * i:240 - 16 * i], uT4[:, ct, i, :],
                        start=(i == 0), stop=(i == 7)),
                        reads=[SELIB] + UB[ct], writes=[PSB[pi]], signal=(gg == 1 and i == 7))
            S.op("act", lambda e, pi=pi, g2=g2: e.activation(out=Ut[:, g2 * 2:g2 * 2 + 2, :],
                                                            in_=PS[pi][:].rearrange("p (a b) -> p a b", a=2), func=AF.Copy),
                 reads=[PSB[pi]], writes=[UTB[g2 * 2], UTB[g2 * 2 + 1]])
        S.op("dve", lambda e: e.memset(Xh[:, 31], 0.0), writes=[XHB[31]])
        for cb in range(16):
            pi = 2 + (cb % 2)
            psv = PS[pi][:].rearrange("p (c o g) -> p c o g", c=16, o=2)
            for g in range(32):
                gh, gl = g // 16, g % 16
                for o in range(2):
                    S.op("pe", lambda e, psv=psv, g=g, gh=gh, gl=gl, o=o, cb=cb: e.matmul(
                        psv[64 * gh:64 * gh + 64, :, o, gl], wbS[:, o, g, :], Ut[:, g, cb * 16:(cb + 1) * 16], start=True, stop=True),
                        reads=[WBB, UTB[g]], writes=[PSB[pi]], signal=(g == 31 and o == 1))
            for cc in range(16):
                c = cb * 16 + cc
                prev = Xh[:, (c - 1) % 32]
                S.op("dve", lambda e, prev=prev: e.tensor_tensor(
                    out=pq, in0=prev.unsqueeze(1).to_broadcast([128, 2, 2, 16]), in1=atS, op=ALU.mult),
                    reads=[XHB[(c - 1) % 32]] + ATB, writes=[PQB])
                S.op("dve", lambda e: e.tensor_tensor(out=rtmp, in0=pq[:, :, 0, :], in1=pq[:, :, 1, :], op=ALU.add),
                     reads=[PQB], writes=[RTB])
                S.op("dve", lambda e, psv=psv, cc=cc, c=c: e.tensor_tensor(out=Xh[:, c % 32], in0=rtmp, in1=psv[:, cc], op=ALU.add),
                     reads=[RTB, PSB[pi]], writes=[XHB[c % 32]])
            half = (cb % 2) * 16
            S.op("act", lambda e, cb=cb, half=half: e.activation(
                out=Xbf[:, :, :, cb * 16:(cb + 1) * 16], in_=Xh[:, half:half + 16].rearrange("p c o g -> p o g c"), func=AF.Copy),
                reads=XHB[half:half + 16], writes=[XBFB])
        for g2 in range(16):
            pi = 4 + (g2 % 2)
            for gg in range(2):
                g = g2 * 2 + gg
                S.op("pe", lambda e, pi=pi, gg=gg, g=g: e.matmul(PS[pi][:, gg * 256:(gg + 1) * 256], toepS[:, g, :], Ut[:, g, :],
                                                                start=True, stop=False),
                     reads=[TOEPB, UTB[g]], writes=[PSB[pi]], signal=False)
                hs = slice(64 * (g // 16), 64 * (g // 16) + 64)
                gl = g % 16
                S.op("pe", lambda e, pi=pi, gg=gg, gl=gl, hs=hs: e.matmul(PS[pi][:, gg * 256 + 1:(gg + 1) * 256], cmS[hs, 0, gl, :], Xbf[hs, 0, gl, 0:255],
                                                                start=False, stop=False),
                     reads=CMB + [XBFB], writes=[PSB[pi]], signal=False)
                S.op("pe", lambda e, pi=pi, gg=gg, gl=gl, hs=hs: e.matmul(PS[pi][:, gg * 256 + 1:(gg + 1) * 256], cmS[hs, 1, gl, :], Xbf[hs, 1, gl, 0:255],
                                                                start=False, stop=True),
                     reads=CMB + [XBFB], writes=[PSB[pi]], signal=(gg == 1))
            S.op("act", lambda e, pi=pi, g2=g2: e.activation(out=Ut[:, g2 * 2:g2 * 2 + 2, :],
                                                            in_=PS[pi][:].rearrange("p (a b) -> p a b", a=2), func=AF.Copy),
                 reads=[PSB[pi]], writes=[UTB[g2 * 2], UTB[g2 * 2 + 1]])
        dump("toep", toepS.rearrange("p g m -> p (g m)"), [TOEPB])
        dump("wb", wbS.rearrange("p o g n -> p (o g n)"), [WBB])
        dump("Y", Ut.rearrange("p g c -> p (g c)"), UTB)
        S.barrier()
        S.dma("pool", woutS, wout_d.rearrange("(kc p) f -> p kc f", p=128), writes=[WOUTB])
        pc = 0
        for ct in range(4):
            for tb in range(4):
                pi = pc % 2
                pc += 1
                for j in range(8):
                    for g8 in range(8):
                        S.op("pe", lambda e, pi=pi, j=j, g8=g8, ct=ct, tb=tb: e.matmul(
                            PS[pi][:, j * 64:(j + 1) * 64], selout[:, j, 112 - 16 * g8:240 - 16 * g8],
                            Ut[:, ct * 8 + g8, tb * 64:(tb + 1) * 64], start=(g8 == 0), stop=(g8 == 7)),
                            reads=[PPB, UTB[ct * 8 + g8]], writes=[PSB[pi]], signal=(j == 7 and g8 == 7))
                ys, x2, x3, sg_ = gt[0], gt[1], gt[2], gt[3]
                sl = slice(tb * TT, (tb + 1) * TT)
                S.op("dve", lambda e, pi=pi, ct=ct, sl=sl: e.scalar_tensor_tensor(
                    out=ys.rearrange("p (c j) -> p c j", j=8), in0=uT4[:, ct, :, tb * 64:(tb + 1) * 64].rearrange("p j c -> p c j"),
                    scalar=chv[:, 3, ct:ct + 1], in1=PS[pi][:].rearrange("p (j c) -> p c j", j=8), op0=ALU.mult, op1=ALU.add),
                    reads=[PSB[pi], UB[ct][tb], M0CB], writes=[GTB[0]])
                S.op("act", lambda e: e.activation(out=x2, in_=ys, func=AF.Square), reads=[GTB[0]], writes=[GTB[1]])
                S.op("dve", lambda e: e.tensor_scalar(out=x2, in0=x2, scalar1=0.044715, scalar2=1.0, op0=ALU.mult, op1=ALU.add),
                     reads=[GTB[1]], writes=[GTB[1]])
                S.op("dve", lambda e: e.tensor_tensor(out=x3, in0=x2, in1=ys, op=ALU.mult), reads=[GTB[1], GTB[0]], writes=[GTB[2]])
                S.op("act", lambda e: e.activation(out=sg_, in_=x3, func=AF.Sigmoid, scale=1.5957691216057308), reads=[GTB[2]], writes=[GTB[3]])
                S.op("dve", lambda e, ct=ct, sl=sl: e.tensor_tensor(out=y1bS[:, ct, sl], in0=ys, in1=sg_, op=ALU.mult),
                     reads=[GTB[0], GTB[3]], writes=[Y1B[ct][tb]])
        for ot in range(4):
            for tb in range(4):
                pi = 2 + (pc % 2)
                pc += 1
                sl = slice(tb * TT, (tb + 1) * TT)
                for ct in range(4):
                    S.op("pe", lambda e, pi=pi, ct=ct, ot=ot, sl=sl: e.matmul(
                        PS[pi][:], gluwS[:, ct, ot * 128:(ot + 1) * 128], y1bS[:, ct, sl], start=(ct == 0), stop=(ct == 3)),
                        reads=[PPB, Y1B[ct][tb]], writes=[PSB[pi]], signal=(ct == 3))
                sg = sgt[pc % 2]
                S.op("act", lambda e, pi=pi, sg=sg, ot=ot: e.activation(out=sg, in_=PS[pi][:], func=AF.Sigmoid, bias=chv[:, 4, ot:ot + 1]),
                     reads=[PSB[pi], M0CB], writes=[SGTB[pc % 2]])
                S.op("dve", lambda e, sg=sg, ot=ot, sl=sl: e.tensor_tensor(out=mcat[:, 4 + ot, sl], in0=y1bS[:, ot, sl], in1=sg, op=ALU.mult),
                     reads=[SGTB[pc % 2], Y1B[ot][tb]], writes=[MCB[4 + ot][tb]])
        dump("mcB", mcat[:, 4:8, :].rearrange("p a b -> p (a b)"), [b for r in MCB[4:8] for b in r])
        dump("y1", y1bS.rearrange("p a b -> p (a b)"), [b for r in Y1B for b in r])
        for dc in range(KC):
            for tb in range(4):
                pi = 4 + (pc % 2)
                pc += 1
                sl = slice(tb * TT, (tb + 1) * TT)
                for mc in range(8):
                    S.op("pe", lambda e, pi=pi, mc=mc, dc=dc, sl=sl: e.matmul(
                        PS[pi][:], woutS[:, mc, dc * 128:(dc + 1) * 128], mcat[:, mc, sl], start=(mc == 0), stop=(mc == 7)),
                        reads=[WOUTB, MCB[mc][tb]], writes=[PSB[pi]], signal=(mc == 7))
                xs = xT[:, dc, sl]
                S.op("dve", lambda e, pi=pi, xs=xs: e.tensor_tensor(out=xs, in0=PS[pi][:], in1=xs, op=ALU.add),
                     reads=[PSB[pi], XB[dc][tb]], writes=[XB[dc][tb]])
        S.barrier()

    wqkv_d = din("w_qkv", [D, 3 * D])
    wo_d = din("w_o", [D, D])
    cbias_d = din("cbias", [128, 2 * 256])
    en_d = din("en", [8, 8 * 128])
    negm_d = din("negm", [128, 4 * 64])
    NEG = -30000.0
    ahT = R(0, [128, KC, L], BF16)
    qring = [R(32 + 8 * i, [128, KC, 512], BF16) for i in range(2)]
    qT = R(48, [128, 4, L], BF16)
    kT = R(64, [128, 4, L], BF16)
    Vh = R(80, [128, 16, 512], BF16)
    oT = R(96, [128, 4, L], BF16)
    woS = R(112, [128, 4, D], BF16)
    pT = [R(120 + i, [128, 2, 256], BF16) for i in range(2)]
    cbiasS = R(122, [128, 2, 256], BF16)
    EnS = R(123, [8, 8, 128], BF16)
    identA = R(125, [128, 128], BF16)
    negmS = R(125.5, [128, 4, 64], F32)
    rden = R(126.5, [128, 256], F32)
    mbT = [R(127.5 + 2 * i, [8, 4, 256], BF16) for i in range(2)]
    kmf = R(131.5, [128, 4, 8], F32)
    kmT = R(131.75, [128, 4, 8], BF16)
    gsb = R(132, [128, 32], F32)
    top8 = R(132.25, [128, 8], F32)
    mball = R(132.5, [128, 8, 32], BF16)
    rden2 = R(133, [128, 256], F32)
    rdens = [rden, rden2]
    RDBS = [Buf("rden0"), Buf("rden1")]
    mbTall = R(32, [8, 4, 4, 256], BF16)
    assert RB + int(134 * KB) <= A.nbytes

    AHB = [[Buf("ah%d_%d" % (k, t)) for t in range(4)] for k in range(KC)]
    QRB = [Buf("qring%d" % i) for i in range(2)]
    QTB = [Buf("qT%d" % h) for h in range(4)]
    KTB = [Buf("kT%d" % h) for h in range(4)]
    VHB = [Buf("vh%d" % k) for k in range(16)]
    OTB = [[Buf("oT%d_%d" % (h, q)) for q in range(8)] for h in range(4)]
    WOB = Buf("woS")
    PTB = [Buf("pT%d" % i) for i in range(2)]
    ACB = Buf("attnconst")
    RDB = Buf("rden")
    MBTB = [Buf("mbT%d" % i) for i in range(2)]
    KMB = Buf("km")
    GSB = Buf("gsb")
    T8B = Buf("top8")
    MBB = Buf("mb")
    SCALE = 128.0 ** -0.5

    def mixer1(s):
        S.barrier()
        S.dma("pool", cbiasS.rearrange("p a b -> p (a b)"), cbias_d, writes=[ACB])
        S.dma("pool", EnS.rearrange("p a b -> p (a b)"), en_d, writes=[ACB])
        S.dma("pool", identA, ident_d, writes=[ACB])
        S.dma("sp", negmS.rearrange("p a b -> p (a b)"), negm_d, writes=[ACB])
        rmsnorm_tile(5, 0, 4, ahT, AHB)
        nring = [0]
        pcs = [0]

        def load_cols(c0):
            i = nring[0] % 2
            nring[0] += 1
            S.dma("pool", qring[i], wqkv_d.rearrange("(kc p) f -> p kc f", p=128)[:, :, c0:c0 + 512], writes=[QRB[i]])
            return qring[i], QRB[i]

        for half in range(2):
            for which, dstT, dstB in ((0, qT, QTB), (1, kT, KTB)):
                if which == 0 and half == 1:
                    wsl, wb_ = pre_q
                else:
                    wsl, wb_ = load_cols(which * D + half * 512)
                for hl in range(4):
                    for tb in range(4):
                        pi = pcs[0] % 2
                        pcs[0] += 1
                        for kc in range(KC):
                            S.op("pe", lambda e, pi=pi, wsl=wsl, kc=kc, hl=hl, tb=tb: e.matmul(
                                PS[pi][:], wsl[:, kc, hl * 128:(hl + 1) * 128], ahT[:, kc, tb * TT:(tb + 1) * TT],
                                start=(kc == 0), stop=(kc == KC - 1)),
                                reads=[wb_, AHB[kc][tb]], writes=[PSB[pi]], signal=(kc == KC - 1))
                        S.op("act", lambda e, pi=pi, dstT=dstT, hl=hl, tb=tb: e.activation(
                            out=dstT[:, hl, tb * TT:(tb + 1) * TT], in_=PS[pi][:], func=AF.Copy),
                            reads=[PSB[pi]], writes=[dstB[hl]])
            wsl, wb_ = load_cols(2 * D + half * 512)
            for kt in range(16):
                pi = pcs[0] % 2
                pcs[0] += 1
                tb = kt // 4
                for kc in range(KC):
                    S.op("pe", lambda e, pi=pi, wsl=wsl, kc=kc, kt=kt: e.matmul(
                        PS[pi][:], ahT[:, kc, kt * 128:(kt + 1) * 128], wsl[:, kc, :], start=(kc == 0), stop=(kc == KC - 1)),
                        reads=[wb_, AHB[kc][tb]], writes=[PSB[pi]], signal=(kc == KC - 1))
                S.op("act", lambda e, pi=pi, kt=kt: e.activation(out=Vh[:, kt, :], in_=PS[pi][:], func=AF.Copy),
                     reads=[PSB[pi]], writes=[VHB[kt]])
            S.dma("pool", woS, wo_d.rearrange("(kc p) f -> p kc f", p=128)[:, half * 4:(half + 1) * 4, :], writes=[WOB])
            if half == 0:
                pre_q = load_cols(0 * D + 1 * 512)
            for hl in range(4):
                S.op("dve", lambda e, hl=hl: e.tensor_reduce(out=kmf[:, hl, :], in_=kT[:, hl, :].rearrange("p (n k) -> p n k", n=8),
                                                            axis=AX.X, op=ALU.add), reads=[KTB[hl]], writes=[KMB])
            S.op("dve", lambda e: e.tensor_copy(out=kmT, in_=kmf), reads=[KMB], writes=[KMB])
            for qb in range(4, 8):
                for q2 in range(2):
                    idx = (qb - 4) * 2 + q2
                    qsl = slice(qb * 256 + q2 * 128, qb * 256 + (q2 + 1) * 128)
                    for hl in range(4):
                        S.op("pe", lambda e, hl=hl, qsl=qsl, idx=idx: e.matmul(PS[6][:, idx * 32 + hl * 8:idx * 32 + (hl + 1) * 8], qT[:, hl, qsl],
                                                                              kmT[:, hl, :], start=True, stop=True),
                             reads=[QTB[hl], KMB], writes=[PSB[6]], signal=(hl == 3))
            for qb in range(4, 8):
                for q2 in range(2):
                    idx = (qb - 4) * 2 + q2
                    S.op("dve", lambda e, idx=idx, qb=qb: e.tensor_tensor(out=gsb, in0=PS[6][:, idx * 32:(idx + 1) * 32], in1=negmS[:, qb - 4, 0:32], op=ALU.add),
                         reads=[PSB[6], ACB], writes=[GSB])
                    for hl in range(4):
                        S.op("dve", lambda e, hl=hl: e.max(out=top8, in_=gsb[:, hl * 8:(hl + 1) * 8]), reads=[GSB], writes=[T8B])
                        S.op("dve", lambda e, hl=hl, idx=idx: e.tensor_scalar(out=mball[:, idx, hl * 8:(hl + 1) * 8], in0=gsb[:, hl * 8:(hl + 1) * 8],
                                                                    scalar1=top8[:, 2:3], scalar2=NEG, op0=ALU.is_lt, op1=ALU.mult),
                             reads=[GSB, T8B], writes=[MBB])

            def mask_transposes():
                for qb in range(4, 8):
                    for q2 in range(2):
                        idx = (qb - 4) * 2 + q2
                        tbk = 6 + (idx % 2)
                        for hl in range(4):
                            S.op("pe", lambda e, hl=hl, idx=idx, tbk=tbk: e.matmul(PS[tbk][0:8, hl * 128:(hl + 1) * 128], mball[:, idx, hl * 8:(hl + 1) * 8], identA,
                                                                         start=True, stop=True),
                                 reads=[MBB, ACB], writes=[PSB[tbk]], signal=(hl == 3))
                        S.op("act", lambda e, qb=qb, q2=q2, tbk=tbk: e.activation(out=mbTall[:, qb - 4, :, q2 * 128:(q2 + 1) * 128],
                                                                        in_=PS[tbk][0:8, :].rearrange("p (h q) -> p h q", h=4), func=AF.Copy),
                             reads=[PSB[tbk]], writes=[MBTB[0]])

            items = []
            for qb in range(8):
                for hl in range(4):
                    npair = qb + 1
                    for kp in range(npair):
                        items.append((qb, hl, kp, kp == 0, kp == npair - 1))

            def emit_S(i):
                qb, hl, kp, _, _ = items[i]
                qs = slice(qb * 256, (qb + 1) * 256)
                gated = qb >= 4
                pi = 2 + (i % 2)
                for k2 in range(2):
                    kt = kp * 2 + k2
                    n = kp
                    osl = PS[pi][:, k2 * 256:(k2 + 1) * 256]
                    extra = (n == qb) or gated
                    S.op("pe", lambda e, osl=osl, hl=hl, kt=kt, extra=extra, qs=qs: e.matmul(
                        osl, kT[:, hl, kt * 128:(kt + 1) * 128], qT[:, hl, qs], start=True, stop=(not extra)),
                        reads=[KTB[hl], QTB[hl]], writes=[PSB[pi]], signal=((not extra) and k2 == 1))
                    if n == qb:
                        S.op("pe", lambda e, osl=osl, k2=k2: e.matmul(osl, identA, cbiasS[:, k2, :], start=False, stop=True),
                             reads=[ACB], writes=[PSB[pi]], signal=(k2 == 1))
                    elif gated:
                        S.op("pe", lambda e, osl=osl, n=n, hl=hl, qb=qb: e.matmul(osl, EnS[:, n, :], mbTall[:, qb - 4, hl, :], start=False, stop=True),
                             reads=[ACB, MBTB[0]], writes=[PSB[pi]], signal=(k2 == 1))

            def emit_rest(i, gi):
                qb, hl, kp, first, last = items[i]
                qs = slice(qb * 256, (qb + 1) * 256)
                pi = 2 + (i % 2)
                ti = i % 2
                po, pd = ((4, 5), (0, 1))[gi % 2]
                S.op("act", lambda e, pi=pi, ti=ti: e.activation(out=pT[ti].rearrange("p a b -> p (a b)"), in_=PS[pi][:],
                                                                func=AF.Exp, scale=SCALE),
                     reads=[PSB[pi]], writes=[PTB[ti]])
                for k2 in range(2):
                    kt = kp * 2 + k2
                    f_ = first and k2 == 0
                    l_ = last and k2 == 1
                    S.op("pe", lambda e, po=po, kt=kt, hl=hl, ti=ti, k2=k2, f_=f_, l_=l_: e.matmul(
                        PS[po][:, 0:256], Vh[:, kt, hl * 128:(hl + 1) * 128], pT[ti][:, k2, :], start=f_, stop=l_),
                        reads=[VHB[kt], PTB[ti]], writes=[PSB[po]], signal=False)
                    S.op("pe", lambda e, pd=pd, ti=ti, k2=k2, f_=f_, l_=l_: e.matmul(
                        PS[pd][:, 0:256], ones_bf, pT[ti][:, k2, :], start=f_, stop=l_),
                        reads=[CONSTB, PTB[ti]], writes=[PSB[pd]], signal=(k2 == 1))
                if last:
                    rd = rdens[gi % 2]
                    S.op("dve", lambda e, pd=pd, rd=rd: e.reciprocal(out=rd, in_=PS[pd][:, 0:256]), reads=[PSB[pd]], writes=[RDBS[gi % 2]])
                    S.op("dve", lambda e, po=po, hl=hl, rd=rd, qs=qs: e.tensor_tensor(out=oT[:, hl, qs], in0=PS[po][:, 0:256], in1=rd, op=ALU.mult),
                         reads=[PSB[po], RDBS[gi % 2]], writes=[OTB[hl][qb]])

            n_items = len(items)
            first_gated = next(i for i, it in enumerate(items) if it[0] >= 4)
            gi = 0
            emit_S(0)
            for i in range(n_items):
                if i + 1 < n_items:
                    if i + 1 == first_gated:
                        mask_transposes()
                    emit_S(i + 1)
                emit_rest(i, gi)
                if items[i][4]:
                    gi += 1
            if half == 0:
                dump("oT0", oT.rearrange("p a b -> p (a b)"), [b for r_ in OTB for b in r_])
                dump("qT0", qT.rearrange("p a b -> p (a b)"), QTB)
                dump("kT0", kT.rearrange("p a b -> p (a b)"), KTB)
                dump("Vh0", Vh.rearrange("p a b -> p (a b)"), VHB)
            for dc in range(KC):
                for tb in range(4):
                    pi = pcs[0] % 2
                    pcs[0] += 1
                    sl = slice(tb * TT, (tb + 1) * TT)
                    for hl in range(4):
                        S.op("pe", lambda e, pi=pi, hl=hl, dc=dc, sl=sl: e.matmul(
                            PS[pi][:], woS[:, hl, dc * 128:(dc + 1) * 128], oT[:, hl, sl], start=(hl == 0), stop=(hl == 3)),
                            reads=[WOB, OTB[hl][2 * tb], OTB[hl][2 * tb + 1]], writes=[PSB[pi]], signal=(hl == 3))
                    xs = xT[:, dc, sl]
                    S.op("dve", lambda e, pi=pi, xs=xs: e.tensor_tensor(out=xs, in0=PS[pi][:], in1=xs, op=ALU.add),
                         reads=[PSB[pi], XB[dc][tb]], writes=[XB[dc][tb]])
            S.barrier()

    OUTB = [Buf("o%d" % i) for i in range(2)]
    for s in range(nseq):
        def load_blocks(sq, blks):
            for blk in blks:
                for kc in range(KC):
                    S.dma("sp", xT[:, kc, blk * TT:(blk + 1) * TT], xT_d[sq, kc * 128:(kc + 1) * 128, blk * TT:(blk + 1) * TT],
                          writes=[XB[kc][blk]])

        if s == 0 or "final" not in stages:
            load_blocks(s, range(L // TT))
        if s == 0 and "mix0" in stages:
            ssm_param_prep()
        if "ffn00" in stages:
            ffn_chain([(0, 0, 0), (0, 0, 1024)])
        if "mix0" in stages:
            mixer0(s)
        if "ffn01" in stages and "ffn10" in stages:
            ffn_chain([(1, 1, 0), (1, 1, 1024), (2, 2, 0), (2, 2, 1024)])
        else:
            if "ffn01" in stages:
                ffn_chain([(1, 1, 0), (1, 1, 1024)])
            if "ffn10" in stages:
                ffn_chain([(2, 2, 0), (2, 2, 1024)])
        if "mix1" in stages:
            mixer1(s)
        def final_tile(t0, s=s):
            rmsnorm_tile(6, t0, 2, hT, HB, inplace=True)
            for blk in (t0 // TT, t0 // TT + 1):
                for kc in range(KC):
                    S.dma("sp", outT_d[s, kc * 128:(kc + 1) * 128, blk * TT:(blk + 1) * TT], xT[:, kc, blk * TT:(blk + 1) * TT],
                          reads=[XB[kc][blk]])
            if s + 1 < nseq:
                load_blocks(s + 1, (t0 // TT, t0 // TT + 1))

        fused_tail = ("ffn11" in stages) and ("final" in stages)
        if "ffn11" in stages:
            ffn_chain([(3, 3, 0), (3, 3, 1024)], last_hook=(lambda: final_tile(0)) if fused_tail else None)
        if "final" in stages:
            for t0 in ((1024,) if fused_tail else (0, 1024)):
                final_tile(t0)
        if "final" not in stages:
            for blk in range(L // TT):
                for kc in range(KC):
                    S.dma("sp", outT_d[s, kc * 128:(kc + 1) * 128, blk * TT:(blk + 1) * TT], xT[:, kc, blk * TT:(blk + 1) * TT],
                          reads=[XB[kc][blk]])
    S.wait_all("sp", [b for row in XB for b in row])
    nc._sched_ninst = S.ninst
    nc._dbg_names = dbg_names
    return nc


def _prep_inputs(inputs):
    f = np.float32
    x = np.asarray(inputs["x"], f)
    ffn_norm = np.asarray(inputs["ffn_norm"], f)
    vecs = [ffn_norm[0, 0], ffn_norm[0, 1], ffn_norm[1, 0], ffn_norm[1, 1],
            np.asarray(inputs["mix_norm"], f)[0], np.asarray(inputs["mix_norm"], f)[1],
            np.asarray(inputs["final_norm"], f)]
    gains = np.stack([v.reshape(KC, 128).T for v in vecs], axis=1)
    gains = np.ascontiguousarray(gains.reshape(128, 7 * KC))
    shared = {
        "gains": gains,
        "w1": np.ascontiguousarray(np.asarray(inputs["ffn_w1"], f).reshape(4, D, FF)),
        "w3": np.ascontiguousarray(np.asarray(inputs["ffn_w3"], f).reshape(4, D, FF)),
        "w2": np.ascontiguousarray(np.asarray(inputs["ffn_w2"], f).reshape(4, FF, D)),
    }
    shared["w_in"] = np.ascontiguousarray(np.asarray(inputs["ab_w_in"], f)[0])
    shared["w_out"] = np.ascontiguousarray(np.asarray(inputs["ab_w_out"], f)[0])
    shared["glu_w"] = np.ascontiguousarray(np.asarray(inputs["ssm_glu_w"], f)[0])
    cwm = np.asarray(inputs["conv_w"], f)[0]
    shared["cw"] = np.ascontiguousarray(cwm.T.reshape(4, 128, 31).transpose(1, 0, 2).reshape(128, 4 * 31))
    chv = [np.asarray(inputs[k], f)[0] for k in ("conv_b", "conv_ln_g", "conv_ln_b", "ssm_d", "ssm_glu_b")]
    shared["chv"] = np.ascontiguousarray(np.stack([v.reshape(4, 128).T for v in chv], axis=1).reshape(128, 20))
    ldt = np.broadcast_to(np.asarray(inputs["ssm_log_dt"], f)[0][None, :], (64, 32))
    are = np.asarray(inputs["ssm_a_re"], f)[0].T
    aim = np.asarray(inputs["ssm_a_im"], f)[0].T
    shared["sp1"] = np.ascontiguousarray(np.stack([ldt, are, aim], axis=1).reshape(64, 96))
    bre = np.asarray(inputs["ssm_b_re"], f)[0].transpose(1, 0, 2).reshape(64, 512)
    bim = np.asarray(inputs["ssm_b_im"], f)[0].transpose(1, 0, 2).reshape(64, 512)
    cre = np.asarray(inputs["ssm_c_re"], f)[0].transpose(2, 0, 1).reshape(64, 512)
    cim = np.asarray(inputs["ssm_c_im"], f)[0].transpose(2, 0, 1).reshape(64, 512)
    shared["sp2"] = np.ascontiguousarray(np.stack([bre, bim, cre, cim], axis=1).reshape(64, 2048))
    shared["w_qkv"] = np.ascontiguousarray(np.asarray(inputs["attn_w_qkv"], f)[0])
    shared["w_o"] = np.ascontiguousarray(np.asarray(inputs["attn_w_o"], f)[0])
    shared.update(_const_tables())
    in_maps = []
    for c in range(NCORES):
        m = dict(shared)
        m["xT"] = np.ascontiguousarray(x[2 * c:2 * c + 2].transpose(0, 2, 1))
        in_maps.append(m)
    return in_maps


def _const_tables():
    f = np.float32
    ident = np.eye(128, dtype=f)
    p = np.arange(128)
    bmask = (p[None, :] // 16 >= p[:, None] // 16).astype(f)
    sel = np.zeros((128, 8, 240), f)
    for g8 in range(8):
        for h in range(16):
            sel[16 * g8 + h, g8, 112 + h] = 1.0
    cb = np.zeros((128, 2, 256), f)
    for par in range(2):
        cb[:, par, :] = np.where((par * 128 + p[:, None]) <= np.arange(256)[None, :], 0.0, -30000.0)
    en = np.zeros((8, 8, 128), f)
    for n in range(8):
        en[n, n, :] = 1.0
    negm = np.zeros((128, 4, 64), f)
    for qb in range(4, 8):
        for h in range(8):
            negm[:, qb - 4, h * 8 + qb:h * 8 + 8] = -1e30
    return {"cbias": np.ascontiguousarray(cb.reshape(128, 512)), "en": np.ascontiguousarray(en.reshape(8, 1024)),
            "negm": np.ascontiguousarray(negm.reshape(128, 256)),
            "ident": ident, "bmask": bmask, "selin": np.ascontiguousarray(sel.reshape(128, 1920)),
            "selout": np.ascontiguousarray(sel.reshape(128, 1920))}


_NC_CACHE = {}


def kernel(**inputs):
    in_maps = _prep_inputs(inputs)
    if "nc" not in _NC_CACHE:
        _NC_CACHE["nc"] = build_program()
    nc = _NC_CACHE["nc"]
    res = run_bass_kernel_spmd(nc, in_maps, core_ids=list(range(NCORES)))
    out = np.empty((2 * NCORES, L, D), np.float32)
    for c in range(NCORES):
        out[2 * c:2 * c + 2] = np.asarray(res.results[c]["outT"]).transpose(0, 2, 1)
    return out
```

```python
import numpy as np
import concourse.bass as bass
import concourse.mybir as mybir
from concourse.bass_utils import run_bass_kernel_spmd

F32 = mybir.dt.float32
BF16 = mybir.dt.bfloat16
AF = mybir.ActivationFunctionType
ALU = mybir.AluOpType
AX = mybir.AxisListType

D = 1024
KC = 8
FF = 2816
FC = 22
L = 2048
NSEQ = 2
TT = 512
RMS_EPS = 1e-6
LN_EPS = 1e-5
NCORES = 8


class Buf:
    __slots__ = ("name", "w", "r", "dsem", "dcnt")

    def __init__(self, name):
        self.name = name
        self.w = None
        self.r = {}
        self.dsem = {}
        self.dcnt = {}


class Sched:
    def __init__(self, nc):
        self.nc = nc
        self.eng = {"pe": nc.tensor, "act": nc.scalar, "dve": nc.vector, "pool": nc.gpsimd, "sp": nc.sync}
        self.sem = {k: nc.alloc_semaphore("s_" + k) for k in self.eng}
        self.cnt = {k: 0 for k in self.eng}
        self.seen = {k: {} for k in self.eng}
        self.pr = {k: [] for k in self.eng}
        self.pw = {k: [] for k in self.eng}
        self.nds = 0
        self.ninst = 0
        self.dbufs = []

    def _deps(self, reads, writes):
        deps = {}

        def need(s, v):
            if deps.get(s, 0) < v:
                deps[s] = v

        for b in reads:
            if b.w is not None:
                need(*b.w)
        for b in writes:
            if b.w is not None:
                need(*b.w)
            for s, v in b.r.items():
                need(s, v)
        return deps

    def _wait(self, eng, deps):
        e = self.eng[eng]
        own = self.sem[eng]
        for s, v in deps.items():
            if s is own and eng == "pe":
                continue
            if self.seen[eng].get(s, 0) >= v:
                continue
            e.wait_ge(s, v)
            self.seen[eng][s] = v

    def op(self, eng, fn, reads=(), writes=(), signal=True):
        self._wait(eng, self._deps(reads, writes))
        ins = fn(self.eng[eng])
        self.ninst += 1
        self.pr[eng].extend(reads)
        self.pw[eng].extend(writes)
        if signal:
            own = self.sem[eng]
            self.cnt[eng] += 1
            ins.then_inc(own, 1)
            v = self.cnt[eng]
            for b in self.pr[eng]:
                b.r[own] = v
            for b in self.pw[eng]:
                b.w = (own, v)
                b.r = {}
            self.pr[eng] = []
            self.pw[eng] = []
        return ins

    def dma(self, q, out, in_, reads=(), writes=()):
        self._wait(q, self._deps(reads, writes))
        owner = writes[0] if writes else reads[0]
        kind = "sw" if q == "pool" else "hw"
        if kind not in owner.dsem:
            owner.dsem[kind] = self.nc.alloc_semaphore("d%d" % self.nds)
            owner.dcnt[kind] = 0
            self.nds += 1
            self.dbufs.append((owner, kind))
        owner.dcnt[kind] += 1
        sem = owner.dsem[kind]
        self.eng[q].dma_start(out=out, in_=in_).then_inc(sem, 16)
        self.ninst += 1
        tag = (sem, 16 * owner.dcnt[kind])
        for b in reads:
            b.r[tag[0]] = tag[1]
        for b in writes:
            b.w = tag
            b.r = {}

    def barrier(self):
        for k in self.eng:
            assert not self.pr[k] and not self.pw[k], "unsignalled ops pending on " + k
        for k in self.eng:
            deps = {self.sem[o]: self.cnt[o] for o in self.eng if o != k and self.cnt[o] > 0}
            for b, kind in self.dbufs:
                deps[b.dsem[kind]] = 16 * b.dcnt[kind]
            self._wait(k, deps)

    def wait_all(self, eng, bufs):
        deps = {}
        for b in bufs:
            if b.w is not None:
                deps[b.w[0]] = max(deps.get(b.w[0], 0), b.w[1])
            for s, v in b.r.items():
                deps[s] = max(deps.get(s, 0), v)
        self._wait(eng, deps)


class Arena:
    def __init__(self, nc, nbytes):
        self.t = nc.alloc_sbuf_tensor("arena", [128, nbytes // 2], BF16)
        self.nbytes = nbytes
        self.off = 0
        self.marks = []

    def alloc(self, shape, dtype, at=None):
        esz = 2 if dtype == BF16 else 4
        n = int(np.prod(shape[1:]))
        nb = n * esz
        off = self.off if at is None else at
        off = (off + 31) // 32 * 32
        assert off + nb <= self.nbytes, ("arena overflow", off, nb, self.nbytes)
        if at is None:
            self.off = off + nb
        ap = self.t[0:shape[0], off // 2:(off + nb) // 2]
        if dtype != BF16:
            ap = ap.bitcast(dtype)
        if len(shape) == 3:
            ap = ap.rearrange("p (a b) -> p a b", a=shape[1])
        elif len(shape) == 4:
            ap = ap.rearrange("p (a b c) -> p a b c", a=shape[1], b=shape[2])
        return ap


def build_program(stages=("ffn00", "mix0", "ffn01", "ffn10", "mix1", "ffn11", "final"), nseq=NSEQ, debug=False):
    nc = bass.Bass("TRN2", target_bir_lowering=False)
    S = Sched(nc)
    dbg_names = []

    def dump(name, ap2d, bufs):
        if not debug or name in dbg_names:
            return
        dbg_names.append(name)
        shp = list(ap2d.shape)
        dt = nc.dram_tensor("dbg_" + name, shp, F32, kind="ExternalOutput").ap()
        S.dma("pool", dt, ap2d, reads=bufs)

    def din(name, shape, dt=F32):
        return nc.dram_tensor(name, list(shape), dt, kind="ExternalInput").ap()

    xT_d = din("xT", [NSEQ, D, L])
    gains_d = din("gains", [128, 7 * KC])
    w1_d = din("w1", [4, D, FF])
    w3_d = din("w3", [4, D, FF])
    w2_d = din("w2", [4, FF, D])
    outT_d = nc.dram_tensor("outT", [NSEQ, D, L], F32, kind="ExternalOutput").ap()

    A = Arena(nc, 207 * 1024)
    xT = A.alloc([128, KC, L], F32)
    gains = A.alloc([128, 7, KC], F32)
    ones_bf = A.alloc([128, 128], BF16)
    epsc = A.alloc([128, 1], F32)
    cw = A.alloc([128, 4, 31], F32)
    chv = A.alloc([128, 5, 4], F32)
    lnepsc = A.alloc([128, 1], F32)
    xsq = [A.alloc([128, TT], BF16) for _ in range(2)]
    rstd = A.alloc([128, TT], F32)
    sil = [A.alloc([128, TT], BF16) for _ in range(2)]
    base_off = A.off
    hT = A.alloc([128, KC, 1024], BF16)
    G = A.alloc([128, FC, 1024], BF16)
    NS13 = 3
    w13 = [A.alloc([128, 2, KC, 256], BF16) for _ in range(NS13)]
    w2s = A.alloc([128, FC, D], BF16)
    ffn_end = A.off

    PS = [nc.alloc_psum_tensor("ps%d" % i, [128, TT], F32) for i in range(8)]
    PSB = [Buf("ps%d" % i) for i in range(8)]

    XB = [[Buf("x%d_%d" % (k, b)) for b in range(L // TT)] for k in range(KC)]
    HB = [[Buf("h%d_%d" % (k, t)) for t in range(2)] for k in range(KC)]
    GB = [[Buf("g%d_%d" % (f, t)) for t in range(2)] for f in range(FC)]
    W13B = [(Buf("w1_%d" % i), Buf("w3_%d" % i)) for i in range(NS13)]
    W2B = [Buf("w2s%d" % i) for i in range(FC // 2)]
    XSQB = [Buf("xsq%d" % i) for i in range(2)]
    RSTDB = Buf("rstd")
    SILB = [Buf("sil%d" % i) for i in range(2)]
    CONSTB = Buf("const")

    S.dma("sp", gains.rearrange("p a b -> p (a b)"), gains_d, writes=[CONSTB])
    S.op("dve", lambda e: e.memset(ones_bf, 1.0), writes=[CONSTB])
    S.op("dve", lambda e: e.memset(epsc, RMS_EPS), writes=[CONSTB])

    w13_state = {"n": 0}

    def load_w13(fi, fg):
        i = w13_state["n"] % NS13
        w13_state["n"] += 1
        slot = w13[i]
        b = W13B[i]
        src1 = w1_d[fi].rearrange("(kc p) f -> p kc f", p=128)[:, :, fg * 256:(fg + 1) * 256]
        src3 = w3_d[fi].rearrange("(kc p) f -> p kc f", p=128)[:, :, fg * 256:(fg + 1) * 256]
        S.dma("pool", slot[:, 0], src1, writes=[b[0]])
        S.dma("pool", slot[:, 1], src3, writes=[b[1]])
        return slot, b

    def load_w2(fi, j):
        src = w2_d[fi].rearrange("(fc p) d -> p fc d", p=128)
        S.dma("pool", w2s[:, 2 * j:2 * j + 2], src[:, 2 * j:2 * j + 2], writes=[W2B[j]])

    def rmsnorm_tile(gidx, t0, ntt, dstT, dstB, inplace=False):
        for tt in range(ntt):
            blk = (t0 + tt * TT) // TT
            ps = PS[6]
            psb = PSB[6]
            for kc in range(KC):
                q = dstT[:, kc, tt * TT:(tt + 1) * TT]
                xs = xT[:, kc, blk * TT:(blk + 1) * TT]
                S.op("act", lambda e, q=q, xs=xs: e.activation(out=q, in_=xs, func=AF.Square),
                     reads=[XB[kc][blk], CONSTB], writes=[dstB[kc][tt]])
            for kc in range(KC):
                q = dstT[:, kc, tt * TT:(tt + 1) * TT]
                S.op("pe", lambda e, q=q, kc=kc: e.matmul(ps[:], ones_bf, q, start=(kc == 0), stop=(kc == KC - 1)),
                     reads=[dstB[kc][tt], CONSTB], writes=[psb], signal=(kc == KC - 1))
            S.op("act", lambda e: e.activation(out=rstd, in_=ps[:], func=AF.Sqrt, scale=1.0 / D, bias=epsc),
                 reads=[psb, CONSTB], writes=[RSTDB])
            S.op("dve", lambda e: e.reciprocal(out=rstd, in_=rstd), reads=[RSTDB], writes=[RSTDB])
            for kc in range(KC):
                xs = xT[:, kc, blk * TT:(blk + 1) * TT]
                if inplace:
                    S.op("dve", lambda e, kc=kc, xs=xs: e.scalar_tensor_tensor(
                        out=xs, in0=xs, scalar=gains[:, gidx, kc:kc + 1], in1=rstd, op0=ALU.mult, op1=ALU.mult),
                        reads=[XB[kc][blk], RSTDB, CONSTB, dstB[kc][tt]], writes=[XB[kc][blk]])
                else:
                    S.op("dve", lambda e, kc=kc, xs=xs: e.scalar_tensor_tensor(
                        out=dstT[:, kc, tt * TT:(tt + 1) * TT], in0=xs,
                        scalar=gains[:, gidx, kc:kc + 1], in1=rstd, op0=ALU.mult, op1=ALU.mult),
                        reads=[XB[kc][blk], RSTDB, CONSTB], writes=[dstB[kc][tt]])

    def ffn_tile(fi, gidx, t0, do_pre=True, hook=None):
        if do_pre:
            rmsnorm_tile(gidx, t0, 2, hT, HB)
        pcount = 0
        slots = {}
        for fg in range(min(NS13, FC // 2)):
            slots[fg] = load_w13(fi, fg)
        for fg in range(FC // 2):
            slot, sb = slots.pop(fg)
            for fh in range(2):
                fc = fg * 2 + fh
                for tt in range(2):
                    pa = (pcount % 2) * 2
                    pcount += 1
                    for j, (pi, wi) in enumerate(((pa, 0), (pa + 1, 1))):
                        for kc in range(KC):
                            S.op("pe", lambda e, pi=pi, wi=wi, kc=kc: e.matmul(
                                PS[pi][:], slot[:, wi, kc, fh * 128:(fh + 1) * 128], hT[:, kc, tt * TT:(tt + 1) * TT],
                                start=(kc == 0), stop=(kc == KC - 1)),
                                reads=[sb[wi], HB[kc][tt]], writes=[PSB[pi]], signal=(kc == KC - 1))
                    sl = sil[pcount % 2]
                    slb = SILB[pcount % 2]
                    S.op("act", lambda e, sl=sl, pa=pa: e.activation(out=sl, in_=PS[pa][:], func=AF.Silu),
                         reads=[PSB[pa]], writes=[slb])
                    S.op("dve", lambda e, sl=sl, pa=pa, fc=fc, tt=tt: e.tensor_tensor(
                        out=G[:, fc, tt * TT:(tt + 1) * TT], in0=sl, in1=PS[pa + 1][:], op=ALU.mult),
                        reads=[slb, PSB[pa + 1]], writes=[GB[fc][tt]])
            if fg + NS13 < FC // 2:
                slots[fg + NS13] = load_w13(fi, fg + NS13)
            load_w2(fi, fg)
        pcount = 0
        for dc in range(KC):
            for tt in range(2):
                pi = 4 + (pcount % 2)
                pcount += 1
                blk = (t0 + tt * TT) // TT
                for fc in range(FC):
                    S.op("pe", lambda e, pi=pi, fc=fc, dc=dc, tt=tt: e.matmul(
                        PS[pi][:], w2s[:, fc, dc * 128:(dc + 1) * 128], G[:, fc, tt * TT:(tt + 1) * TT],
                        start=(fc == 0), stop=(fc == FC - 1)),
                        reads=[W2B[fc // 2], GB[fc][tt]], writes=[PSB[pi]], signal=(fc == FC - 1))
                xs = xT[:, dc, blk * TT:(blk + 1) * TT]
                S.op("dve", lambda e, pi=pi, xs=xs: e.scalar_tensor_tensor(
                    out=xs, in0=PS[pi][:], scalar=0.5, in1=xs, op0=ALU.mult, op1=ALU.add),
                    reads=[PSB[pi], XB[dc][blk]], writes=[XB[dc][blk]])
            if dc == 1 and hook is not None:
                hook()

    def ffn_chain(jobs, last_hook=None):
        for j, (fi, gidx, t0) in enumerate(jobs):
            nxt = jobs[j + 1] if j + 1 < len(jobs) else None
            hook = last_hook
            if nxt is not None:
                assert nxt[2] != t0
                hook = (lambda nxt=nxt: rmsnorm_tile(nxt[1], nxt[2], 2, hT, HB))
            ffn_tile(fi, gidx, t0, do_pre=(j == 0), hook=hook)

    RB = base_off
    KB = 1024

    def R(off_kb, shape, dt):
        return A.alloc(shape, dt, at=RB + int(off_kb * KB))

    I32 = mybir.dt.int32
    win_d = din("w_in", [D, 1536])
    wout_d = din("w_out", [D, D])
    gluw_d = din("glu_w", [512, 512])
    cw_d = din("cw", [128, 4 * 31])
    chv_d = din("chv", [128, 5 * 4])
    sp1_d = din("sp1", [64, 3 * 32])
    sp2_d = din("sp2", [64, 4 * 512])
    ident_d = din("ident", [128, 128])
    bmask_d = din("bmask", [128, 128])
    selin_d = din("selin", [128, 8 * 240])
    selout_d = din("selout", [128, 8 * 240])
    scr_toep = nc.dram_tensor("scr_toep", [128, 32 * 128], BF16, kind="Internal").ap()
    scr_wb = nc.dram_tensor("scr_wb", [128, 2 * 32 * 64], BF16, kind="Internal").ap()
    scr_cm = nc.dram_tensor("scr_cm", [64, 2 * 32 * 128], BF16, kind="Internal").ap()
    scr_at = nc.dram_tensor("scr_at", [64, 128], F32, kind="Internal").ap()

    M0CB = Buf("m0const")
    S.dma("sp", cw.rearrange("p a b -> p (a b)"), cw_d, writes=[M0CB])
    S.dma("sp", chv.rearrange("p a b -> p (a b)"), chv_d, writes=[M0CB])
    S.op("dve", lambda e: e.memset(lnepsc, LN_EPS), writes=[M0CB])

    uT = R(0, [128, 4, L], BF16)
    uT4 = R(0, [128, 4, 8, 256], BF16)
    mcat = R(16, [128, 8, L], BF16)
    Xbf = R(32, [128, 2, 16, 256], BF16)
    woutS = R(116, [128, 8, D], BF16)
    hT2 = R(64, [128, KC, L], BF16)
    winS = R(96, [128, KC, 1536], BF16)
    vpad = R(47.5, [128, 4, 30 + L], BF16)
    ycv = R(64, [128, 4, L], F32)
    lnt = [R(96 + 2 * i, [128, TT], F32) for i in range(5)]
    selin = R(130, [128, 8, 240], BF16)
    selout = R(68, [128, 8, 240], BF16)
    gluwS = R(72, [128, 4, 512], BF16)
    toepS = R(76, [128, 32, 128], BF16)
    wbS = R(84, [128, 2, 32, 64], BF16)
    cmS = R(92, [128, 2, 16, 128], BF16)
    atS = R(108, [128, 2, 2, 16], F32)
    Xh = R(109, [128, 32, 2, 16], F32)
    pq = R(113, [128, 2, 2, 16], F32)
    rtmp = R(114, [128, 2, 16], F32)
    Ut = R(48, [128, 32, 256], BF16)
    gt = [R(76 + 2 * i, [128, TT], F32) for i in range(6)]
    y1bS = R(92, [128, 4, L], BF16)
    sgt = [R(88 + i, [128, TT], BF16) for i in range(2)]

    diag = [R(o_, [128, 31, 128], BF16) for o_ in (32, 39.75, 114, 121.75)]
    identM = R(135.25, [128, 128], BF16)
    DIAGB = [Buf("diag%d" % i) for i in range(4)]
    IDMB = Buf("identM")
    UB = [[Buf("u%d_%d" % (c, t)) for t in range(4)] for c in range(4)]
    MCB = [[Buf("mc%d_%d" % (c, t)) for t in range(4)] for c in range(8)]
    H2B = [[Buf("h2_%d_%d" % (k, t)) for t in range(4)] for k in range(KC)]
    WINB = [Buf("win%d" % i) for i in range(3)]
    WOUTB = Buf("woutS")
    VB = [Buf("v%d" % c) for c in range(4)]
    YCB = [Buf("yc%d" % c) for c in range(4)]
    LNTB = [Buf("lnt%d" % i) for i in range(5)]
    PPB = Buf("ssmparams")
    SELIB = Buf("selin"); WBB = Buf("wbS"); TOEPB = Buf("toepS")
    CMB = [Buf("cm%d" % i) for i in range(4)]
    ATB = [Buf("at%d" % i) for i in range(8)]
    UTB = [Buf("ut%d" % g) for g in range(32)]
    XBFB = Buf("xbf")
    XHB = [Buf("xh%d" % i) for i in range(32)]
    PQB = Buf("pq")
    RTB = Buf("rtmp")
    GTB = [Buf("gt%d" % i) for i in range(6)]
    Y1B = [[Buf("y1_%d_%d" % (c, t)) for t in range(4)] for c in range(4)]
    SGTB = [Buf("sgt%d" % i) for i in range(2)]

    def ssm_param_prep():
        S.barrier()
        P = PPB
        cnt = {"o": 0}

        def T(shape, dt=F32):
            n = int(np.prod(shape[1:])) * (4 if dt != BF16 else 2)
            off = cnt["o"]
            cnt["o"] = (off + n + 31) // 32 * 32
            assert RB + cnt["o"] <= A.nbytes, cnt["o"]
            return A.alloc(shape, dt, at=RB + off)

        def v_tt(out, a, b, op):
            S.op("dve", lambda e: e.tensor_tensor(out=out, in0=a, in1=b, op=op), reads=[P], writes=[P])

        def v_ts(out, a, s1, s2, op0, op1=None):
            if op1 is None:
                S.op("dve", lambda e: e.tensor_scalar(out=out, in0=a, scalar1=s1, scalar2=None, op0=op0), reads=[P], writes=[P])
            else:
                S.op("dve", lambda e: e.tensor_scalar(out=out, in0=a, scalar1=s1, scalar2=s2, op0=op0, op1=op1), reads=[P], writes=[P])

        def a_act(out, a, func, scale=1.0):
            S.op("act", lambda e: e.activation(out=out, in_=a, func=func, scale=scale), reads=[P], writes=[P])

        def v_cp(out, a):
            S.op("dve", lambda e: e.tensor_copy(out=out, in_=a), reads=[P], writes=[P])

        sp1 = T([64, 3, 32]); sp2 = T([64, 4, 512])
        S.dma("act", sp1.rearrange("p a b -> p (a b)"), sp1_d, writes=[P])
        S.dma("act", sp2.rearrange("p a b -> p (a b)"), sp2_d, writes=[P])
        identS = T([128, 128], BF16); bmaskS = T([128, 128])
        S.dma("pool", identS, ident_d, writes=[P])
        S.dma("act", bmaskS, bmask_d, writes=[P])
        ldt, are, aim = sp1[:, 0], sp1[:, 1], sp1[:, 2]
        bre = sp2[:, 0].rearrange("p (g h) -> p g h", g=32)
        bim = sp2[:, 1].rearrange("p (g h) -> p g h", g=32)
        cre = sp2[:, 2].rearrange("p (g h) -> p g h", g=32)
        cim = sp2[:, 3].rearrange("p (g h) -> p g h", g=32)
        dt_ = T([64, 32]); mag = T([64, 32]); ang = T([64, 32]); t1 = T([64, 32]); t2 = T([64, 32])
        ti = T([64, 32], I32); cosv = T([64, 32]); sinv = T([64, 32])
        a_act(dt_, ldt, AF.Exp)
        v_tt(t1, dt_, are, ALU.mult)
        a_act(mag, t1, AF.Exp)
        v_tt(ang, dt_, aim, ALU.mult)
        TWO_PI = 2.0 * np.pi

        def sin_of(out, shift):
            v_ts(t1, ang, 1.0 / TWO_PI, float(shift), ALU.mult, ALU.add)
            v_cp(ti, t1)
            v_cp(t2, ti)
            v_tt(t1, t1, t2, ALU.subtract)
            v_ts(t2, t1, 0.5, None, ALU.is_gt)
            v_tt(t1, t1, t2, ALU.subtract)
            v_ts(t2, t1, -0.5, None, ALU.is_lt)
            v_tt(t1, t1, t2, ALU.add)
            a_act(out, t1, AF.Sin, scale=TWO_PI)

        dump("dt", dt_, [P]); dump("mag", mag, [P]); dump("ang", ang, [P])
        sin_of(sinv, 0.0)
        dump("frac_s", t1, [P]); dump("sinv", sinv, [P])
        sin_of(cosv, 0.25)
        dump("cosv", cosv, [P])
        pwr = T([64, 9, 32]); pwi = T([64, 9, 32])
        S.op("dve", lambda e: e.memset(pwr[:, 0], 1.0), reads=[P], writes=[P])
        S.op("dve", lambda e: e.memset(pwi[:, 0], 0.0), reads=[P], writes=[P])
        v_tt(pwr[:, 1], mag, cosv, ALU.mult)
        v_tt(pwi[:, 1], mag, sinv, ALU.mult)
        for k in range(1, 8):
            v_tt(t1, pwr[:, k], pwr[:, 1], ALU.mult)
            v_tt(t2, pwi[:, k], pwi[:, 1], ALU.mult)
            v_tt(pwr[:, k + 1], t1, t2, ALU.subtract)
            v_tt(t1, pwr[:, k], pwi[:, 1], ALU.mult)
            v_tt(t2, pwi[:, k], pwr[:, 1], ALU.mult)
            v_tt(pwi[:, k + 1], t1, t2, ALU.add)
        dump("pwr", pwr.rearrange("p k g -> p (k g)"), [P]); dump("pwi", pwi.rearrange("p k g -> p (k g)"), [P])
        ipr = T([64, 9, 32]); ipi = T([64, 9, 32]); n2 = T([64, 9, 32]); n3 = T([64, 9, 32])
        v_tt(n2, pwr, pwr, ALU.mult)
        v_tt(n3, pwi, pwi, ALU.mult)
        v_tt(n2, n2, n3, ALU.add)
        S.op("dve", lambda e: e.reciprocal(out=n2, in_=n2), reads=[P], writes=[P])
        v_tt(ipr, pwr, n2, ALU.mult)
        v_tt(ipi, pwi, n2, ALU.mult)
        v_ts(ipi, ipi, -1.0, None, ALU.mult)
        den = T([64, 32]); nr = T([64, 32]); qre = T([64, 32]); qim = T([64, 32])
        v_tt(den, are, are, ALU.mult)
        v_tt(t1, aim, aim, ALU.mult)
        v_tt(den, den, t1, ALU.add)
        S.op("dve", lambda e: e.reciprocal(out=den, in_=den), reads=[P], writes=[P])
        v_ts(nr, pwr[:, 1], -1.0, None, ALU.add)
        v_tt(t1, nr, are, ALU.mult)
        v_tt(t2, pwi[:, 1], aim, ALU.mult)
        v_tt(t1, t1, t2, ALU.add)
        v_tt(qre, t1, den, ALU.mult)
        v_tt(t1, pwi[:, 1], are, ALU.mult)
        v_tt(t2, nr, aim, ALU.mult)
        v_tt(t1, t1, t2, ALU.subtract)
        v_tt(qim, t1, den, ALU.mult)
        Bre = T([64, 32, 16]); Bim = T([64, 32, 16]); w1_ = T([64, 32, 16]); w2_ = T([64, 32, 16])
        qre_b = qre.unsqueeze(2).to_broadcast([64, 32, 16])
        qim_b = qim.unsqueeze(2).to_broadcast([64, 32, 16])
        v_tt(w1_, bre, qre_b, ALU.mult); v_tt(w2_, bim, qim_b, ALU.mult); v_tt(Bre, w1_, w2_, ALU.subtract)
        v_tt(w1_, bim, qre_b, ALU.mult); v_tt(w2_, bre, qim_b, ALU.mult); v_tt(Bim, w1_, w2_, ALU.add)
        big_off = cnt["o"]
        big1 = T([64, 32, 8, 16]); big2 = T([64, 32, 8, 16])
        Cmr = T([64, 32, 8, 16]); Cmi = T([64, 32, 8, 16])
        q0 = cnt["o"]
        Bsr = T([64, 32, 8, 16], BF16); Bsi = T([64, 32, 8, 16], BF16)
        cmr_bf = T([64, 32, 8, 16], BF16); cmi_bf = T([64, 32, 8, 16], BF16)
        Bmr = A.alloc([64, 32, 8, 16], F32, at=RB + q0); Bmi = A.alloc([64, 32, 8, 16], F32, at=RB + q0 + 16 * KB)

        def bc_x(x):
            return x.unsqueeze(2).to_broadcast([64, 32, 8, 16])

        def bc_p(p, lo, hi, rev=False):
            sl = p[:, lo:hi, :].rearrange("p k g -> p g k")
            return sl.unsqueeze(3).to_broadcast([64, 32, 8, 16])

        def cmul_big(o_re, o_im, xr, xi, pr, pi, neg_im=False):
            v_tt(big1, bc_x(xr), pr, ALU.mult); v_tt(big2, bc_x(xi), pi, ALU.mult)
            v_tt(o_re, big1, big2, ALU.subtract)
            v_tt(big1, bc_x(xr), pi, ALU.mult); v_tt(big2, bc_x(xi), pr, ALU.mult)
            if neg_im:
                v_tt(o_im, big1, big2, ALU.add)
                v_ts(o_im, o_im, -1.0, None, ALU.mult)
            else:
                v_tt(o_im, big1, big2, ALU.add)

        cmul_big(Cmr, Cmi, cre, cim, bc_p(pwr, 1, 9), bc_p(pwi, 1, 9), neg_im=True)
        v_cp(cmr_bf, Cmr); v_cp(cmi_bf, Cmi)
        S.dma("sp", scr_cm[:, 0:4096], cmr_bf.rearrange("p g j h -> p (g j h)"), reads=[P])
        S.dma("sp", scr_cm[:, 4096:8192], cmi_bf.rearrange("p g j h -> p (g j h)"), reads=[P])
        pwr_rev = T([64, 8, 32]); pwi_rev = T([64, 8, 32])
        for i in range(8):
            v_cp(pwr_rev[:, i], pwr[:, 7 - i]); v_cp(pwi_rev[:, i], pwi[:, 7 - i])
        prr = pwr_rev.rearrange("p k g -> p g k").unsqueeze(3).to_broadcast([64, 32, 8, 16])
        pri = pwi_rev.rearrange("p k g -> p g k").unsqueeze(3).to_broadcast([64, 32, 8, 16])
        cmul_big(Bsr, Bsi, Bre, Bim, prr, pri)
        wb_bf = T([128, 2, 32, 64], BF16)
        for o, src in ((0, Bsr), (1, Bsi)):
            for g8 in range(4):
                ps = PS[7]
                for gg in range(8):
                    g = g8 * 8 + gg
                    S.op("pe", lambda e, g=g, gg=gg, src=src: e.matmul(ps[:, gg * 64:(gg + 1) * 64], src[:, g].rearrange("p i h -> p (i h)"),
                                                                       identS[0:64, 0:64], start=True, stop=True),
                         reads=[P], writes=[PSB[7]], signal=(gg == 7))
                S.op("dve", lambda e, o=o, g8=g8: e.tensor_copy(out=wb_bf[:, o, g8 * 8:(g8 + 1) * 8, :],
                                                               in_=ps[:].rearrange("p (a b) -> p a b", a=8)), reads=[PSB[7], P], writes=[P])
        S.dma("sp", scr_wb, wb_bf.rearrange("p o g n -> p (o g n)"), reads=[P])
        at_ = T([64, 2, 2, 32])
        v_cp(at_[:, 0, 0], pwr[:, 8]); v_cp(at_[:, 1, 1], pwr[:, 8]); v_cp(at_[:, 1, 0], pwi[:, 8])
        v_ts(at_[:, 0, 1], pwi[:, 8], -1.0, None, ALU.mult)
        S.dma("sp", scr_at, at_.rearrange("p o k g -> p (o k g)"), reads=[P])
        cmul_big(Bmr, Bmi, Bre, Bim, bc_p(ipr, 1, 9), bc_p(ipi, 1, 9))
        toep_bf = A.alloc([128, 32, 128], BF16, at=RB + big_off)
        for g4 in range(8):
            ps = PS[7]
            for gg in range(4):
                g = g4 * 4 + gg
                S.op("pe", lambda e, g=g, gg=gg: e.matmul(ps[:, gg * 128:(gg + 1) * 128], Bmr[:, g].rearrange("p i h -> p (i h)"),
                                                          Cmr[:, g].rearrange("p j h -> p (j h)"), start=True, stop=False),
                     reads=[P], writes=[PSB[7]], signal=False)
                S.op("pe", lambda e, g=g, gg=gg: e.matmul(ps[:, gg * 128:(gg + 1) * 128], Bmi[:, g].rearrange("p i h -> p (i h)"),
                                                          Cmi[:, g].rearrange("p j h -> p (j h)"), start=False, stop=True),
                     reads=[P], writes=[PSB[7]], signal=(gg == 3))
            S.op("dve", lambda e, g4=g4: e.tensor_tensor(
                out=toep_bf[:, g4 * 4:(g4 + 1) * 4, :], in0=ps[:].rearrange("p (a b) -> p a b", a=4),
                in1=bmaskS.unsqueeze(1).to_broadcast([128, 4, 128]), op=ALU.mult), reads=[PSB[7], P], writes=[P])
        S.dma("sp", scr_toep, toep_bf.rearrange("p g m -> p (g m)"), reads=[P])
        S.barrier()

    def mixer0(s):
        S.barrier()
        for i in range(3):
            S.dma("pool", winS[:, :, i * 512:(i + 1) * 512],
                  win_d.rearrange("(kc p) f -> p kc f", p=128)[:, :, i * 512:(i + 1) * 512], writes=[WINB[i]])
        S.dma("pool", identM, ident_d, writes=[IDMB])
        for ct in range(2):
            for k in range(31):
                S.op("dve", lambda e, ct=ct, k=k: e.tensor_scalar(out=diag[ct][:, k, :], in0=identM, scalar1=cw[:, ct, k:k + 1], scalar2=None,
                                                              op0=ALU.mult), reads=[IDMB, M0CB], writes=[DIAGB[ct]])
        rmsnorm_tile(4, 0, 4, hT2, H2B)
        S.op("pool", lambda e: e.memset(vpad[:, :, 0:30], 0.0), writes=VB)
        pc = 0
        for ct in range(4):
            for tb in range(4):
                pa, pg = (pc % 2) * 2, (pc % 2) * 2 + 1
                pc += 1
                for pi, oc in ((pa, ct), (pg, ct + 4)):
                    for kc in range(KC):
                        S.op("pe", lambda e, pi=pi, oc=oc, kc=kc, tb=tb: e.matmul(
                            PS[pi][:], winS[:, kc, oc * 128:(oc + 1) * 128], hT2[:, kc, tb * TT:(tb + 1) * TT],
                            start=(kc == 0), stop=(kc == KC - 1)),
                            reads=[WINB[oc // 4], H2B[kc][tb]], writes=[PSB[pi]], signal=(kc == KC - 1))
                sg = sil[pc % 2]
                S.op("act", lambda e, sg=sg, pg=pg: e.activation(out=sg, in_=PS[pg][:], func=AF.Sigmoid),
                     reads=[PSB[pg]], writes=[SILB[pc % 2]])
                S.op("dve", lambda e, sg=sg, pa=pa, ct=ct, tb=tb: e.tensor_tensor(
                    out=vpad[:, ct, 30 + tb * TT:30 + (tb + 1) * TT], in0=sg, in1=PS[pa][:], op=ALU.mult),
                    reads=[SILB[pc % 2], PSB[pa]], writes=[VB[ct]])
        for ct in range(4):
            for tb in range(4):
                pi = 4 + (pc % 2)
                pc += 1
                oc = 8 + ct
                for kc in range(KC):
                    S.op("pe", lambda e, pi=pi, oc=oc, kc=kc, tb=tb: e.matmul(
                        PS[pi][:], winS[:, kc, oc * 128:(oc + 1) * 128], hT2[:, kc, tb * TT:(tb + 1) * TT],
                        start=(kc == 0), stop=(kc == KC - 1)),
                        reads=[WINB[2], H2B[kc][tb]], writes=[PSB[pi]], signal=(kc == KC - 1))
                S.op("act", lambda e, pi=pi, ct=ct, tb=tb: e.activation(out=uT4[:, ct, :, tb * 64:(tb + 1) * 64],
                                                                       in_=PS[pi][:].rearrange("p (c i) -> p i c", i=8), func=AF.Copy),
                     reads=[PSB[pi]], writes=[UB[ct][tb]])
        S.barrier()
        S.dma("pool", selin.rearrange("p a b -> p (a b)"), selin_d, writes=[SELIB])
        for ct in range(2, 4):
            for k in range(31):
                S.op("dve", lambda e, ct=ct, k=k: e.tensor_scalar(out=diag[ct][:, k, :], in0=identM, scalar1=cw[:, ct, k:k + 1], scalar2=None,
                                                              op0=ALU.mult), reads=[IDMB, M0CB], writes=[DIAGB[ct]])
        lnb = [R(96 + 2 * i, [128, TT], F32) for i in range(7)]
        ybf = [R(110 + i, [128, TT], BF16) for i in range(2)]
        ysq = [R(112 + i, [128, TT], BF16) for i in range(2)]
        LNB = [Buf("lnb%d" % i) for i in range(7)]
        YBB = [Buf("ybf%d" % i) for i in range(2)]
        YSB = [Buf("ysq%d" % i) for i in range(2)]
        YC2 = [[Buf("yc%d_%d" % (c, t)) for t in range(4)] for c in range(4)]
        tiles = [(tb, ct) for tb in range(4) for ct in range(4)]

        def conv_tile(n):
            tb, ct = tiles[n]
            pi = 2 + (n % 2)
            j = n % 2
            for k in range(31):
                S.op("pe", lambda e, pi=pi, ct=ct, k=k, tb=tb: e.matmul(
                    PS[pi][:], diag[ct][:, k, :], vpad[:, ct, k + tb * TT:k + (tb + 1) * TT], start=(k == 0), stop=(k == 30)),
                    reads=[DIAGB[ct], VB[ct]], writes=[PSB[pi]], signal=(k == 30))
            bia = chv[:, 0, ct:ct + 1]
            S.op("act", lambda e, pi=pi, ct=ct, tb=tb: e.activation(out=ycv[:, ct, tb * TT:(tb + 1) * TT], in_=PS[pi][:], func=AF.Identity, bias=bia),
                 reads=[PSB[pi], M0CB], writes=[YC2[ct][tb]])
            S.op("act", lambda e, pi=pi, j=j: e.activation(out=ybf[j], in_=PS[pi][:], func=AF.Identity, bias=bia),
                 reads=[PSB[pi], M0CB], writes=[YBB[j]])
            S.op("act", lambda e, pi=pi, j=j: e.activation(out=ysq[j], in_=PS[pi][:], func=AF.Square, bias=bia),
                 reads=[PSB[pi], M0CB], writes=[YSB[j]])

        def stat_mm(n):
            tb, ct = tiles[n]
            j = n % 2
            sb_, qb_ = ((0, 1), (4, 5))[tb % 2]
            S.op("pe", lambda e: e.matmul(PS[sb_][:], ones_bf, ybf[j], start=(ct == 0), stop=(ct == 3)),
                 reads=[YBB[j], CONSTB], writes=[PSB[sb_]], signal=True)
            S.op("pe", lambda e: e.matmul(PS[qb_][:], ones_bf, ysq[j], start=(ct == 0), stop=(ct == 3)),
                 reads=[YSB[j], CONSTB], writes=[PSB[qb_]], signal=True)

        def ln_stages(tb):
            sb_, qb_ = ((0, 1), (4, 5))[tb % 2]
            mean, var, msq = lnb[2 + tb % 2], lnb[4 + tb % 2], lnb[6]
            MB_, VB_, QB_ = LNB[2 + tb % 2], LNB[4 + tb % 2], LNB[6]
            sl = slice(tb * TT, (tb + 1) * TT)

            def st_a():
                S.op("act", lambda e: e.activation(out=mean, in_=PS[sb_][:], func=AF.Copy, scale=1.0 / 512), reads=[PSB[sb_]], writes=[MB_])
                S.op("act", lambda e: e.activation(out=msq, in_=PS[sb_][:], func=AF.Square, scale=1.0 / 512), reads=[PSB[sb_]], writes=[QB_])
                S.op("dve", lambda e: e.scalar_tensor_tensor(out=var, in0=PS[qb_][:], scalar=1.0 / 512, in1=msq, op0=ALU.mult, op1=ALU.subtract),
                     reads=[PSB[qb_], QB_], writes=[VB_])

            def st_b():
                S.op("act", lambda e: e.activation(out=var, in_=var, func=AF.Sqrt, bias=lnepsc), reads=[VB_, M0CB], writes=[VB_])
                S.op("dve", lambda e: e.reciprocal(out=var, in_=var), reads=[VB_], writes=[VB_])

            def t_ops(ct):
                t_ = lnb[ct % 2]
                TB_ = LNB[ct % 2]
                S.op("dve", lambda e: e.tensor_tensor(out=t_, in0=ycv[:, ct, sl], in1=mean, op=ALU.subtract),
                     reads=[YC2[ct][tb], MB_], writes=[TB_])
                S.op("dve", lambda e: e.tensor_tensor(out=t_, in0=t_, in1=var, op=ALU.mult),
                     reads=[TB_, VB_], writes=[TB_])

            def silu(ct):
                t_ = lnb[ct % 2]
                S.op("act", lambda e: e.activation(out=mcat[:, ct, sl], in_=t_, func=AF.Silu,
                                                   scale=chv[:, 1, ct:ct + 1], bias=chv[:, 2, ct:ct + 1]),
                     reads=[LNB[ct % 2], M0CB], writes=[MCB[ct][tb]])

            return [st_a, st_b, lambda: (t_ops(0), t_ops(1)), lambda: (silu(0), silu(1), t_ops(2), t_ops(3)), lambda: (silu(2), silu(3))]

        pending = []
        conv_tile(0)
        for n in range(len(tiles)):
            if n + 1 < len(tiles):
                conv_tile(n + 1)
            stat_mm(n)
            for st in pending:
                if st:
                    st.pop(0)()
            if tiles[n][1] == 3:
                pending.append(ln_stages(tiles[n][0]))
        while any(pending):
            for st in pending:
                if st:
                    st.pop(0)()
        dump("mcA", mcat[:, 0:4, :].rearrange("p a b -> p (a b)"), [b for r in MCB[0:4] for b in r])
        dump("u", uT.rearrange("p a b -> p (a b)"), [b for r in UB for b in r])
        S.barrier()
        S.dma("sp", wbS.rearrange("p o g n -> p (o g n)"), scr_wb, writes=[WBB])
        S.dma("sp", toepS.rearrange("p g m -> p (g m)"), scr_toep, writes=[TOEPB])
        S.dma("pool", selout.rearrange("p a b -> p (a b)"), selout_d, writes=[PPB])
        S.dma("pool", gluwS, gluw_d.rearrange("(kc p) f -> p kc f", p=128), writes=[PPB])
        scr_cm4 = scr_cm.rearrange("p (o g m) -> p o g m", o=2, g=32)
        scr_at4 = scr_at.rearrange("p (o k g) -> p o k g", o=2, k=2)
        for gh in range(2):
            for o in range(2):
                S.dma("sp", cmS[64 * gh:64 * gh + 64, o, :, :], scr_cm4[:, o, gh * 16:(gh + 1) * 16, :], writes=[CMB[gh * 2 + o]])
                for k in range(2):
                    S.dma("sp", atS[64 * gh:64 * gh + 64, o, k, :], scr_at4[:, o, k, gh * 16:(gh + 1) * 16], writes=[ATB[gh * 4 + o * 2 + k]])
        for g2 in range(16):
            pi = g2 % 2
            for gg in range(2):
                g = g2 * 2 + gg
                ct, g8 = g // 8, g % 8
                for i in range(8):
                    S.op("pe", lambda e, pi=pi, gg=gg, ct=ct, g8=g8, i=i: e.matmul(
                        PS[pi][:, gg * 256:(gg + 1) * 256], selin[:, g8, 112 - 16 * i:240 - 16 * i], uT4[:, ct, i, :],
                        start=(i == 0), stop=(i == 7)),
                        reads=[SELIB] + UB[ct], writes=[PSB[pi]], signal=(gg == 1 and i == 7))
            S.op("act", lambda e, pi=pi, g2=g2: e.activation(out=Ut[:, g2 * 2:g2 * 2 + 2, :],
                                                            in_=PS[pi][:].rearrange("p (a b) -> p a b", a=2), func=AF.Copy),
                 reads=[PSB[pi]], writes=[UTB[g2 * 2], UTB[g2 * 2 + 1]])
        S.op("dve", lambda e: e.memset(Xh[:, 31], 0.0), writes=[XHB[31]])
        for cb in range(16):
            pi = 2 + (cb % 2)
            psv = PS[pi][:].rearrange("p (c o g) -> p c o g", c=16, o=2)
            for g in range(32):
                gh, gl = g // 16, g % 16
                for o in range(2):
                    S.op("pe", lambda e, psv=psv, g=g, gh=gh, gl=gl, o=o, cb=cb: e.matmul(
                        psv[64 * gh:64 * gh + 64, :, o, gl], wbS[:, o, g, :], Ut[:, g, cb * 16:(cb + 1) * 16], start=True, stop=True),
                        reads=[WBB, UTB[g]], writes=[PSB[pi]], signal=(g == 31 and o == 1))
            for cc in range(16):
                c = cb * 16 + cc
                prev = Xh[:, (c - 1) % 32]
                S.op("dve", lambda e, prev=prev: e.tensor_tensor(
                    out=pq, in0=prev.unsqueeze(1).to_broadcast([128, 2, 2, 16]), in1=atS, op=ALU.mult),
                    reads=[XHB[(c - 1) % 32]] + ATB, writes=[PQB])
                S.op("dve", lambda e: e.tensor_tensor(out=rtmp, in0=pq[:, :, 0, :], in1=pq[:, :, 1, :], op=ALU.add),
                     reads=[PQB], writes=[RTB])
                S.op("dve", lambda e, psv=psv, cc=cc, c=c: e.tensor_tensor(out=Xh[:, c % 32], in0=rtmp, in1=psv[:, cc], op=ALU.add),
                     reads=[RTB, PSB[pi]], writes=[XHB[c % 32]])
            half = (cb % 2) * 16
            S.op("act", lambda e, cb=cb, half=half: e.activation(
                out=Xbf[:, :, :, cb * 16:(cb + 1) * 16], in_=Xh[:, half:half + 16].rearrange("p c o g -> p o g c"), func=AF.Copy),
                reads=XHB[half:half + 16], writes=[XBFB])
        for g2 in range(16):
            pi = 4 + (g2 % 2)
            for gg in range(2):
                g = g2 * 2 + gg
                S.op("pe", lambda e, pi=pi, gg=gg, g=g: e.matmul(PS[pi][:, gg * 256:(gg + 1) * 256], toepS[:, g, :], Ut[:, g, :],
                                                                start=True, stop=False),
                     reads=[TOEPB, UTB[g]], writes=[PSB[pi]], signal=False)
                hs = slice(64 * (g // 16), 64 * (g // 16) + 64)
                gl = g % 16
                S.op("pe", lambda e, pi=pi, gg=gg, gl=gl, hs=hs: e.matmul(PS[pi][:, gg * 256 + 1:(gg + 1) * 256], cmS[hs, 0, gl, :], Xbf[hs, 0, gl, 0:255],
                                                                start=False, stop=False),
                     reads=CMB + [XBFB], writes=[PSB[pi]], signal=False)
                S.op("pe", lambda e, pi=pi, gg=gg, gl=gl, hs=hs: e.matmul(PS[pi][:, gg * 256 + 1:(gg + 1) * 256], cmS[hs, 1, gl, :], Xbf[hs, 1, gl, 0:255],
                                                                start=False, stop=True),
                     reads=CMB + [XBFB], writes=[PSB[pi]], signal=(gg == 1))
            S.op("act", lambda e, pi=pi, g2=g2: e.activation(out=Ut[:, g2 * 2:g2 * 2 + 2, :],
                                                            in_=PS[pi][:].rearrange("p (a b) -> p a b", a=2), func=AF.Copy),
                 reads=[PSB[pi]], writes=[UTB[g2 * 2], UTB[g2 * 2 + 1]])
        dump("toep", toepS.rearrange("p g m -> p (g m)"), [TOEPB])
        dump("wb", wbS.rearrange("p o g n -> p (o g n)"), [WBB])
        dump("Y", Ut.rearrange("p g c -> p (g c)"), UTB)
        S.barrier()
        S.dma("pool", woutS, wout_d.rearrange("(kc p) f -> p kc f", p=128), writes=[WOUTB])
        pc = 0
        for ct in range(4):
            for tb in range(4):
                pi = pc % 2
                pc += 1
                for j in range(8):
                    for g8 in range(8):
                        S.op("pe", lambda e, pi=pi, j=j, g8=g8, ct=ct, tb=tb: e.matmul(
                            PS[pi][:, j * 64:(j + 1) * 64], selout[:, j, 112 - 16 * g8:240 - 16 * g8],
                            Ut[:, ct * 8 + g8, tb * 64:(tb + 1) * 64], start=(g8 == 0), stop=(g8 == 7)),
                            reads=[PPB, UTB[ct * 8 + g8]], writes=[PSB[pi]], signal=(j == 7 and g8 == 7))
                ys, x2, x3, sg_ = gt[0], gt[1], gt[2], gt[3]
                sl = slice(tb * TT, (tb + 1) * TT)
                S.op("dve", lambda e, pi=pi, ct=ct, sl=sl: e.scalar_tensor_tensor(
                    out=ys.rearrange("p (c j) -> p c j", j=8), in0=uT4[:, ct, :, tb * 64:(tb + 1) * 64].rearrange("p j c -> p c j"),
                    scalar=chv[:, 3, ct:ct + 1], in1=PS[pi][:].rearrange("p (j c) -> p c j", j=8), op0=ALU.mult, op1=ALU.add),
                    reads=[PSB[pi], UB[ct][tb], M0CB], writes=[GTB[0]])
                S.op("act", lambda e: e.activation(out=x2, in_=ys, func=AF.Square), reads=[GTB[0]], writes=[GTB[1]])
                S.op("dve", lambda e: e.tensor_scalar(out=x2, in0=x2, scalar1=0.044715, scalar2=1.0, op0=ALU.mult, op1=ALU.add),
                     reads=[GTB[1]], writes=[GTB[1]])
                S.op("dve", lambda e: e.tensor_tensor(out=x3, in0=x2, in1=ys, op=ALU.mult), reads=[GTB[1], GTB[0]], writes=[GTB[2]])
                S.op("act", lambda e: e.activation(out=sg_, in_=x3, func=AF.Sigmoid, scale=1.5957691216057308), reads=[GTB[2]], writes=[GTB[3]])
                S.op("dve", lambda e, ct=ct, sl=sl: e.tensor_tensor(out=y1bS[:, ct, sl], in0=ys, in1=sg_, op=ALU.mult),
                     reads=[GTB[0], GTB[3]], writes=[Y1B[ct][tb]])
        for ot in range(4):
            for tb in range(4):
                pi = 2 + (pc % 2)
                pc += 1
                sl = slice(tb * TT, (tb + 1) * TT)
                for ct in range(4):
                    S.op("pe", lambda e, pi=pi, ct=ct, ot=ot, sl=sl: e.matmul(
                        PS[pi][:], gluwS[:, ct, ot * 128:(ot + 1) * 128], y1bS[:, ct, sl], start=(ct == 0), stop=(ct == 3)),
                        reads=[PPB, Y1B[ct][tb]], writes=[PSB[pi]], signal=(ct == 3))
                sg = sgt[pc % 2]
                S.op("act", lambda e, pi=pi, sg=sg, ot=ot: e.activation(out=sg, in_=PS[pi][:], func=AF.Sigmoid, bias=chv[:, 4, ot:ot + 1]),
                     reads=[PSB[pi], M0CB], writes=[SGTB[pc % 2]])
                S.op("dve", lambda e, sg=sg, ot=ot, sl=sl: e.tensor_tensor(out=mcat[:, 4 + ot, sl], in0=y1bS[:, ot, sl], in1=sg, op=ALU.mult),
                     reads=[SGTB[pc % 2], Y1B[ot][tb]], writes=[MCB[4 + ot][tb]])
        dump("mcB", mcat[:, 4:8, :].rearrange("p a b -> p (a b)"), [b for r in MCB[4:8] for b in r])
        dump("y1", y1bS.rearrange("p a b -> p (a b)"), [b for r in Y1B for b in r])
        for dc in range(KC):
            for tb in range(4):
                pi = 4 + (pc % 2)
                pc += 1
                sl = slice(tb * TT, (tb + 1) * TT)
                for mc in range(8):
                    S.op("pe", lambda e, pi=pi, mc=mc, dc=dc, sl=sl: e.matmul(
                        PS[pi][:], woutS[:, mc, dc * 128:(dc + 1) * 128], mcat[:, mc, sl], start=(mc == 0), stop=(mc == 7)),
                        reads=[WOUTB, MCB[mc][tb]], writes=[PSB[pi]], signal=(mc == 7))
                xs = xT[:, dc, sl]
                S.op("dve", lambda e, pi=pi, xs=xs: e.tensor_tensor(out=xs, in0=PS[pi][:], in1=xs, op=ALU.add),
                     reads=[PSB[pi], XB[dc][tb]], writes=[XB[dc][tb]])
        S.barrier()

    wqkv_d = din("w_qkv", [D, 3 * D])
    wo_d = din("w_o", [D, D])
    cbias_d = din("cbias", [128, 2 * 256])
    en_d = din("en", [8, 8 * 128])
    negm_d = din("negm", [128, 4 * 64])
    NEG = -30000.0
    ahT = R(0, [128, KC, L], BF16)
    qring = [R(32 + 8 * i, [128, KC, 512], BF16) for i in range(2)]
    qT = R(48, [128, 4, L], BF16)
    kT = R(64, [128, 4, L], BF16)
    Vh = R(80, [128, 16, 512], BF16)
    oT = R(96, [128, 4, L], BF16)
    woS = R(112, [128, 4, D], BF16)
    pT = [R(120 + i, [128, 2, 256], BF16) for i in range(2)]
    cbiasS = R(122, [128, 2, 256], BF16)
    EnS = R(123, [8, 8, 128], BF16)
    identA = R(125, [128, 128], BF16)
    negmS = R(125.5, [128, 4, 64], F32)
    rden = R(126.5, [128, 256], F32)
    mbT = [R(127.5 + 2 * i, [8, 4, 256], BF16) for i in range(2)]
    kmf = R(131.5, [128, 4, 8], F32)
    kmT = R(131.75, [128, 4, 8], BF16)
    gsb = R(132, [128, 32], F32)
    top8 = R(132.25, [128, 8], F32)
    mball = R(132.5, [128, 8, 32], BF16)
    rden2 = R(133, [128, 256], F32)
    rdens = [rden, rden2]
    RDBS = [Buf("rden0"), Buf("rden1")]
    mbTall = R(32, [8, 4, 4, 256], BF16)
    assert RB + int(134 * KB) <= A.nbytes

    AHB = [[Buf("ah%d_%d" % (k, t)) for t in range(4)] for k in range(KC)]
    QRB = [Buf("qring%d" % i) for i in range(2)]
    QTB = [Buf("qT%d" % h) for h in range(4)]
    KTB = [Buf("kT%d" % h) for h in range(4)]
    VHB = [Buf("vh%d" % k) for k in range(16)]
    OTB = [[Buf("oT%d_%d" % (h, q)) for q in range(8)] for h in range(4)]
    WOB = Buf("woS")
    PTB = [Buf("pT%d" % i) for i in range(2)]
    ACB = Buf("attnconst")
    RDB = Buf("rden")
    MBTB = [Buf("mbT%d" % i) for i in range(2)]
    KMB = Buf("km")
    GSB = Buf("gsb")
    T8B = Buf("top8")
    MBB = Buf("mb")
    SCALE = 128.0 ** -0.5

    def mixer1(s):
        S.barrier()
        S.dma("pool", cbiasS.rearrange("p a b -> p (a b)"), cbias_d, writes=[ACB])
        S.dma("pool", EnS.rearrange("p a b -> p (a b)"), en_d, writes=[ACB])
        S.dma("pool", identA, ident_d, writes=[ACB])
        S.dma("sp", negmS.rearrange("p a b -> p (a b)"), negm_d, writes=[ACB])
        rmsnorm_tile(5, 0, 4, ahT, AHB)
        nring = [0]
        pcs = [0]

        def load_cols(c0):
            i = nring[0] % 2
            nring[0] += 1
            S.dma("pool", qring[i], wqkv_d.rearrange("(kc p) f -> p kc f", p=128)[:, :, c0:c0 + 512], writes=[QRB[i]])
            return qring[i], QRB[i]

        for half in range(2):
            for which, dstT, dstB in ((0, qT, QTB), (1, kT, KTB)):
                if which == 0 and half == 1:
                    wsl, wb_ = pre_q
                else:
                    wsl, wb_ = load_cols(which * D + half * 512)
                for hl in range(4):
                    for tb in range(4):
                        pi = pcs[0] % 2
                        pcs[0] += 1
                        for kc in range(KC):
                            S.op("pe", lambda e, pi=pi, wsl=wsl, kc=kc, hl=hl, tb=tb: e.matmul(
                                PS[pi][:], wsl[:, kc, hl * 128:(hl + 1) * 128], ahT[:, kc, tb * TT:(tb + 1) * TT],
                                start=(kc == 0), stop=(kc == KC - 1)),
                                reads=[wb_, AHB[kc][tb]], writes=[PSB[pi]], signal=(kc == KC - 1))
                        S.op("act", lambda e, pi=pi, dstT=dstT, hl=hl, tb=tb: e.activation(
                            out=dstT[:, hl, tb * TT:(tb + 1) * TT], in_=PS[pi][:], func=AF.Copy),
                            reads=[PSB[pi]], writes=[dstB[hl]])
            wsl, wb_ = load_cols(2 * D + half * 512)
            for kt in range(16):
                pi = pcs[0] % 2
                pcs[0] += 1
                tb = kt // 4
                for kc in range(KC):
                    S.op("pe", lambda e, pi=pi, wsl=wsl, kc=kc, kt=kt: e.matmul(
                        PS[pi][:], ahT[:, kc, kt * 128:(kt + 1) * 128], wsl[:, kc, :], start=(kc == 0), stop=(kc == KC - 1)),
                        reads=[wb_, AHB[kc][tb]], writes=[PSB[pi]], signal=(kc == KC - 1))
                S.op("act", lambda e, pi=pi, kt=kt: e.activation(out=Vh[:, kt, :], in_=PS[pi][:], func=AF.Copy),
                     reads=[PSB[pi]], writes=[VHB[kt]])
            S.dma("pool", woS, wo_d.rearrange("(kc p) f -> p kc f", p=128)[:, half * 4:(half + 1) * 4, :], writes=[WOB])
            if half == 0:
                pre_q = load_cols(0 * D + 1 * 512)
            for hl in range(4):
                S.op("dve", lambda e, hl=hl: e.tensor_reduce(out=kmf[:, hl, :], in_=kT[:, hl, :].rearrange("p (n k) -> p n k", n=8),
                                                            axis=AX.X, op=ALU.add), reads=[KTB[hl]], writes=[KMB])
            S.op("dve", lambda e: e.tensor_copy(out=kmT, in_=kmf), reads=[KMB], writes=[KMB])
            for qb in range(4, 8):
                for q2 in range(2):
                    idx = (qb - 4) * 2 + q2
                    qsl = slice(qb * 256 + q2 * 128, qb * 256 + (q2 + 1) * 128)
                    for hl in range(4):
                        S.op("pe", lambda e, hl=hl, qsl=qsl, idx=idx: e.matmul(PS[6][:, idx * 32 + hl * 8:idx * 32 + (hl + 1) * 8], qT[:, hl, qsl],
                                                                              kmT[:, hl, :], start=True, stop=True),
                             reads=[QTB[hl], KMB], writes=[PSB[6]], signal=(hl == 3))
            for qb in range(4, 8):
                for q2 in range(2):
                    idx = (qb - 4) * 2 + q2
                    S.op("dve", lambda e, idx=idx, qb=qb: e.tensor_tensor(out=gsb, in0=PS[6][:, idx * 32:(idx + 1) * 32], in1=negmS[:, qb - 4, 0:32], op=ALU.add),
                         reads=[PSB[6], ACB], writes=[GSB])
                    for hl in range(4):
                        S.op("dve", lambda e, hl=hl: e.max(out=top8, in_=gsb[:, hl * 8:(hl + 1) * 8]), reads=[GSB], writes=[T8B])
                        S.op("dve", lambda e, hl=hl, idx=idx: e.tensor_scalar(out=mball[:, idx, hl * 8:(hl + 1) * 8], in0=gsb[:, hl * 8:(hl + 1) * 8],
                                                                    scalar1=top8[:, 2:3], scalar2=NEG, op0=ALU.is_lt, op1=ALU.mult),
                             reads=[GSB, T8B], writes=[MBB])

            def mask_transposes():
                for qb in range(4, 8):
                    for q2 in range(2):
                        idx = (qb - 4) * 2 + q2
                        tbk = 6 + (idx % 2)
                        for hl in range(4):
                            S.op("pe", lambda e, hl=hl, idx=idx, tbk=tbk: e.matmul(PS[tbk][0:8, hl * 128:(hl + 1) * 128], mball[:, idx, hl * 8:(hl + 1) * 8], identA,
                                                                         start=True, stop=True),
                                 reads=[MBB, ACB], writes=[PSB[tbk]], signal=(hl == 3))
                        S.op("act", lambda e, qb=qb, q2=q2, tbk=tbk: e.activation(out=mbTall[:, qb - 4, :, q2 * 128:(q2 + 1) * 128],
                                                                        in_=PS[tbk][0:8, :].rearrange("p (h q) -> p h q", h=4), func=AF.Copy),
                             reads=[PSB[tbk]], writes=[MBTB[0]])

            items = []
            for qb in range(8):
                for hl in range(4):
                    npair = qb + 1
                    for kp in range(npair):
                        items.append((qb, hl, kp, kp == 0, kp == npair - 1))

            def emit_S(i):
                qb, hl, kp, _, _ = items[i]
                qs = slice(qb * 256, (qb + 1) * 256)
                gated = qb >= 4
                pi = 2 + (i % 2)
                for k2 in range(2):
                    kt = kp * 2 + k2
                    n = kp
                    osl = PS[pi][:, k2 * 256:(k2 + 1) * 256]
                    extra = (n == qb) or gated
                    S.op("pe", lambda e, osl=osl, hl=hl, kt=kt, extra=extra, qs=qs: e.matmul(
                        osl, kT[:, hl, kt * 128:(kt + 1) * 128], qT[:, hl, qs], start=True, stop=(not extra)),
                        reads=[KTB[hl], QTB[hl]], writes=[PSB[pi]], signal=((not extra) and k2 == 1))
                    if n == qb:
                        S.op("pe", lambda e, osl=osl, k2=k2: e.matmul(osl, identA, cbiasS[:, k2, :], start=False, stop=True),
                             reads=[ACB], writes=[PSB[pi]], signal=(k2 == 1))
                    elif gated:
                        S.op("pe", lambda e, osl=osl, n=n, hl=hl, qb=qb: e.matmul(osl, EnS[:, n, :], mbTall[:, qb - 4, hl, :], start=False, stop=True),
                             reads=[ACB, MBTB[0]], writes=[PSB[pi]], signal=(k2 == 1))

            def emit_rest(i, gi):
                qb, hl, kp, first, last = items[i]
                qs = slice(qb * 256, (qb + 1) * 256)
                pi = 2 + (i % 2)
                ti = i % 2
                po, pd = ((4, 5), (0, 1))[gi % 2]
                S.op("act", lambda e, pi=pi, ti=ti: e.activation(out=pT[ti].rearrange("p a b -> p (a b)"), in_=PS[pi][:],
                                                                func=AF.Exp, scale=SCALE),
                     reads=[PSB[pi]], writes=[PTB[ti]])
                for k2 in range(2):
                    kt = kp * 2 + k2
                    f_ = first and k2 == 0
                    l_ = last and k2 == 1
                    S.op("pe", lambda e, po=po, kt=kt, hl=hl, ti=ti, k2=k2, f_=f_, l_=l_: e.matmul(
                        PS[po][:, 0:256], Vh[:, kt, hl * 128:(hl + 1) * 128], pT[ti][:, k2, :], start=f_, stop=l_),
                        reads=[VHB[kt], PTB[ti]], writes=[PSB[po]], signal=False)
                    S.op("pe", lambda e, pd=pd, ti=ti, k2=k2, f_=f_, l_=l_: e.matmul(
                        PS[pd][:, 0:256], ones_bf, pT[ti][:, k2, :], start=f_, stop=l_),
                        reads=[CONSTB, PTB[ti]], writes=[PSB[pd]], signal=(k2 == 1))
                if last:
                    rd = rdens[gi % 2]
                    S.op("dve", lambda e, pd=pd, rd=rd: e.reciprocal(out=rd, in_=PS[pd][:, 0:256]), reads=[PSB[pd]], writes=[RDBS[gi % 2]])
                    S.op("dve", lambda e, po=po, hl=hl, rd=rd, qs=qs: e.tensor_tensor(out=oT[:, hl, qs], in0=PS[po][:, 0:256], in1=rd, op=ALU.mult),
                         reads=[PSB[po], RDBS[gi % 2]], writes=[OTB[hl][qb]])

            n_items = len(items)
            first_gated = next(i for i, it in enumerate(items) if it[0] >= 4)
            gi = 0
            emit_S(0)
            for i in range(n_items):
                if i + 1 < n_items:
                    if i + 1 == first_gated:
                        mask_transposes()
                    emit_S(i + 1)
                emit_rest(i, gi)
                if items[i][4]:
                    gi += 1
            if half == 0:
                dump("oT0", oT.rearrange("p a b -> p (a b)"), [b for r_ in OTB for b in r_])
                dump("qT0", qT.rearrange("p a b -> p (a b)"), QTB)
                dump("kT0", kT.rearrange("p a b -> p (a b)"), KTB)
                dump("Vh0", Vh.rearrange("p a b -> p (a b)"), VHB)
            for dc in range(KC):
                for tb in range(4):
                    pi = pcs[0] % 2
                    pcs[0] += 1
                    sl = slice(tb * TT, (tb + 1) * TT)
                    for hl in range(4):
                        S.op("pe", lambda e, pi=pi, hl=hl, dc=dc, sl=sl: e.matmul(
                            PS[pi][:], woS[:, hl, dc * 128:(dc + 1) * 128], oT[:, hl, sl], start=(hl == 0), stop=(hl == 3)),
                            reads=[WOB, OTB[hl][2 * tb], OTB[hl][2 * tb + 1]], writes=[PSB[pi]], signal=(hl == 3))
                    xs = xT[:, dc, sl]
                    S.op("dve", lambda e, pi=pi, xs=xs: e.tensor_tensor(out=xs, in0=PS[pi][:], in1=xs, op=ALU.add),
                         reads=[PSB[pi], XB[dc][tb]], writes=[XB[dc][tb]])
            S.barrier()

    OUTB = [Buf("o%d" % i) for i in range(2)]
    for s in range(nseq):
        def load_blocks(sq, blks):
            for blk in blks:
                for kc in range(KC):
                    S.dma("sp", xT[:, kc, blk * TT:(blk + 1) * TT], xT_d[sq, kc * 128:(kc + 1) * 128, blk * TT:(blk + 1) * TT],
                          writes=[XB[kc][blk]])

        if s == 0 or "final" not in stages:
            load_blocks(s, range(L // TT))
        if s == 0 and "mix0" in stages:
            ssm_param_prep()
        if "ffn00" in stages:
            ffn_chain([(0, 0, 0), (0, 0, 1024)])
        if "mix0" in stages:
            mixer0(s)
        if "ffn01" in stages and "ffn10" in stages:
            ffn_chain([(1, 1, 0), (1, 1, 1024), (2, 2, 0), (2, 2, 1024)])
        else:
            if "ffn01" in stages:
                ffn_chain([(1, 1, 0), (1, 1, 1024)])
            if "ffn10" in stages:
                ffn_chain([(2, 2, 0), (2, 2, 1024)])
        if "mix1" in stages:
            mixer1(s)
        def final_tile(t0, s=s):
            rmsnorm_tile(6, t0, 2, hT, HB, inplace=True)
            for blk in (t0 // TT, t0 // TT + 1):
                for kc in range(KC):
                    S.dma("sp", outT_d[s, kc * 128:(kc + 1) * 128, blk * TT:(blk + 1) * TT], xT[:, kc, blk * TT:(blk + 1) * TT],
                          reads=[XB[kc][blk]])
            if s + 1 < nseq:
                load_blocks(s + 1, (t0 // TT, t0 // TT + 1))

        fused_tail = ("ffn11" in stages) and ("final" in stages)
        if "ffn11" in stages:
            ffn_chain([(3, 3, 0), (3, 3, 1024)], last_hook=(lambda: final_tile(0)) if fused_tail else None)
        if "final" in stages:
            for t0 in ((1024,) if fused_tail else (0, 1024)):
                final_tile(t0)
        if "final" not in stages:
            for blk in range(L // TT):
                for kc in range(KC):
                    S.dma("sp", outT_d[s, kc * 128:(kc + 1) * 128, blk * TT:(blk + 1) * TT], xT[:, kc, blk * TT:(blk + 1) * TT],
                          reads=[XB[kc][blk]])
    S.wait_all("sp", [b for row in XB for b in row])
    nc._sched_ninst = S.ninst
    nc._dbg_names = dbg_names
    return nc


def _prep_inputs(inputs):
    f = np.float32
    x = np.asarray(inputs["x"], f)
    ffn_norm = np.asarray(inputs["ffn_norm"], f)
    vecs = [ffn_norm[0, 0], ffn_norm[0, 1], ffn_norm[1, 0], ffn_norm[1, 1],
            np.asarray(inputs["mix_norm"], f)[0], np.asarray(inputs["mix_norm"], f)[1],
            np.asarray(inputs["final_norm"], f)]
    gains = np.stack([v.reshape(KC, 128).T for v in vecs], axis=1)
    gains = np.ascontiguousarray(gains.reshape(128, 7 * KC))
    shared = {
        "gains": gains,
        "w1": np.ascontiguousarray(np.asarray(inputs["ffn_w1"], f).reshape(4, D, FF)),
        "w3": np.ascontiguousarray(np.asarray(inputs["ffn_w3"], f).reshape(4, D, FF)),
        "w2": np.ascontiguousarray(np.asarray(inputs["ffn_w2"], f).reshape(4, FF, D)),
    }
    shared["w_in"] = np.ascontiguousarray(np.asarray(inputs["ab_w_in"], f)[0])
    shared["w_out"] = np.ascontiguousarray(np.asarray(inputs["ab_w_out"], f)[0])
    shared["glu_w"] = np.ascontiguousarray(np.asarray(inputs["ssm_glu_w"], f)[0])
    cwm = np.asarray(inputs["conv_w"], f)[0]
    shared["cw"] = np.ascontiguousarray(cwm.T.reshape(4, 128, 31).transpose(1, 0, 2).reshape(128, 4 * 31))
    chv = [np.asarray(inputs[k], f)[0] for k in ("conv_b", "conv_ln_g", "conv_ln_b", "ssm_d", "ssm_glu_b")]
    shared["chv"] = np.ascontiguousarray(np.stack([v.reshape(4, 128).T for v in chv], axis=1).reshape(128, 20))
    ldt = np.broadcast_to(np.asarray(inputs["ssm_log_dt"], f)[0][None, :], (64, 32))
    are = np.asarray(inputs["ssm_a_re"], f)[0].T
    aim = np.asarray(inputs["ssm_a_im"], f)[0].T
    shared["sp1"] = np.ascontiguousarray(np.stack([ldt, are, aim], axis=1).reshape(64, 96))
    bre = np.asarray(inputs["ssm_b_re"], f)[0].transpose(1, 0, 2).reshape(64, 512)
    bim = np.asarray(inputs["ssm_b_im"], f)[0].transpose(1, 0, 2).reshape(64, 512)
    cre = np.asarray(inputs["ssm_c_re"], f)[0].transpose(2, 0, 1).reshape(64, 512)
    cim = np.asarray(inputs["ssm_c_im"], f)[0].transpose(2, 0, 1).reshape(64, 512)
    shared["sp2"] = np.ascontiguousarray(np.stack([bre, bim, cre, cim], axis=1).reshape(64, 2048))
    shared["w_qkv"] = np.ascontiguousarray(np.asarray(inputs["attn_w_qkv"], f)[0])
    shared["w_o"] = np.ascontiguousarray(np.asarray(inputs["attn_w_o"], f)[0])
    shared.update(_const_tables())
    in_maps = []
    for c in range(NCORES):
        m = dict(shared)
        m["xT"] = np.ascontiguousarray(x[2 * c:2 * c + 2].transpose(0, 2, 1))
        in_maps.append(m)
    return in_maps


def _const_tables():
    f = np.float32
    ident = np.eye(128, dtype=f)
    p = np.arange(128)
    bmask = (p[None, :] // 16 >= p[:, None] // 16).astype(f)
    sel = np.zeros((128, 8, 240), f)
    for g8 in range(8):
        for h in range(16):
            sel[16 * g8 + h, g8, 112 + h] = 1.0
    cb = np.zeros((128, 2, 256), f)
    for par in range(2):
        cb[:, par, :] = np.where((par * 128 + p[:, None]) <= np.arange(256)[None, :], 0.0, -30000.0)
    en = np.zeros((8, 8, 128), f)
    for n in range(8):
        en[n, n, :] = 1.0
    negm = np.zeros((128, 4, 64), f)
    for qb in range(4, 8):
        for h in range(8):
            negm[:, qb - 4, h * 8 + qb:h * 8 + 8] = -1e30
    return {"cbias": np.ascontiguousarray(cb.reshape(128, 512)), "en": np.ascontiguousarray(en.reshape(8, 1024)),
            "negm": np.ascontiguousarray(negm.reshape(128, 256)),
            "ident": ident, "bmask": bmask, "selin": np.ascontiguousarray(sel.reshape(128, 1920)),
            "selout": np.ascontiguousarray(sel.reshape(128, 1920))}


_NC_CACHE = {}


def kernel(**inputs):
    in_maps = _prep_inputs(inputs)
    if "nc" not in _NC_CACHE:
        _NC_CACHE["nc"] = build_program()
    nc = _NC_CACHE["nc"]
    res = run_bass_kernel_spmd(nc, in_maps, core_ids=list(range(NCORES)))
    out = np.empty((2 * NCORES, L, D), np.float32)
    for c in range(NCORES):
        out[2 * c:2 * c + 2] = np.asarray(res.results[c]["outT"]).transpose(0, 2, 1)
    return out
```

```python
import numpy as np
import concourse.bass as bass
import concourse.mybir as mybir
from concourse.bass_utils import run_bass_kernel_spmd

F32 = mybir.dt.float32
BF16 = mybir.dt.bfloat16
AF = mybir.ActivationFunctionType
ALU = mybir.AluOpType
AX = mybir.AxisListType

D = 1024
KC = 8
FF = 2816
FC = 22
L = 2048
NSEQ = 2
TT = 512
RMS_EPS = 1e-6
LN_EPS = 1e-5
NCORES = 8


class Buf:
    __slots__ = ("name", "w", "r", "dsem", "dcnt")

    def __init__(self, name):
        self.name = name
        self.w = None
        self.r = {}
        self.dsem = {}
        self.dcnt = {}


class Sched:
    def __init__(self, nc):
        self.nc = nc
        self.eng = {"pe": nc.tensor, "act": nc.scalar, "dve": nc.vector, "pool": nc.gpsimd, "sp": nc.sync}
        self.sem = {k: nc.alloc_semaphore("s_" + k) for k in self.eng}
        self.cnt = {k: 0 for k in self.eng}
        self.seen = {k: {} for k in self.eng}
        self.pr = {k: [] for k in self.eng}
        self.pw = {k: [] for k in self.eng}
        self.nds = 0
        self.ninst = 0
        self.dbufs = []

    def _deps(self, reads, writes):
        deps = {}

        def need(s, v):
            if deps.get(s, 0) < v:
                deps[s] = v

        for b in reads:
            if b.w is not None:
                need(*b.w)
        for b in writes:
            if b.w is not None:
                need(*b.w)
            for s, v in b.r.items():
                need(s, v)
        return deps

    def _wait(self, eng, deps):
        e = self.eng[eng]
        own = self.sem[eng]
        for s, v in deps.items():
            if s is own and eng == "pe":
                continue
            if self.seen[eng].get(s, 0) >= v:
                continue
            e.wait_ge(s, v)
            self.seen[eng][s] = v

    def op(self, eng, fn, reads=(), writes=(), signal=True):
        self._wait(eng, self._deps(reads, writes))
        ins = fn(self.eng[eng])
        self.ninst += 1
        self.pr[eng].extend(reads)
        self.pw[eng].extend(writes)
        if signal:
            own = self.sem[eng]
            self.cnt[eng] += 1
            ins.then_inc(own, 1)
            v = self.cnt[eng]
            for b in self.pr[eng]:
                b.r[own] = v
            for b in self.pw[eng]:
                b.w = (own, v)
                b.r = {}
            self.pr[eng] = []
            self.pw[eng] = []
        return ins

    def dma(self, q, out, in_, reads=(), writes=()):
        self._wait(q, self._deps(reads, writes))
        owner = writes[0] if writes else reads[0]
        kind = "sw" if q == "pool" else "hw"
        if kind not in owner.dsem:
            owner.dsem[kind] = self.nc.alloc_semaphore("d%d" % self.nds)
            owner.dcnt[kind] = 0
            self.nds += 1
            self.dbufs.append((owner, kind))
        owner.dcnt[kind] += 1
        sem = owner.dsem[kind]
        self.eng[q].dma_start(out=out, in_=in_).then_inc(sem, 16)
        self.ninst += 1
        tag = (sem, 16 * owner.dcnt[kind])
        for b in reads:
            b.r[tag[0]] = tag[1]
        for b in writes:
            b.w = tag
            b.r = {}

    def barrier(self):
        for k in self.eng:
            assert not self.pr[k] and not self.pw[k], "unsignalled ops pending on " + k
        for k in self.eng:
            deps = {self.sem[o]: self.cnt[o] for o in self.eng if o != k and self.cnt[o] > 0}
            for b, kind in self.dbufs:
                deps[b.dsem[kind]] = 16 * b.dcnt[kind]
            self._wait(k, deps)

    def wait_all(self, eng, bufs):
        deps = {}
        for b in bufs:
            if b.w is not None:
                deps[b.w[0]] = max(deps.get(b.w[0], 0), b.w[1])
            for s, v in b.r.items():
                deps[s] = max(deps.get(s, 0), v)
        self._wait(eng, deps)


class Arena:
    def __init__(self, nc, nbytes):
        self.t = nc.alloc_sbuf_tensor("arena", [128, nbytes // 2], BF16)
        self.nbytes = nbytes
        self.off = 0
        self.marks = []

    def alloc(self, shape, dtype, at=None):
        esz = 2 if dtype == BF16 else 4
        n = int(np.prod(shape[1:]))
        nb = n * esz
        off = self.off if at is None else at
        off = (off + 31) // 32 * 32
        assert off + nb <= self.nbytes, ("arena overflow", off, nb, self.nbytes)
        if at is None:
            self.off = off + nb
        ap = self.t[0:shape[0], off // 2:(off + nb) // 2]
        if dtype != BF16:
            ap = ap.bitcast(dtype)
        if len(shape) == 3:
            ap = ap.rearrange("p (a b) -> p a b", a=shape[1])
        elif len(shape) == 4:
            ap = ap.rearrange("p (a b c) -> p a b c", a=shape[1], b=shape[2])
        return ap


def build_program(stages=("ffn00", "mix0", "ffn01", "ffn10", "mix1", "ffn11", "final"), nseq=NSEQ, debug=False):
    nc = bass.Bass("TRN2", target_bir_lowering=False)
    S = Sched(nc)
    dbg_names = []

    def dump(name, ap2d, bufs):
        if not debug or name in dbg_names:
            return
        dbg_names.append(name)
        shp = list(ap2d.shape)
        dt = nc.dram_tensor("dbg_" + name, shp, F32, kind="ExternalOutput").ap()
        S.dma("pool", dt, ap2d, reads=bufs)

    def din(name, shape, dt=F32):
        return nc.dram_tensor(name, list(shape), dt, kind="ExternalInput").ap()

    xT_d = din("xT", [NSEQ, D, L])
    gains_d = din("gains", [128, 7 * KC])
    w1_d = din("w1", [4, D, FF])
    w3_d = din("w3", [4, D, FF])
    w2_d = din("w2", [4, FF, D])
    outT_d = nc.dram_tensor("outT", [NSEQ, D, L], F32, kind="ExternalOutput").ap()

    A = Arena(nc, 207 * 1024)
    xT = A.alloc([128, KC, L], F32)
    gains = A.alloc([128, 7, KC], F32)
    ones_bf = A.alloc([128, 128], BF16)
    epsc = A.alloc([128, 1], F32)
    cw = A.alloc([128, 4, 31], F32)
    chv = A.alloc([128, 5, 4], F32)
    lnepsc = A.alloc([128, 1], F32)
    xsq = [A.alloc([128, TT], BF16) for _ in range(2)]
    rstd = A.alloc([128, TT], F32)
    sil = [A.alloc([128, TT], BF16) for _ in range(2)]
    base_off = A.off
    hT = A.alloc([128, KC, 1024], BF16)
    G = A.alloc([128, FC, 1024], BF16)
    NS13 = 3
    w13 = [A.alloc([128, 2, KC, 256], BF16) for _ in range(NS13)]
    w2s = A.alloc([128, FC, D], BF16)
    ffn_end = A.off

    PS = [nc.alloc_psum_tensor("ps%d" % i, [128, TT], F32) for i in range(8)]
    PSB = [Buf("ps%d" % i) for i in range(8)]

    XB = [[Buf("x%d_%d" % (k, b)) for b in range(L // TT)] for k in range(KC)]
    HB = [[Buf("h%d_%d" % (k, t)) for t in range(2)] for k in range(KC)]
    GB = [[Buf("g%d_%d" % (f, t)) for t in range(2)] for f in range(FC)]
    W13B = [(Buf("w1_%d" % i), Buf("w3_%d" % i)) for i in range(NS13)]
    W2B = [Buf("w2s%d" % i) for i in range(FC // 2)]
    XSQB = [Buf("xsq%d" % i) for i in range(2)]
    RSTDB = Buf("rstd")
    SILB = [Buf("sil%d" % i) for i in range(2)]
    CONSTB = Buf("const")

    S.dma("sp", gains.rearrange("p a b -> p (a b)"), gains_d, writes=[CONSTB])
    S.op("dve", lambda e: e.memset(ones_bf, 1.0), writes=[CONSTB])
    S.op("dve", lambda e: e.memset(epsc, RMS_EPS), writes=[CONSTB])

    w13_state = {"n": 0}

    def load_w13(fi, fg):
        i = w13_state["n"] % NS13
        w13_state["n"] += 1
        slot = w13[i]
        b = W13B[i]
        src1 = w1_d[fi].rearrange("(kc p) f -> p kc f", p=128)[:, :, fg * 256:(fg + 1) * 256]
        src3 = w3_d[fi].rearrange("(kc p) f -> p kc f", p=128)[:, :, fg * 256:(fg + 1) * 256]
        S.dma("pool", slot[:, 0], src1, writes=[b[0]])
        S.dma("pool", slot[:, 1], src3, writes=[b[1]])
        return slot, b

    def load_w2(fi, j):
        src = w2_d[fi].rearrange("(fc p) d -> p fc d", p=128)
        S.dma("pool", w2s[:, 2 * j:2 * j + 2], src[:, 2 * j:2 * j + 2], writes=[W2B[j]])

    def rmsnorm_tile(gidx, t0, ntt, dstT, dstB, inplace=False):
        for tt in range(ntt):
            blk = (t0 + tt * TT) // TT
            ps = PS[6]
            psb = PSB[6]
            for kc in range(KC):
                q = dstT[:, kc, tt * TT:(tt + 1) * TT]
                xs = xT[:, kc, blk * TT:(blk + 1) * TT]
                S.op("act", lambda e, q=q, xs=xs: e.activation(out=q, in_=xs, func=AF.Square),
                     reads=[XB[kc][blk], CONSTB], writes=[dstB[kc][tt]])
            for kc in range(KC):
                q = dstT[:, kc, tt * TT:(tt + 1) * TT]
                S.op("pe", lambda e, q=q, kc=kc: e.matmul(ps[:], ones_bf, q, start=(kc == 0), stop=(kc == KC - 1)),
                     reads=[dstB[kc][tt], CONSTB], writes=[psb], signal=(kc == KC - 1))
            S.op("act", lambda e: e.activation(out=rstd, in_=ps[:], func=AF.Sqrt, scale=1.0 / D, bias=epsc),
                 reads=[psb, CONSTB], writes=[RSTDB])
            S.op("dve", lambda e: e.reciprocal(out=rstd, in_=rstd), reads=[RSTDB], writes=[RSTDB])
            for kc in range(KC):
                xs = xT[:, kc, blk * TT:(blk + 1) * TT]
                if inplace:
                    S.op("dve", lambda e, kc=kc, xs=xs: e.scalar_tensor_tensor(
                        out=xs, in0=xs, scalar=gains[:, gidx, kc:kc + 1], in1=rstd, op0=ALU.mult, op1=ALU.mult),
                        reads=[XB[kc][blk], RSTDB, CONSTB, dstB[kc][tt]], writes=[XB[kc][blk]])
                else:
                    S.op("dve", lambda e, kc=kc, xs=xs: e.scalar_tensor_tensor(
                        out=dstT[:, kc, tt * TT:(tt + 1) * TT], in0=xs,
                        scalar=gains[:, gidx, kc:kc + 1], in1=rstd, op0=ALU.mult, op1=ALU.mult),
                        reads=[XB[kc][blk], RSTDB, CONSTB], writes=[dstB[kc][tt]])

    def ffn_tile(fi, gidx, t0, do_pre=True, hook=None):
        if do_pre:
            rmsnorm_tile(gidx, t0, 2, hT, HB)
        pcount = 0
        slots = {}
        for fg in range(min(NS13, FC // 2)):
            slots[fg] = load_w13(fi, fg)
        for fg in range(FC // 2):
            slot, sb = slots.pop(fg)
            for fh in range(2):
                fc = fg * 2 + fh
                for tt in range(2):
                    pa = (pcount % 2) * 2
                    pcount += 1
                    for j, (pi, wi) in enumerate(((pa, 0), (pa + 1, 1))):
                        for kc in range(KC):
                            S.op("pe", lambda e, pi=pi, wi=wi, kc=kc: e.matmul(
                                PS[pi][:], slot[:, wi, kc, fh * 128:(fh + 1) * 128], hT[:, kc, tt * TT:(tt + 1) * TT],
                                start=(kc == 0), stop=(kc == KC - 1)),
                                reads=[sb[wi], HB[kc][tt]], writes=[PSB[pi]], signal=(kc == KC - 1))
                    sl = sil[pcount % 2]
                    slb = SILB[pcount % 2]
                    S.op("act", lambda e, sl=sl, pa=pa: e.activation(out=sl, in_=PS[pa][:], func=AF.Silu),
                         reads=[PSB[pa]], writes=[slb])
                    S.op("dve", lambda e, sl=sl, pa=pa, fc=fc, tt=tt: e.tensor_tensor(
                        out=G[:, fc, tt * TT:(tt + 1) * TT], in0=sl, in1=PS[pa + 1][:], op=ALU.mult),
                        reads=[slb, PSB[pa + 1]], writes=[GB[fc][tt]])
            if fg + NS13 < FC // 2:
                slots[fg + NS13] = load_w13(fi, fg + NS13)
            load_w2(fi, fg)
        pcount = 0
        for dc in range(KC):
            for tt in range(2):
                pi = 4 + (pcount % 2)
                pcount += 1
                blk = (t0 + tt * TT) // TT
                for fc in range(FC):
                    S.op("pe", lambda e, pi=pi, fc=fc, dc=dc, tt=tt: e.matmul(
                        PS[pi][:], w2s[:, fc, dc * 128:(dc + 1) * 128], G[:, fc, tt * TT:(tt + 1) * TT],
                        start=(fc == 0), stop=(fc == FC - 1)),
                        reads=[W2B[fc // 2], GB[fc][tt]], writes=[PSB[pi]], signal=(fc == FC - 1))
                xs = xT[:, dc, blk * TT:(blk + 1) * TT]
                S.op("dve", lambda e, pi=pi, xs=xs: e.scalar_tensor_tensor(
                    out=xs, in0=PS[pi][:], scalar=0.5, in1=xs, op0=ALU.mult, op1=ALU.add),
                    reads=[PSB[pi], XB[dc][blk]], writes=[XB[dc][blk]])
            if dc == 1 and hook is not None:
                hook()

    def ffn_chain(jobs, last_hook=None):
        for j, (fi, gidx, t0) in enumerate(jobs):
            nxt = jobs[j + 1] if j + 1 < len(jobs) else None
            hook = last_hook
            if nxt is not None:
                assert nxt[2] != t0
                hook = (lambda nxt=nxt: rmsnorm_tile(nxt[1], nxt[2], 2, hT, HB))
            ffn_tile(fi, gidx, t0, do_pre=(j == 0), hook=hook)

    RB = base_off
    KB = 1024

    def R(off_kb, shape, dt):
        return A.alloc(shape, dt, at=RB + int(off_kb * KB))

    I32 = mybir.dt.int32
    win_d = din("w_in", [D, 1536])
    wout_d = din("w_out", [D, D])
    gluw_d = din("glu_w", [512, 512])
    cw_d = din("cw", [128, 4 * 31])
    chv_d = din("chv", [128, 5 * 4])
    sp1_d = din("sp1", [64, 3 * 32])
    sp2_d = din("sp2", [64, 4 * 512])
    ident_d = din("ident", [128, 128])
    bmask_d = din("bmask", [128, 128])
    selin_d = din("selin", [128, 8 * 240])
    selout_d = din("selout", [128, 8 * 240])
    scr_toep = nc.dram_tensor("scr_toep", [128, 32 * 128], BF16, kind="Internal").ap()
    scr_wb = nc.dram_tensor("scr_wb", [128, 2 * 32 * 64], BF16, kind="Internal").ap()
    scr_cm = nc.dram_tensor("scr_cm", [64, 2 * 32 * 128], BF16, kind="Internal").ap()
    scr_at = nc.dram_tensor("scr_at", [64, 128], F32, kind="Internal").ap()

    M0CB = Buf("m0const")
    S.dma("sp", cw.rearrange("p a b -> p (a b)"), cw_d, writes=[M0CB])
    S.dma("sp", chv.rearrange("p a b -> p (a b)"), chv_d, writes=[M0CB])
    S.op("dve", lambda e: e.memset(lnepsc, LN_EPS), writes=[M0CB])

    uT = R(0, [128, 4, L], BF16)
    uT4 = R(0, [128, 4, 8, 256], BF16)
    mcat = R(16, [128, 8, L], BF16)
    Xbf = R(32, [128, 2, 16, 256], BF16)
    woutS = R(116, [128, 8, D], BF16)
    hT2 = R(64, [128, KC, L], BF16)
    winS = R(96, [128, KC, 1536], BF16)
    vpad = R(47.5, [128, 4, 30 + L], BF16)
    ycv = R(64, [128, 4, L], F32)
    lnt = [R(96 + 2 * i, [128, TT], F32) for i in range(5)]
    selin = R(130, [128, 8, 240], BF16)
    selout = R(68, [128, 8, 240], BF16)
    gluwS = R(72, [128, 4, 512], BF16)
    toepS = R(76, [128, 32, 128], BF16)
    wbS = R(84, [128, 2, 32, 64], BF16)
    cmS = R(92, [128, 2, 16, 128], BF16)
    atS = R(108, [128, 2, 2, 16], F32)
    Xh = R(109, [128, 32, 2, 16], F32)
    pq = R(113, [128, 2, 2, 16], F32)
    rtmp = R(114, [128, 2, 16], F32)
    Ut = R(48, [128, 32, 256], BF16)
    gt = [R(76 + 2 * i, [128, TT], F32) for i in range(6)]
    y1bS = R(92, [128, 4, L], BF16)
    sgt = [R(88 + i, [128, TT], BF16) for i in range(2)]

    diag = [R(o_, [128, 31, 128], BF16) for o_ in (32, 39.75, 114, 121.75)]
    identM = R(135.25, [128, 128], BF16)
    DIAGB = [Buf("diag%d" % i) for i in range(4)]
    IDMB = Buf("identM")
    UB = [[Buf("u%d_%d" % (c, t)) for t in range(4)] for c in range(4)]
    MCB = [[Buf("mc%d_%d" % (c, t)) for t in range(4)] for c in range(8)]
    H2B = [[Buf("h2_%d_%d" % (k, t)) for t in range(4)] for k in range(KC)]
    WINB = [Buf("win%d" % i) for i in range(3)]
    WOUTB = Buf("woutS")
    VB = [Buf("v%d" % c) for c in range(4)]
    YCB = [Buf("yc%d" % c) for c in range(4)]
    LNTB = [Buf("lnt%d" % i) for i in range(5)]
    PPB = Buf("ssmparams")
    SELIB = Buf("selin"); WBB = Buf("wbS"); TOEPB = Buf("toepS")
    CMB = [Buf("cm%d" % i) for i in range(4)]
    ATB = [Buf("at%d" % i) for i in range(8)]
    UTB = [Buf("ut%d" % g) for g in range(32)]
    XBFB = Buf("xbf")
    XHB = [Buf("xh%d" % i) for i in range(32)]
    PQB = Buf("pq")
    RTB = Buf("rtmp")
    GTB = [Buf("gt%d" % i) for i in range(6)]
    Y1B = [[Buf("y1_%d_%d" % (c, t)) for t in range(4)] for c in range(4)]
    SGTB = [Buf("sgt%d" % i) for i in range(2)]

    def ssm_param_prep():
        S.barrier()
        P = PPB
        cnt = {"o": 0}

        def T(shape, dt=F32):
            n = int(np.prod(shape[1:])) * (4 if dt != BF16 else 2)
            off = cnt["o"]
            cnt["o"] = (off + n + 31) // 32 * 32
            assert RB + cnt["o"] <= A.nbytes, cnt["o"]
            return A.alloc(shape, dt, at=RB + off)

        def v_tt(out, a, b, op):
            S.op("dve", lambda e: e.tensor_tensor(out=out, in0=a, in1=b, op=op), reads=[P], writes=[P])

        def v_ts(out, a, s1, s2, op0, op1=None):
            if op1 is None:
                S.op("dve", lambda e: e.tensor_scalar(out=out, in0=a, scalar1=s1, scalar2=None, op0=op0), reads=[P], writes=[P])
            else:
                S.op("dve", lambda e: e.tensor_scalar(out=out, in0=a, scalar1=s1, scalar2=s2, op0=op0, op1=op1), reads=[P], writes=[P])

        def a_act(out, a, func, scale=1.0):
            S.op("act", lambda e: e.activation(out=out, in_=a, func=func, scale=scale), reads=[P], writes=[P])

        def v_cp(out, a):
            S.op("dve", lambda e: e.tensor_copy(out=out, in_=a), reads=[P], writes=[P])

        sp1 = T([64, 3, 32]); sp2 = T([64, 4, 512])
        S.dma("act", sp1.rearrange("p a b -> p (a b)"), sp1_d, writes=[P])
        S.dma("act", sp2.rearrange("p a b -> p (a b)"), sp2_d, writes=[P])
        identS = T([128, 128], BF16); bmaskS = T([128, 128])
        S.dma("pool", identS, ident_d, writes=[P])
        S.dma("act", bmaskS, bmask_d, writes=[P])
        ldt, are, aim = sp1[:, 0], sp1[:, 1], sp1[:, 2]
        bre = sp2[:, 0].rearrange("p (g h) -> p g h", g=32)
        bim = sp2[:, 1].rearrange("p (g h) -> p g h", g=32)
        cre = sp2[:, 2].rearrange("p (g h) -> p g h", g=32)
        cim = sp2[:, 3].rearrange("p (g h) -> p g h", g=32)
        dt_ = T([64, 32]); mag = T([64, 32]); ang = T([64, 32]); t1 = T([64, 32]); t2 = T([64, 32])
        ti = T([64, 32], I32); cosv = T([64, 32]); sinv = T([64, 32])
        a_act(dt_, ldt, AF.Exp)
        v_tt(t1, dt_, are, ALU.mult)
        a_act(mag, t1, AF.Exp)
        v_tt(ang, dt_, aim, ALU.mult)
        TWO_PI = 2.0 * np.pi

        def sin_of(out, shift):
            v_ts(t1, ang, 1.0 / TWO_PI, float(shift), ALU.mult, ALU.add)
            v_cp(ti, t1)
            v_cp(t2, ti)
            v_tt(t1, t1, t2, ALU.subtract)
            v_ts(t2, t1, 0.5, None, ALU.is_gt)
            v_tt(t1, t1, t2, ALU.subtract)
            v_ts(t2, t1, -0.5, None, ALU.is_lt)
            v_tt(t1, t1, t2, ALU.add)
            a_act(out, t1, AF.Sin, scale=TWO_PI)

        dump("dt", dt_, [P]); dump("mag", mag, [P]); dump("ang", ang, [P])
        sin_of(sinv, 0.0)
        dump("frac_s", t1, [P]); dump("sinv", sinv, [P])
        sin_of(cosv, 0.25)
        dump("cosv", cosv, [P])
        pwr = T([64, 9, 32]); pwi = T([64, 9, 32])
        S.op("dve", lambda e: e.memset(pwr[:, 0], 1.0), reads=[P], writes=[P])
        S.op("dve", lambda e: e.memset(pwi[:, 0], 0.0), reads=[P], writes=[P])
        v_tt(pwr[:, 1], mag, cosv, ALU.mult)
        v_tt(pwi[:, 1], mag, sinv, ALU.mult)
        for k in range(1, 8):
            v_tt(t1, pwr[:, k], pwr[:, 1], ALU.mult)
            v_tt(t2, pwi[:, k], pwi[:, 1], ALU.mult)
            v_tt(pwr[:, k + 1], t1, t2, ALU.subtract)
            v_tt(t1, pwr[:, k], pwi[:, 1], ALU.mult)
            v_tt(t2, pwi[:, k], pwr[:, 1], ALU.mult)
            v_tt(pwi[:, k + 1], t1, t2, ALU.add)
        dump("pwr", pwr.rearrange("p k g -> p (k g)"), [P]); dump("pwi", pwi.rearrange("p k g -> p (k g)"), [P])
        ipr = T([64, 9, 32]); ipi = T([64, 9, 32]); n2 = T([64, 9, 32]); n3 = T([64, 9, 32])
        v_tt(n2, pwr, pwr, ALU.mult)
        v_tt(n3, pwi, pwi, ALU.mult)
        v_tt(n2, n2, n3, ALU.add)
        S.op("dve", lambda e: e.reciprocal(out=n2, in_=n2), reads=[P], writes=[P])
        v_tt(ipr, pwr, n2, ALU.mult)
        v_tt(ipi, pwi, n2, ALU.mult)
        v_ts(ipi, ipi, -1.0, None, ALU.mult)
        den = T([64, 32]); nr = T([64, 32]); qre = T([64, 32]); qim = T([64, 32])
        v_tt(den, are, are, ALU.mult)
        v_tt(t1, aim, aim, ALU.mult)
        v_tt(den, den, t1, ALU.add)
        S.op("dve", lambda e: e.reciprocal(out=den, in_=den), reads=[P], writes=[P])
        v_ts(nr, pwr[:, 1], -1.0, None, ALU.add)
        v_tt(t1, nr, are, ALU.mult)
        v_tt(t2, pwi[:, 1], aim, ALU.mult)
        v_tt(t1, t1, t2, ALU.add)
        v_tt(qre, t1, den, ALU.mult)
        v_tt(t1, pwi[:, 1], are, ALU.mult)
        v_tt(t2, nr, aim, ALU.mult)
        v_tt(t1, t1, t2, ALU.subtract)
        v_tt(qim, t1, den, ALU.mult)
        Bre = T([64, 32, 16]); Bim = T([64, 32, 16]); w1_ = T([64, 32, 16]); w2_ = T([64, 32, 16])
        qre_b = qre.unsqueeze(2).to_broadcast([64, 32, 16])
        qim_b = qim.unsqueeze(2).to_broadcast([64, 32, 16])
        v_tt(w1_, bre, qre_b, ALU.mult); v_tt(w2_, bim, qim_b, ALU.mult); v_tt(Bre, w1_, w2_, ALU.subtract)
        v_tt(w1_, bim, qre_b, ALU.mult); v_tt(w2_, bre, qim_b, ALU.mult); v_tt(Bim, w1_, w2_, ALU.add)
        big_off = cnt["o"]
        big1 = T([64, 32, 8, 16]); big2 = T([64, 32, 8, 16])
        Cmr = T([64, 32, 8, 16]); Cmi = T([64, 32, 8, 16])
        q0 = cnt["o"]
        Bsr = T([64, 32, 8, 16], BF16); Bsi = T([64, 32, 8, 16], BF16)
        cmr_bf = T([64, 32, 8, 16], BF16); cmi_bf = T([64, 32, 8, 16], BF16)
        Bmr = A.alloc([64, 32, 8, 16], F32, at=RB + q0); Bmi = A.alloc([64, 32, 8, 16], F32, at=RB + q0 + 16 * KB)

        def bc_x(x):
            return x.unsqueeze(2).to_broadcast([64, 32, 8, 16])

        def bc_p(p, lo, hi, rev=False):
            sl = p[:, lo:hi, :].rearrange("p k g -> p g k")
            return sl.unsqueeze(3).to_broadcast([64, 32, 8, 16])

        def cmul_big(o_re, o_im, xr, xi, pr, pi, neg_im=False):
            v_tt(big1, bc_x(xr), pr, ALU.mult); v_tt(big2, bc_x(xi), pi, ALU.mult)
            v_tt(o_re, big1, big2, ALU.subtract)
            v_tt(big1, bc_x(xr), pi, ALU.mult); v_tt(big2, bc_x(xi), pr, ALU.mult)
            if neg_im:
                v_tt(o_im, big1, big2, ALU.add)
                v_ts(o_im, o_im, -1.0, None, ALU.mult)
            else:
                v_tt(o_im, big1, big2, ALU.add)

        cmul_big(Cmr, Cmi, cre, cim, bc_p(pwr, 1, 9), bc_p(pwi, 1, 9), neg_im=True)
        v_cp(cmr_bf, Cmr); v_cp(cmi_bf, Cmi)
        S.dma("sp", scr_cm[:, 0:4096], cmr_bf.rearrange("p g j h -> p (g j h)"), reads=[P])
        S.dma("sp", scr_cm[:, 4096:8192], cmi_bf.rearrange("p g j h -> p (g j h)"), reads=[P])
        pwr_rev = T([64, 8, 32]); pwi_rev = T([64, 8, 32])
        for i in range(8):
            v_cp(pwr_rev[:, i], pwr[:, 7 - i]); v_cp(pwi_rev[:, i], pwi[:, 7 - i])
        prr = pwr_rev.rearrange("p k g -> p g k").unsqueeze(3).to_broadcast([64, 32, 8, 16])
        pri = pwi_rev.rearrange("p k g -> p g k").unsqueeze(3).to_broadcast([64, 32, 8, 16])
        cmul_big(Bsr, Bsi, Bre, Bim, prr, pri)
        wb_bf = T([128, 2, 32, 64], BF16)
        for o, src in ((0, Bsr), (1, Bsi)):
            for g8 in range(4):
                ps = PS[7]
                for gg in range(8):
                    g = g8 * 8 + gg
                    S.op("pe", lambda e, g=g, gg=gg, src=src: e.matmul(ps[:, gg * 64:(gg + 1) * 64], src[:, g].rearrange("p i h -> p (i h)"),
                                                                       identS[0:64, 0:64], start=True, stop=True),
                         reads=[P], writes=[PSB[7]], signal=(gg == 7))
                S.op("dve", lambda e, o=o, g8=g8: e.tensor_copy(out=wb_bf[:, o, g8 * 8:(g8 + 1) * 8, :],
                                                               in_=ps[:].rearrange("p (a b) -> p a b", a=8)), reads=[PSB[7], P], writes=[P])
        S.dma("sp", scr_wb, wb_bf.rearrange("p o g n -> p (o g n)"), reads=[P])
        at_ = T([64, 2, 2, 32])
        v_cp(at_[:, 0, 0], pwr[:, 8]); v_cp(at_[:, 1, 1], pwr[:, 8]); v_cp(at_[:, 1, 0], pwi[:, 8])
        v_ts(at_[:, 0, 1], pwi[:, 8], -1.0, None, ALU.mult)
        S.dma("sp", scr_at, at_.rearrange("p o k g -> p (o k g)"), reads=[P])
        cmul_big(Bmr, Bmi, Bre, Bim, bc_p(ipr, 1, 9), bc_p(ipi, 1, 9))
        toep_bf = A.alloc([128, 32, 128], BF16, at=RB + big_off)
        for g4 in range(8):
            ps = PS[7]
            for gg in range(4):
                g = g4 * 4 + gg
                S.op("pe", lambda e, g=g, gg=gg: e.matmul(ps[:, gg * 128:(gg + 1) * 128], Bmr[:, g].rearrange("p i h -> p (i h)"),
                                                          Cmr[:, g].rearrange("p j h -> p (j h)"), start=True, stop=False),
                     reads=[P], writes=[PSB[7]], signal=False)
                S.op("pe", lambda e, g=g, gg=gg: e.matmul(ps[:, gg * 128:(gg + 1) * 128], Bmi[:, g].rearrange("p i h -> p (i h)"),
                                                          Cmi[:, g].rearrange("p j h -> p (j h)"), start=False, stop=True),
                     reads=[P], writes=[PSB[7]], signal=(gg == 3))
            S.op("dve", lambda e, g4=g4: e.tensor_tensor(
                out=toep_bf[:, g4 * 4:(g4 + 1) * 4, :], in0=ps[:].rearrange("p (a b) -> p a b", a=4),
                in1=bmaskS.unsqueeze(1).to_broadcast([128, 4, 128]), op=ALU.mult), reads=[PSB[7], P], writes=[P])
        S.dma("sp", scr_toep, toep_bf.rearrange("p g m -> p (g m)"), reads=[P])
        S.barrier()

    def mixer0(s):
        S.barrier()
        for i in range(3):
            S.dma("pool", winS[:, :, i * 512:(i + 1) * 512],
                  win_d.rearrange("(kc p) f -> p kc f", p=128)[:, :, i * 512:(i + 1) * 512], writes=[WINB[i]])
        S.dma("pool", identM, ident_d, writes=[IDMB])
        for ct in range(2):
            for k in range(31):
                S.op("dve", lambda e, ct=ct, k=k: e.tensor_scalar(out=diag[ct][:, k, :], in0=identM, scalar1=cw[:, ct, k:k + 1], scalar2=None,
                                                              op0=ALU.mult), reads=[IDMB, M0CB], writes=[DIAGB[ct]])
        rmsnorm_tile(4, 0, 4, hT2, H2B)
        S.op("pool", lambda e: e.memset(vpad[:, :, 0:30], 0.0), writes=VB)
        pc = 0
        for ct in range(4):
            for tb in range(4):
                pa, pg = (pc % 2) * 2, (pc % 2) * 2 + 1
                pc += 1
                for pi, oc in ((pa, ct), (pg, ct + 4)):
                    for kc in range(KC):
                        S.op("pe", lambda e, pi=pi, oc=oc, kc=kc, tb=tb: e.matmul(
                            PS[pi][:], winS[:, kc, oc * 128:(oc + 1) * 128], hT2[:, kc, tb * TT:(tb + 1) * TT],
                            start=(kc == 0), stop=(kc == KC - 1)),
                            reads=[WINB[oc // 4], H2B[kc][tb]], writes=[PSB[pi]], signal=(kc == KC - 1))
                sg = sil[pc % 2]
                S.op("act", lambda e, sg=sg, pg=pg: e.activation(out=sg, in_=PS[pg][:], func=AF.Sigmoid),
                     reads=[PSB[pg]], writes=[SILB[pc % 2]])
                S.op("dve", lambda e, sg=sg, pa=pa, ct=ct, tb=tb: e.tensor_tensor(
                    out=vpad[:, ct, 30 + tb * TT:30 + (tb + 1) * TT], in0=sg, in1=PS[pa][:], op=ALU.mult),
                    reads=[SILB[pc % 2], PSB[pa]], writes=[VB[ct]])
        for ct in range(4):
            for tb in range(4):
                pi = 4 + (pc % 2)
                pc += 1
                oc = 8 + ct
                for kc in range(KC):
                    S.op("pe", lambda e, pi=pi, oc=oc, kc=kc, tb=tb: e.matmul(
                        PS[pi][:], winS[:, kc, oc * 128:(oc + 1) * 128], hT2[:, kc, tb * TT:(tb + 1) * TT],
                        start=(kc == 0), stop=(kc == KC - 1)),
                        reads=[WINB[2], H2B[kc][tb]], writes=[PSB[pi]], signal=(kc == KC - 1))
                S.op("act", lambda e, pi=pi, ct=ct, tb=tb: e.activation(out=uT4[:, ct, :, tb * 64:(tb + 1) * 64],
                                                                       in_=PS[pi][:].rearrange("p (c i) -> p i c", i=8), func=AF.Copy),
                     reads=[PSB[pi]], writes=[UB[ct][tb]])
        S.barrier()
        S.dma("pool", selin.rearrange("p a b -> p (a b)"), selin_d, writes=[SELIB])
        for ct in range(2, 4):
            for k in range(31):
                S.op("dve", lambda e, ct=ct, k=k: e.tensor_scalar(out=diag[ct][:, k, :], in0=identM, scalar1=cw[:, ct, k:k + 1], scalar2=None,
                                                              op0=ALU.mult), reads=[IDMB, M0CB], writes=[DIAGB[ct]])
        lnb = [R(96 + 2 * i, [128, TT], F32) for i in range(7)]
        ybf = [R(110 + i, [128, TT], BF16) for i in range(2)]
        ysq = [R(112 + i, [128, TT], BF16) for i in range(2)]
        LNB = [Buf("lnb%d" % i) for i in range(7)]
        YBB = [Buf("ybf%d" % i) for i in range(2)]
        YSB = [Buf("ysq%d" % i) for i in range(2)]
        YC2 = [[Buf("yc%d_%d" % (c, t)) for t in range(4)] for c in range(4)]
        tiles = [(tb, ct) for tb in range(4) for ct in range(4)]

        def conv_tile(n):
            tb, ct = tiles[n]
            pi = 2 + (n % 2)
            j = n % 2
            for k in range(31):
                S.op("pe", lambda e, pi=pi, ct=ct, k=k, tb=tb: e.matmul(
                    PS[pi][:], diag[ct][:, k, :], vpad[:, ct, k + tb * TT:k + (tb + 1) * TT], start=(k == 0), stop=(k == 30)),
                    reads=[DIAGB[ct], VB[ct]], writes=[PSB[pi]], signal=(k == 30))
            bia = chv[:, 0, ct:ct + 1]
            S.op("act", lambda e, pi=pi, ct=ct, tb=tb: e.activation(out=ycv[:, ct, tb * TT:(tb + 1) * TT], in_=PS[pi][:], func=AF.Identity, bias=bia),
                 reads=[PSB[pi], M0CB], writes=[YC2[ct][tb]])
            S.op("act", lambda e, pi=pi, j=j: e.activation(out=ybf[j], in_=PS[pi][:], func=AF.Identity, bias=bia),
                 reads=[PSB[pi], M0CB], writes=[YBB[j]])
            S.op("act", lambda e, pi=pi, j=j: e.activation(out=ysq[j], in_=PS[pi][:], func=AF.Square, bias=bia),
                 reads=[PSB[pi], M0CB], writes=[YSB[j]])

        def stat_mm(n):
            tb, ct = tiles[n]
            j = n % 2
            sb_, qb_ = ((0, 1), (4, 5))[tb % 2]
            S.op("pe", lambda e: e.matmul(PS[sb_][:], ones_bf, ybf[j], start=(ct == 0), stop=(ct == 3)),
                 reads=[YBB[j], CONSTB], writes=[PSB[sb_]], signal=True)
            S.op("pe", lambda e: e.matmul(PS[qb_][:], ones_bf, ysq[j], start=(ct == 0), stop=(ct == 3)),
                 reads=[YSB[j], CONSTB], writes=[PSB[qb_]], signal=True)

        def ln_stages(tb):
            sb_, qb_ = ((0, 1), (4, 5))[tb % 2]
            mean, var, msq = lnb[2 + tb % 2], lnb[4 + tb % 2], lnb[6]
            MB_, VB_, QB_ = LNB[2 + tb % 2], LNB[4 + tb % 2], LNB[6]
            sl = slice(tb * TT, (tb + 1) * TT)

            def st_a():
                S.op("act", lambda e: e.activation(out=mean, in_=PS[sb_][:], func=AF.Copy, scale=1.0 / 512), reads=[PSB[sb_]], writes=[MB_])
                S.op("act", lambda e: e.activation(out=msq, in_=PS[sb_][:], func=AF.Square, scale=1.0 / 512), reads=[PSB[sb_]], writes=[QB_])
                S.op("dve", lambda e: e.scalar_tensor_tensor(out=var, in0=PS[qb_][:], scalar=1.0 / 512, in1=msq, op0=ALU.mult, op1=ALU.subtract),
                     reads=[PSB[qb_], QB_], writes=[VB_])

            def st_b():
                S.op("act", lambda e: e.activation(out=var, in_=var, func=AF.Sqrt, bias=lnepsc), reads=[VB_, M0CB], writes=[VB_])
                S.op("dve", lambda e: e.reciprocal(out=var, in_=var), reads=[VB_], writes=[VB_])

            def t_ops(ct):
                t_ = lnb[ct % 2]
                TB_ = LNB[ct % 2]
                S.op("dve", lambda e: e.tensor_tensor(out=t_, in0=ycv[:, ct, sl], in1=mean, op=ALU.subtract),
                     reads=[YC2[ct][tb], MB_], writes=[TB_])
                S.op("dve", lambda e: e.tensor_tensor(out=t_, in0=t_, in1=var, op=ALU.mult),
                     reads=[TB_, VB_], writes=[TB_])

            def silu(ct):
                t_ = lnb[ct % 2]
                S.op("act", lambda e: e.activation(out=mcat[:, ct, sl], in_=t_, func=AF.Silu,
                                                   scale=chv[:, 1, ct:ct + 1], bias=chv[:, 2, ct:ct + 1]),
                     reads=[LNB[ct % 2], M0CB], writes=[MCB[ct][tb]])

            return [st_a, st_b, lambda: (t_ops(0), t_ops(1)), lambda: (silu(0), silu(1), t_ops(2), t_ops(3)), lambda: (silu(2), silu(3))]

        pending = []
        conv_tile(0)
        for n in range(len(tiles)):
            if n + 1 < len(tiles):
                conv_tile(n + 1)
            stat_mm(n)
            for st in pending:
                if st:
                    st.pop(0)()
            if tiles[n][1] == 3:
                pending.append(ln_stages(tiles[n][0]))
        while any(pending):
            for st in pending:
                if st:
                    st.pop(0)()
        dump("mcA", mcat[:, 0:4, :].rearrange("p a b -> p (a b)"), [b for r in MCB[0:4] for b in r])
        dump("u", uT.rearrange("p a b -> p (a b)"), [b for r in UB for b in r])
        S.barrier()
        S.dma("sp", wbS.rearrange("p o g n -> p (o g n)"), scr_wb, writes=[WBB])
        S.dma("sp", toepS.rearrange("p g m -> p (g m)"), scr_toep, writes=[TOEPB])
        S.dma("pool", selout.rearrange("p a b -> p (a b)"), selout_d, writes=[PPB])
        S.dma("pool", gluwS, gluw_d.rearrange("(kc p) f -> p kc f", p=128), writes=[PPB])
        scr_cm4 = scr_cm.rearrange("p (o g m) -> p o g m", o=2, g=32)
        scr_at4 = scr_at.rearrange("p (o k g) -> p o k g", o=2, k=2)
        for gh in range(2):
            for o in range(2):
                S.dma("sp", cmS[64 * gh:64 * gh + 64, o, :, :], scr_cm4[:, o, gh * 16:(gh + 1) * 16, :], writes=[CMB[gh * 2 + o]])
                for k in range(2):
                    S.dma("sp", atS[64 * gh:64 * gh + 64, o, k, :], scr_at4[:, o, k, gh * 16:(gh + 1) * 16], writes=[ATB[gh * 4 + o * 2 + k]])
        for g2 in range(16):
            pi = g2 % 2
            for gg in range(2):
                g = g2 * 2 + gg
                ct, g8 = g // 8, g % 8
                for i in range(8):
                    S.op("pe", lambda e, pi=pi, gg=gg, ct=ct, g8=g8, i=i: e.matmul(
                        PS[pi][:, gg * 256:(gg + 1) * 256], selin[:, g8, 112 - 16 * i:240 - 16 * i], uT4[:, ct, i, :],
                        start=(i == 0), stop=(i == 7)),
                        reads=[SELIB] + UB[ct], writes=[PSB[pi]], signal=(gg == 1 and i == 7))
            S.op("act", lambda e, pi=pi, g2=g2: e.activation(out=Ut[:, g2 * 2:g2 * 2 + 2, :],
                                                            in_=PS[pi][:].rearrange("p (a b) -> p a b", a=2), func=AF.Copy),
                 reads=[PSB[pi]], writes=[UTB[g2 * 2], UTB[g2 * 2 + 1]])
        S.op("dve", lambda e: e.memset(Xh[:, 31], 0.0), writes=[XHB[31]])
        for cb in range(16):
            pi = 2 + (cb % 2)
            psv = PS[pi][:].rearrange("p (c o g) -> p c o g", c=16, o=2)
            for g in range(32):
                gh, gl = g // 16, g % 16
                for o in range(2):
                    S.op("pe", lambda e, psv=psv, g=g, gh=gh, gl=gl, o=o, cb=cb: e.matmul(
                        psv[64 * gh:64 * gh + 64, :, o, gl], wbS[:, o, g, :], Ut[:, g, cb * 16:(cb + 1) * 16], start=True, stop=True),
                        reads=[WBB, UTB[g]], writes=[PSB[pi]], signal=(g == 31 and o == 1))
            for cc in range(16):
                c = cb * 16 + cc
                prev = Xh[:, (c - 1) % 32]
                S.op("dve", lambda e, prev=prev: e.tensor_tensor(
                    out=pq, in0=prev.unsqueeze(1).to_broadcast([128, 2, 2, 16]), in1=atS, op=ALU.mult),
                    reads=[XHB[(c - 1) % 32]] + ATB, writes=[PQB])
                S.op("dve", lambda e: e.tensor_tensor(out=rtmp, in0=pq[:, :, 0, :], in1=pq[:, :, 1, :], op=ALU.add),
                     reads=[PQB], writes=[RTB])
                S.op("dve", lambda e, psv=psv, cc=cc, c=c: e.tensor_tensor(out=Xh[:, c % 32], in0=rtmp, in1=psv[:, cc], op=ALU.add),
                     reads=[RTB, PSB[pi]], writes=[XHB[c % 32]])
            half = (cb % 2) * 16
            S.op("act", lambda e, cb=cb, half=half: e.activation(
                out=Xbf[:, :, :, cb * 16:(cb + 1) * 16], in_=Xh[:, half:half + 16].rearrange("p c o g -> p o g c"), func=AF.Copy),
                reads=XHB[half:half + 16], writes=[XBFB])
        for g2 in range(16):
            pi = 4 + (g2 % 2)
            for gg in range(2):
                g = g2 * 2 + gg
                S.op("pe", lambda e, pi=pi, gg=gg, g=g: e.matmul(PS[pi][:, gg * 256:(gg + 1) * 256], toepS[:, g, :], Ut[:, g, :],
                                                                start=True, stop=False),
                     reads=[TOEPB, UTB[g]], writes=[PSB[pi]], signal=False)
                hs = slice(64 * (g // 16), 64 * (g // 16) + 64)
                gl = g % 16
                S.op("pe", lambda e, pi=pi, gg=gg, gl=gl, hs=hs: e.matmul(PS[pi][:, gg * 256 + 1:(gg + 1) * 256], cmS[hs, 0, gl, :], Xbf[hs, 0, gl, 0:255],
                                                                start=False, stop=False),
                     reads=CMB + [XBFB], writes=[PSB[pi]], signal=False)
                S.op("pe", lambda e, pi=pi, gg=gg, gl=gl, hs=hs: e.matmul(PS[pi][:, gg * 256 + 1:(gg + 1) * 256], cmS[hs, 1, gl, :], Xbf[hs, 1, gl, 0:255],
                                                                start=False, stop=True),
                     reads=CMB + [XBFB], writes=[PSB[pi]], signal=(gg == 1))
            S.op("act", lambda e, pi=pi, g2=g2: e.activation(out=Ut[:, g2 * 2:g2 * 2 + 2, :],
                                                            in_=PS[pi][:].rearrange("p (a b) -> p a b", a=2), func=AF.Copy),
                 reads=[PSB[pi]], writes=[UTB[g2 * 2], UTB[g2 * 2 + 1]])
        dump("toep", toepS.rearrange("p g m -> p (g m)"), [TOEPB])
        dump("wb", wbS.rearrange("p o g n -> p (o g n)"), [WBB])
        dump("Y", Ut.rearrange("p g c -> p (g c)"), UTB)
        S.barrier()
        S.dma("pool", woutS, wout_d.rearrange("(kc p) f -> p kc f", p=128), writes=[WOUTB])
        pc = 0
        for ct in range(4):
            for tb in range(4):
                pi = pc % 2
                pc += 1
                for j in range(8):
                    for g8 in range(8):
                        S.op("pe", lambda e, pi=pi, j=j, g8=g8, ct=ct, tb=tb: e.matmul(
                            PS[pi][:, j * 64:(j + 1) * 64], selout[:, j, 112 - 16 * g8:240 - 16 * g8],
                            Ut[:, ct * 8 + g8, tb * 64:(tb + 1) * 64], start=(g8 == 0), stop=(g8 == 7)),
                            reads=[PPB, UTB[ct * 8 + g8]], writes=[PSB[pi]], signal=(j == 7 and g8 == 7))
                ys, x2, x3, sg_ = gt[0], gt[1], gt[2], gt[3]
                sl = slice(tb * TT, (tb + 1) * TT)
                S.op("dve", lambda e, pi=pi, ct=ct, sl=sl: e.scalar_tensor_tensor(
                    out=ys.rearrange("p (c j) -> p c j", j=8), in0=uT4[:, ct, :, tb * 64:(tb + 1) * 64].rearrange("p j c -> p c j"),
                    scalar=chv[:, 3, ct:ct + 1], in1=PS[pi][:].rearrange("p (j c) -> p c j", j=8), op0=ALU.mult, op1=ALU.add),
                    reads=[PSB[pi], UB[ct][tb], M0CB], writes=[GTB[0]])
                S.op("act", lambda e: e.activation(out=x2, in_=ys, func=AF.Square), reads=[GTB[0]], writes=[GTB[1]])
                S.op("dve", lambda e: e.tensor_scalar(out=x2, in0=x2, scalar1=0.044715, scalar2=1.0, op0=ALU.mult, op1=ALU.add),
                     reads=[GTB[1]], writes=[GTB[1]])
                S.op("dve", lambda e: e.tensor_tensor(out=x3, in0=x2, in1=ys, op=ALU.mult), reads=[GTB[1], GTB[0]], writes=[GTB[2]])
                S.op("act", lambda e: e.activation(out=sg_, in_=x3, func=AF.Sigmoid, scale=1.5957691216057308), reads=[GTB[2]], writes=[GTB[3]])
                S.op("dve", lambda e, ct=ct, sl=sl: e.tensor_tensor(out=y1bS[:, ct, sl], in0=ys, in1=sg_, op=ALU.mult),
                     reads=[GTB[0], GTB[3]], writes=[Y1B[ct][tb]])
        for ot in range(4):
            for tb in range(4):
                pi = 2 + (pc % 2)
                pc += 1
                sl = slice(tb * TT, (tb + 1) * TT)
                for ct in range(4):
                    S.op("pe", lambda e, pi=pi, ct=ct, ot=ot, sl=sl: e.matmul(
                        PS[pi][:], gluwS[:, ct, ot * 128:(ot + 1) * 128], y1bS[:, ct, sl], start=(ct == 0), stop=(ct == 3)),
                        reads=[PPB, Y1B[ct][tb]], writes=[PSB[pi]], signal=(ct == 3))
                sg = sgt[pc % 2]
                S.op("act", lambda e, pi=pi, sg=sg, ot=ot: e.activation(out=sg, in_=PS[pi][:], func=AF.Sigmoid, bias=chv[:, 4, ot:ot + 1]),
                     reads=[PSB[pi], M0CB], writes=[SGTB[pc % 2]])
                S.op("dve", lambda e, sg=sg, ot=ot, sl=sl: e.tensor_tensor(out=mcat[:, 4 + ot, sl], in0=y1bS[:, ot, sl], in1=sg, op=ALU.mult),
                     reads=[SGTB[pc % 2], Y1B[ot][tb]], writes=[MCB[4 + ot][tb]])
        dump("mcB", mcat[:, 4:8, :].rearrange("p a b -> p (a b)"), [b for r in MCB[4:8] for b in r])
        dump("y1", y1bS.rearrange("p a b -> p (a b)"), [b for r in Y1B for b in r])
        for dc in range(KC):
            for tb in range(4):
                pi = 4 + (pc % 2)
                pc += 1
                sl = slice(tb * TT, (tb + 1) * TT)
                for mc in range(8):
                    S.op("pe", lambda e, pi=pi, mc=mc, dc=dc, sl=sl: e.matmul(
                        PS[pi][:], woutS[:, mc, dc * 128:(dc + 1) * 128], mcat[:, mc, sl], start=(mc == 0), stop=(mc == 7)),
                        reads=[WOUTB, MCB[mc][tb]], writes=[PSB[pi]], signal=(mc == 7))
                xs = xT[:, dc, sl]
                S.op("dve", lambda e, pi=pi, xs=xs: e.tensor_tensor(out=xs, in0=PS[pi][:], in1=xs, op=ALU.add),
                     reads=[PSB[pi], XB[dc][tb]], writes=[XB[dc][tb]])
        S.barrier()

    wqkv_d = din("w_qkv", [D, 3 * D])
    wo_d = din("w_o", [D, D])
    cbias_d = din("cbias", [128, 2 * 256])
    en_d = din("en", [8, 8 * 128])
    negm_d = din("negm", [128, 4 * 64])
    NEG = -30000.0
    ahT = R(0, [128, KC, L], BF16)
    qring = [R(32 + 8 * i, [128, KC, 512], BF16) for i in range(2)]
    qT = R(48, [128, 4, L], BF16)
    kT = R(64, [128, 4, L], BF16)
    Vh = R(80, [128, 16, 512], BF16)
    oT = R(96, [128, 4, L], BF16)
    woS = R(112, [128, 4, D], BF16)
    pT = [R(120 + i, [128, 2, 256], BF16) for i in range(2)]
    cbiasS = R(122, [128, 2, 256], BF16)
    EnS = R(123, [8, 8, 128], BF16)
    identA = R(125, [128, 128], BF16)
    negmS = R(125.5, [128, 4, 64], F32)
    rden = R(126.5, [128, 256], F32)
    mbT = [R(127.5 + 2 * i, [8, 4, 256], BF16) for i in range(2)]
    kmf = R(131.5, [128, 4, 8], F32)
    kmT = R(131.75, [128, 4, 8], BF16)
    gsb = R(132, [128, 32], F32)
    top8 = R(132.25, [128, 8], F32)
    mball = R(132.5, [128, 8, 32], BF16)
    rden2 = R(133, [128, 256], F32)
    rdens = [rden, rden2]
    RDBS = [Buf("rden0"), Buf("rden1")]
    mbTall = R(32, [8, 4, 4, 256], BF16)
    assert RB + int(134 * KB) <= A.nbytes

    AHB = [[Buf("ah%d_%d" % (k, t)) for t in range(4)] for k in range(KC)]
    QRB = [Buf("qring%d" % i) for i in range(2)]
    QTB = [Buf("qT%d" % h) for h in range(4)]
    KTB = [Buf("kT%d" % h) for h in range(4)]
    VHB = [Buf("vh%d" % k) for k in range(16)]
    OTB = [[Buf("oT%d_%d" % (h, q)) for q in range(8)] for h in range(4)]
    WOB = Buf("woS")
    PTB = [Buf("pT%d" % i) for i in range(2)]
    ACB = Buf("attnconst")
    RDB = Buf("rden")
    MBTB = [Buf("mbT%d" % i) for i in range(2)]
    KMB = Buf("km")
    GSB = Buf("gsb")
    T8B = Buf("top8")
    MBB = Buf("mb")
    SCALE = 128.0 ** -0.5

    def mixer1(s):
        S.barrier()
        S.dma("pool", cbiasS.rearrange("p a b -> p (a b)"), cbias_d, writes=[ACB])
        S.dma("pool", EnS.rearrange("p a b -> p (a b)"), en_d, writes=[ACB])
        S.dma("pool", identA, ident_d, writes=[ACB])
        S.dma("sp", negmS.rearrange("p a b -> p (a b)"), negm_d, writes=[ACB])
        rmsnorm_tile(5, 0, 4, ahT, AHB)
        nring = [0]
        pcs = [0]

        def load_cols(c0):
            i = nring[0] % 2
            nring[0] += 1
            S.dma("pool", qring[i], wqkv_d.rearrange("(kc p) f -> p kc f", p=128)[:, :, c0:c0 + 512], writes=[QRB[i]])
            return qring[i], QRB[i]

        for half in range(2):
            for which, dstT, dstB in ((0, qT, QTB), (1, kT, KTB)):
                if which == 0 and half == 1:
                    wsl, wb_ = pre_q
                else:
                    wsl, wb_ = load_cols(which * D + half * 512)
                for hl in range(4):
                    for tb in range(4):
                        pi = pcs[0] % 2
                        pcs[0] += 1
                        for kc in range(KC):
                            S.op("pe", lambda e, pi=pi, wsl=wsl, kc=kc, hl=hl, tb=tb: e.matmul(
                                PS[pi][:], wsl[:, kc, hl * 128:(hl + 1) * 128], ahT[:, kc, tb * TT:(tb + 1) * TT],
                                start=(kc == 0), stop=(kc == KC - 1)),
                                reads=[wb_, AHB[kc][tb]], writes=[PSB[pi]], signal=(kc == KC - 1))
                        S.op("act", lambda e, pi=pi, dstT=dstT, hl=hl, tb=tb: e.activation(
                            out=dstT[:, hl, tb * TT:(tb + 1) * TT], in_=PS[pi][:], func=AF.Copy),
                            reads=[PSB[pi]], writes=[dstB[hl]])
            wsl, wb_ = load_cols(2 * D + half * 512)
            for kt in range(16):
                pi = pcs[0] % 2
                pcs[0] += 1
                tb = kt // 4
                for kc in range(KC):
                    S.op("pe", lambda e, pi=pi, wsl=wsl, kc=kc, kt=kt: e.matmul(
                        PS[pi][:], ahT[:, kc, kt * 128:(kt + 1) * 128], wsl[:, kc, :], start=(kc == 0), stop=(kc == KC - 1)),
                        reads=[wb_, AHB[kc][tb]], writes=[PSB[pi]], signal=(kc == KC - 1))
                S.op("act", lambda e, pi=pi, kt=kt: e.activation(out=Vh[:, kt, :], in_=PS[pi][:], func=AF.Copy),
                     reads=[PSB[pi]], writes=[VHB[kt]])
            S.dma("pool", woS, wo_d.rearrange("(kc p) f -> p kc f", p=128)[:, half * 4:(half + 1) * 4, :], writes=[WOB])
            if half == 0:
                pre_q = load_cols(0 * D + 1 * 512)
            for hl in range(4):
                S.op("dve", lambda e, hl=hl: e.tensor_reduce(out=kmf[:, hl, :], in_=kT[:, hl, :].rearrange("p (n k) -> p n k", n=8),
                                                            axis=AX.X, op=ALU.add), reads=[KTB[hl]], writes=[KMB])
            S.op("dve", lambda e: e.tensor_copy(out=kmT, in_=kmf), reads=[KMB], writes=[KMB])
            for qb in range(4, 8):
                for q2 in range(2):
                    idx = (qb - 4) * 2 + q2
                    qsl = slice(qb * 256 + q2 * 128, qb * 256 + (q2 + 1) * 128)
                    for hl in range(4):
                        S.op("pe", lambda e, hl=hl, qsl=qsl, idx=idx: e.matmul(PS[6][:, idx * 32 + hl * 8:idx * 32 + (hl + 1) * 8], qT[:, hl, qsl],
                                                                              kmT[:, hl, :], start=True, stop=True),
                             reads=[QTB[hl], KMB], writes=[PSB[6]], signal=(hl == 3))
            def mask_job(qb, q2):
                idx = (qb - 4) * 2 + q2
                S.op("dve", lambda e: e.tensor_tensor(out=gsb, in0=PS[6][:, idx * 32:(idx + 1) * 32], in1=negmS[:, qb - 4, 0:32], op=ALU.add),
                     reads=[PSB[6], ACB], writes=[GSB])
                for hl in range(4):
                    S.op("dve", lambda e, hl=hl: e.max(out=top8, in_=gsb[:, hl * 8:(hl + 1) * 8]), reads=[GSB], writes=[T8B])
                    S.op("dve", lambda e, hl=hl: e.tensor_scalar(out=mball[:, idx, hl * 8:(hl + 1) * 8], in0=gsb[:, hl * 8:(hl + 1) * 8],
                                                                scalar1=top8[:, 2:3], scalar2=NEG, op0=ALU.is_lt, op1=ALU.mult),
                         reads=[GSB, T8B], writes=[MBB])

            mask_jobs = [(qb, q2) for qb in range(4, 8) for q2 in range(2)]

            def mask_transposes():
                for qb in range(4, 8):
                    for q2 in range(2):
                        idx = (qb - 4) * 2 + q2
                        tbk = 6 + (idx % 2)
                        for hl in range(4):
                            S.op("pe", lambda e, hl=hl, idx=idx, tbk=tbk: e.matmul(PS[tbk][0:8, hl * 128:(hl + 1) * 128], mball[:, idx, hl * 8:(hl + 1) * 8], identA,
                                                                         start=True, stop=True),
                                 reads=[MBB, ACB], writes=[PSB[tbk]], signal=(hl == 3))
                        S.op("act", lambda e, qb=qb, q2=q2, tbk=tbk: e.activation(out=mbTall[:, qb - 4, :, q2 * 128:(q2 + 1) * 128],
                                                                        in_=PS[tbk][0:8, :].rearrange("p (h q) -> p h q", h=4), func=AF.Copy),
                             reads=[PSB[tbk]], writes=[MBTB[0]])

            items = []
            for qb in range(8):
                for hl in range(4):
                    npair = qb + 1
                    for kp in range(npair):
                        items.append((qb, hl, kp, kp == 0, kp == npair - 1))

            def emit_S(i):
                qb, hl, kp, _, _ = items[i]
                qs = slice(qb * 256, (qb + 1) * 256)
                gated = qb >= 4
                pi = 2 + (i % 2)
                for k2 in range(2):
                    kt = kp * 2 + k2
                    n = kp
                    osl = PS[pi][:, k2 * 256:(k2 + 1) * 256]
                    extra = (n == qb) or gated
                    S.op("pe", lambda e, osl=osl, hl=hl, kt=kt, extra=extra, qs=qs: e.matmul(
                        osl, kT[:, hl, kt * 128:(kt + 1) * 128], qT[:, hl, qs], start=True, stop=(not extra)),
                        reads=[KTB[hl], QTB[hl]], writes=[PSB[pi]], signal=((not extra) and k2 == 1))
                    if n == qb:
                        S.op("pe", lambda e, osl=osl, k2=k2: e.matmul(osl, identA, cbiasS[:, k2, :], start=False, stop=True),
                             reads=[ACB], writes=[PSB[pi]], signal=(k2 == 1))
                    elif gated:
                        S.op("pe", lambda e, osl=osl, n=n, hl=hl, qb=qb: e.matmul(osl, EnS[:, n, :], mbTall[:, qb - 4, hl, :], start=False, stop=True),
                             reads=[ACB, MBTB[0]], writes=[PSB[pi]], signal=(k2 == 1))

            def emit_rest(i, gi):
                qb, hl, kp, first, last = items[i]
                qs = slice(qb * 256, (qb + 1) * 256)
                pi = 2 + (i % 2)
                ti = i % 2
                po, pd = ((4, 5), (0, 1))[gi % 2]
                S.op("act", lambda e, pi=pi, ti=ti: e.activation(out=pT[ti].rearrange("p a b -> p (a b)"), in_=PS[pi][:],
                                                                func=AF.Exp, scale=SCALE),
                     reads=[PSB[pi]], writes=[PTB[ti]])
                for k2 in range(2):
                    kt = kp * 2 + k2
                    f_ = first and k2 == 0
                    l_ = last and k2 == 1
                    S.op("pe", lambda e, po=po, kt=kt, hl=hl, ti=ti, k2=k2, f_=f_, l_=l_: e.matmul(
                        PS[po][:, 0:256], Vh[:, kt, hl * 128:(hl + 1) * 128], pT[ti][:, k2, :], start=f_, stop=l_),
                        reads=[VHB[kt], PTB[ti]], writes=[PSB[po]], signal=False)
                    S.op("pe", lambda e, pd=pd, ti=ti, k2=k2, f_=f_, l_=l_: e.matmul(
                        PS[pd][:, 0:256], ones_bf, pT[ti][:, k2, :], start=f_, stop=l_),
                        reads=[CONSTB, PTB[ti]], writes=[PSB[pd]], signal=(k2 == 1))
                if last:
                    rd = rdens[gi % 2]
                    S.op("dve", lambda e, pd=pd, rd=rd: e.reciprocal(out=rd, in_=PS[pd][:, 0:256]), reads=[PSB[pd]], writes=[RDBS[gi % 2]])
                    S.op("dve", lambda e, po=po, hl=hl, rd=rd, qs=qs: e.tensor_tensor(out=oT[:, hl, qs], in0=PS[po][:, 0:256], in1=rd, op=ALU.mult),
                         reads=[PSB[po], RDBS[gi % 2]], writes=[OTB[hl][qb]])

            n_items = len(items)
            first_gated = next(i for i, it in enumerate(items) if it[0] >= 4)
            gi = 0
            emit_S(0)
            for i in range(n_items):
                if i + 1 < n_items:
                    if i + 1 == first_gated:
                        while mask_jobs:
                            mask_job(*mask_jobs.pop(0))
                        mask_transposes()
                    emit_S(i + 1)
                emit_rest(i, gi)
                if items[i][4]:
                    gi += 1
                    if mask_jobs:
                        mask_job(*mask_jobs.pop(0))
            if half == 0:
                dump("oT0", oT.rearrange("p a b -> p (a b)"), [b for r_ in OTB for b in r_])
                dump("qT0", qT.rearrange("p a b -> p (a b)"), QTB)
                dump("kT0", kT.rearrange("p a b -> p (a b)"), KTB)
                dump("Vh0", Vh.rearrange("p a b -> p (a b)"), VHB)
            for dc in range(KC):
                for tb in range(4):
                    pi = pcs[0] % 2
                    pcs[0] += 1
                    sl = slice(tb * TT, (tb + 1) * TT)
                    for hl in range(4):
                        S.op("pe", lambda e, pi=pi, hl=hl, dc=dc, sl=sl: e.matmul(
                            PS[pi][:], woS[:, hl, dc * 128:(dc + 1) * 128], oT[:, hl, sl], start=(hl == 0), stop=(hl == 3)),
                            reads=[WOB, OTB[hl][2 * tb], OTB[hl][2 * tb + 1]], writes=[PSB[pi]], signal=(hl == 3))
                    xs = xT[:, dc, sl]
                    S.op("dve", lambda e, pi=pi, xs=xs: e.tensor_tensor(out=xs, in0=PS[pi][:], in1=xs, op=ALU.add),
                         reads=[PSB[pi], XB[dc][tb]], writes=[XB[dc][tb]])
            S.barrier()

    OUTB = [Buf("o%d" % i) for i in range(2)]
    for s in range(nseq):
        def load_blocks(sq, blks):
            for blk in blks:
                for kc in range(KC):
                    S.dma("sp", xT[:, kc, blk * TT:(blk + 1) * TT], xT_d[sq, kc * 128:(kc + 1) * 128, blk * TT:(blk + 1) * TT],
                          writes=[XB[kc][blk]])

        if s == 0 or "final" not in stages:
            load_blocks(s, range(L // TT))
        if s == 0 and "mix0" in stages:
            ssm_param_prep()
        if "ffn00" in stages:
            ffn_chain([(0, 0, 0), (0, 0, 1024)])
        if "mix0" in stages:
            mixer0(s)
        if "ffn01" in stages and "ffn10" in stages:
            ffn_chain([(1, 1, 0), (1, 1, 1024), (2, 2, 0), (2, 2, 1024)])
        else:
            if "ffn01" in stages:
                ffn_chain([(1, 1, 0), (1, 1, 1024)])
            if "ffn10" in stages:
                ffn_chain([(2, 2, 0), (2, 2, 1024)])
        if "mix1" in stages:
            mixer1(s)
        def final_tile(t0, s=s):
            rmsnorm_tile(6, t0, 2, hT, HB, inplace=True)
            for blk in (t0 // TT, t0 // TT + 1):
                for kc in range(KC):
                    S.dma("sp", outT_d[s, kc * 128:(kc + 1) * 128, blk * TT:(blk + 1) * TT], xT[:, kc, blk * TT:(blk + 1) * TT],
                          reads=[XB[kc][blk]])
            if s + 1 < nseq:
                load_blocks(s + 1, (t0 // TT, t0 // TT + 1))

        fused_tail = ("ffn11" in stages) and ("final" in stages)
        if "ffn11" in stages:
            ffn_chain([(3, 3, 0), (3, 3, 1024)], last_hook=(lambda: final_tile(0)) if fused_tail else None)
        if "final" in stages:
            for t0 in ((1024,) if fused_tail else (0, 1024)):
                final_tile(t0)
        if "final" not in stages:
            for blk in range(L // TT):
                for kc in range(KC):
                    S.dma("sp", outT_d[s, kc * 128:(kc + 1) * 128, blk * TT:(blk + 1) * TT], xT[:, kc, blk * TT:(blk + 1) * TT],
                          reads=[XB[kc][blk]])
    S.wait_all("sp", [b for row in XB for b in row])
    nc._sched_ninst = S.ninst
    nc._dbg_names = dbg_names
    return nc


def _prep_inputs(inputs):
    f = np.float32
    x = np.asarray(inputs["x"], f)
    ffn_norm = np.asarray(inputs["ffn_norm"], f)
    vecs = [ffn_norm[0, 0], ffn_norm[0, 1], ffn_norm[1, 0], ffn_norm[1, 1],
            np.asarray(inputs["mix_norm"], f)[0], np.asarray(inputs["mix_norm"], f)[1],
            np.asarray(inputs["final_norm"], f)]
    gains = np.stack([v.reshape(KC, 128).T for v in vecs], axis=1)
    gains = np.ascontiguousarray(gains.reshape(128, 7 * KC))
    shared = {
        "gains": gains,
        "w1": np.ascontiguousarray(np.asarray(inputs["ffn_w1"], f).reshape(4, D, FF)),
        "w3": np.ascontiguousarray(np.asarray(inputs["ffn_w3"], f).reshape(4, D, FF)),
        "w2": np.ascontiguousarray(np.asarray(inputs["ffn_w2"], f).reshape(4, FF, D)),
    }
    shared["w_in"] = np.ascontiguousarray(np.asarray(inputs["ab_w_in"], f)[0])
    shared["w_out"] = np.ascontiguousarray(np.asarray(inputs["ab_w_out"], f)[0])
    shared["glu_w"] = np.ascontiguousarray(np.asarray(inputs["ssm_glu_w"], f)[0])
    cwm = np.asarray(inputs["conv_w"], f)[0]
    shared["cw"] = np.ascontiguousarray(cwm.T.reshape(4, 128, 31).transpose(1, 0, 2).reshape(128, 4 * 31))
    chv = [np.asarray(inputs[k], f)[0] for k in ("conv_b", "conv_ln_g", "conv_ln_b", "ssm_d", "ssm_glu_b")]
    shared["chv"] = np.ascontiguousarray(np.stack([v.reshape(4, 128).T for v in chv], axis=1).reshape(128, 20))
    ldt = np.broadcast_to(np.asarray(inputs["ssm_log_dt"], f)[0][None, :], (64, 32))
    are = np.asarray(inputs["ssm_a_re"], f)[0].T
    aim = np.asarray(inputs["ssm_a_im"], f)[0].T
    shared["sp1"] = np.ascontiguousarray(np.stack([ldt, are, aim], axis=1).reshape(64, 96))
    bre = np.asarray(inputs["ssm_b_re"], f)[0].transpose(1, 0, 2).reshape(64, 512)
    bim = np.asarray(inputs["ssm_b_im"], f)[0].transpose(1, 0, 2).reshape(64, 512)
    cre = np.asarray(inputs["ssm_c_re"], f)[0].transpose(2, 0, 1).reshape(64, 512)
    cim = np.asarray(inputs["ssm_c_im"], f)[0].transpose(2, 0, 1).reshape(64, 512)
    shared["sp2"] = np.ascontiguousarray(np.stack([bre, bim, cre, cim], axis=1).reshape(64, 2048))
    shared["w_qkv"] = np.ascontiguousarray(np.asarray(inputs["attn_w_qkv"], f)[0])
    shared["w_o"] = np.ascontiguousarray(np.asarray(inputs["attn_w_o"], f)[0])
    shared.update(_const_tables())
    in_maps = []
    for c in range(NCORES):
        m = dict(shared)
        m["xT"] = np.ascontiguousarray(x[2 * c:2 * c + 2].transpose(0, 2, 1))
        in_maps.append(m)
    return in_maps


def _const_tables():
    f = np.float32
    ident = np.eye(128, dtype=f)
    p = np.arange(128)
    bmask = (p[None, :] // 16 >= p[:, None] // 16).astype(f)
    sel = np.zeros((128, 8, 240), f)
    for g8 in range(8):
        for h in range(16):
            sel[16 * g8 + h, g8, 112 + h] = 1.0
    cb = np.zeros((128, 2, 256), f)
    for par in range(2):
        cb[:, par, :] = np.where((par * 128 + p[:, None]) <= np.arange(256)[None, :], 0.0, -30000.0)
    en = np.zeros((8, 8, 128), f)
    for n in range(8):
        en[n, n, :] = 1.0
    negm = np.zeros((128, 4, 64), f)
    for qb in range(4, 8):
        for h in range(8):
            negm[:, qb - 4, h * 8 + qb:h * 8 + 8] = -1e30
    return {"cbias": np.ascontiguousarray(cb.reshape(128, 512)), "en": np.ascontiguousarray(en.reshape(8, 1024)),
            "negm": np.ascontiguousarray(negm.reshape(128, 256)),
            "ident": ident, "bmask": bmask, "selin": np.ascontiguousarray(sel.reshape(128, 1920)),
            "selout": np.ascontiguousarray(sel.reshape(128, 1920))}


_NC_CACHE = {}


def kernel(**inputs):
    in_maps = _prep_inputs(inputs)
    if "nc" not in _NC_CACHE:
        _NC_CACHE["nc"] = build_program()
    nc = _NC_CACHE["nc"]
    res = run_bass_kernel_spmd(nc, in_maps, core_ids=list(range(NCORES)))
    out = np.empty((2 * NCORES, L, D), np.float32)
    for c in range(NCORES):
        out[2 * c:2 * c + 2] = np.asarray(res.results[c]["outT"]).transpose(0, 2, 1)
    return out
```

```python
import numpy as np
import concourse.bass as bass
import concourse.mybir as mybir
from concourse.bass_utils import run_bass_kernel_spmd

F32 = mybir.dt.float32
BF16 = mybir.dt.bfloat16
AF = mybir.ActivationFunctionType
ALU = mybir.AluOpType
AX = mybir.AxisListType

D = 1024
KC = 8
FF = 2816
FC = 22
L = 2048
NSEQ = 2
TT = 512
RMS_EPS = 1e-6
LN_EPS = 1e-5
NCORES = 8


class Buf:
    __slots__ = ("name", "w", "r", "dsem", "dcnt")

    def __init__(self, name):
        self.name = name
        self.w = None
        self.r = {}
        self.dsem = {}
        self.dcnt = {}


class Sched:
    def __init__(self, nc):
        self.nc = nc
        self.eng = {"pe": nc.tensor, "act": nc.scalar, "dve": nc.vector, "pool": nc.gpsimd, "sp": nc.sync}
        self.sem = {k: nc.alloc_semaphore("s_" + k) for k in self.eng}
        self.cnt = {k: 0 for k in self.eng}
        self.seen = {k: {} for k in self.eng}
        self.pr = {k: [] for k in self.eng}
        self.pw = {k: [] for k in self.eng}
        self.nds = 0
        self.ninst = 0
        self.dbufs = []

    def _deps(self, reads, writes):
        deps = {}

        def need(s, v):
            if deps.get(s, 0) < v:
                deps[s] = v

        for b in reads:
            if b.w is not None:
                need(*b.w)
        for b in writes:
            if b.w is not None:
                need(*b.w)
            for s, v in b.r.items():
                need(s, v)
        return deps

    def _wait(self, eng, deps):
        e = self.eng[eng]
        own = self.sem[eng]
        for s, v in deps.items():
            if s is own and eng == "pe":
                continue
            if self.seen[eng].get(s, 0) >= v:
                continue
            e.wait_ge(s, v)
            self.seen[eng][s] = v

    def op(self, eng, fn, reads=(), writes=(), signal=True):
        self._wait(eng, self._deps(reads, writes))
        ins = fn(self.eng[eng])
        self.ninst += 1
        self.pr[eng].extend(reads)
        self.pw[eng].extend(writes)
        if signal:
            own = self.sem[eng]
            self.cnt[eng] += 1
            ins.then_inc(own, 1)
            v = self.cnt[eng]
            for b in self.pr[eng]:
                b.r[own] = v
            for b in self.pw[eng]:
                b.w = (own, v)
                b.r = {}
            self.pr[eng] = []
            self.pw[eng] = []
        return ins

    def dma(self, q, out, in_, reads=(), writes=()):
        self._wait(q, self._deps(reads, writes))
        owner = writes[0] if writes else reads[0]
        kind = "sw" if q == "pool" else "hw"
        if kind not in owner.dsem:
            owner.dsem[kind] = self.nc.alloc_semaphore("d%d" % self.nds)
            owner.dcnt[kind] = 0
            self.nds += 1
            self.dbufs.append((owner, kind))
        owner.dcnt[kind] += 1
        sem = owner.dsem[kind]
        self.eng[q].dma_start(out=out, in_=in_).then_inc(sem, 16)
        self.ninst += 1
        tag = (sem, 16 * owner.dcnt[kind])
        for b in reads:
            b.r[tag[0]] = tag[1]
        for b in writes:
            b.w = tag
            b.r = {}

    def barrier(self):
        for k in self.eng:
            assert not self.pr[k] and not self.pw[k], "unsignalled ops pending on " + k
        for k in self.eng:
            deps = {self.sem[o]: self.cnt[o] for o in self.eng if o != k and self.cnt[o] > 0}
            for b, kind in self.dbufs:
                deps[b.dsem[kind]] = 16 * b.dcnt[kind]
            self._wait(k, deps)

    def wait_all(self, eng, bufs):
        deps = {}
        for b in bufs:
            if b.w is not None:
                deps[b.w[0]] = max(deps.get(b.w[0], 0), b.w[1])
            for s, v in b.r.items():
                deps[s] = max(deps.get(s, 0), v)
        self._wait(eng, deps)


class Arena:
    def __init__(self, nc, nbytes):
        self.t = nc.alloc_sbuf_tensor("arena", [128, nbytes // 2], BF16)
        self.nbytes = nbytes
        self.off = 0
        self.marks = []

    def alloc(self, shape, dtype, at=None):
        esz = 2 if dtype == BF16 else 4
        n = int(np.prod(shape[1:]))
        nb = n * esz
        off = self.off if at is None else at
        off = (off + 31) // 32 * 32
        assert off + nb <= self.nbytes, ("arena overflow", off, nb, self.nbytes)
        if at is None:
            self.off = off + nb
        ap = self.t[0:shape[0], off // 2:(off + nb) // 2]
        if dtype != BF16:
            ap = ap.bitcast(dtype)
        if len(shape) == 3:
            ap = ap.rearrange("p (a b) -> p a b", a=shape[1])
        elif len(shape) == 4:
            ap = ap.rearrange("p (a b c) -> p a b c", a=shape[1], b=shape[2])
        return ap


def build_program(stages=("ffn00", "mix0", "ffn01", "ffn10", "mix1", "ffn11", "final"), nseq=NSEQ, debug=False):
    nc = bass.Bass("TRN2", target_bir_lowering=False)
    S = Sched(nc)
    dbg_names = []

    def dump(name, ap2d, bufs):
        if not debug or name in dbg_names:
            return
        dbg_names.append(name)
        shp = list(ap2d.shape)
        dt = nc.dram_tensor("dbg_" + name, shp, F32, kind="ExternalOutput").ap()
        S.dma("pool", dt, ap2d, reads=bufs)

    def din(name, shape, dt=F32):
        return nc.dram_tensor(name, list(shape), dt, kind="ExternalInput").ap()

    xT_d = din("xT", [NSEQ, D, L])
    gains_d = din("gains", [128, 7 * KC])
    w1_d = din("w1", [4, D, FF])
    w3_d = din("w3", [4, D, FF])
    w2_d = din("w2", [4, FF, D])
    outT_d = nc.dram_tensor("outT", [NSEQ, D, L], F32, kind="ExternalOutput").ap()

    A = Arena(nc, 207 * 1024)
    xT = A.alloc([128, KC, L], F32)
    gains = A.alloc([128, 7, KC], F32)
    ones_bf = A.alloc([128, 128], BF16)
    epsc = A.alloc([128, 1], F32)
    cw = A.alloc([128, 4, 31], F32)
    chv = A.alloc([128, 5, 4], F32)
    lnepsc = A.alloc([128, 1], F32)
    xsq = [A.alloc([128, TT], BF16) for _ in range(2)]
    rstd = A.alloc([128, TT], F32)
    sil = [A.alloc([128, TT], BF16) for _ in range(2)]
    base_off = A.off
    hT = A.alloc([128, KC, 1024], BF16)
    G = A.alloc([128, FC, 1024], BF16)
    NS13 = 3
    w13 = [A.alloc([128, 2, KC, 256], BF16) for _ in range(NS13)]
    w2s = A.alloc([128, FC, D], BF16)
    ffn_end = A.off

    PS = [nc.alloc_psum_tensor("ps%d" % i, [128, TT], F32) for i in range(8)]
    PSB = [Buf("ps%d" % i) for i in range(8)]

    XB = [[Buf("x%d_%d" % (k, b)) for b in range(L // TT)] for k in range(KC)]
    HB = [[Buf("h%d_%d" % (k, t)) for t in range(2)] for k in range(KC)]
    GB = [[Buf("g%d_%d" % (f, t)) for t in range(2)] for f in range(FC)]
    W13B = [(Buf("w1_%d" % i), Buf("w3_%d" % i)) for i in range(NS13)]
    W2B = [Buf("w2s%d" % i) for i in range(FC // 2)]
    XSQB = [Buf("xsq%d" % i) for i in range(2)]
    RSTDB = Buf("rstd")
    SILB = [Buf("sil%d" % i) for i in range(2)]
    CONSTB = Buf("const")

    S.dma("sp", gains.rearrange("p a b -> p (a b)"), gains_d, writes=[CONSTB])
    S.op("dve", lambda e: e.memset(ones_bf, 1.0), writes=[CONSTB])
    S.op("dve", lambda e: e.memset(epsc, RMS_EPS), writes=[CONSTB])

    w13_state = {"n": 0}

    def load_w13(fi, fg):
        i = w13_state["n"] % NS13
        w13_state["n"] += 1
        slot = w13[i]
        b = W13B[i]
        src1 = w1_d[fi].rearrange("(kc p) f -> p kc f", p=128)[:, :, fg * 256:(fg + 1) * 256]
        src3 = w3_d[fi].rearrange("(kc p) f -> p kc f", p=128)[:, :, fg * 256:(fg + 1) * 256]
        S.dma("pool", slot[:, 0], src1, writes=[b[0]])
        S.dma("pool", slot[:, 1], src3, writes=[b[1]])
        return slot, b

    def load_w2(fi, j):
        src = w2_d[fi].rearrange("(fc p) d -> p fc d", p=128)
        S.dma("pool", w2s[:, 2 * j:2 * j + 2], src[:, 2 * j:2 * j + 2], writes=[W2B[j]])

    def rmsnorm_tile(gidx, t0, ntt, dstT, dstB, inplace=False):
        for tt in range(ntt):
            blk = (t0 + tt * TT) // TT
            ps = PS[6]
            psb = PSB[6]
            for kc in range(KC):
                q = dstT[:, kc, tt * TT:(tt + 1) * TT]
                xs = xT[:, kc, blk * TT:(blk + 1) * TT]
                S.op("act", lambda e, q=q, xs=xs: e.activation(out=q, in_=xs, func=AF.Square),
                     reads=[XB[kc][blk], CONSTB], writes=[dstB[kc][tt]])
            for kc in range(KC):
                q = dstT[:, kc, tt * TT:(tt + 1) * TT]
                S.op("pe", lambda e, q=q, kc=kc: e.matmul(ps[:], ones_bf, q, start=(kc == 0), stop=(kc == KC - 1)),
                     reads=[dstB[kc][tt], CONSTB], writes=[psb], signal=(kc == KC - 1))
            S.op("act", lambda e: e.activation(out=rstd, in_=ps[:], func=AF.Sqrt, scale=1.0 / D, bias=epsc),
                 reads=[psb, CONSTB], writes=[RSTDB])
            S.op("dve", lambda e: e.reciprocal(out=rstd, in_=rstd), reads=[RSTDB], writes=[RSTDB])
            for kc in range(KC):
                xs = xT[:, kc, blk * TT:(blk + 1) * TT]
                if inplace:
                    S.op("dve", lambda e, kc=kc, xs=xs: e.scalar_tensor_tensor(
                        out=xs, in0=xs, scalar=gains[:, gidx, kc:kc + 1], in1=rstd, op0=ALU.mult, op1=ALU.mult),
                        reads=[XB[kc][blk], RSTDB, CONSTB, dstB[kc][tt]], writes=[XB[kc][blk]])
                else:
                    S.op("dve", lambda e, kc=kc, xs=xs: e.scalar_tensor_tensor(
                        out=dstT[:, kc, tt * TT:(tt + 1) * TT], in0=xs,
                        scalar=gains[:, gidx, kc:kc + 1], in1=rstd, op0=ALU.mult, op1=ALU.mult),
                        reads=[XB[kc][blk], RSTDB, CONSTB], writes=[dstB[kc][tt]])

    def ffn_tile(fi, gidx, t0, do_pre=True, hook=None):
        if do_pre:
            rmsnorm_tile(gidx, t0, 2, hT, HB)
        pcount = 0
        slots = {}
        for fg in range(min(NS13, FC // 2)):
            slots[fg] = load_w13(fi, fg)
        for fg in range(FC // 2):
            slot, sb = slots.pop(fg)
            for fh in range(2):
                fc = fg * 2 + fh
                for tt in range(2):
                    pa = (pcount % 2) * 2
                    pcount += 1
                    for j, (pi, wi) in enumerate(((pa, 0), (pa + 1, 1))):
                        for kc in range(KC):
                            S.op("pe", lambda e, pi=pi, wi=wi, kc=kc: e.matmul(
                                PS[pi][:], slot[:, wi, kc, fh * 128:(fh + 1) * 128], hT[:, kc, tt * TT:(tt + 1) * TT],
                                start=(kc == 0), stop=(kc == KC - 1)),
                                reads=[sb[wi], HB[kc][tt]], writes=[PSB[pi]], signal=(kc == KC - 1))
                    sl = sil[pcount % 2]
                    slb = SILB[pcount % 2]
                    S.op("act", lambda e, sl=sl, pa=pa: e.activation(out=sl, in_=PS[pa][:], func=AF.Silu),
                         reads=[PSB[pa]], writes=[slb])
                    S.op("dve", lambda e, sl=sl, pa=pa, fc=fc, tt=tt: e.tensor_tensor(
                        out=G[:, fc, tt * TT:(tt + 1) * TT], in0=sl, in1=PS[pa + 1][:], op=ALU.mult),
                        reads=[slb, PSB[pa + 1]], writes=[GB[fc][tt]])
            if fg + NS13 < FC // 2:
                slots[fg + NS13] = load_w13(fi, fg + NS13)
            load_w2(fi, fg)
        pcount = 0
        for dc in range(KC):
            for tt in range(2):
                pi = 4 + (pcount % 2)
                pcount += 1
                blk = (t0 + tt * TT) // TT
                for fc in range(FC):
                    S.op("pe", lambda e, pi=pi, fc=fc, dc=dc, tt=tt: e.matmul(
                        PS[pi][:], w2s[:, fc, dc * 128:(dc + 1) * 128], G[:, fc, tt * TT:(tt + 1) * TT],
                        start=(fc == 0), stop=(fc == FC - 1)),
                        reads=[W2B[fc // 2], GB[fc][tt]], writes=[PSB[pi]], signal=(fc == FC - 1))
                xs = xT[:, dc, blk * TT:(blk + 1) * TT]
                S.op("dve", lambda e, pi=pi, xs=xs: e.scalar_tensor_tensor(
                    out=xs, in0=PS[pi][:], scalar=0.5, in1=xs, op0=ALU.mult, op1=ALU.add),
                    reads=[PSB[pi], XB[dc][blk]], writes=[XB[dc][blk]])
            if dc == 1 and hook is not None:
                hook()

    def ffn_chain(jobs, last_hook=None):
        for j, (fi, gidx, t0) in enumerate(jobs):
            nxt = jobs[j + 1] if j + 1 < len(jobs) else None
            hook = last_hook
            if nxt is not None:
                assert nxt[2] != t0
                hook = (lambda nxt=nxt: rmsnorm_tile(nxt[1], nxt[2], 2, hT, HB))
            ffn_tile(fi, gidx, t0, do_pre=(j == 0), hook=hook)

    RB = base_off
    KB = 1024

    def R(off_kb, shape, dt):
        return A.alloc(shape, dt, at=RB + int(off_kb * KB))

    I32 = mybir.dt.int32
    win_d = din("w_in", [D, 1536])
    wout_d = din("w_out", [D, D])
    gluw_d = din("glu_w", [512, 512])
    cw_d = din("cw", [128, 4 * 31])
    chv_d = din("chv", [128, 5 * 4])
    sp1_d = din("sp1", [64, 3 * 32])
    sp2_d = din("sp2", [64, 4 * 512])
    ident_d = din("ident", [128, 128])
    bmask_d = din("bmask", [128, 128])
    selin_d = din("selin", [128, 8 * 240])
    selout_d = din("selout", [128, 8 * 240])
    scr_toep = nc.dram_tensor("scr_toep", [128, 32 * 128], BF16, kind="Internal").ap()
    scr_wb = nc.dram_tensor("scr_wb", [128, 2 * 32 * 64], BF16, kind="Internal").ap()
    scr_cm = nc.dram_tensor("scr_cm", [64, 2 * 32 * 128], BF16, kind="Internal").ap()
    scr_at = nc.dram_tensor("scr_at", [64, 128], F32, kind="Internal").ap()

    M0CB = Buf("m0const")
    S.dma("sp", cw.rearrange("p a b -> p (a b)"), cw_d, writes=[M0CB])
    S.dma("sp", chv.rearrange("p a b -> p (a b)"), chv_d, writes=[M0CB])
    S.op("dve", lambda e: e.memset(lnepsc, LN_EPS), writes=[M0CB])

    uT = R(0, [128, 4, L], BF16)
    uT4 = R(0, [128, 4, 8, 256], BF16)
    mcat = R(16, [128, 8, L], BF16)
    Xbf = R(32, [128, 2, 16, 256], BF16)
    woutS = R(116, [128, 8, D], BF16)
    hT2 = R(64, [128, KC, L], BF16)
    winS = R(96, [128, KC, 1536], BF16)
    vpad = R(47.5, [128, 4, 30 + L], BF16)
    ycv = R(64, [128, 4, L], F32)
    lnt = [R(96 + 2 * i, [128, TT], F32) for i in range(5)]
    selin = R(130, [128, 8, 240], BF16)
    selout = R(68, [128, 8, 240], BF16)
    gluwS = R(72, [128, 4, 512], BF16)
    toepS = R(76, [128, 32, 128], BF16)
    wbS = R(84, [128, 2, 32, 64], BF16)
    cmS = R(92, [128, 2, 16, 128], BF16)
    atS = R(108, [128, 2, 2, 16], F32)
    Xh = R(109, [128, 32, 2, 16], F32)
    pq = R(113, [128, 2, 2, 16], F32)
    rtmp = R(114, [128, 2, 16], F32)
    Ut = R(48, [128, 32, 256], BF16)
    gt = [R(76 + 2 * i, [128, TT], F32) for i in range(6)]
    y1bS = R(92, [128, 4, L], BF16)
    sgt = [R(88 + i, [128, TT], BF16) for i in range(2)]

    diag = [R(o_, [128, 31, 128], BF16) for o_ in (32, 39.75, 114, 121.75)]
    identM = R(135.25, [128, 128], BF16)
    DIAGB = [Buf("diag%d" % i) for i in range(4)]
    IDMB = Buf("identM")
    UB = [[Buf("u%d_%d" % (c, t)) for t in range(4)] for c in range(4)]
    MCB = [[Buf("mc%d_%d" % (c, t)) for t in range(4)] for c in range(8)]
    H2B = [[Buf("h2_%d_%d" % (k, t)) for t in range(4)] for k in range(KC)]
    WINB = [Buf("win%d" % i) for i in range(3)]
    WOUTB = Buf("woutS")
    VB = [Buf("v%d" % c) for c in range(4)]
    YCB = [Buf("yc%d" % c) for c in range(4)]
    LNTB = [Buf("lnt%d" % i) for i in range(5)]
    PPB = Buf("ssmparams")
    SELIB = Buf("selin"); WBB = Buf("wbS"); TOEPB = Buf("toepS")
    CMB = [Buf("cm%d" % i) for i in range(4)]
    ATB = [Buf("at%d" % i) for i in range(8)]
    UTB = [Buf("ut%d" % g) for g in range(32)]
    XBFB = Buf("xbf")
    XHB = [Buf("xh%d" % i) for i in range(32)]
    PQB = Buf("pq")
    RTB = Buf("rtmp")
    GTB = [Buf("gt%d" % i) for i in range(6)]
    Y1B = [[Buf("y1_%d_%d" % (c, t)) for t in range(4)] for c in range(4)]
    SGTB = [Buf("sgt%d" % i) for i in range(2)]

    def ssm_param_prep():
        S.barrier()
        P = PPB
        cnt = {"o": 0}

        def T(shape, dt=F32):
            n = int(np.prod(shape[1:])) * (4 if dt != BF16 else 2)
            off = cnt["o"]
            cnt["o"] = (off + n + 31) // 32 * 32
            assert RB + cnt["o"] <= A.nbytes, cnt["o"]
            return A.alloc(shape, dt, at=RB + off)

        def v_tt(out, a, b, op):
            S.op("dve", lambda e: e.tensor_tensor(out=out, in0=a, in1=b, op=op), reads=[P], writes=[P])

        def v_ts(out, a, s1, s2, op0, op1=None):
            if op1 is None:
                S.op("dve", lambda e: e.tensor_scalar(out=out, in0=a, scalar1=s1, scalar2=None, op0=op0), reads=[P], writes=[P])
            else:
                S.op("dve", lambda e: e.tensor_scalar(out=out, in0=a, scalar1=s1, scalar2=s2, op0=op0, op1=op1), reads=[P], writes=[P])

        def a_act(out, a, func, scale=1.0):
            S.op("act", lambda e: e.activation(out=out, in_=a, func=func, scale=scale), reads=[P], writes=[P])

        def v_cp(out, a):
            S.op("dve", lambda e: e.tensor_copy(out=out, in_=a), reads=[P], writes=[P])

        sp1 = T([64, 3, 32]); sp2 = T([64, 4, 512])
        S.dma("act", sp1.rearrange("p a b -> p (a b)"), sp1_d, writes=[P])
        S.dma("act", sp2.rearrange("p a b -> p (a b)"), sp2_d, writes=[P])
        identS = T([128, 128], BF16); bmaskS = T([128, 128])
        S.dma("pool", identS, ident_d, writes=[P])
        S.dma("act", bmaskS, bmask_d, writes=[P])
        ldt, are, aim = sp1[:, 0], sp1[:, 1], sp1[:, 2]
        bre = sp2[:, 0].rearrange("p (g h) -> p g h", g=32)
        bim = sp2[:, 1].rearrange("p (g h) -> p g h", g=32)
        cre = sp2[:, 2].rearrange("p (g h) -> p g h", g=32)
        cim = sp2[:, 3].rearrange("p (g h) -> p g h", g=32)
        dt_ = T([64, 32]); mag = T([64, 32]); ang = T([64, 32]); t1 = T([64, 32]); t2 = T([64, 32])
        ti = T([64, 32], I32); cosv = T([64, 32]); sinv = T([64, 32])
        a_act(dt_, ldt, AF.Exp)
        v_tt(t1, dt_, are, ALU.mult)
        a_act(mag, t1, AF.Exp)
        v_tt(ang, dt_, aim, ALU.mult)
        TWO_PI = 2.0 * np.pi

        def sin_of(out, shift):
            v_ts(t1, ang, 1.0 / TWO_PI, float(shift), ALU.mult, ALU.add)
            v_cp(ti, t1)
            v_cp(t2, ti)
            v_tt(t1, t1, t2, ALU.subtract)
            v_ts(t2, t1, 0.5, None, ALU.is_gt)
            v_tt(t1, t1, t2, ALU.subtract)
            v_ts(t2, t1, -0.5, None, ALU.is_lt)
            v_tt(t1, t1, t2, ALU.add)
            a_act(out, t1, AF.Sin, scale=TWO_PI)

        dump("dt", dt_, [P]); dump("mag", mag, [P]); dump("ang", ang, [P])
        sin_of(sinv, 0.0)
        dump("frac_s", t1, [P]); dump("sinv", sinv, [P])
        sin_of(cosv, 0.25)
        dump("cosv", cosv, [P])
        pwr = T([64, 9, 32]); pwi = T([64, 9, 32])
        S.op("dve", lambda e: e.memset(pwr[:, 0], 1.0), reads=[P], writes=[P])
        S.op("dve", lambda e: e.memset(pwi[:, 0], 0.0), reads=[P], writes=[P])
        v_tt(pwr[:, 1], mag, cosv, ALU.mult)
        v_tt(pwi[:, 1], mag, sinv, ALU.mult)
        for k in range(1, 8):
            v_tt(t1, pwr[:, k], pwr[:, 1], ALU.mult)
            v_tt(t2, pwi[:, k], pwi[:, 1], ALU.mult)
            v_tt(pwr[:, k + 1], t1, t2, ALU.subtract)
            v_tt(t1, pwr[:, k], pwi[:, 1], ALU.mult)
            v_tt(t2, pwi[:, k], pwr[:, 1], ALU.mult)
            v_tt(pwi[:, k + 1], t1, t2, ALU.add)
        dump("pwr", pwr.rearrange("p k g -> p (k g)"), [P]); dump("pwi", pwi.rearrange("p k g -> p (k g)"), [P])
        ipr = T([64, 9, 32]); ipi = T([64, 9, 32]); n2 = T([64, 9, 32]); n3 = T([64, 9, 32])
        v_tt(n2, pwr, pwr, ALU.mult)
        v_tt(n3, pwi, pwi, ALU.mult)
        v_tt(n2, n2, n3, ALU.add)
        S.op("dve", lambda e: e.reciprocal(out=n2, in_=n2), reads=[P], writes=[P])
        v_tt(ipr, pwr, n2, ALU.mult)
        v_tt(ipi, pwi, n2, ALU.mult)
        v_ts(ipi, ipi, -1.0, None, ALU.mult)
        den = T([64, 32]); nr = T([64, 32]); qre = T([64, 32]); qim = T([64, 32])
        v_tt(den, are, are, ALU.mult)
        v_tt(t1, aim, aim, ALU.mult)
        v_tt(den, den, t1, ALU.add)
        S.op("dve", lambda e: e.reciprocal(out=den, in_=den), reads=[P], writes=[P])
        v_ts(nr, pwr[:, 1], -1.0, None, ALU.add)
        v_tt(t1, nr, are, ALU.mult)
        v_tt(t2, pwi[:, 1], aim, ALU.mult)
        v_tt(t1, t1, t2, ALU.add)
        v_tt(qre, t1, den, ALU.mult)
        v_tt(t1, pwi[:, 1], are, ALU.mult)
        v_tt(t2, nr, aim, ALU.mult)
        v_tt(t1, t1, t2, ALU.subtract)
        v_tt(qim, t1, den, ALU.mult)
        Bre = T([64, 32, 16]); Bim = T([64, 32, 16]); w1_ = T([64, 32, 16]); w2_ = T([64, 32, 16])
        qre_b = qre.unsqueeze(2).to_broadcast([64, 32, 16])
        qim_b = qim.unsqueeze(2).to_broadcast([64, 32, 16])
        v_tt(w1_, bre, qre_b, ALU.mult); v_tt(w2_, bim, qim_b, ALU.mult); v_tt(Bre, w1_, w2_, ALU.subtract)
        v_tt(w1_, bim, qre_b, ALU.mult); v_tt(w2_, bre, qim_b, ALU.mult); v_tt(Bim, w1_, w2_, ALU.add)
        big_off = cnt["o"]
        big1 = T([64, 32, 8, 16]); big2 = T([64, 32, 8, 16])
        Cmr = T([64, 32, 8, 16]); Cmi = T([64, 32, 8, 16])
        q0 = cnt["o"]
        Bsr = T([64, 32, 8, 16], BF16); Bsi = T([64, 32, 8, 16], BF16)
        cmr_bf = T([64, 32, 8, 16], BF16); cmi_bf = T([64, 32, 8, 16], BF16)
        Bmr = A.alloc([64, 32, 8, 16], F32, at=RB + q0); Bmi = A.alloc([64, 32, 8, 16], F32, at=RB + q0 + 16 * KB)

        def bc_x(x):
            return x.unsqueeze(2).to_broadcast([64, 32, 8, 16])

        def bc_p(p, lo, hi, rev=False):
            sl = p[:, lo:hi, :].rearrange("p k g -> p g k")
            return sl.unsqueeze(3).to_broadcast([64, 32, 8, 16])

        def cmul_big(o_re, o_im, xr, xi, pr, pi, neg_im=False):
            v_tt(big1, bc_x(xr), pr, ALU.mult); v_tt(big2, bc_x(xi), pi, ALU.mult)
            v_tt(o_re, big1, big2, ALU.subtract)
            v_tt(big1, bc_x(xr), pi, ALU.mult); v_tt(big2, bc_x(xi), pr, ALU.mult)
            if neg_im:
                v_tt(o_im, big1, big2, ALU.add)
                v_ts(o_im, o_im, -1.0, None, ALU.mult)
            else:
                v_tt(o_im, big1, big2, ALU.add)

        cmul_big(Cmr, Cmi, cre, cim, bc_p(pwr, 1, 9), bc_p(pwi, 1, 9), neg_im=True)
        v_cp(cmr_bf, Cmr); v_cp(cmi_bf, Cmi)
        S.dma("sp", scr_cm[:, 0:4096], cmr_bf.rearrange("p g j h -> p (g j h)"), reads=[P])
        S.dma("sp", scr_cm[:, 4096:8192], cmi_bf.rearrange("p g j h -> p (g j h)"), reads=[P])
        pwr_rev = T([64, 8, 32]); pwi_rev = T([64, 8, 32])
        for i in range(8):
            v_cp(pwr_rev[:, i], pwr[:, 7 - i]); v_cp(pwi_rev[:, i], pwi[:, 7 - i])
        prr = pwr_rev.rearrange("p k g -> p g k").unsqueeze(3).to_broadcast([64, 32, 8, 16])
        pri = pwi_rev.rearrange("p k g -> p g k").unsqueeze(3).to_broadcast([64, 32, 8, 16])
        cmul_big(Bsr, Bsi, Bre, Bim, prr, pri)
        wb_bf = T([128, 2, 32, 64], BF16)
        for o, src in ((0, Bsr), (1, Bsi)):
            for g8 in range(4):
                ps = PS[7]
                for gg in range(8):
                    g = g8 * 8 + gg
                    S.op("pe", lambda e, g=g, gg=gg, src=src: e.matmul(ps[:, gg * 64:(gg + 1) * 64], src[:, g].rearrange("p i h -> p (i h)"),
                                                                       identS[0:64, 0:64], start=True, stop=True),
                         reads=[P], writes=[PSB[7]], signal=(gg == 7))
                S.op("dve", lambda e, o=o, g8=g8: e.tensor_copy(out=wb_bf[:, o, g8 * 8:(g8 + 1) * 8, :],
                                                               in_=ps[:].rearrange("p (a b) -> p a b", a=8)), reads=[PSB[7], P], writes=[P])
        S.dma("sp", scr_wb, wb_bf.rearrange("p o g n -> p (o g n)"), reads=[P])
        at_ = T([64, 2, 2, 32])
        v_cp(at_[:, 0, 0], pwr[:, 8]); v_cp(at_[:, 1, 1], pwr[:, 8]); v_cp(at_[:, 1, 0], pwi[:, 8])
        v_ts(at_[:, 0, 1], pwi[:, 8], -1.0, None, ALU.mult)
        S.dma("sp", scr_at, at_.rearrange("p o k g -> p (o k g)"), reads=[P])
        cmul_big(Bmr, Bmi, Bre, Bim, bc_p(ipr, 1, 9), bc_p(ipi, 1, 9))
        toep_bf = A.alloc([128, 32, 128], BF16, at=RB + big_off)
        for g4 in range(8):
            ps = PS[7]
            for gg in range(4):
                g = g4 * 4 + gg
                S.op("pe", lambda e, g=g, gg=gg: e.matmul(ps[:, gg * 128:(gg + 1) * 128], Bmr[:, g].rearrange("p i h -> p (i h)"),
                                                          Cmr[:, g].rearrange("p j h -> p (j h)"), start=True, stop=False),
                     reads=[P], writes=[PSB[7]], signal=False)
                S.op("pe", lambda e, g=g, gg=gg: e.matmul(ps[:, gg * 128:(gg + 1) * 128], Bmi[:, g].rearrange("p i h -> p (i h)"),
                                                          Cmi[:, g].rearrange("p j h -> p (j h)"), start=False, stop=True),
                     reads=[P], writes=[PSB[7]], signal=(gg == 3))
            S.op("dve", lambda e, g4=g4: e.tensor_tensor(
                out=toep_bf[:, g4 * 4:(g4 + 1) * 4, :], in0=ps[:].rearrange("p (a b) -> p a b", a=4),
                in1=bmaskS.unsqueeze(1).to_broadcast([128, 4, 128]), op=ALU.mult), reads=[PSB[7], P], writes=[P])
        S.dma("sp", scr_toep, toep_bf.rearrange("p g m -> p (g m)"), reads=[P])
        S.barrier()

    def mixer0(s):
        S.barrier()
        for i in range(3):
            S.dma("pool", winS[:, :, i * 512:(i + 1) * 512],
                  win_d.rearrange("(kc p) f -> p kc f", p=128)[:, :, i * 512:(i + 1) * 512], writes=[WINB[i]])
        S.dma("pool", identM, ident_d, writes=[IDMB])
        for ct in range(2):
            for k in range(31):
                S.op("dve", lambda e, ct=ct, k=k: e.tensor_scalar(out=diag[ct][:, k, :], in0=identM, scalar1=cw[:, ct, k:k + 1], scalar2=None,
                                                              op0=ALU.mult), reads=[IDMB, M0CB], writes=[DIAGB[ct]])
        rmsnorm_tile(4, 0, 4, hT2, H2B)
        S.op("pool", lambda e: e.memset(vpad[:, :, 0:30], 0.0), writes=VB)
        pc = 0
        for ct in range(4):
            for tb in range(4):
                pa, pg = (pc % 2) * 2, (pc % 2) * 2 + 1
                pc += 1
                for pi, oc in ((pa, ct), (pg, ct + 4)):
                    for kc in range(KC):
                        S.op("pe", lambda e, pi=pi, oc=oc, kc=kc, tb=tb: e.matmul(
                            PS[pi][:], winS[:, kc, oc * 128:(oc + 1) * 128], hT2[:, kc, tb * TT:(tb + 1) * TT],
                            start=(kc == 0), stop=(kc == KC - 1)),
                            reads=[WINB[oc // 4], H2B[kc][tb]], writes=[PSB[pi]], signal=(kc == KC - 1))
                sg = sil[pc % 2]
                S.op("act", lambda e, sg=sg, pg=pg: e.activation(out=sg, in_=PS[pg][:], func=AF.Sigmoid),
                     reads=[PSB[pg]], writes=[SILB[pc % 2]])
                S.op("dve", lambda e, sg=sg, pa=pa, ct=ct, tb=tb: e.tensor_tensor(
                    out=vpad[:, ct, 30 + tb * TT:30 + (tb + 1) * TT], in0=sg, in1=PS[pa][:], op=ALU.mult),
                    reads=[SILB[pc % 2], PSB[pa]], writes=[VB[ct]])
        for ct in range(4):
            for tb in range(4):
                pi = 4 + (pc % 2)
                pc += 1
                oc = 8 + ct
                for kc in range(KC):
                    S.op("pe", lambda e, pi=pi, oc=oc, kc=kc, tb=tb: e.matmul(
                        PS[pi][:], winS[:, kc, oc * 128:(oc + 1) * 128], hT2[:, kc, tb * TT:(tb + 1) * TT],
                        start=(kc == 0), stop=(kc == KC - 1)),
                        reads=[WINB[2], H2B[kc][tb]], writes=[PSB[pi]], signal=(kc == KC - 1))
                S.op("act", lambda e, pi=pi, ct=ct, tb=tb: e.activation(out=uT4[:, ct, :, tb * 64:(tb + 1) * 64],
                                                                       in_=PS[pi][:].rearrange("p (c i) -> p i c", i=8), func=AF.Copy),
                     reads=[PSB[pi]], writes=[UB[ct][tb]])
        S.barrier()
        S.dma("pool", selin.rearrange("p a b -> p (a b)"), selin_d, writes=[SELIB])
        for ct in range(2, 4):
            for k in range(31):
                S.op("dve", lambda e, ct=ct, k=k: e.tensor_scalar(out=diag[ct][:, k, :], in0=identM, scalar1=cw[:, ct, k:k + 1], scalar2=None,
                                                              op0=ALU.mult), reads=[IDMB, M0CB], writes=[DIAGB[ct]])
        lnb = [R(96 + 2 * i, [128, TT], F32) for i in range(7)]
        ybf = [R(110 + i, [128, TT], BF16) for i in range(2)]
        ysq = [R(112 + i, [128, TT], BF16) for i in range(2)]
        LNB = [Buf("lnb%d" % i) for i in range(7)]
        YBB = [Buf("ybf%d" % i) for i in range(2)]
        YSB = [Buf("ysq%d" % i) for i in range(2)]
        YC2 = [[Buf("yc%d_%d" % (c, t)) for t in range(4)] for c in range(4)]
        tiles = [(tb, ct) for tb in range(4) for ct in range(4)]

        def conv_tile(n):
            tb, ct = tiles[n]
            pi = 2 + (n % 2)
            j = n % 2
            for k in range(31):
                S.op("pe", lambda e, pi=pi, ct=ct, k=k, tb=tb: e.matmul(
                    PS[pi][:], diag[ct][:, k, :], vpad[:, ct, k + tb * TT:k + (tb + 1) * TT], start=(k == 0), stop=(k == 30)),
                    reads=[DIAGB[ct], VB[ct]], writes=[PSB[pi]], signal=(k == 30))
            bia = chv[:, 0, ct:ct + 1]
            S.op("act", lambda e, pi=pi, ct=ct, tb=tb: e.activation(out=ycv[:, ct, tb * TT:(tb + 1) * TT], in_=PS[pi][:], func=AF.Identity, bias=bia),
                 reads=[PSB[pi], M0CB], writes=[YC2[ct][tb]])
            S.op("act", lambda e, pi=pi, j=j: e.activation(out=ybf[j], in_=PS[pi][:], func=AF.Identity, bias=bia),
                 reads=[PSB[pi], M0CB], writes=[YBB[j]])
            S.op("act", lambda e, pi=pi, j=j: e.activation(out=ysq[j], in_=PS[pi][:], func=AF.Square, bias=bia),
                 reads=[PSB[pi], M0CB], writes=[YSB[j]])

        def stat_mm(n):
            tb, ct = tiles[n]
            j = n % 2
            sb_, qb_ = ((0, 1), (4, 5))[tb % 2]
            S.op("pe", lambda e: e.matmul(PS[sb_][:], ones_bf, ybf[j], start=(ct == 0), stop=(ct == 3)),
                 reads=[YBB[j], CONSTB], writes=[PSB[sb_]], signal=True)
            S.op("pe", lambda e: e.matmul(PS[qb_][:], ones_bf, ysq[j], start=(ct == 0), stop=(ct == 3)),
                 reads=[YSB[j], CONSTB], writes=[PSB[qb_]], signal=True)

        def ln_stages(tb):
            sb_, qb_ = ((0, 1), (4, 5))[tb % 2]
            mean, var, msq = lnb[2 + tb % 2], lnb[4 + tb % 2], lnb[6]
            MB_, VB_, QB_ = LNB[2 + tb % 2], LNB[4 + tb % 2], LNB[6]
            sl = slice(tb * TT, (tb + 1) * TT)

            def st_a():
                S.op("act", lambda e: e.activation(out=mean, in_=PS[sb_][:], func=AF.Copy, scale=1.0 / 512), reads=[PSB[sb_]], writes=[MB_])
                S.op("act", lambda e: e.activation(out=msq, in_=PS[sb_][:], func=AF.Square, scale=1.0 / 512), reads=[PSB[sb_]], writes=[QB_])
                S.op("dve", lambda e: e.scalar_tensor_tensor(out=var, in0=PS[qb_][:], scalar=1.0 / 512, in1=msq, op0=ALU.mult, op1=ALU.subtract),
                     reads=[PSB[qb_], QB_], writes=[VB_])

            def st_b():
                S.op("act", lambda e: e.activation(out=var, in_=var, func=AF.Sqrt, bias=lnepsc), reads=[VB_, M0CB], writes=[VB_])
                S.op("dve", lambda e: e.reciprocal(out=var, in_=var), reads=[VB_], writes=[VB_])

            def t_ops(ct):
                t_ = lnb[ct % 2]
                TB_ = LNB[ct % 2]
                S.op("dve", lambda e: e.tensor_tensor(out=t_, in0=ycv[:, ct, sl], in1=mean, op=ALU.subtract),
                     reads=[YC2[ct][tb], MB_], writes=[TB_])
                S.op("dve", lambda e: e.tensor_tensor(out=t_, in0=t_, in1=var, op=ALU.mult),
                     reads=[TB_, VB_], writes=[TB_])

            def silu(ct):
                t_ = lnb[ct % 2]
                S.op("act", lambda e: e.activation(out=mcat[:, ct, sl], in_=t_, func=AF.Silu,
                                                   scale=chv[:, 1, ct:ct + 1], bias=chv[:, 2, ct:ct + 1]),
                     reads=[LNB[ct % 2], M0CB], writes=[MCB[ct][tb]])

            return [st_a, st_b, lambda: (t_ops(0), t_ops(1)), lambda: (silu(0), silu(1), t_ops(2), t_ops(3)), lambda: (silu(2), silu(3))]

        pending = []
        conv_tile(0)
        for n in range(len(tiles)):
            if n + 1 < len(tiles):
                conv_tile(n + 1)
            stat_mm(n)
            for st in pending:
                if st:
                    st.pop(0)()
            if tiles[n][1] == 3:
                pending.append(ln_stages(tiles[n][0]))
        while any(pending):
            for st in pending:
                if st:
                    st.pop(0)()
        dump("mcA", mcat[:, 0:4, :].rearrange("p a b -> p (a b)"), [b for r in MCB[0:4] for b in r])
        dump("u", uT.rearrange("p a b -> p (a b)"), [b for r in UB for b in r])
        S.barrier()
        S.dma("sp", wbS.rearrange("p o g n -> p (o g n)"), scr_wb, writes=[WBB])
        S.dma("sp", toepS.rearrange("p g m -> p (g m)"), scr_toep, writes=[TOEPB])
        S.dma("pool", selout.rearrange("p a b -> p (a b)"), selout_d, writes=[PPB])
        S.dma("pool", gluwS, gluw_d.rearrange("(kc p) f -> p kc f", p=128), writes=[PPB])
        scr_cm4 = scr_cm.rearrange("p (o g m) -> p o g m", o=2, g=32)
        scr_at4 = scr_at.rearrange("p (o k g) -> p o k g", o=2, k=2)
        for gh in range(2):
            for o in range(2):
                S.dma("sp", cmS[64 * gh:64 * gh + 64, o, :, :], scr_cm4[:, o, gh * 16:(gh + 1) * 16, :], writes=[CMB[gh * 2 + o]])
                for k in range(2):
                    S.dma("sp", atS[64 * gh:64 * gh + 64, o, k, :], scr_at4[:, o, k, gh * 16:(gh + 1) * 16], writes=[ATB[gh * 4 + o * 2 + k]])
        for g2 in range(16):
            pi = g2 % 2
            for gg in range(2):
                g = g2 * 2 + gg
                ct, g8 = g // 8, g % 8
                for i in range(8):
                    S.op("pe", lambda e, pi=pi, gg=gg, ct=ct, g8=g8, i=i: e.matmul(
                        PS[pi][:, gg * 256:(gg + 1) * 256], selin[:, g8, 112 - 16 * i:240 - 16 * i], uT4[:, ct, i, :],
                        start=(i == 0), stop=(i == 7)),
                        reads=[SELIB] + UB[ct], writes=[PSB[pi]], signal=(gg == 1 and i == 7))
            S.op("act", lambda e, pi=pi, g2=g2: e.activation(out=Ut[:, g2 * 2:g2 * 2 + 2, :],
                                                            in_=PS[pi][:].rearrange("p (a b) -> p a b", a=2), func=AF.Copy),
                 reads=[PSB[pi]], writes=[UTB[g2 * 2], UTB[g2 * 2 + 1]])
        S.op("dve", lambda e: e.memset(Xh[:, 31], 0.0), writes=[XHB[31]])
        for cb in range(16):
            pi = 2 + (cb % 2)
            psv = PS[pi][:].rearrange("p (c o g) -> p c o g", c=16, o=2)
            for g in range(32):
                gh, gl = g // 16, g % 16
                for o in range(2):
                    S.op("pe", lambda e, psv=psv, g=g, gh=gh, gl=gl, o=o, cb=cb: e.matmul(
                        psv[64 * gh:64 * gh + 64, :, o, gl], wbS[:, o, g, :], Ut[:, g, cb * 16:(cb + 1) * 16], start=True, stop=True),
                        reads=[WBB, UTB[g]], writes=[PSB[pi]], signal=(g == 31 and o == 1))
            for cc in range(16):
                c = cb * 16 + cc
                prev = Xh[:, (c - 1) % 32]
                S.op("dve", lambda e, prev=prev: e.tensor_tensor(
                    out=pq, in0=prev.unsqueeze(1).to_broadcast([128, 2, 2, 16]), in1=atS, op=ALU.mult),
                    reads=[XHB[(c - 1) % 32]] + ATB, writes=[PQB])
                S.op("dve", lambda e: e.tensor_tensor(out=rtmp, in0=pq[:, :, 0, :], in1=pq[:, :, 1, :], op=ALU.add),
                     reads=[PQB], writes=[RTB])
                S.op("dve", lambda e, psv=psv, cc=cc, c=c: e.tensor_tensor(out=Xh[:, c % 32], in0=rtmp, in1=psv[:, cc], op=ALU.add),
                     reads=[RTB, PSB[pi]], writes=[XHB[c % 32]])
            half = (cb % 2) * 16
            S.op("act", lambda e, cb=cb, half=half: e.activation(
                out=Xbf[:, :, :, cb * 16:(cb + 1) * 16], in_=Xh[:, half:half + 16].rearrange("p c o g -> p o g c"), func=AF.Copy),
                reads=XHB[half:half + 16], writes=[XBFB])
        for g2 in range(16):
            pi = 4 + (g2 % 2)
            for gg in range(2):
                g = g2 * 2 + gg
                S.op("pe", lambda e, pi=pi, gg=gg, g=g: e.matmul(PS[pi][:, gg * 256:(gg + 1) * 256], toepS[:, g, :], Ut[:, g, :],
                                                                start=True, stop=False),
                     reads=[TOEPB, UTB[g]], writes=[PSB[pi]], signal=False)
                hs = slice(64 * (g // 16), 64 * (g // 16) + 64)
                gl = g % 16
                S.op("pe", lambda e, pi=pi, gg=gg, gl=gl, hs=hs: e.matmul(PS[pi][:, gg * 256 + 1:(gg + 1) * 256], cmS[hs, 0, gl, :], Xbf[hs, 0, gl, 0:255],
                                                                start=False, stop=False),
                     reads=CMB + [XBFB], writes=[PSB[pi]], signal=False)
                S.op("pe", lambda e, pi=pi, gg=gg, gl=gl, hs=hs: e.matmul(PS[pi][:, gg * 256 + 1:(gg + 1) * 256], cmS[hs, 1, gl, :], Xbf[hs, 1, gl, 0:255],
                                                                start=False, stop=True),
                     reads=CMB + [XBFB], writes=[PSB[pi]], signal=(gg == 1))
            S.op("act", lambda e, pi=pi, g2=g2: e.activation(out=Ut[:, g2 * 2:g2 * 2 + 2, :],
                                                            in_=PS[pi][:].rearrange("p (a b) -> p a b", a=2), func=AF.Copy),
                 reads=[PSB[pi]], writes=[UTB[g2 * 2], UTB[g2 * 2 + 1]])
        dump("toep", toepS.rearrange("p g m -> p (g m)"), [TOEPB])
        dump("wb", wbS.rearrange("p o g n -> p (o g n)"), [WBB])
        dump("Y", Ut.rearrange("p g c -> p (g c)"), UTB)
        S.barrier()
        S.dma("pool", woutS, wout_d.rearrange("(kc p) f -> p kc f", p=128), writes=[WOUTB])
        pc = 0
        for ct in range(4):
            for tb in range(4):
                pi = pc % 2
                pc += 1
                for j in range(8):
                    for g8 in range(8):
                        S.op("pe", lambda e, pi=pi, j=j, g8=g8, ct=ct, tb=tb: e.matmul(
                            PS[pi][:, j * 64:(j + 1) * 64], selout[:, j, 112 - 16 * g8:240 - 16 * g8],
                            Ut[:, ct * 8 + g8, tb * 64:(tb + 1) * 64], start=(g8 == 0), stop=(g8 == 7)),
                            reads=[PPB, UTB[ct * 8 + g8]], writes=[PSB[pi]], signal=(j == 7 and g8 == 7))
                gb = 3 * (pc % 2)
                ys, x2, x3 = gt[gb], gt[gb + 1], gt[gb + 2]
                sg_ = x2
                GY, G2, G3 = GTB[gb], GTB[gb + 1], GTB[gb + 2]
                sl = slice(tb * TT, (tb + 1) * TT)
                S.op("dve", lambda e, pi=pi, ct=ct, sl=sl: e.scalar_tensor_tensor(
                    out=ys.rearrange("p (c j) -> p c j", j=8), in0=uT4[:, ct, :, tb * 64:(tb + 1) * 64].rearrange("p j c -> p c j"),
                    scalar=chv[:, 3, ct:ct + 1], in1=PS[pi][:].rearrange("p (j c) -> p c j", j=8), op0=ALU.mult, op1=ALU.add),
                    reads=[PSB[pi], UB[ct][tb], M0CB], writes=[GY])
                S.op("act", lambda e: e.activation(out=x2, in_=ys, func=AF.Square), reads=[GY], writes=[G2])
                S.op("dve", lambda e: e.tensor_scalar(out=x2, in0=x2, scalar1=0.044715, scalar2=1.0, op0=ALU.mult, op1=ALU.add),
                     reads=[G2], writes=[G2])
                S.op("dve", lambda e: e.tensor_tensor(out=x3, in0=x2, in1=ys, op=ALU.mult), reads=[G2, GY], writes=[G3])
                S.op("act", lambda e: e.activation(out=sg_, in_=x3, func=AF.Sigmoid, scale=1.5957691216057308), reads=[G3], writes=[G2])
                S.op("dve", lambda e, ct=ct, sl=sl: e.tensor_tensor(out=y1bS[:, ct, sl], in0=ys, in1=sg_, op=ALU.mult),
                     reads=[GY, G2], writes=[Y1B[ct][tb]])
        for ot in range(4):
            for tb in range(4):
                pi = 2 + (pc % 2)
                pc += 1
                sl = slice(tb * TT, (tb + 1) * TT)
                for ct in range(4):
                    S.op("pe", lambda e, pi=pi, ct=ct, ot=ot, sl=sl: e.matmul(
                        PS[pi][:], gluwS[:, ct, ot * 128:(ot + 1) * 128], y1bS[:, ct, sl], start=(ct == 0), stop=(ct == 3)),
                        reads=[PPB, Y1B[ct][tb]], writes=[PSB[pi]], signal=(ct == 3))
                sg = sgt[pc % 2]
                S.op("act", lambda e, pi=pi, sg=sg, ot=ot: e.activation(out=sg, in_=PS[pi][:], func=AF.Sigmoid, bias=chv[:, 4, ot:ot + 1]),
                     reads=[PSB[pi], M0CB], writes=[SGTB[pc % 2]])
                S.op("dve", lambda e, sg=sg, ot=ot, sl=sl: e.tensor_tensor(out=mcat[:, 4 + ot, sl], in0=y1bS[:, ot, sl], in1=sg, op=ALU.mult),
                     reads=[SGTB[pc % 2], Y1B[ot][tb]], writes=[MCB[4 + ot][tb]])
        dump("mcB", mcat[:, 4:8, :].rearrange("p a b -> p (a b)"), [b for r in MCB[4:8] for b in r])
        dump("y1", y1bS.rearrange("p a b -> p (a b)"), [b for r in Y1B for b in r])
        for dc in range(KC):
            for tb in range(4):
                pi = 4 + (pc % 2)
                pc += 1
                sl = slice(tb * TT, (tb + 1) * TT)
                for mc in range(8):
                    S.op("pe", lambda e, pi=pi, mc=mc, dc=dc, sl=sl: e.matmul(
                        PS[pi][:], woutS[:, mc, dc * 128:(dc + 1) * 128], mcat[:, mc, sl], start=(mc == 0), stop=(mc == 7)),
                        reads=[WOUTB, MCB[mc][tb]], writes=[PSB[pi]], signal=(mc == 7))
                xs = xT[:, dc, sl]
                S.op("dve", lambda e, pi=pi, xs=xs: e.tensor_tensor(out=xs, in0=PS[pi][:], in1=xs, op=ALU.add),
                     reads=[PSB[pi], XB[dc][tb]], writes=[XB[dc][tb]])
        S.barrier()

    wqkv_d = din("w_qkv", [D, 3 * D])
    wo_d = din("w_o", [D, D])
    cbias_d = din("cbias", [128, 2 * 256])
    en_d = din("en", [8, 8 * 128])
    negm_d = din("negm", [128, 4 * 64])
    NEG = -30000.0
    ahT = R(0, [128, KC, L], BF16)
    qring = [R(32 + 8 * i, [128, KC, 512], BF16) for i in range(2)]
    qT = R(48, [128, 4, L], BF16)
    kT = R(64, [128, 4, L], BF16)
    Vh = R(80, [128, 16, 512], BF16)
    oT = R(96, [128, 4, L], BF16)
    woS = R(112, [128, 4, D], BF16)
    pT = [R(120 + i, [128, 2, 256], BF16) for i in range(2)]
    cbiasS = R(122, [128, 2, 256], BF16)
    EnS = R(123, [8, 8, 128], BF16)
    identA = R(125, [128, 128], BF16)
    negmS = R(125.5, [128, 4, 64], F32)
    rden = R(126.5, [128, 256], F32)
    mbT = [R(127.5 + 2 * i, [8, 4, 256], BF16) for i in range(2)]
    kmf = R(131.5, [128, 4, 8], F32)
    kmT = R(131.75, [128, 4, 8], BF16)
    gsb = R(132, [128, 32], F32)
    top8 = R(132.25, [128, 8], F32)
    mball = R(132.5, [128, 8, 32], BF16)
    rden2 = R(133, [128, 256], F32)
    rdens = [rden, rden2]
    RDBS = [Buf("rden0"), Buf("rden1")]
    mbTall = R(32, [8, 4, 4, 256], BF16)
    assert RB + int(134 * KB) <= A.nbytes

    AHB = [[Buf("ah%d_%d" % (k, t)) for t in range(4)] for k in range(KC)]
    QRB = [Buf("qring%d" % i) for i in range(2)]
    QTB = [Buf("qT%d" % h) for h in range(4)]
    KTB = [Buf("kT%d" % h) for h in range(4)]
    VHB = [Buf("vh%d" % k) for k in range(16)]
    OTB = [[Buf("oT%d_%d" % (h, q)) for q in range(8)] for h in range(4)]
    WOB = Buf("woS")
    PTB = [Buf("pT%d" % i) for i in range(2)]
    ACB = Buf("attnconst")
    RDB = Buf("rden")
    MBTB = [Buf("mbT%d" % i) for i in range(2)]
    KMB = Buf("km")
    GSB = Buf("gsb")
    T8B = Buf("top8")
    MBB = Buf("mb")
    SCALE = 128.0 ** -0.5

    def mixer1(s):
        S.barrier()
        S.dma("pool", cbiasS.rearrange("p a b -> p (a b)"), cbias_d, writes=[ACB])
        S.dma("pool", EnS.rearrange("p a b -> p (a b)"), en_d, writes=[ACB])
        S.dma("pool", identA, ident_d, writes=[ACB])
        S.dma("sp", negmS.rearrange("p a b -> p (a b)"), negm_d, writes=[ACB])
        rmsnorm_tile(5, 0, 4, ahT, AHB)
        nring = [0]
        pcs = [0]

        def load_cols(c0):
            i = nring[0] % 2
            nring[0] += 1
            S.dma("pool", qring[i], wqkv_d.rearrange("(kc p) f -> p kc f", p=128)[:, :, c0:c0 + 512], writes=[QRB[i]])
            return qring[i], QRB[i]

        for half in range(2):
            for which, dstT, dstB in ((0, qT, QTB), (1, kT, KTB)):
                if which == 0 and half == 1:
                    wsl, wb_ = pre_q
                else:
                    wsl, wb_ = load_cols(which * D + half * 512)
                for hl in range(4):
                    for tb in range(4):
                        pi = pcs[0] % 2
                        pcs[0] += 1
                        for kc in range(KC):
                            S.op("pe", lambda e, pi=pi, wsl=wsl, kc=kc, hl=hl, tb=tb: e.matmul(
                                PS[pi][:], wsl[:, kc, hl * 128:(hl + 1) * 128], ahT[:, kc, tb * TT:(tb + 1) * TT],
                                start=(kc == 0), stop=(kc == KC - 1)),
                                reads=[wb_, AHB[kc][tb]], writes=[PSB[pi]], signal=(kc == KC - 1))
                        S.op("act", lambda e, pi=pi, dstT=dstT, hl=hl, tb=tb: e.activation(
                            out=dstT[:, hl, tb * TT:(tb + 1) * TT], in_=PS[pi][:], func=AF.Copy),
                            reads=[PSB[pi]], writes=[dstB[hl]])
            wsl, wb_ = load_cols(2 * D + half * 512)
            for kt in range(16):
                pi = pcs[0] % 2
                pcs[0] += 1
                tb = kt // 4
                for kc in range(KC):
                    S.op("pe", lambda e, pi=pi, wsl=wsl, kc=kc, kt=kt: e.matmul(
                        PS[pi][:], ahT[:, kc, kt * 128:(kt + 1) * 128], wsl[:, kc, :], start=(kc == 0), stop=(kc == KC - 1)),
                        reads=[wb_, AHB[kc][tb]], writes=[PSB[pi]], signal=(kc == KC - 1))
                S.op("act", lambda e, pi=pi, kt=kt: e.activation(out=Vh[:, kt, :], in_=PS[pi][:], func=AF.Copy),
                     reads=[PSB[pi]], writes=[VHB[kt]])
            S.dma("pool", woS, wo_d.rearrange("(kc p) f -> p kc f", p=128)[:, half * 4:(half + 1) * 4, :], writes=[WOB])
            if half == 0:
                pre_q = load_cols(0 * D + 1 * 512)
            for hl in range(4):
                S.op("dve", lambda e, hl=hl: e.tensor_reduce(out=kmf[:, hl, :], in_=kT[:, hl, :].rearrange("p (n k) -> p n k", n=8),
                                                            axis=AX.X, op=ALU.add), reads=[KTB[hl]], writes=[KMB])
            S.op("dve", lambda e: e.tensor_copy(out=kmT, in_=kmf), reads=[KMB], writes=[KMB])
            for qb in range(4, 8):
                for q2 in range(2):
                    idx = (qb - 4) * 2 + q2
                    qsl = slice(qb * 256 + q2 * 128, qb * 256 + (q2 + 1) * 128)
                    for hl in range(4):
                        S.op("pe", lambda e, hl=hl, qsl=qsl, idx=idx: e.matmul(PS[6][:, idx * 32 + hl * 8:idx * 32 + (hl + 1) * 8], qT[:, hl, qsl],
                                                                              kmT[:, hl, :], start=True, stop=True),
                             reads=[QTB[hl], KMB], writes=[PSB[6]], signal=(hl == 3))
            def mask_job(qb, q2):
                idx = (qb - 4) * 2 + q2
                S.op("dve", lambda e: e.tensor_tensor(out=gsb, in0=PS[6][:, idx * 32:(idx + 1) * 32], in1=negmS[:, qb - 4, 0:32], op=ALU.add),
                     reads=[PSB[6], ACB], writes=[GSB])
                for hl in range(4):
                    S.op("dve", lambda e, hl=hl: e.max(out=top8, in_=gsb[:, hl * 8:(hl + 1) * 8]), reads=[GSB], writes=[T8B])
                    S.op("dve", lambda e, hl=hl: e.tensor_scalar(out=mball[:, idx, hl * 8:(hl + 1) * 8], in0=gsb[:, hl * 8:(hl + 1) * 8],
                                                                scalar1=top8[:, 2:3], scalar2=NEG, op0=ALU.is_lt, op1=ALU.mult),
                         reads=[GSB, T8B], writes=[MBB])

            mask_jobs = [(qb, q2) for qb in range(4, 8) for q2 in range(2)]

            def mask_transposes():
                for qb in range(4, 8):
                    for q2 in range(2):
                        idx = (qb - 4) * 2 + q2
                        tbk = 6 + (idx % 2)
                        for hl in range(4):
                            S.op("pe", lambda e, hl=hl, idx=idx, tbk=tbk: e.matmul(PS[tbk][0:8, hl * 128:(hl + 1) * 128], mball[:, idx, hl * 8:(hl + 1) * 8], identA,
                                                                         start=True, stop=True),
                                 reads=[MBB, ACB], writes=[PSB[tbk]], signal=(hl == 3))
                        S.op("act", lambda e, qb=qb, q2=q2, tbk=tbk: e.activation(out=mbTall[:, qb - 4, :, q2 * 128:(q2 + 1) * 128],
                                                                        in_=PS[tbk][0:8, :].rearrange("p (h q) -> p h q", h=4), func=AF.Copy),
                             reads=[PSB[tbk]], writes=[MBTB[0]])

            items = []
            for qb in range(8):
                for hl in range(4):
                    npair = qb + 1
                    for kp in range(npair):
                        items.append((qb, hl, kp, kp == 0, kp == npair - 1))

            def emit_S(i):
                qb, hl, kp, _, _ = items[i]
                qs = slice(qb * 256, (qb + 1) * 256)
                gated = qb >= 4
                pi = 2 + (i % 2)
                for k2 in range(2):
                    kt = kp * 2 + k2
                    n = kp
                    osl = PS[pi][:, k2 * 256:(k2 + 1) * 256]
                    extra = (n == qb) or gated
                    S.op("pe", lambda e, osl=osl, hl=hl, kt=kt, extra=extra, qs=qs: e.matmul(
                        osl, kT[:, hl, kt * 128:(kt + 1) * 128], qT[:, hl, qs], start=True, stop=(not extra)),
                        reads=[KTB[hl], QTB[hl]], writes=[PSB[pi]], signal=((not extra) and k2 == 1))
                    if n == qb:
                        S.op("pe", lambda e, osl=osl, k2=k2: e.matmul(osl, identA, cbiasS[:, k2, :], start=False, stop=True),
                             reads=[ACB], writes=[PSB[pi]], signal=(k2 == 1))
                    elif gated:
                        S.op("pe", lambda e, osl=osl, n=n, hl=hl, qb=qb: e.matmul(osl, EnS[:, n, :], mbTall[:, qb - 4, hl, :], start=False, stop=True),
                             reads=[ACB, MBTB[0]], writes=[PSB[pi]], signal=(k2 == 1))

            def emit_rest(i, gi):
                qb, hl, kp, first, last = items[i]
                qs = slice(qb * 256, (qb + 1) * 256)
                pi = 2 + (i % 2)
                ti = i % 2
                po, pd = ((4, 5), (0, 1))[gi % 2]
                S.op("act", lambda e, pi=pi, ti=ti: e.activation(out=pT[ti].rearrange("p a b -> p (a b)"), in_=PS[pi][:],
                                                                func=AF.Exp, scale=SCALE),
                     reads=[PSB[pi]], writes=[PTB[ti]])
                for k2 in range(2):
                    kt = kp * 2 + k2
                    f_ = first and k2 == 0
                    l_ = last and k2 == 1
                    S.op("pe", lambda e, po=po, kt=kt, hl=hl, ti=ti, k2=k2, f_=f_, l_=l_: e.matmul(
                        PS[po][:, 0:256], Vh[:, kt, hl * 128:(hl + 1) * 128], pT[ti][:, k2, :], start=f_, stop=l_),
                        reads=[VHB[kt], PTB[ti]], writes=[PSB[po]], signal=False)
                    S.op("pe", lambda e, pd=pd, ti=ti, k2=k2, f_=f_, l_=l_: e.matmul(
                        PS[pd][:, 0:256], ones_bf, pT[ti][:, k2, :], start=f_, stop=l_),
                        reads=[CONSTB, PTB[ti]], writes=[PSB[pd]], signal=(k2 == 1))
                if last:
                    rd = rdens[gi % 2]
                    S.op("dve", lambda e, pd=pd, rd=rd: e.reciprocal(out=rd, in_=PS[pd][:, 0:256]), reads=[PSB[pd]], writes=[RDBS[gi % 2]])
                    S.op("dve", lambda e, po=po, hl=hl, rd=rd, qs=qs: e.tensor_tensor(out=oT[:, hl, qs], in0=PS[po][:, 0:256], in1=rd, op=ALU.mult),
                         reads=[PSB[po], RDBS[gi % 2]], writes=[OTB[hl][qb]])

            n_items = len(items)
            first_gated = next(i for i, it in enumerate(items) if it[0] >= 4)
            gi = 0
            emit_S(0)
            for i in range(n_items):
                if i + 1 < n_items:
                    if i + 1 == first_gated:
                        while mask_jobs:
                            mask_job(*mask_jobs.pop(0))
                        mask_transposes()
                    emit_S(i + 1)
                emit_rest(i, gi)
                if items[i][4]:
                    gi += 1
                    if mask_jobs:
                        mask_job(*mask_jobs.pop(0))
            if half == 0:
                dump("oT0", oT.rearrange("p a b -> p (a b)"), [b for r_ in OTB for b in r_])
                dump("qT0", qT.rearrange("p a b -> p (a b)"), QTB)
                dump("kT0", kT.rearrange("p a b -> p (a b)"), KTB)
                dump("Vh0", Vh.rearrange("p a b -> p (a b)"), VHB)
            for dc in range(KC):
                for tb in range(4):
                    pi = pcs[0] % 2
                    pcs[0] += 1
                    sl = slice(tb * TT, (tb + 1) * TT)
                    for hl in range(4):
                        S.op("pe", lambda e, pi=pi, hl=hl, dc=dc, sl=sl: e.matmul(
                            PS[pi][:], woS[:, hl, dc * 128:(dc + 1) * 128], oT[:, hl, sl], start=(hl == 0), stop=(hl == 3)),
                            reads=[WOB, OTB[hl][2 * tb], OTB[hl][2 * tb + 1]], writes=[PSB[pi]], signal=(hl == 3))
                    xs = xT[:, dc, sl]
                    S.op("dve", lambda e, pi=pi, xs=xs: e.tensor_tensor(out=xs, in0=PS[pi][:], in1=xs, op=ALU.add),
                         reads=[PSB[pi], XB[dc][tb]], writes=[XB[dc][tb]])
            S.barrier()

    OUTB = [Buf("o%d" % i) for i in range(2)]
    for s in range(nseq):
        def load_blocks(sq, blks):
            for blk in blks:
                for kc in range(KC):
                    S.dma("sp", xT[:, kc, blk * TT:(blk + 1) * TT], xT_d[sq, kc * 128:(kc + 1) * 128, blk * TT:(blk + 1) * TT],
                          writes=[XB[kc][blk]])

        if s == 0 or "final" not in stages:
            load_blocks(s, range(L // TT))
        if s == 0 and "mix0" in stages:
            ssm_param_prep()
        if "ffn00" in stages:
            ffn_chain([(0, 0, 0), (0, 0, 1024)])
        if "mix0" in stages:
            mixer0(s)
        if "ffn01" in stages and "ffn10" in stages:
            ffn_chain([(1, 1, 0), (1, 1, 1024), (2, 2, 0), (2, 2, 1024)])
        else:
            if "ffn01" in stages:
                ffn_chain([(1, 1, 0), (1, 1, 1024)])
            if "ffn10" in stages:
                ffn_chain([(2, 2, 0), (2, 2, 1024)])
        if "mix1" in stages:
            mixer1(s)
        def final_tile(t0, s=s):
            rmsnorm_tile(6, t0, 2, hT, HB, inplace=True)
            for blk in (t0 // TT, t0 // TT + 1):
                for kc in range(KC):
                    S.dma("sp", outT_d[s, kc * 128:(kc + 1) * 128, blk * TT:(blk + 1) * TT], xT[:, kc, blk * TT:(blk + 1) * TT],
                          reads=[XB[kc][blk]])
            if s + 1 < nseq:
                load_blocks(s + 1, (t0 // TT, t0 // TT + 1))

        fused_tail = ("ffn11" in stages) and ("final" in stages)
        if "ffn11" in stages:
            ffn_chain([(3, 3, 0), (3, 3, 1024)], last_hook=(lambda: final_tile(0)) if fused_tail else None)
        if "final" in stages:
            for t0 in ((1024,) if fused_tail else (0, 1024)):
                final_tile(t0)
        if "final" not in stages:
            for blk in range(L // TT):
                for kc in range(KC):
                    S.dma("sp", outT_d[s, kc * 128:(kc + 1) * 128, blk * TT:(blk + 1) * TT], xT[:, kc, blk * TT:(blk + 1) * TT],
                          reads=[XB[kc][blk]])
    S.wait_all("sp", [b for row in XB for b in row])
    nc._sched_ninst = S.ninst
    nc._dbg_names = dbg_names
    return nc


def _prep_inputs(inputs):
    f = np.float32
    x = np.asarray(inputs["x"], f)
    ffn_norm = np.asarray(inputs["ffn_norm"], f)
    vecs = [ffn_norm[0, 0], ffn_norm[0, 1], ffn_norm[1, 0], ffn_norm[1, 1],
            np.asarray(inputs["mix_norm"], f)[0], np.asarray(inputs["mix_norm"], f)[1],
            np.asarray(inputs["final_norm"], f)]
    gains = np.stack([v.reshape(KC, 128).T for v in vecs], axis=1)
    gains = np.ascontiguousarray(gains.reshape(128, 7 * KC))
    shared = {
        "gains": gains,
        "w1": np.ascontiguousarray(np.asarray(inputs["ffn_w1"], f).reshape(4, D, FF)),
        "w3": np.ascontiguousarray(np.asarray(inputs["ffn_w3"], f).reshape(4, D, FF)),
        "w2": np.ascontiguousarray(np.asarray(inputs["ffn_w2"], f).reshape(4, FF, D)),
    }
    shared["w_in"] = np.ascontiguousarray(np.asarray(inputs["ab_w_in"], f)[0])
    shared["w_out"] = np.ascontiguousarray(np.asarray(inputs["ab_w_out"], f)[0])
    shared["glu_w"] = np.ascontiguousarray(np.asarray(inputs["ssm_glu_w"], f)[0])
    cwm = np.asarray(inputs["conv_w"], f)[0]
    shared["cw"] = np.ascontiguousarray(cwm.T.reshape(4, 128, 31).transpose(1, 0, 2).reshape(128, 4 * 31))
    chv = [np.asarray(inputs[k], f)[0] for k in ("conv_b", "conv_ln_g", "conv_ln_b", "ssm_d", "ssm_glu_b")]
    shared["chv"] = np.ascontiguousarray(np.stack([v.reshape(4, 128).T for v in chv], axis=1).reshape(128, 20))
    ldt = np.broadcast_to(np.asarray(inputs["ssm_log_dt"], f)[0][None, :], (64, 32))
    are = np.asarray(inputs["ssm_a_re"], f)[0].T
    aim = np.asarray(inputs["ssm_a_im"], f)[0].T
    shared["sp1"] = np.ascontiguousarray(np.stack([ldt, are, aim], axis=1).reshape(64, 96))
    bre = np.asarray(inputs["ssm_b_re"], f)[0].transpose(1, 0, 2).reshape(64, 512)
    bim = np.asarray(inputs["ssm_b_im"], f)[0].transpose(1, 0, 2).reshape(64, 512)
    cre = np.asarray(inputs["ssm_c_re"], f)[0].transpose(2, 0, 1).reshape(64, 512)
    cim = np.asarray(inputs["ssm_c_im"], f)[0].transpose(2, 0, 1).reshape(64, 512)
    shared["sp2"] = np.ascontiguousarray(np.stack([bre, bim, cre, cim], axis=1).reshape(64, 2048))
    shared["w_qkv"] = np.ascontiguousarray(np.asarray(inputs["attn_w_qkv"], f)[0])
    shared["w_o"] = np.ascontiguousarray(np.asarray(inputs["attn_w_o"], f)[0])
    shared.update(_const_tables())
    in_maps = []
    for c in range(NCORES):
        m = dict(shared)
        m["xT"] = np.ascontiguousarray(x[2 * c:2 * c + 2].transpose(0, 2, 1))
        in_maps.append(m)
    return in_maps


def _const_tables():
    f = np.float32
    ident = np.eye(128, dtype=f)
    p = np.arange(128)
    bmask = (p[None, :] // 16 >= p[:, None] // 16).astype(f)
    sel = np.zeros((128, 8, 240), f)
    for g8 in range(8):
        for h in range(16):
            sel[16 * g8 + h, g8, 112 + h] = 1.0
    cb = np.zeros((128, 2, 256), f)
    for par in range(2):
        cb[:, par, :] = np.where((par * 128 + p[:, None]) <= np.arange(256)[None, :], 0.0, -30000.0)
    en = np.zeros((8, 8, 128), f)
    for n in range(8):
        en[n, n, :] = 1.0
    negm = np.zeros((128, 4, 64), f)
    for qb in range(4, 8):
        for h in range(8):
            negm[:, qb - 4, h * 8 + qb:h * 8 + 8] = -1e30
    return {"cbias": np.ascontiguousarray(cb.reshape(128, 512)), "en": np.ascontiguousarray(en.reshape(8, 1024)),
            "negm": np.ascontiguousarray(negm.reshape(128, 256)),
            "ident": ident, "bmask": bmask, "selin": np.ascontiguousarray(sel.reshape(128, 1920)),
            "selout": np.ascontiguousarray(sel.reshape(128, 1920))}


_NC_CACHE = {}


def kernel(**inputs):
    in_maps = _prep_inputs(inputs)
    if "nc" not in _NC_CACHE:
        _NC_CACHE["nc"] = build_program()
    nc = _NC_CACHE["nc"]
    res = run_bass_kernel_spmd(nc, in_maps, core_ids=list(range(NCORES)))
    out = np.empty((2 * NCORES, L, D), np.float32)
    for c in range(NCORES):
        out[2 * c:2 * c + 2] = np.asarray(res.results[c]["outT"]).transpose(0, 2, 1)
    return out
```

```python
import numpy as np
import concourse.bass as bass
import concourse.mybir as mybir
from concourse.bass_utils import run_bass_kernel_spmd

F32 = mybir.dt.float32
BF16 = mybir.dt.bfloat16
AF = mybir.ActivationFunctionType
ALU = mybir.AluOpType
AX = mybir.AxisListType

D = 1024
KC = 8
FF = 2816
FC = 22
L = 2048
NSEQ = 2
TT = 512
RMS_EPS = 1e-6
LN_EPS = 1e-5
NCORES = 8


class Buf:
    __slots__ = ("name", "w", "r", "dsem", "dcnt")

    def __init__(self, name):
        self.name = name
        self.w = None
        self.r = {}
        self.dsem = {}
        self.dcnt = {}


class Sched:
    def __init__(self, nc):
        self.nc = nc
        self.eng = {"pe": nc.tensor, "act": nc.scalar, "dve": nc.vector, "pool": nc.gpsimd, "sp": nc.sync}
        self.sem = {k: nc.alloc_semaphore("s_" + k) for k in self.eng}
        self.cnt = {k: 0 for k in self.eng}
        self.seen = {k: {} for k in self.eng}
        self.pr = {k: [] for k in self.eng}
        self.pw = {k: [] for k in self.eng}
        self.nds = 0
        self.ninst = 0
        self.dbufs = []

    def _deps(self, reads, writes):
        deps = {}

        def need(s, v):
            if deps.get(s, 0) < v:
                deps[s] = v

        for b in reads:
            if b.w is not None:
                need(*b.w)
        for b in writes:
            if b.w is not None:
                need(*b.w)
            for s, v in b.r.items():
                need(s, v)
        return deps

    def _wait(self, eng, deps):
        e = self.eng[eng]
        own = self.sem[eng]
        for s, v in deps.items():
            if s is own and eng == "pe":
                continue
            if self.seen[eng].get(s, 0) >= v:
                continue
            e.wait_ge(s, v)
            self.seen[eng][s] = v

    def op(self, eng, fn, reads=(), writes=(), signal=True):
        self._wait(eng, self._deps(reads, writes))
        ins = fn(self.eng[eng])
        self.ninst += 1
        self.pr[eng].extend(reads)
        self.pw[eng].extend(writes)
        if signal:
            own = self.sem[eng]
            self.cnt[eng] += 1
            ins.then_inc(own, 1)
            v = self.cnt[eng]
            for b in self.pr[eng]:
                b.r[own] = v
            for b in self.pw[eng]:
                b.w = (own, v)
                b.r = {}
            self.pr[eng] = []
            self.pw[eng] = []
        return ins

    def dma(self, q, out, in_, reads=(), writes=()):
        self._wait(q, self._deps(reads, writes))
        owner = writes[0] if writes else reads[0]
        kind = "sw" if q == "pool" else "hw"
        if kind not in owner.dsem:
            owner.dsem[kind] = self.nc.alloc_semaphore("d%d" % self.nds)
            owner.dcnt[kind] = 0
            self.nds += 1
            self.dbufs.append((owner, kind))
        owner.dcnt[kind] += 1
        sem = owner.dsem[kind]
        self.eng[q].dma_start(out=out, in_=in_).then_inc(sem, 16)
        self.ninst += 1
        tag = (sem, 16 * owner.dcnt[kind])
        for b in reads:
            b.r[tag[0]] = tag[1]
        for b in writes:
            b.w = tag
            b.r = {}

    def barrier(self):
        for k in self.eng:
            assert not self.pr[k] and not self.pw[k], "unsignalled ops pending on " + k
        for k in self.eng:
            deps = {self.sem[o]: self.cnt[o] for o in self.eng if o != k and self.cnt[o] > 0}
            for b, kind in self.dbufs:
                deps[b.dsem[kind]] = 16 * b.dcnt[kind]
            self._wait(k, deps)

    def wait_all(self, eng, bufs):
        deps = {}
        for b in bufs:
            if b.w is not None:
                deps[b.w[0]] = max(deps.get(b.w[0], 0), b.w[1])
            for s, v in b.r.items():
                deps[s] = max(deps.get(s, 0), v)
        self._wait(eng, deps)


class Arena:
    def __init__(self, nc, nbytes):
        self.t = nc.alloc_sbuf_tensor("arena", [128, nbytes // 2], BF16)
        self.nbytes = nbytes
        self.off = 0
        self.marks = []

    def alloc(self, shape, dtype, at=None):
        esz = 2 if dtype == BF16 else 4
        n = int(np.prod(shape[1:]))
        nb = n * esz
        off = self.off if at is None else at
        off = (off + 31) // 32 * 32
        assert off + nb <= self.nbytes, ("arena overflow", off, nb, self.nbytes)
        if at is None:
            self.off = off + nb
        ap = self.t[0:shape[0], off // 2:(off + nb) // 2]
        if dtype != BF16:
            ap = ap.bitcast(dtype)
        if len(shape) == 3:
            ap = ap.rearrange("p (a b) -> p a b", a=shape[1])
        elif len(shape) == 4:
            ap = ap.rearrange("p (a b c) -> p a b c", a=shape[1], b=shape[2])
        return ap


def build_program(stages=("ffn00", "mix0", "ffn01", "ffn10", "mix1", "ffn11", "final"), nseq=NSEQ, debug=False):
    nc = bass.Bass("TRN2", target_bir_lowering=False)
    S = Sched(nc)
    dbg_names = []

    def dump(name, ap2d, bufs):
        if not debug or name in dbg_names:
            return
        dbg_names.append(name)
        shp = list(ap2d.shape)
        dt = nc.dram_tensor("dbg_" + name, shp, F32, kind="ExternalOutput").ap()
        S.dma("pool", dt, ap2d, reads=bufs)

    def din(name, shape, dt=F32):
        return nc.dram_tensor(name, list(shape), dt, kind="ExternalInput").ap()

    xT_d = din("xT", [NSEQ, D, L])
    gains_d = din("gains", [128, 7 * KC])
    w1_d = din("w1", [4, D, FF])
    w3_d = din("w3", [4, D, FF])
    w2_d = din("w2", [4, FF, D])
    outT_d = nc.dram_tensor("outT", [NSEQ, D, L], F32, kind="ExternalOutput").ap()

    A = Arena(nc, 207 * 1024)
    xT = A.alloc([128, KC, L], F32)
    gains = A.alloc([128, 7, KC], F32)
    ones_bf = A.alloc([128, 128], BF16)
    epsc = A.alloc([128, 1], F32)
    cw = A.alloc([128, 4, 31], F32)
    chv = A.alloc([128, 5, 4], F32)
    lnepsc = A.alloc([128, 1], F32)
    xsq = [A.alloc([128, TT], BF16) for _ in range(2)]
    rstd = A.alloc([128, TT], F32)
    sil = [A.alloc([128, TT], BF16) for _ in range(2)]
    base_off = A.off
    hT = A.alloc([128, KC, 1024], BF16)
    G = A.alloc([128, FC, 1024], BF16)
    NS13 = 3
    w13 = [A.alloc([128, 2, KC, 256], BF16) for _ in range(NS13)]
    w2s = A.alloc([128, FC, D], BF16)
    ffn_end = A.off

    PS = [nc.alloc_psum_tensor("ps%d" % i, [128, TT], F32) for i in range(8)]
    PSB = [Buf("ps%d" % i) for i in range(8)]

    XB = [[Buf("x%d_%d" % (k, b)) for b in range(L // TT)] for k in range(KC)]
    HB = [[Buf("h%d_%d" % (k, t)) for t in range(2)] for k in range(KC)]
    GB = [[Buf("g%d_%d" % (f, t)) for t in range(2)] for f in range(FC)]
    W13B = [(Buf("w1_%d" % i), Buf("w3_%d" % i)) for i in range(NS13)]
    W2B = [Buf("w2s%d" % i) for i in range(FC // 2)]
    XSQB = [Buf("xsq%d" % i) for i in range(2)]
    RSTDB = Buf("rstd")
    SILB = [Buf("sil%d" % i) for i in range(2)]
    CONSTB = Buf("const")

    S.dma("sp", gains.rearrange("p a b -> p (a b)"), gains_d, writes=[CONSTB])
    S.op("dve", lambda e: e.memset(ones_bf, 1.0), writes=[CONSTB])
    S.op("dve", lambda e: e.memset(epsc, RMS_EPS), writes=[CONSTB])

    w13_state = {"n": 0}

    def load_w13(fi, fg):
        i = w13_state["n"] % NS13
        w13_state["n"] += 1
        slot = w13[i]
        b = W13B[i]
        src1 = w1_d[fi].rearrange("(kc p) f -> p kc f", p=128)[:, :, fg * 256:(fg + 1) * 256]
        src3 = w3_d[fi].rearrange("(kc p) f -> p kc f", p=128)[:, :, fg * 256:(fg + 1) * 256]
        S.dma("pool", slot[:, 0], src1, writes=[b[0]])
        S.dma("pool", slot[:, 1], src3, writes=[b[1]])
        return slot, b

    def load_w2(fi, j):
        src = w2_d[fi].rearrange("(fc p) d -> p fc d", p=128)
        S.dma("pool", w2s[:, 2 * j:2 * j + 2], src[:, 2 * j:2 * j + 2], writes=[W2B[j]])

    def rmsnorm_tile(gidx, t0, ntt, dstT, dstB, inplace=False):
        for tt in range(ntt):
            blk = (t0 + tt * TT) // TT
            ps = PS[6]
            psb = PSB[6]
            for kc in range(KC):
                q = dstT[:, kc, tt * TT:(tt + 1) * TT]
                xs = xT[:, kc, blk * TT:(blk + 1) * TT]
                S.op("act", lambda e, q=q, xs=xs: e.activation(out=q, in_=xs, func=AF.Square),
                     reads=[XB[kc][blk], CONSTB], writes=[dstB[kc][tt]])
            for kc in range(KC):
                q = dstT[:, kc, tt * TT:(tt + 1) * TT]
                S.op("pe", lambda e, q=q, kc=kc: e.matmul(ps[:], ones_bf, q, start=(kc == 0), stop=(kc == KC - 1)),
                     reads=[dstB[kc][tt], CONSTB], writes=[psb], signal=(kc == KC - 1))
            S.op("act", lambda e: e.activation(out=rstd, in_=ps[:], func=AF.Sqrt, scale=1.0 / D, bias=epsc),
                 reads=[psb, CONSTB], writes=[RSTDB])
            S.op("dve", lambda e: e.reciprocal(out=rstd, in_=rstd), reads=[RSTDB], writes=[RSTDB])
            for kc in range(KC):
                xs = xT[:, kc, blk * TT:(blk + 1) * TT]
                if inplace:
                    S.op("dve", lambda e, kc=kc, xs=xs: e.scalar_tensor_tensor(
                        out=xs, in0=xs, scalar=gains[:, gidx, kc:kc + 1], in1=rstd, op0=ALU.mult, op1=ALU.mult),
                        reads=[XB[kc][blk], RSTDB, CONSTB, dstB[kc][tt]], writes=[XB[kc][blk]])
                else:
                    S.op("dve", lambda e, kc=kc, xs=xs: e.scalar_tensor_tensor(
                        out=dstT[:, kc, tt * TT:(tt + 1) * TT], in0=xs,
                        scalar=gains[:, gidx, kc:kc + 1], in1=rstd, op0=ALU.mult, op1=ALU.mult),
                        reads=[XB[kc][blk], RSTDB, CONSTB], writes=[dstB[kc][tt]])

    def ffn_tile(fi, gidx, t0, do_pre=True, hook=None):
        if do_pre:
            rmsnorm_tile(gidx, t0, 2, hT, HB)
        pcount = 0
        slots = {}
        for fg in range(min(NS13, FC // 2)):
            slots[fg] = load_w13(fi, fg)
        for fg in range(FC // 2):
            slot, sb = slots.pop(fg)
            for fh in range(2):
                fc = fg * 2 + fh
                for tt in range(2):
                    pa = (pcount % 2) * 2
                    pcount += 1
                    for j, (pi, wi) in enumerate(((pa, 0), (pa + 1, 1))):
                        for kc in range(KC):
                            S.op("pe", lambda e, pi=pi, wi=wi, kc=kc: e.matmul(
                                PS[pi][:], slot[:, wi, kc, fh * 128:(fh + 1) * 128], hT[:, kc, tt * TT:(tt + 1) * TT],
                                start=(kc == 0), stop=(kc == KC - 1)),
                                reads=[sb[wi], HB[kc][tt]], writes=[PSB[pi]], signal=(kc == KC - 1))
                    sl = sil[pcount % 2]
                    slb = SILB[pcount % 2]
                    S.op("act", lambda e, sl=sl, pa=pa: e.activation(out=sl, in_=PS[pa][:], func=AF.Silu),
                         reads=[PSB[pa]], writes=[slb])
                    S.op("dve", lambda e, sl=sl, pa=pa, fc=fc, tt=tt: e.tensor_tensor(
                        out=G[:, fc, tt * TT:(tt + 1) * TT], in0=sl, in1=PS[pa + 1][:], op=ALU.mult),
                        reads=[slb, PSB[pa + 1]], writes=[GB[fc][tt]])
            if fg + NS13 < FC // 2:
                slots[fg + NS13] = load_w13(fi, fg + NS13)
            load_w2(fi, fg)
        pcount = 0
        for dc in range(KC):
            for tt in range(2):
                pi = 4 + (pcount % 2)
                pcount += 1
                blk = (t0 + tt * TT) // TT
                for fc in range(FC):
                    S.op("pe", lambda e, pi=pi, fc=fc, dc=dc, tt=tt: e.matmul(
                        PS[pi][:], w2s[:, fc, dc * 128:(dc + 1) * 128], G[:, fc, tt * TT:(tt + 1) * TT],
                        start=(fc == 0), stop=(fc == FC - 1)),
                        reads=[W2B[fc // 2], GB[fc][tt]], writes=[PSB[pi]], signal=(fc == FC - 1))
                xs = xT[:, dc, blk * TT:(blk + 1) * TT]
                S.op("dve", lambda e, pi=pi, xs=xs: e.scalar_tensor_tensor(
                    out=xs, in0=PS[pi][:], scalar=0.5, in1=xs, op0=ALU.mult, op1=ALU.add),
                    reads=[PSB[pi], XB[dc][blk]], writes=[XB[dc][blk]])
            if dc == 1 and hook is not None:
                hook()

    def ffn_chain(jobs, last_hook=None):
        for j, (fi, gidx, t0) in enumerate(jobs):
            nxt = jobs[j + 1] if j + 1 < len(jobs) else None
            hook = last_hook
            if nxt is not None:
                assert nxt[2] != t0
                hook = (lambda nxt=nxt: rmsnorm_tile(nxt[1], nxt[2], 2, hT, HB))
            ffn_tile(fi, gidx, t0, do_pre=(j == 0), hook=hook)

    RB = base_off
    KB = 1024

    def R(off_kb, shape, dt):
        return A.alloc(shape, dt, at=RB + int(off_kb * KB))

    I32 = mybir.dt.int32
    win_d = din("w_in", [D, 1536])
    wout_d = din("w_out", [D, D])
    gluw_d = din("glu_w", [512, 512])
    cw_d = din("cw", [128, 4 * 31])
    chv_d = din("chv", [128, 5 * 4])
    sp1_d = din("sp1", [64, 3 * 32])
    sp2_d = din("sp2", [64, 4 * 512])
    ident_d = din("ident", [128, 128])
    bmask_d = din("bmask", [128, 128])
    selin_d = din("selin", [128, 8 * 240])
    selout_d = din("selout", [128, 8 * 240])
    scr_toep = nc.dram_tensor("scr_toep", [128, 32 * 128], BF16, kind="Internal").ap()
    scr_wb = nc.dram_tensor("scr_wb", [128, 2 * 32 * 64], BF16, kind="Internal").ap()
    scr_cm = nc.dram_tensor("scr_cm", [64, 2 * 32 * 128], BF16, kind="Internal").ap()
    scr_at = nc.dram_tensor("scr_at", [64, 128], F32, kind="Internal").ap()

    M0CB = Buf("m0const")
    S.dma("sp", cw.rearrange("p a b -> p (a b)"), cw_d, writes=[M0CB])
    S.dma("sp", chv.rearrange("p a b -> p (a b)"), chv_d, writes=[M0CB])
    S.op("dve", lambda e: e.memset(lnepsc, LN_EPS), writes=[M0CB])

    uT = R(0, [128, 4, L], BF16)
    uT4 = R(0, [128, 4, 8, 256], BF16)
    mcat = R(16, [128, 8, L], BF16)
    Xbf = R(32, [128, 2, 16, 256], BF16)
    woutS = R(116, [128, 8, D], BF16)
    hT2 = R(64, [128, KC, L], BF16)
    winS = R(96, [128, KC, 1536], BF16)
    vpad = R(47.5, [128, 4, 30 + L], BF16)
    ycv = R(64, [128, 4, L], F32)
    lnt = [R(96 + 2 * i, [128, TT], F32) for i in range(5)]
    selin = R(130, [128, 8, 240], BF16)
    selout = R(68, [128, 8, 240], BF16)
    gluwS = R(72, [128, 4, 512], BF16)
    toepS = R(76, [128, 32, 128], BF16)
    wbS = R(84, [128, 2, 32, 64], BF16)
    cmS = R(92, [128, 2, 16, 128], BF16)
    atS = R(108, [128, 2, 2, 16], F32)
    Xh = R(109, [128, 32, 2, 16], F32)
    pq = R(113, [128, 2, 2, 16], F32)
    rtmp = R(114, [128, 2, 16], F32)
    Ut = R(48, [128, 32, 256], BF16)
    gt = [R(76 + 2 * i, [128, TT], F32) for i in range(6)]
    y1bS = R(92, [128, 4, L], BF16)
    sgt = [R(88 + i, [128, TT], BF16) for i in range(2)]

    diag = [R(o_, [128, 31, 128], BF16) for o_ in (32, 39.75, 114, 121.75)]
    identM = R(135.25, [128, 128], BF16)
    DIAGB = [Buf("diag%d" % i) for i in range(4)]
    IDMB = Buf("identM")
    UB = [[Buf("u%d_%d" % (c, t)) for t in range(4)] for c in range(4)]
    MCB = [[Buf("mc%d_%d" % (c, t)) for t in range(4)] for c in range(8)]
    H2B = [[Buf("h2_%d_%d" % (k, t)) for t in range(4)] for k in range(KC)]
    WINB = [Buf("win%d" % i) for i in range(3)]
    WOUTB = Buf("woutS")
    VB = [Buf("v%d" % c) for c in range(4)]
    YCB = [Buf("yc%d" % c) for c in range(4)]
    LNTB = [Buf("lnt%d" % i) for i in range(5)]
    PPB = Buf("ssmparams")
    SELIB = Buf("selin"); WBB = Buf("wbS"); TOEPB = Buf("toepS")
    CMB = [Buf("cm%d" % i) for i in range(4)]
    ATB = [Buf("at%d" % i) for i in range(8)]
    UTB = [Buf("ut%d" % g) for g in range(32)]
    XBFB = Buf("xbf")
    XHB = [Buf("xh%d" % i) for i in range(32)]
    PQB = Buf("pq")
    RTB = Buf("rtmp")
    GTB = [Buf("gt%d" % i) for i in range(6)]
    Y1B = [[Buf("y1_%d_%d" % (c, t)) for t in range(4)] for c in range(4)]
    SGTB = [Buf("sgt%d" % i) for i in range(2)]

    def ssm_param_prep():
        S.barrier()
        P = PPB
        cnt = {"o": 0}

        def T(shape, dt=F32):
            n = int(np.prod(shape[1:])) * (4 if dt != BF16 else 2)
            off = cnt["o"]
            cnt["o"] = (off + n + 31) // 32 * 32
            assert RB + cnt["o"] <= A.nbytes, cnt["o"]
            return A.alloc(shape, dt, at=RB + off)

        def v_tt(out, a, b, op):
            S.op("dve", lambda e: e.tensor_tensor(out=out, in0=a, in1=b, op=op), reads=[P], writes=[P])

        def v_ts(out, a, s1, s2, op0, op1=None):
            if op1 is None:
                S.op("dve", lambda e: e.tensor_scalar(out=out, in0=a, scalar1=s1, scalar2=None, op0=op0), reads=[P], writes=[P])
            else:
                S.op("dve", lambda e: e.tensor_scalar(out=out, in0=a, scalar1=s1, scalar2=s2, op0=op0, op1=op1), reads=[P], writes=[P])

        def a_act(out, a, func, scale=1.0):
            S.op("act", lambda e: e.activation(out=out, in_=a, func=func, scale=scale), reads=[P], writes=[P])

        def v_cp(out, a):
            S.op("dve", lambda e: e.tensor_copy(out=out, in_=a), reads=[P], writes=[P])

        sp1 = T([64, 3, 32]); sp2 = T([64, 4, 512])
        S.dma("act", sp1.rearrange("p a b -> p (a b)"), sp1_d, writes=[P])
        S.dma("act", sp2.rearrange("p a b -> p (a b)"), sp2_d, writes=[P])
        identS = T([128, 128], BF16); bmaskS = T([128, 128])
        S.dma("pool", identS, ident_d, writes=[P])
        S.dma("act", bmaskS, bmask_d, writes=[P])
        ldt, are, aim = sp1[:, 0], sp1[:, 1], sp1[:, 2]
        bre = sp2[:, 0].rearrange("p (g h) -> p g h", g=32)
        bim = sp2[:, 1].rearrange("p (g h) -> p g h", g=32)
        cre = sp2[:, 2].rearrange("p (g h) -> p g h", g=32)
        cim = sp2[:, 3].rearrange("p (g h) -> p g h", g=32)
        dt_ = T([64, 32]); mag = T([64, 32]); ang = T([64, 32]); t1 = T([64, 32]); t2 = T([64, 32])
        ti = T([64, 32], I32); cosv = T([64, 32]); sinv = T([64, 32])
        a_act(dt_, ldt, AF.Exp)
        v_tt(t1, dt_, are, ALU.mult)
        a_act(mag, t1, AF.Exp)
        v_tt(ang, dt_, aim, ALU.mult)
        TWO_PI = 2.0 * np.pi

        def sin_of(out, shift):
            v_ts(t1, ang, 1.0 / TWO_PI, float(shift), ALU.mult, ALU.add)
            v_cp(ti, t1)
            v_cp(t2, ti)
            v_tt(t1, t1, t2, ALU.subtract)
            v_ts(t2, t1, 0.5, None, ALU.is_gt)
            v_tt(t1, t1, t2, ALU.subtract)
            v_ts(t2, t1, -0.5, None, ALU.is_lt)
            v_tt(t1, t1, t2, ALU.add)
            a_act(out, t1, AF.Sin, scale=TWO_PI)

        dump("dt", dt_, [P]); dump("mag", mag, [P]); dump("ang", ang, [P])
        sin_of(sinv, 0.0)
        dump("frac_s", t1, [P]); dump("sinv", sinv, [P])
        sin_of(cosv, 0.25)
        dump("cosv", cosv, [P])
        pwr = T([64, 9, 32]); pwi = T([64, 9, 32])
        S.op("dve", lambda e: e.memset(pwr[:, 0], 1.0), reads=[P], writes=[P])
        S.op("dve", lambda e: e.memset(pwi[:, 0], 0.0), reads=[P], writes=[P])
        v_tt(pwr[:, 1], mag, cosv, ALU.mult)
        v_tt(pwi[:, 1], mag, sinv, ALU.mult)
        for k in range(1, 8):
            v_tt(t1, pwr[:, k], pwr[:, 1], ALU.mult)
            v_tt(t2, pwi[:, k], pwi[:, 1], ALU.mult)
            v_tt(pwr[:, k + 1], t1, t2, ALU.subtract)
            v_tt(t1, pwr[:, k], pwi[:, 1], ALU.mult)
            v_tt(t2, pwi[:, k], pwr[:, 1], ALU.mult)
            v_tt(pwi[:, k + 1], t1, t2, ALU.add)
        dump("pwr", pwr.rearrange("p k g -> p (k g)"), [P]); dump("pwi", pwi.rearrange("p k g -> p (k g)"), [P])
        ipr = T([64, 9, 32]); ipi = T([64, 9, 32]); n2 = T([64, 9, 32]); n3 = T([64, 9, 32])
        v_tt(n2, pwr, pwr, ALU.mult)
        v_tt(n3, pwi, pwi, ALU.mult)
        v_tt(n2, n2, n3, ALU.add)
        S.op("dve", lambda e: e.reciprocal(out=n2, in_=n2), reads=[P], writes=[P])
        v_tt(ipr, pwr, n2, ALU.mult)
        v_tt(ipi, pwi, n2, ALU.mult)
        v_ts(ipi, ipi, -1.0, None, ALU.mult)
        den = T([64, 32]); nr = T([64, 32]); qre = T([64, 32]); qim = T([64, 32])
        v_tt(den, are, are, ALU.mult)
        v_tt(t1, aim, aim, ALU.mult)
        v_tt(den, den, t1, ALU.add)
        S.op("dve", lambda e: e.reciprocal(out=den, in_=den), reads=[P], writes=[P])
        v_ts(nr, pwr[:, 1], -1.0, None, ALU.add)
        v_tt(t1, nr, are, ALU.mult)
        v_tt(t2, pwi[:, 1], aim, ALU.mult)
        v_tt(t1, t1, t2, ALU.add)
        v_tt(qre, t1, den, ALU.mult)
        v_tt(t1, pwi[:, 1], are, ALU.mult)
        v_tt(t2, nr, aim, ALU.mult)
        v_tt(t1, t1, t2, ALU.subtract)
        v_tt(qim, t1, den, ALU.mult)
        Bre = T([64, 32, 16]); Bim = T([64, 32, 16]); w1_ = T([64, 32, 16]); w2_ = T([64, 32, 16])
        qre_b = qre.unsqueeze(2).to_broadcast([64, 32, 16])
        qim_b = qim.unsqueeze(2).to_broadcast([64, 32, 16])
        v_tt(w1_, bre, qre_b, ALU.mult); v_tt(w2_, bim, qim_b, ALU.mult); v_tt(Bre, w1_, w2_, ALU.subtract)
        v_tt(w1_, bim, qre_b, ALU.mult); v_tt(w2_, bre, qim_b, ALU.mult); v_tt(Bim, w1_, w2_, ALU.add)
        big_off = cnt["o"]
        big1 = T([64, 32, 8, 16]); big2 = T([64, 32, 8, 16])
        Cmr = T([64, 32, 8, 16]); Cmi = T([64, 32, 8, 16])
        q0 = cnt["o"]
        Bsr = T([64, 32, 8, 16], BF16); Bsi = T([64, 32, 8, 16], BF16)
        cmr_bf = T([64, 32, 8, 16], BF16); cmi_bf = T([64, 32, 8, 16], BF16)
        Bmr = A.alloc([64, 32, 8, 16], F32, at=RB + q0); Bmi = A.alloc([64, 32, 8, 16], F32, at=RB + q0 + 16 * KB)

        def bc_x(x):
            return x.unsqueeze(2).to_broadcast([64, 32, 8, 16])

        def bc_p(p, lo, hi, rev=False):
            sl = p[:, lo:hi, :].rearrange("p k g -> p g k")
            return sl.unsqueeze(3).to_broadcast([64, 32, 8, 16])

        def cmul_big(o_re, o_im, xr, xi, pr, pi, neg_im=False):
            v_tt(big1, bc_x(xr), pr, ALU.mult); v_tt(big2, bc_x(xi), pi, ALU.mult)
            v_tt(o_re, big1, big2, ALU.subtract)
            v_tt(big1, bc_x(xr), pi, ALU.mult); v_tt(big2, bc_x(xi), pr, ALU.mult)
            if neg_im:
                v_tt(o_im, big1, big2, ALU.add)
                v_ts(o_im, o_im, -1.0, None, ALU.mult)
            else:
                v_tt(o_im, big1, big2, ALU.add)

        cmul_big(Cmr, Cmi, cre, cim, bc_p(pwr, 1, 9), bc_p(pwi, 1, 9), neg_im=True)
        v_cp(cmr_bf, Cmr); v_cp(cmi_bf, Cmi)
        S.dma("sp", scr_cm[:, 0:4096], cmr_bf.rearrange("p g j h -> p (g j h)"), reads=[P])
        S.dma("sp", scr_cm[:, 4096:8192], cmi_bf.rearrange("p g j h -> p (g j h)"), reads=[P])
        pwr_rev = T([64, 8, 32]); pwi_rev = T([64, 8, 32])
        for i in range(8):
            v_cp(pwr_rev[:, i], pwr[:, 7 - i]); v_cp(pwi_rev[:, i], pwi[:, 7 - i])
        prr = pwr_rev.rearrange("p k g -> p g k").unsqueeze(3).to_broadcast([64, 32, 8, 16])
        pri = pwi_rev.rearrange("p k g -> p g k").unsqueeze(3).to_broadcast([64, 32, 8, 16])
        cmul_big(Bsr, Bsi, Bre, Bim, prr, pri)
        wb_bf = T([128, 2, 32, 64], BF16)
        for o, src in ((0, Bsr), (1, Bsi)):
            for g8 in range(4):
                ps = PS[7]
                for gg in range(8):
                    g = g8 * 8 + gg
                    S.op("pe", lambda e, g=g, gg=gg, src=src: e.matmul(ps[:, gg * 64:(gg + 1) * 64], src[:, g].rearrange("p i h -> p (i h)"),
                                                                       identS[0:64, 0:64], start=True, stop=True),
                         reads=[P], writes=[PSB[7]], signal=(gg == 7))
                S.op("dve", lambda e, o=o, g8=g8: e.tensor_copy(out=wb_bf[:, o, g8 * 8:(g8 + 1) * 8, :],
                                                               in_=ps[:].rearrange("p (a b) -> p a b", a=8)), reads=[PSB[7], P], writes=[P])
        S.dma("sp", scr_wb, wb_bf.rearrange("p o g n -> p (o g n)"), reads=[P])
        at_ = T([64, 2, 2, 32])
        v_cp(at_[:, 0, 0], pwr[:, 8]); v_cp(at_[:, 1, 1], pwr[:, 8]); v_cp(at_[:, 1, 0], pwi[:, 8])
        v_ts(at_[:, 0, 1], pwi[:, 8], -1.0, None, ALU.mult)
        S.dma("sp", scr_at, at_.rearrange("p o k g -> p (o k g)"), reads=[P])
        cmul_big(Bmr, Bmi, Bre, Bim, bc_p(ipr, 1, 9), bc_p(ipi, 1, 9))
        toep_bf = A.alloc([128, 32, 128], BF16, at=RB + big_off)
        for g4 in range(8):
            ps = PS[7]
            for gg in range(4):
                g = g4 * 4 + gg
                S.op("pe", lambda e, g=g, gg=gg: e.matmul(ps[:, gg * 128:(gg + 1) * 128], Bmr[:, g].rearrange("p i h -> p (i h)"),
                                                          Cmr[:, g].rearrange("p j h -> p (j h)"), start=True, stop=False),
                     reads=[P], writes=[PSB[7]], signal=False)
                S.op("pe", lambda e, g=g, gg=gg: e.matmul(ps[:, gg * 128:(gg + 1) * 128], Bmi[:, g].rearrange("p i h -> p (i h)"),
                                                          Cmi[:, g].rearrange("p j h -> p (j h)"), start=False, stop=True),
                     reads=[P], writes=[PSB[7]], signal=(gg == 3))
            S.op("dve", lambda e, g4=g4: e.tensor_tensor(
                out=toep_bf[:, g4 * 4:(g4 + 1) * 4, :], in0=ps[:].rearrange("p (a b) -> p a b", a=4),
                in1=bmaskS.unsqueeze(1).to_broadcast([128, 4, 128]), op=ALU.mult), reads=[PSB[7], P], writes=[P])
        S.dma("sp", scr_toep, toep_bf.rearrange("p g m -> p (g m)"), reads=[P])
        S.barrier()

    def mixer0(s):
        S.barrier()
        for i in range(3):
            S.dma("pool", winS[:, :, i * 512:(i + 1) * 512],
                  win_d.rearrange("(kc p) f -> p kc f", p=128)[:, :, i * 512:(i + 1) * 512], writes=[WINB[i]])
        S.dma("pool", identM, ident_d, writes=[IDMB])
        diag_jobs = [(ct, k) for ct in range(2) for k in range(31)]

        def diag_some(n):
            for _ in range(min(n, len(diag_jobs))):
                ct, k = diag_jobs.pop(0)
                S.op("dve", lambda e, ct=ct, k=k: e.tensor_scalar(out=diag[ct][:, k, :], in0=identM, scalar1=cw[:, ct, k:k + 1], scalar2=None,
                                                              op0=ALU.mult), reads=[IDMB, M0CB], writes=[DIAGB[ct]])
        rmsnorm_tile(4, 0, 4, hT2, H2B)
        S.op("pool", lambda e: e.memset(vpad[:, :, 0:30], 0.0), writes=VB)
        pc = 0
        for ct in range(4):
            for tb in range(4):
                pa, pg = (pc % 2) * 2, (pc % 2) * 2 + 1
                pc += 1
                for pi, oc in ((pa, ct), (pg, ct + 4)):
                    for kc in range(KC):
                        S.op("pe", lambda e, pi=pi, oc=oc, kc=kc, tb=tb: e.matmul(
                            PS[pi][:], winS[:, kc, oc * 128:(oc + 1) * 128], hT2[:, kc, tb * TT:(tb + 1) * TT],
                            start=(kc == 0), stop=(kc == KC - 1)),
                            reads=[WINB[oc // 4], H2B[kc][tb]], writes=[PSB[pi]], signal=(kc == KC - 1))
                sg = sil[pc % 2]
                S.op("act", lambda e, sg=sg, pg=pg: e.activation(out=sg, in_=PS[pg][:], func=AF.Sigmoid),
                     reads=[PSB[pg]], writes=[SILB[pc % 2]])
                S.op("dve", lambda e, sg=sg, pa=pa, ct=ct, tb=tb: e.tensor_tensor(
                    out=vpad[:, ct, 30 + tb * TT:30 + (tb + 1) * TT], in0=sg, in1=PS[pa][:], op=ALU.mult),
                    reads=[SILB[pc % 2], PSB[pa]], writes=[VB[ct]])
                diag_some(4)
        for ct in range(4):
            for tb in range(4):
                pi = 4 + (pc % 2)
                pc += 1
                oc = 8 + ct
                for kc in range(KC):
                    S.op("pe", lambda e, pi=pi, oc=oc, kc=kc, tb=tb: e.matmul(
                        PS[pi][:], winS[:, kc, oc * 128:(oc + 1) * 128], hT2[:, kc, tb * TT:(tb + 1) * TT],
                        start=(kc == 0), stop=(kc == KC - 1)),
                        reads=[WINB[2], H2B[kc][tb]], writes=[PSB[pi]], signal=(kc == KC - 1))
                S.op("act", lambda e, pi=pi, ct=ct, tb=tb: e.activation(out=uT4[:, ct, :, tb * 64:(tb + 1) * 64],
                                                                       in_=PS[pi][:].rearrange("p (c i) -> p i c", i=8), func=AF.Copy),
                     reads=[PSB[pi]], writes=[UB[ct][tb]])
        diag_some(len(diag_jobs))
        S.barrier()
        S.dma("pool", selin.rearrange("p a b -> p (a b)"), selin_d, writes=[SELIB])
        for ct in range(2, 4):
            for k in range(31):
                S.op("dve", lambda e, ct=ct, k=k: e.tensor_scalar(out=diag[ct][:, k, :], in0=identM, scalar1=cw[:, ct, k:k + 1], scalar2=None,
                                                              op0=ALU.mult), reads=[IDMB, M0CB], writes=[DIAGB[ct]])
        lnb = [R(96 + 2 * i, [128, TT], F32) for i in range(7)]
        ybf = [R(110 + i, [128, TT], BF16) for i in range(2)]
        ysq = [R(112 + i, [128, TT], BF16) for i in range(2)]
        LNB = [Buf("lnb%d" % i) for i in range(7)]
        YBB = [Buf("ybf%d" % i) for i in range(2)]
        YSB = [Buf("ysq%d" % i) for i in range(2)]
        YC2 = [[Buf("yc%d_%d" % (c, t)) for t in range(4)] for c in range(4)]
        tiles = [(tb, ct) for tb in range(4) for ct in range(4)]

        def conv_tile(n):
            tb, ct = tiles[n]
            pi = 2 + (n % 2)
            j = n % 2
            for k in range(31):
                S.op("pe", lambda e, pi=pi, ct=ct, k=k, tb=tb: e.matmul(
                    PS[pi][:], diag[ct][:, k, :], vpad[:, ct, k + tb * TT:k + (tb + 1) * TT], start=(k == 0), stop=(k == 30)),
                    reads=[DIAGB[ct], VB[ct]], writes=[PSB[pi]], signal=(k == 30))
            bia = chv[:, 0, ct:ct + 1]
            S.op("act", lambda e, pi=pi, ct=ct, tb=tb: e.activation(out=ycv[:, ct, tb * TT:(tb + 1) * TT], in_=PS[pi][:], func=AF.Identity, bias=bia),
                 reads=[PSB[pi], M0CB], writes=[YC2[ct][tb]])
            S.op("act", lambda e, pi=pi, j=j: e.activation(out=ybf[j], in_=PS[pi][:], func=AF.Identity, bias=bia),
                 reads=[PSB[pi], M0CB], writes=[YBB[j]])
            S.op("act", lambda e, pi=pi, j=j: e.activation(out=ysq[j], in_=PS[pi][:], func=AF.Square, bias=bia),
                 reads=[PSB[pi], M0CB], writes=[YSB[j]])

        def stat_mm(n):
            tb, ct = tiles[n]
            j = n % 2
            sb_, qb_ = ((0, 1), (4, 5))[tb % 2]
            S.op("pe", lambda e: e.matmul(PS[sb_][:], ones_bf, ybf[j], start=(ct == 0), stop=(ct == 3)),
                 reads=[YBB[j], CONSTB], writes=[PSB[sb_]], signal=True)
            S.op("pe", lambda e: e.matmul(PS[qb_][:], ones_bf, ysq[j], start=(ct == 0), stop=(ct == 3)),
                 reads=[YSB[j], CONSTB], writes=[PSB[qb_]], signal=True)

        def ln_stages(tb):
            sb_, qb_ = ((0, 1), (4, 5))[tb % 2]
            mean, var, msq = lnb[2 + tb % 2], lnb[4 + tb % 2], lnb[6]
            MB_, VB_, QB_ = LNB[2 + tb % 2], LNB[4 + tb % 2], LNB[6]
            sl = slice(tb * TT, (tb + 1) * TT)

            def st_a():
                S.op("act", lambda e: e.activation(out=mean, in_=PS[sb_][:], func=AF.Copy, scale=1.0 / 512), reads=[PSB[sb_]], writes=[MB_])
                S.op("act", lambda e: e.activation(out=msq, in_=PS[sb_][:], func=AF.Square, scale=1.0 / 512), reads=[PSB[sb_]], writes=[QB_])
                S.op("dve", lambda e: e.scalar_tensor_tensor(out=var, in0=PS[qb_][:], scalar=1.0 / 512, in1=msq, op0=ALU.mult, op1=ALU.subtract),
                     reads=[PSB[qb_], QB_], writes=[VB_])

            def st_b():
                S.op("act", lambda e: e.activation(out=var, in_=var, func=AF.Sqrt, bias=lnepsc), reads=[VB_, M0CB], writes=[VB_])
                S.op("dve", lambda e: e.reciprocal(out=var, in_=var), reads=[VB_], writes=[VB_])

            def t_ops(ct):
                t_ = lnb[ct % 2]
                TB_ = LNB[ct % 2]
                S.op("dve", lambda e: e.tensor_tensor(out=t_, in0=ycv[:, ct, sl], in1=mean, op=ALU.subtract),
                     reads=[YC2[ct][tb], MB_], writes=[TB_])
                S.op("dve", lambda e: e.tensor_tensor(out=t_, in0=t_, in1=var, op=ALU.mult),
                     reads=[TB_, VB_], writes=[TB_])

            def silu(ct):
                t_ = lnb[ct % 2]
                S.op("act", lambda e: e.activation(out=mcat[:, ct, sl], in_=t_, func=AF.Silu,
                                                   scale=chv[:, 1, ct:ct + 1], bias=chv[:, 2, ct:ct + 1]),
                     reads=[LNB[ct % 2], M0CB], writes=[MCB[ct][tb]])

            return [st_a, st_b, lambda: (t_ops(0), t_ops(1)), lambda: (silu(0), silu(1), t_ops(2), t_ops(3)), lambda: (silu(2), silu(3))]

        pending = []
        conv_tile(0)
        for n in range(len(tiles)):
            if n + 1 < len(tiles):
                conv_tile(n + 1)
            stat_mm(n)
            for st in pending:
                if st:
                    st.pop(0)()
            if tiles[n][1] == 3:
                pending.append(ln_stages(tiles[n][0]))
        while any(pending):
            for st in pending:
                if st:
                    st.pop(0)()
        dump("mcA", mcat[:, 0:4, :].rearrange("p a b -> p (a b)"), [b for r in MCB[0:4] for b in r])
        dump("u", uT.rearrange("p a b -> p (a b)"), [b for r in UB for b in r])
        S.barrier()
        S.dma("sp", wbS.rearrange("p o g n -> p (o g n)"), scr_wb, writes=[WBB])
        S.dma("sp", toepS.rearrange("p g m -> p (g m)"), scr_toep, writes=[TOEPB])
        S.dma("pool", selout.rearrange("p a b -> p (a b)"), selout_d, writes=[PPB])
        S.dma("pool", gluwS, gluw_d.rearrange("(kc p) f -> p kc f", p=128), writes=[PPB])
        scr_cm4 = scr_cm.rearrange("p (o g m) -> p o g m", o=2, g=32)
        scr_at4 = scr_at.rearrange("p (o k g) -> p o k g", o=2, k=2)
        for gh in range(2):
            for o in range(2):
                S.dma("sp", cmS[64 * gh:64 * gh + 64, o, :, :], scr_cm4[:, o, gh * 16:(gh + 1) * 16, :], writes=[CMB[gh * 2 + o]])
                for k in range(2):
                    S.dma("sp", atS[64 * gh:64 * gh + 64, o, k, :], scr_at4[:, o, k, gh * 16:(gh + 1) * 16], writes=[ATB[gh * 4 + o * 2 + k]])
        for g2 in range(16):
            pi = g2 % 2
            for gg in range(2):
                g = g2 * 2 + gg
                ct, g8 = g // 8, g % 8
                for i in range(8):
                    S.op("pe", lambda e, pi=pi, gg=gg, ct=ct, g8=g8, i=i: e.matmul(
                        PS[pi][:, gg * 256:(gg + 1) * 256], selin[:, g8, 112 - 16 * i:240 - 16 * i], uT4[:, ct, i, :],
                        start=(i == 0), stop=(i == 7)),
                        reads=[SELIB] + UB[ct], writes=[PSB[pi]], signal=(gg == 1 and i == 7))
            S.op("act", lambda e, pi=pi, g2=g2: e.activation(out=Ut[:, g2 * 2:g2 * 2 + 2, :],
                                                            in_=PS[pi][:].rearrange("p (a b) -> p a b", a=2), func=AF.Copy),
                 reads=[PSB[pi]], writes=[UTB[g2 * 2], UTB[g2 * 2 + 1]])
        S.op("dve", lambda e: e.memset(Xh[:, 31], 0.0), writes=[XHB[31]])
        for cb in range(16):
            pi = 2 + (cb % 2)
            psv = PS[pi][:].rearrange("p (c o g) -> p c o g", c=16, o=2)
            for g in range(32):
                gh, gl = g // 16, g % 16
                for o in range(2):
                    S.op("pe", lambda e, psv=psv, g=g, gh=gh, gl=gl, o=o, cb=cb: e.matmul(
                        psv[64 * gh:64 * gh + 64, :, o, gl], wbS[:, o, g, :], Ut[:, g, cb * 16:(cb + 1) * 16], start=True, stop=True),
                        reads=[WBB, UTB[g]], writes=[PSB[pi]], signal=(g == 31 and o == 1))
            for cc in range(16):
                c = cb * 16 + cc
                prev = Xh[:, (c - 1) % 32]
                S.op("dve", lambda e, prev=prev: e.tensor_tensor(
                    out=pq, in0=prev.unsqueeze(1).to_broadcast([128, 2, 2, 16]), in1=atS, op=ALU.mult),
                    reads=[XHB[(c - 1) % 32]] + ATB, writes=[PQB])
                S.op("dve", lambda e: e.tensor_tensor(out=rtmp, in0=pq[:, :, 0, :], in1=pq[:, :, 1, :], op=ALU.add),
                     reads=[PQB], writes=[RTB])
                S.op("dve", lambda e, psv=psv, cc=cc, c=c: e.tensor_tensor(out=Xh[:, c % 32], in0=rtmp, in1=psv[:, cc], op=ALU.add),
                     reads=[RTB, PSB[pi]], writes=[XHB[c % 32]])
            half = (cb % 2) * 16
            S.op("act", lambda e, cb=cb, half=half: e.activation(
                out=Xbf[:, :, :, cb * 16:(cb + 1) * 16], in_=Xh[:, half:half + 16].rearrange("p c o g -> p o g c"), func=AF.Copy),
                reads=XHB[half:half + 16], writes=[XBFB])
        for g2 in range(16):
            pi = 4 + (g2 % 2)
            for gg in range(2):
                g = g2 * 2 + gg
                S.op("pe", lambda e, pi=pi, gg=gg, g=g: e.matmul(PS[pi][:, gg * 256:(gg + 1) * 256], toepS[:, g, :], Ut[:, g, :],
                                                                start=True, stop=False),
                     reads=[TOEPB, UTB[g]], writes=[PSB[pi]], signal=False)
                hs = slice(64 * (g // 16), 64 * (g // 16) + 64)
                gl = g % 16
                S.op("pe", lambda e, pi=pi, gg=gg, gl=gl, hs=hs: e.matmul(PS[pi][:, gg * 256 + 1:(gg + 1) * 256], cmS[hs, 0, gl, :], Xbf[hs, 0, gl, 0:255],
                                                                start=False, stop=False),
                     reads=CMB + [XBFB], writes=[PSB[pi]], signal=False)
                S.op("pe", lambda e, pi=pi, gg=gg, gl=gl, hs=hs: e.matmul(PS[pi][:, gg * 256 + 1:(gg + 1) * 256], cmS[hs, 1, gl, :], Xbf[hs, 1, gl, 0:255],
                                                                start=False, stop=True),
                     reads=CMB + [XBFB], writes=[PSB[pi]], signal=(gg == 1))
            S.op("act", lambda e, pi=pi, g2=g2: e.activation(out=Ut[:, g2 * 2:g2 * 2 + 2, :],
                                                            in_=PS[pi][:].rearrange("p (a b) -> p a b", a=2), func=AF.Copy),
                 reads=[PSB[pi]], writes=[UTB[g2 * 2], UTB[g2 * 2 + 1]])
        dump("toep", toepS.rearrange("p g m -> p (g m)"), [TOEPB])
        dump("wb", wbS.rearrange("p o g n -> p (o g n)"), [WBB])
        dump("Y", Ut.rearrange("p g c -> p (g c)"), UTB)
        S.barrier()
        S.dma("pool", woutS, wout_d.rearrange("(kc p) f -> p kc f", p=128), writes=[WOUTB])
        pc = 0
        for ct in range(4):
            for tb in range(4):
                pi = pc % 2
                pc += 1
                for j in range(8):
                    for g8 in range(8):
                        S.op("pe", lambda e, pi=pi, j=j, g8=g8, ct=ct, tb=tb: e.matmul(
                            PS[pi][:, j * 64:(j + 1) * 64], selout[:, j, 112 - 16 * g8:240 - 16 * g8],
                            Ut[:, ct * 8 + g8, tb * 64:(tb + 1) * 64], start=(g8 == 0), stop=(g8 == 7)),
                            reads=[PPB, UTB[ct * 8 + g8]], writes=[PSB[pi]], signal=(j == 7 and g8 == 7))
                gb = 3 * (pc % 2)
                ys, x2, x3 = gt[gb], gt[gb + 1], gt[gb + 2]
                sg_ = x2
                GY, G2, G3 = GTB[gb], GTB[gb + 1], GTB[gb + 2]
                sl = slice(tb * TT, (tb + 1) * TT)
                S.op("dve", lambda e, pi=pi, ct=ct, sl=sl: e.scalar_tensor_tensor(
                    out=ys.rearrange("p (c j) -> p c j", j=8), in0=uT4[:, ct, :, tb * 64:(tb + 1) * 64].rearrange("p j c -> p c j"),
                    scalar=chv[:, 3, ct:ct + 1], in1=PS[pi][:].rearrange("p (j c) -> p c j", j=8), op0=ALU.mult, op1=ALU.add),
                    reads=[PSB[pi], UB[ct][tb], M0CB], writes=[GY])
                S.op("act", lambda e: e.activation(out=x2, in_=ys, func=AF.Square), reads=[GY], writes=[G2])
                S.op("dve", lambda e: e.tensor_scalar(out=x2, in0=x2, scalar1=0.044715, scalar2=1.0, op0=ALU.mult, op1=ALU.add),
                     reads=[G2], writes=[G2])
                S.op("dve", lambda e: e.tensor_tensor(out=x3, in0=x2, in1=ys, op=ALU.mult), reads=[G2, GY], writes=[G3])
                S.op("act", lambda e: e.activation(out=sg_, in_=x3, func=AF.Sigmoid, scale=1.5957691216057308), reads=[G3], writes=[G2])
                S.op("dve", lambda e, ct=ct, sl=sl: e.tensor_tensor(out=y1bS[:, ct, sl], in0=ys, in1=sg_, op=ALU.mult),
                     reads=[GY, G2], writes=[Y1B[ct][tb]])
        for ot in range(4):
            for tb in range(4):
                pi = 2 + (pc % 2)
                pc += 1
                sl = slice(tb * TT, (tb + 1) * TT)
                for ct in range(4):
                    S.op("pe", lambda e, pi=pi, ct=ct, ot=ot, sl=sl: e.matmul(
                        PS[pi][:], gluwS[:, ct, ot * 128:(ot + 1) * 128], y1bS[:, ct, sl], start=(ct == 0), stop=(ct == 3)),
                        reads=[PPB, Y1B[ct][tb]], writes=[PSB[pi]], signal=(ct == 3))
                sg = sgt[pc % 2]
                S.op("act", lambda e, pi=pi, sg=sg, ot=ot: e.activation(out=sg, in_=PS[pi][:], func=AF.Sigmoid, bias=chv[:, 4, ot:ot + 1]),
                     reads=[PSB[pi], M0CB], writes=[SGTB[pc % 2]])
                S.op("dve", lambda e, sg=sg, ot=ot, sl=sl: e.tensor_tensor(out=mcat[:, 4 + ot, sl], in0=y1bS[:, ot, sl], in1=sg, op=ALU.mult),
                     reads=[SGTB[pc % 2], Y1B[ot][tb]], writes=[MCB[4 + ot][tb]])
        dump("mcB", mcat[:, 4:8, :].rearrange("p a b -> p (a b)"), [b for r in MCB[4:8] for b in r])
        dump("y1", y1bS.rearrange("p a b -> p (a b)"), [b for r in Y1B for b in r])
        for dc in range(KC):
            for tb in range(4):
                pi = 4 + (pc % 2)
                pc += 1
                sl = slice(tb * TT, (tb + 1) * TT)
                for mc in range(8):
                    S.op("pe", lambda e, pi=pi, mc=mc, dc=dc, sl=sl: e.matmul(
                        PS[pi][:], woutS[:, mc, dc * 128:(dc + 1) * 128], mcat[:, mc, sl], start=(mc == 0), stop=(mc == 7)),
                        reads=[WOUTB, MCB[mc][tb]], writes=[PSB[pi]], signal=(mc == 7))
                xs = xT[:, dc, sl]
                S.op("dve", lambda e, pi=pi, xs=xs: e.tensor_tensor(out=xs, in0=PS[pi][:], in1=xs, op=ALU.add),
                     reads=[PSB[pi], XB[dc][tb]], writes=[XB[dc][tb]])
        S.barrier()

    wqkv_d = din("w_qkv", [D, 3 * D])
    wo_d = din("w_o", [D, D])
    cbias_d = din("cbias", [128, 2 * 256])
    en_d = din("en", [8, 8 * 128])
    negm_d = din("negm", [128, 4 * 64])
    NEG = -30000.0
    ahT = R(0, [128, KC, L], BF16)
    qring = [R(32 + 8 * i, [128, KC, 512], BF16) for i in range(2)]
    qT = R(48, [128, 4, L], BF16)
    kT = R(64, [128, 4, L], BF16)
    Vh = R(80, [128, 16, 512], BF16)
    oT = R(96, [128, 4, L], BF16)
    woS = R(112, [128, 4, D], BF16)
    pT = [R(120 + i, [128, 2, 256], BF16) for i in range(2)]
    cbiasS = R(122, [128, 2, 256], BF16)
    EnS = R(123, [8, 8, 128], BF16)
    identA = R(125, [128, 128], BF16)
    negmS = R(125.5, [128, 4, 64], F32)
    rden = R(126.5, [128, 256], F32)
    mbT = [R(127.5 + 2 * i, [8, 4, 256], BF16) for i in range(2)]
    kmf = R(131.5, [128, 4, 8], F32)
    kmT = R(131.75, [128, 4, 8], BF16)
    gsb = R(132, [128, 32], F32)
    top8 = R(132.25, [128, 8], F32)
    mball = R(132.5, [128, 8, 32], BF16)
    rden2 = R(133, [128, 256], F32)
    rdens = [rden, rden2]
    RDBS = [Buf("rden0"), Buf("rden1")]
    mbTall = R(32, [8, 4, 4, 256], BF16)
    assert RB + int(134 * KB) <= A.nbytes

    AHB = [[Buf("ah%d_%d" % (k, t)) for t in range(4)] for k in range(KC)]
    QRB = [Buf("qring%d" % i) for i in range(2)]
    QTB = [Buf("qT%d" % h) for h in range(4)]
    KTB = [Buf("kT%d" % h) for h in range(4)]
    VHB = [Buf("vh%d" % k) for k in range(16)]
    OTB = [[Buf("oT%d_%d" % (h, q)) for q in range(8)] for h in range(4)]
    WOB = Buf("woS")
    PTB = [Buf("pT%d" % i) for i in range(2)]
    ACB = Buf("attnconst")
    RDB = Buf("rden")
    MBTB = [Buf("mbT%d" % i) for i in range(2)]
    KMB = Buf("km")
    GSB = Buf("gsb")
    T8B = Buf("top8")
    MBB = Buf("mb")
    SCALE = 128.0 ** -0.5

    def mixer1(s):
        S.barrier()
        S.dma("pool", cbiasS.rearrange("p a b -> p (a b)"), cbias_d, writes=[ACB])
        S.dma("pool", EnS.rearrange("p a b -> p (a b)"), en_d, writes=[ACB])
        S.dma("pool", identA, ident_d, writes=[ACB])
        S.dma("sp", negmS.rearrange("p a b -> p (a b)"), negm_d, writes=[ACB])
        rmsnorm_tile(5, 0, 4, ahT, AHB)
        nring = [0]
        pcs = [0]

        def load_cols(c0):
            i = nring[0] % 2
            nring[0] += 1
            S.dma("pool", qring[i], wqkv_d.rearrange("(kc p) f -> p kc f", p=128)[:, :, c0:c0 + 512], writes=[QRB[i]])
            return qring[i], QRB[i]

        for half in range(2):
            for which, dstT, dstB in ((0, qT, QTB), (1, kT, KTB)):
                if which == 0 and half == 1:
                    wsl, wb_ = pre_q
                else:
                    wsl, wb_ = load_cols(which * D + half * 512)
                for hl in range(4):
                    for tb in range(4):
                        pi = pcs[0] % 2
                        pcs[0] += 1
                        for kc in range(KC):
                            S.op("pe", lambda e, pi=pi, wsl=wsl, kc=kc, hl=hl, tb=tb: e.matmul(
                                PS[pi][:], wsl[:, kc, hl * 128:(hl + 1) * 128], ahT[:, kc, tb * TT:(tb + 1) * TT],
                                start=(kc == 0), stop=(kc == KC - 1)),
                                reads=[wb_, AHB[kc][tb]], writes=[PSB[pi]], signal=(kc == KC - 1))
                        S.op("act", lambda e, pi=pi, dstT=dstT, hl=hl, tb=tb: e.activation(
                            out=dstT[:, hl, tb * TT:(tb + 1) * TT], in_=PS[pi][:], func=AF.Copy),
                            reads=[PSB[pi]], writes=[dstB[hl]])
            wsl, wb_ = load_cols(2 * D + half * 512)
            for kt in range(16):
                pi = pcs[0] % 2
                pcs[0] += 1
                tb = kt // 4
                for kc in range(KC):
                    S.op("pe", lambda e, pi=pi, wsl=wsl, kc=kc, kt=kt: e.matmul(
                        PS[pi][:], ahT[:, kc, kt * 128:(kt + 1) * 128], wsl[:, kc, :], start=(kc == 0), stop=(kc == KC - 1)),
                        reads=[wb_, AHB[kc][tb]], writes=[PSB[pi]], signal=(kc == KC - 1))
                S.op("act", lambda e, pi=pi, kt=kt: e.activation(out=Vh[:, kt, :], in_=PS[pi][:], func=AF.Copy),
                     reads=[PSB[pi]], writes=[VHB[kt]])
            S.dma("pool", woS, wo_d.rearrange("(kc p) f -> p kc f", p=128)[:, half * 4:(half + 1) * 4, :], writes=[WOB])
            if half == 0:
                pre_q = load_cols(0 * D + 1 * 512)
            for hl in range(4):
                S.op("dve", lambda e, hl=hl: e.tensor_reduce(out=kmf[:, hl, :], in_=kT[:, hl, :].rearrange("p (n k) -> p n k", n=8),
                                                            axis=AX.X, op=ALU.add), reads=[KTB[hl]], writes=[KMB])
            S.op("dve", lambda e: e.tensor_copy(out=kmT, in_=kmf), reads=[KMB], writes=[KMB])
            for qb in range(4, 8):
                for q2 in range(2):
                    idx = (qb - 4) * 2 + q2
                    qsl = slice(qb * 256 + q2 * 128, qb * 256 + (q2 + 1) * 128)
                    for hl in range(4):
                        S.op("pe", lambda e, hl=hl, qsl=qsl, idx=idx: e.matmul(PS[6][:, idx * 32 + hl * 8:idx * 32 + (hl + 1) * 8], qT[:, hl, qsl],
                                                                              kmT[:, hl, :], start=True, stop=True),
                             reads=[QTB[hl], KMB], writes=[PSB[6]], signal=(hl == 3))
            def mask_job(qb, q2):
                idx = (qb - 4) * 2 + q2
                S.op("dve", lambda e: e.tensor_tensor(out=gsb, in0=PS[6][:, idx * 32:(idx + 1) * 32], in1=negmS[:, qb - 4, 0:32], op=ALU.add),
                     reads=[PSB[6], ACB], writes=[GSB])
                for hl in range(4):
                    S.op("dve", lambda e, hl=hl: e.max(out=top8, in_=gsb[:, hl * 8:(hl + 1) * 8]), reads=[GSB], writes=[T8B])
                    S.op("dve", lambda e, hl=hl: e.tensor_scalar(out=mball[:, idx, hl * 8:(hl + 1) * 8], in0=gsb[:, hl * 8:(hl + 1) * 8],
                                                                scalar1=top8[:, 2:3], scalar2=NEG, op0=ALU.is_lt, op1=ALU.mult),
                         reads=[GSB, T8B], writes=[MBB])

            mask_jobs = [(qb, q2) for qb in range(4, 8) for q2 in range(2)]

            def mask_transposes():
                for qb in range(4, 8):
                    for q2 in range(2):
                        idx = (qb - 4) * 2 + q2
                        tbk = 6 + (idx % 2)
                        for hl in range(4):
                            S.op("pe", lambda e, hl=hl, idx=idx, tbk=tbk: e.matmul(PS[tbk][0:8, hl * 128:(hl + 1) * 128], mball[:, idx, hl * 8:(hl + 1) * 8], identA,
                                                                         start=True, stop=True),
                                 reads=[MBB, ACB], writes=[PSB[tbk]], signal=(hl == 3))
                        S.op("act", lambda e, qb=qb, q2=q2, tbk=tbk: e.activation(out=mbTall[:, qb - 4, :, q2 * 128:(q2 + 1) * 128],
                                                                        in_=PS[tbk][0:8, :].rearrange("p (h q) -> p h q", h=4), func=AF.Copy),
                             reads=[PSB[tbk]], writes=[MBTB[0]])

            items = []
            for qb in range(8):
                for hl in range(4):
                    npair = qb + 1
                    for kp in range(npair):
                        items.append((qb, hl, kp, kp == 0, kp == npair - 1))

            def emit_S(i):
                qb, hl, kp, _, _ = items[i]
                qs = slice(qb * 256, (qb + 1) * 256)
                gated = qb >= 4
                pi = 2 + (i % 2)
                for k2 in range(2):
                    kt = kp * 2 + k2
                    n = kp
                    osl = PS[pi][:, k2 * 256:(k2 + 1) * 256]
                    extra = (n == qb) or gated
                    S.op("pe", lambda e, osl=osl, hl=hl, kt=kt, extra=extra, qs=qs: e.matmul(
                        osl, kT[:, hl, kt * 128:(kt + 1) * 128], qT[:, hl, qs], start=True, stop=(not extra)),
                        reads=[KTB[hl], QTB[hl]], writes=[PSB[pi]], signal=((not extra) and k2 == 1))
                    if n == qb:
                        S.op("pe", lambda e, osl=osl, k2=k2: e.matmul(osl, identA, cbiasS[:, k2, :], start=False, stop=True),
                             reads=[ACB], writes=[PSB[pi]], signal=(k2 == 1))
                    elif gated:
                        S.op("pe", lambda e, osl=osl, n=n, hl=hl, qb=qb: e.matmul(osl, EnS[:, n, :], mbTall[:, qb - 4, hl, :], start=False, stop=True),
                             reads=[ACB, MBTB[0]], writes=[PSB[pi]], signal=(k2 == 1))

            def emit_rest(i, gi):
                qb, hl, kp, first, last = items[i]
                qs = slice(qb * 256, (qb + 1) * 256)
                pi = 2 + (i % 2)
                ti = i % 2
                po, pd = ((4, 5), (0, 1))[gi % 2]
                S.op("act", lambda e, pi=pi, ti=ti: e.activation(out=pT[ti].rearrange("p a b -> p (a b)"), in_=PS[pi][:],
                                                                func=AF.Exp, scale=SCALE),
                     reads=[PSB[pi]], writes=[PTB[ti]])
                for k2 in range(2):
                    kt = kp * 2 + k2
                    f_ = first and k2 == 0
                    l_ = last and k2 == 1
                    S.op("pe", lambda e, po=po, kt=kt, hl=hl, ti=ti, k2=k2, f_=f_, l_=l_: e.matmul(
                        PS[po][:, 0:256], Vh[:, kt, hl * 128:(hl + 1) * 128], pT[ti][:, k2, :], start=f_, stop=l_),
                        reads=[VHB[kt], PTB[ti]], writes=[PSB[po]], signal=False)
                    S.op("pe", lambda e, pd=pd, ti=ti, k2=k2, f_=f_, l_=l_: e.matmul(
                        PS[pd][:, 0:256], ones_bf, pT[ti][:, k2, :], start=f_, stop=l_),
                        reads=[CONSTB, PTB[ti]], writes=[PSB[pd]], signal=(k2 == 1))
                if last:
                    rd = rdens[gi % 2]
                    S.op("dve", lambda e, pd=pd, rd=rd: e.reciprocal(out=rd, in_=PS[pd][:, 0:256]), reads=[PSB[pd]], writes=[RDBS[gi % 2]])
                    S.op("dve", lambda e, po=po, hl=hl, rd=rd, qs=qs: e.tensor_tensor(out=oT[:, hl, qs], in0=PS[po][:, 0:256], in1=rd, op=ALU.mult),
                         reads=[PSB[po], RDBS[gi % 2]], writes=[OTB[hl][qb]])

            n_items = len(items)
            first_gated = next(i for i, it in enumerate(items) if it[0] >= 4)
            gi = 0
            emit_S(0)
            for i in range(n_items):
                if i + 1 < n_items:
                    if i + 1 == first_gated:
                        while mask_jobs:
                            mask_job(*mask_jobs.pop(0))
                        mask_transposes()
                    emit_S(i + 1)
                emit_rest(i, gi)
                if items[i][4]:
                    gi += 1
                    if mask_jobs:
                        mask_job(*mask_jobs.pop(0))
            if half == 0:
                dump("oT0", oT.rearrange("p a b -> p (a b)"), [b for r_ in OTB for b in r_])
                dump("qT0", qT.rearrange("p a b -> p (a b)"), QTB)
                dump("kT0", kT.rearrange("p a b -> p (a b)"), KTB)
                dump("Vh0", Vh.rearrange("p a b -> p (a b)"), VHB)
            for dc in range(KC):
                for tb in range(4):
                    pi = pcs[0] % 2
                    pcs[0] += 1
                    sl = slice(tb * TT, (tb + 1) * TT)
                    for hl in range(4):
                        S.op("pe", lambda e, pi=pi, hl=hl, dc=dc, sl=sl: e.matmul(
                            PS[pi][:], woS[:, hl, dc * 128:(dc + 1) * 128], oT[:, hl, sl], start=(hl == 0), stop=(hl == 3)),
                            reads=[WOB, OTB[hl][2 * tb], OTB[hl][2 * tb + 1]], writes=[PSB[pi]], signal=(hl == 3))
                    xs = xT[:, dc, sl]
                    S.op("dve", lambda e, pi=pi, xs=xs: e.tensor_tensor(out=xs, in0=PS[pi][:], in1=xs, op=ALU.add),
                         reads=[PSB[pi], XB[dc][tb]], writes=[XB[dc][tb]])
            S.barrier()

    OUTB = [Buf("o%d" % i) for i in range(2)]
    for s in range(nseq):
        def load_blocks(sq, blks):
            for blk in blks:
                for kc in range(KC):
                    S.dma("sp", xT[:, kc, blk * TT:(blk + 1) * TT], xT_d[sq, kc * 128:(kc + 1) * 128, blk * TT:(blk + 1) * TT],
                          writes=[XB[kc][blk]])

        if s == 0 or "final" not in stages:
            load_blocks(s, range(L // TT))
        if s == 0 and "mix0" in stages:
            ssm_param_prep()
        if "ffn00" in stages:
            ffn_chain([(0, 0, 0), (0, 0, 1024)])
        if "mix0" in stages:
            mixer0(s)
        if "ffn01" in stages and "ffn10" in stages:
            ffn_chain([(1, 1, 0), (1, 1, 1024), (2, 2, 0), (2, 2, 1024)])
        else:
            if "ffn01" in stages:
                ffn_chain([(1, 1, 0), (1, 1, 1024)])
            if "ffn10" in stages:
                ffn_chain([(2, 2, 0), (2, 2, 1024)])
        if "mix1" in stages:
            mixer1(s)
        def final_tile(t0, s=s):
            rmsnorm_tile(6, t0, 2, hT, HB, inplace=True)
            for blk in (t0 // TT, t0 // TT + 1):
                for kc in range(KC):
                    S.dma("sp", outT_d[s, kc * 128:(kc + 1) * 128, blk * TT:(blk + 1) * TT], xT[:, kc, blk * TT:(blk + 1) * TT],
                          reads=[XB[kc][blk]])
            if s + 1 < nseq:
                load_blocks(s + 1, (t0 // TT, t0 // TT + 1))

        fused_tail = ("ffn11" in stages) and ("final" in stages)
        if "ffn11" in stages:
            ffn_chain([(3, 3, 0), (3, 3, 1024)], last_hook=(lambda: final_tile(0)) if fused_tail else None)
        if "final" in stages:
            for t0 in ((1024,) if fused_tail else (0, 1024)):
                final_tile(t0)
        if "final" not in stages:
            for blk in range(L // TT):
                for kc in range(KC):
                    S.dma("sp", outT_d[s, kc * 128:(kc + 1) * 128, blk * TT:(blk + 1) * TT], xT[:, kc, blk * TT:(blk + 1) * TT],
                          reads=[XB[kc][blk]])
    S.wait_all("sp", [b for row in XB for b in row])
    nc._sched_ninst = S.ninst
    nc._dbg_names = dbg_names
    return nc


def _prep_inputs(inputs):
    f = np.float32
    x = np.asarray(inputs["x"], f)
    ffn_norm = np.asarray(inputs["ffn_norm"], f)
    vecs = [ffn_norm[0, 0], ffn_norm[0, 1], ffn_norm[1, 0], ffn_norm[1, 1],
            np.asarray(inputs["mix_norm"], f)[0], np.asarray(inputs["mix_norm"], f)[1],
            np.asarray(inputs["final_norm"], f)]
    gains = np.stack([v.reshape(KC, 128).T for v in vecs], axis=1)
    gains = np.ascontiguousarray(gains.reshape(128, 7 * KC))
    shared = {
        "gains": gains,
        "w1": np.ascontiguousarray(np.asarray(inputs["ffn_w1"], f).reshape(4, D, FF)),
        "w3": np.ascontiguousarray(np.asarray(inputs["ffn_w3"], f).reshape(4, D, FF)),
        "w2": np.ascontiguousarray(np.asarray(inputs["ffn_w2"], f).reshape(4, FF, D)),
    }
    shared["w_in"] = np.ascontiguousarray(np.asarray(inputs["ab_w_in"], f)[0])
    shared["w_out"] = np.ascontiguousarray(np.asarray(inputs["ab_w_out"], f)[0])
    shared["glu_w"] = np.ascontiguousarray(np.asarray(inputs["ssm_glu_w"], f)[0])
    cwm = np.asarray(inputs["conv_w"], f)[0]
    shared["cw"] = np.ascontiguousarray(cwm.T.reshape(4, 128, 31).transpose(1, 0, 2).reshape(128, 4 * 31))
    chv = [np.asarray(inputs[k], f)[0] for k in ("conv_b", "conv_ln_g", "conv_ln_b", "ssm_d", "ssm_glu_b")]
    shared["chv"] = np.ascontiguousarray(np.stack([v.reshape(4, 128).T for v in chv], axis=1).reshape(128, 20))
    ldt = np.broadcast_to(np.asarray(inputs["ssm_log_dt"], f)[0][None, :], (64, 32))
    are = np.asarray(inputs["ssm_a_re"], f)[0].T
    aim = np.asarray(inputs["ssm_a_im"], f)[0].T
    shared["sp1"] = np.ascontiguousarray(np.stack([ldt, are, aim], axis=1).reshape(64, 96))
    bre = np.asarray(inputs["ssm_b_re"], f)[0].transpose(1, 0, 2).reshape(64, 512)
    bim = np.asarray(inputs["ssm_b_im"], f)[0].transpose(1, 0, 2).reshape(64, 512)
    cre = np.asarray(inputs["ssm_c_re"], f)[0].transpose(2, 0, 1).reshape(64, 512)
    cim = np.asarray(inputs["ssm_c_im"], f)[0].transpose(2, 0, 1).reshape(64, 512)
    shared["sp2"] = np.ascontiguousarray(np.stack([bre, bim, cre, cim], axis=1).reshape(64, 2048))
    shared["w_qkv"] = np.ascontiguousarray(np.asarray(inputs["attn_w_qkv"], f)[0])
    shared["w_o"] = np.ascontiguousarray(np.asarray(inputs["attn_w_o"], f)[0])
    shared.update(_const_tables())
    in_maps = []
    for c in range(NCORES):
        m = dict(shared)
        m["xT"] = np.ascontiguousarray(x[2 * c:2 * c + 2].transpose(0, 2, 1))
        in_maps.append(m)
    return in_maps


def _const_tables():
    f = np.float32
    ident = np.eye(128, dtype=f)
    p = np.arange(128)
    bmask = (p[None, :] // 16 >= p[:, None] // 16).astype(f)
    sel = np.zeros((128, 8, 240), f)
    for g8 in range(8):
        for h in range(16):
            sel[16 * g8 + h, g8, 112 + h] = 1.0
    cb = np.zeros((128, 2, 256), f)
    for par in range(2):
        cb[:, par, :] = np.where((par * 128 + p[:, None]) <= np.arange(256)[None, :], 0.0, -30000.0)
    en = np.zeros((8, 8, 128), f)
    for n in range(8):
        en[n, n, :] = 1.0
    negm = np.zeros((128, 4, 64), f)
    for qb in range(4, 8):
        for h in range(8):
            negm[:, qb - 4, h * 8 + qb:h * 8 + 8] = -1e30
    return {"cbias": np.ascontiguousarray(cb.reshape(128, 512)), "en": np.ascontiguousarray(en.reshape(8, 1024)),
            "negm": np.ascontiguousarray(negm.reshape(128, 256)),
            "ident": ident, "bmask": bmask, "selin": np.ascontiguousarray(sel.reshape(128, 1920)),
            "selout": np.ascontiguousarray(sel.reshape(128, 1920))}


_NC_CACHE = {}


def kernel(**inputs):
    in_maps = _prep_inputs(inputs)
    if "nc" not in _NC_CACHE:
        _NC_CACHE["nc"] = build_program()
    nc = _NC_CACHE["nc"]
    res = run_bass_kernel_spmd(nc, in_maps, core_ids=list(range(NCORES)))
    out = np.empty((2 * NCORES, L, D), np.float32)
    for c in range(NCORES):
        out[2 * c:2 * c + 2] = np.asarray(res.results[c]["outT"]).transpose(0, 2, 1)
    return out
```

```python
import numpy as np
import concourse.bass as bass
import concourse.mybir as mybir
from concourse.bass_utils import run_bass_kernel_spmd

F32 = mybir.dt.float32
BF16 = mybir.dt.bfloat16
AF = mybir.ActivationFunctionType
ALU = mybir.AluOpType
AX = mybir.AxisListType

D = 1024
KC = 8
FF = 2816
FC = 22
L = 2048
NSEQ = 2
TT = 512
RMS_EPS = 1e-6
LN_EPS = 1e-5
NCORES = 8


class Buf:
    __slots__ = ("name", "w", "r", "dsem", "dcnt")

    def __init__(self, name):
        self.name = name
        self.w = None
        self.r = {}
        self.dsem = {}
        self.dcnt = {}


class Sched:
    def __init__(self, nc):
        self.nc = nc
        self.eng = {"pe": nc.tensor, "act": nc.scalar, "dve": nc.vector, "pool": nc.gpsimd, "sp": nc.sync}
        self.sem = {k: nc.alloc_semaphore("s_" + k) for k in self.eng}
        self.cnt = {k: 0 for k in self.eng}
        self.seen = {k: {} for k in self.eng}
        self.pr = {k: [] for k in self.eng}
        self.pw = {k: [] for k in self.eng}
        self.nds = 0
        self.ninst = 0
        self.dbufs = []

    def _deps(self, reads, writes):
        deps = {}

        def need(s, v):
            if deps.get(s, 0) < v:
                deps[s] = v

        for b in reads:
            if b.w is not None:
                need(*b.w)
        for b in writes:
            if b.w is not None:
                need(*b.w)
            for s, v in b.r.items():
                need(s, v)
        return deps

    def _wait(self, eng, deps):
        e = self.eng[eng]
        own = self.sem[eng]
        for s, v in deps.items():
            if s is own and eng == "pe":
                continue
            if self.seen[eng].get(s, 0) >= v:
                continue
            e.wait_ge(s, v)
            self.seen[eng][s] = v

    def op(self, eng, fn, reads=(), writes=(), signal=True):
        self._wait(eng, self._deps(reads, writes))
        ins = fn(self.eng[eng])
        self.ninst += 1
        self.pr[eng].extend(reads)
        self.pw[eng].extend(writes)
        if signal:
            own = self.sem[eng]
            self.cnt[eng] += 1
            ins.then_inc(own, 1)
            v = self.cnt[eng]
            for b in self.pr[eng]:
                b.r[own] = v
            for b in self.pw[eng]:
                b.w = (own, v)
                b.r = {}
            self.pr[eng] = []
            self.pw[eng] = []
        return ins

    def dma(self, q, out, in_, reads=(), writes=()):
        self._wait(q, self._deps(reads, writes))
        owner = writes[0] if writes else reads[0]
        kind = "sw" if q == "pool" else "hw"
        if kind not in owner.dsem:
            owner.dsem[kind] = self.nc.alloc_semaphore("d%d" % self.nds)
            owner.dcnt[kind] = 0
            self.nds += 1
            self.dbufs.append((owner, kind))
        owner.dcnt[kind] += 1
        sem = owner.dsem[kind]
        self.eng[q].dma_start(out=out, in_=in_).then_inc(sem, 16)
        self.ninst += 1
        tag = (sem, 16 * owner.dcnt[kind])
        for b in reads:
            b.r[tag[0]] = tag[1]
        for b in writes:
            b.w = tag
            b.r = {}

    def barrier(self):
        for k in self.eng:
            assert not self.pr[k] and not self.pw[k], "unsignalled ops pending on " + k
        for k in self.eng:
            deps = {self.sem[o]: self.cnt[o] for o in self.eng if o != k and self.cnt[o] > 0}
            for b, kind in self.dbufs:
                deps[b.dsem[kind]] = 16 * b.dcnt[kind]
            self._wait(k, deps)

    def wait_all(self, eng, bufs):
        deps = {}
        for b in bufs:
            if b.w is not None:
                deps[b.w[0]] = max(deps.get(b.w[0], 0), b.w[1])
            for s, v in b.r.items():
                deps[s] = max(deps.get(s, 0), v)
        self._wait(eng, deps)


class Arena:
    def __init__(self, nc, nbytes):
        self.t = nc.alloc_sbuf_tensor("arena", [128, nbytes // 2], BF16)
        self.nbytes = nbytes
        self.off = 0
        self.marks = []

    def alloc(self, shape, dtype, at=None):
        esz = 2 if dtype == BF16 else 4
        n = int(np.prod(shape[1:]))
        nb = n * esz
        off = self.off if at is None else at
        off = (off + 31) // 32 * 32
        assert off + nb <= self.nbytes, ("arena overflow", off, nb, self.nbytes)
        if at is None:
            self.off = off + nb
        ap = self.t[0:shape[0], off // 2:(off + nb) // 2]
        if dtype != BF16:
            ap = ap.bitcast(dtype)
        if len(shape) == 3:
            ap = ap.rearrange("p (a b) -> p a b", a=shape[1])
        elif len(shape) == 4:
            ap = ap.rearrange("p (a b c) -> p a b c", a=shape[1], b=shape[2])
        return ap


def build_program(stages=("ffn00", "mix0", "ffn01", "ffn10", "mix1", "ffn11", "final"), nseq=NSEQ, debug=False):
    nc = bass.Bass("TRN2", target_bir_lowering=False)
    S = Sched(nc)
    dbg_names = []

    def dump(name, ap2d, bufs):
        if not debug or name in dbg_names:
            return
        dbg_names.append(name)
        shp = list(ap2d.shape)
        dt = nc.dram_tensor("dbg_" + name, shp, F32, kind="ExternalOutput").ap()
        S.dma("pool", dt, ap2d, reads=bufs)

    def din(name, shape, dt=F32):
        return nc.dram_tensor(name, list(shape), dt, kind="ExternalInput").ap()

    xT_d = din("xT", [NSEQ, D, L])
    gains_d = din("gains", [128, 7 * KC])
    w1_d = din("w1", [4, D, FF])
    w3_d = din("w3", [4, D, FF])
    w2_d = din("w2", [4, FF, D])
    outT_d = nc.dram_tensor("outT", [NSEQ, D, L], F32, kind="ExternalOutput").ap()

    A = Arena(nc, 207 * 1024)
    xT = A.alloc([128, KC, L], F32)
    gains = A.alloc([128, 7, KC], F32)
    ones_bf = A.alloc([128, 128], BF16)
    epsc = A.alloc([128, 1], F32)
    cw = A.alloc([128, 4, 31], F32)
    chv = A.alloc([128, 5, 4], F32)
    lnepsc = A.alloc([128, 1], F32)
    xsq = [A.alloc([128, TT], BF16) for _ in range(2)]
    rstd = A.alloc([128, TT], F32)
    sil = [A.alloc([128, TT], BF16) for _ in range(2)]
    base_off = A.off
    hT = A.alloc([128, KC, 1024], BF16)
    G = A.alloc([128, FC, 1024], BF16)
    NS13 = 3
    w13 = [A.alloc([128, 2, KC, 256], BF16) for _ in range(NS13)]
    w2s = A.alloc([128, FC, D], BF16)
    ffn_end = A.off

    PS = [nc.alloc_psum_tensor("ps%d" % i, [128, TT], F32) for i in range(8)]
    PSB = [Buf("ps%d" % i) for i in range(8)]

    XB = [[Buf("x%d_%d" % (k, b)) for b in range(L // TT)] for k in range(KC)]
    HB = [[Buf("h%d_%d" % (k, t)) for t in range(2)] for k in range(KC)]
    GB = [[Buf("g%d_%d" % (f, t)) for t in range(2)] for f in range(FC)]
    W13B = [(Buf("w1_%d" % i), Buf("w3_%d" % i)) for i in range(NS13)]
    W2B = [Buf("w2s%d" % i) for i in range(FC // 2)]
    XSQB = [Buf("xsq%d" % i) for i in range(2)]
    RSTDB = Buf("rstd")
    SILB = [Buf("sil%d" % i) for i in range(2)]
    CONSTB = Buf("const")

    S.dma("sp", gains.rearrange("p a b -> p (a b)"), gains_d, writes=[CONSTB])
    S.op("dve", lambda e: e.memset(ones_bf, 1.0), writes=[CONSTB])
    S.op("dve", lambda e: e.memset(epsc, RMS_EPS), writes=[CONSTB])

    w13_state = {"n": 0}

    def load_w13(fi, fg):
        i = w13_state["n"] % NS13
        w13_state["n"] += 1
        slot = w13[i]
        b = W13B[i]
        src1 = w1_d[fi].rearrange("(kc p) f -> p kc f", p=128)[:, :, fg * 256:(fg + 1) * 256]
        src3 = w3_d[fi].rearrange("(kc p) f -> p kc f", p=128)[:, :, fg * 256:(fg + 1) * 256]
        S.dma("pool", slot[:, 0], src1, writes=[b[0]])
        S.dma("pool", slot[:, 1], src3, writes=[b[1]])
        return slot, b

    def load_w2(fi, j):
        src = w2_d[fi].rearrange("(fc p) d -> p fc d", p=128)
        S.dma("pool", w2s[:, 2 * j:2 * j + 2], src[:, 2 * j:2 * j + 2], writes=[W2B[j]])

    def rmsnorm_tile(gidx, t0, ntt, dstT, dstB, inplace=False):
        for tt in range(ntt):
            blk = (t0 + tt * TT) // TT
            ps = PS[6]
            psb = PSB[6]
            for kc in range(KC):
                q = dstT[:, kc, tt * TT:(tt + 1) * TT]
                xs = xT[:, kc, blk * TT:(blk + 1) * TT]
                S.op("act", lambda e, q=q, xs=xs: e.activation(out=q, in_=xs, func=AF.Square),
                     reads=[XB[kc][blk], CONSTB], writes=[dstB[kc][tt]])
            for kc in range(KC):
                q = dstT[:, kc, tt * TT:(tt + 1) * TT]
                S.op("pe", lambda e, q=q, kc=kc: e.matmul(ps[:], ones_bf, q, start=(kc == 0), stop=(kc == KC - 1)),
                     reads=[dstB[kc][tt], CONSTB], writes=[psb], signal=(kc == KC - 1))
            S.op("act", lambda e: e.activation(out=rstd, in_=ps[:], func=AF.Sqrt, scale=1.0 / D, bias=epsc),
                 reads=[psb, CONSTB], writes=[RSTDB])
            S.op("dve", lambda e: e.reciprocal(out=rstd, in_=rstd), reads=[RSTDB], writes=[RSTDB])
            for kc in range(KC):
                xs = xT[:, kc, blk * TT:(blk + 1) * TT]
                if inplace:
                    S.op("dve", lambda e, kc=kc, xs=xs: e.scalar_tensor_tensor(
                        out=xs, in0=xs, scalar=gains[:, gidx, kc:kc + 1], in1=rstd, op0=ALU.mult, op1=ALU.mult),
                        reads=[XB[kc][blk], RSTDB, CONSTB, dstB[kc][tt]], writes=[XB[kc][blk]])
                else:
                    S.op("dve", lambda e, kc=kc, xs=xs: e.scalar_tensor_tensor(
                        out=dstT[:, kc, tt * TT:(tt + 1) * TT], in0=xs,
                        scalar=gains[:, gidx, kc:kc + 1], in1=rstd, op0=ALU.mult, op1=ALU.mult),
                        reads=[XB[kc][blk], RSTDB, CONSTB], writes=[dstB[kc][tt]])

    def ffn_tile(fi, gidx, t0, do_pre=True, hook=None):
        if do_pre:
            rmsnorm_tile(gidx, t0, 2, hT, HB)
        pcount = 0
        slots = {}
        for fg in range(min(NS13, FC // 2)):
            slots[fg] = load_w13(fi, fg)
        for fg in range(FC // 2):
            slot, sb = slots.pop(fg)
            for fh in range(2):
                fc = fg * 2 + fh
                for tt in range(2):
                    pa = (pcount % 2) * 2
                    pcount += 1
                    for j, (pi, wi) in enumerate(((pa, 0), (pa + 1, 1))):
                        for kc in range(KC):
                            S.op("pe", lambda e, pi=pi, wi=wi, kc=kc: e.matmul(
                                PS[pi][:], slot[:, wi, kc, fh * 128:(fh + 1) * 128], hT[:, kc, tt * TT:(tt + 1) * TT],
                                start=(kc == 0), stop=(kc == KC - 1)),
                                reads=[sb[wi], HB[kc][tt]], writes=[PSB[pi]], signal=(kc == KC - 1))
                    sl = sil[pcount % 2]
                    slb = SILB[pcount % 2]
                    S.op("act", lambda e, sl=sl, pa=pa: e.activation(out=sl, in_=PS[pa][:], func=AF.Silu),
                         reads=[PSB[pa]], writes=[slb])
                    S.op("dve", lambda e, sl=sl, pa=pa, fc=fc, tt=tt: e.tensor_tensor(
                        out=G[:, fc, tt * TT:(tt + 1) * TT], in0=sl, in1=PS[pa + 1][:], op=ALU.mult),
                        reads=[slb, PSB[pa + 1]], writes=[GB[fc][tt]])
            if fg + NS13 < FC // 2:
                slots[fg + NS13] = load_w13(fi, fg + NS13)
            load_w2(fi, fg)
        pcount = 0
        for dc in range(KC):
            for tt in range(2):
                pi = 4 + (pcount % 2)
                pcount += 1
                blk = (t0 + tt * TT) // TT
                for fc in range(FC):
                    S.op("pe", lambda e, pi=pi, fc=fc, dc=dc, tt=tt: e.matmul(
                        PS[pi][:], w2s[:, fc, dc * 128:(dc + 1) * 128], G[:, fc, tt * TT:(tt + 1) * TT],
                        start=(fc == 0), stop=(fc == FC - 1)),
                        reads=[W2B[fc // 2], GB[fc][tt]], writes=[PSB[pi]], signal=(fc == FC - 1))
                xs = xT[:, dc, blk * TT:(blk + 1) * TT]
                S.op("dve", lambda e, pi=pi, xs=xs: e.scalar_tensor_tensor(
                    out=xs, in0=PS[pi][:], scalar=0.5, in1=xs, op0=ALU.mult, op1=ALU.add),
                    reads=[PSB[pi], XB[dc][blk]], writes=[XB[dc][blk]])
            if dc == 1 and hook is not None:
                hook()

    def ffn_chain(jobs, last_hook=None):
        for j, (fi, gidx, t0) in enumerate(jobs):
            nxt = jobs[j + 1] if j + 1 < len(jobs) else None
            hook = last_hook
            if nxt is not None:
                assert nxt[2] != t0
                hook = (lambda nxt=nxt: rmsnorm_tile(nxt[1], nxt[2], 2, hT, HB))
            ffn_tile(fi, gidx, t0, do_pre=(j == 0), hook=hook)

    RB = base_off
    KB = 1024

    def R(off_kb, shape, dt):
        return A.alloc(shape, dt, at=RB + int(off_kb * KB))

    I32 = mybir.dt.int32
    win_d = din("w_in", [D, 1536])
    wout_d = din("w_out", [D, D])
    gluw_d = din("glu_w", [512, 512])
    cw_d = din("cw", [128, 4 * 31])
    chv_d = din("chv", [128, 5 * 4])
    sp1_d = din("sp1", [64, 3 * 32])
    sp2_d = din("sp2", [64, 4 * 512])
    ident_d = din("ident", [128, 128])
    bmask_d = din("bmask", [128, 128])
    selin_d = din("selin", [128, 8 * 240])
    selout_d = din("selout", [128, 8 * 240])
    scr_toep = nc.dram_tensor("scr_toep", [128, 32 * 128], BF16, kind="Internal").ap()
    scr_wb = nc.dram_tensor("scr_wb", [128, 2 * 32 * 64], BF16, kind="Internal").ap()
    scr_cm = nc.dram_tensor("scr_cm", [64, 2 * 32 * 128], BF16, kind="Internal").ap()
    scr_at = nc.dram_tensor("scr_at", [64, 128], F32, kind="Internal").ap()

    M0CB = Buf("m0const")
    S.dma("sp", cw.rearrange("p a b -> p (a b)"), cw_d, writes=[M0CB])
    S.dma("sp", chv.rearrange("p a b -> p (a b)"), chv_d, writes=[M0CB])
    S.op("dve", lambda e: e.memset(lnepsc, LN_EPS), writes=[M0CB])

    uT = R(0, [128, 4, L], BF16)
    uT4 = R(0, [128, 4, 8, 256], BF16)
    mcat = R(16, [128, 8, L], BF16)
    Xbf = R(32, [128, 2, 16, 256], BF16)
    woutS = R(116, [128, 8, D], BF16)
    hT2 = R(64, [128, KC, L], BF16)
    winS = R(96, [128, KC, 1536], BF16)
    vpad = R(47.5, [128, 4, 30 + L], BF16)
    ycv = R(64, [128, 4, L], F32)
    lnt = [R(96 + 2 * i, [128, TT], F32) for i in range(5)]
    selin = R(130, [128, 8, 240], BF16)
    selout = R(68, [128, 8, 240], BF16)
    gluwS = R(72, [128, 4, 512], BF16)
    toepS = R(76, [128, 32, 128], BF16)
    wbS = R(84, [128, 2, 32, 64], BF16)
    cmS = R(92, [128, 2, 16, 128], BF16)
    atS = R(108, [128, 2, 2, 16], F32)
    Xh = R(109, [128, 32, 2, 16], F32)
    pq = R(113, [128, 2, 2, 16], F32)
    rtmp = R(114, [128, 2, 16], F32)
    Ut = R(48, [128, 32, 256], BF16)
    gt = [R(76 + 2 * i, [128, TT], F32) for i in range(6)]
    y1bS = R(92, [128, 4, L], BF16)
    sgt = [R(88 + i, [128, TT], BF16) for i in range(2)]

    diag = [R(o_, [128, 31, 128], BF16) for o_ in (32, 39.75, 114, 121.75)]
    identM = R(135.25, [128, 128], BF16)
    DIAGB = [Buf("diag%d" % i) for i in range(4)]
    IDMB = Buf("identM")
    UB = [[Buf("u%d_%d" % (c, t)) for t in range(4)] for c in range(4)]
    MCB = [[Buf("mc%d_%d" % (c, t)) for t in range(4)] for c in range(8)]
    H2B = [[Buf("h2_%d_%d" % (k, t)) for t in range(4)] for k in range(KC)]
    WINB = [Buf("win%d" % i) for i in range(3)]
    WOUTB = Buf("woutS")
    VB = [Buf("v%d" % c) for c in range(4)]
    YCB = [Buf("yc%d" % c) for c in range(4)]
    LNTB = [Buf("lnt%d" % i) for i in range(5)]
    PPB = Buf("ssmparams")
    SELIB = Buf("selin"); WBB = Buf("wbS"); TOEPB = Buf("toepS")
    CMB = [Buf("cm%d" % i) for i in range(4)]
    ATB = [Buf("at%d" % i) for i in range(8)]
    UTB = [Buf("ut%d" % g) for g in range(32)]
    XBFB = Buf("xbf")
    XHB = [Buf("xh%d" % i) for i in range(32)]
    PQB = Buf("pq")
    RTB = Buf("rtmp")
    GTB = [Buf("gt%d" % i) for i in range(6)]
    Y1B = [[Buf("y1_%d_%d" % (c, t)) for t in range(4)] for c in range(4)]
    SGTB = [Buf("sgt%d" % i) for i in range(2)]

    def ssm_param_prep():
        S.barrier()
        P = PPB
        cnt = {"o": 0}

        def T(shape, dt=F32):
            n = int(np.prod(shape[1:])) * (4 if dt != BF16 else 2)
            off = cnt["o"]
            cnt["o"] = (off + n + 31) // 32 * 32
            assert RB + cnt["o"] <= A.nbytes, cnt["o"]
            return A.alloc(shape, dt, at=RB + off)

        def v_tt(out, a, b, op):
            S.op("dve", lambda e: e.tensor_tensor(out=out, in0=a, in1=b, op=op), reads=[P], writes=[P])

        def v_ts(out, a, s1, s2, op0, op1=None):
            if op1 is None:
                S.op("dve", lambda e: e.tensor_scalar(out=out, in0=a, scalar1=s1, scalar2=None, op0=op0), reads=[P], writes=[P])
            else:
                S.op("dve", lambda e: e.tensor_scalar(out=out, in0=a, scalar1=s1, scalar2=s2, op0=op0, op1=op1), reads=[P], writes=[P])

        def a_act(out, a, func, scale=1.0):
            S.op("act", lambda e: e.activation(out=out, in_=a, func=func, scale=scale), reads=[P], writes=[P])

        def v_cp(out, a):
            S.op("dve", lambda e: e.tensor_copy(out=out, in_=a), reads=[P], writes=[P])

        sp1 = T([64, 3, 32]); sp2 = T([64, 4, 512])
        S.dma("act", sp1.rearrange("p a b -> p (a b)"), sp1_d, writes=[P])
        S.dma("act", sp2.rearrange("p a b -> p (a b)"), sp2_d, writes=[P])
        identS = T([128, 128], BF16); bmaskS = T([128, 128])
        S.dma("pool", identS, ident_d, writes=[P])
        S.dma("act", bmaskS, bmask_d, writes=[P])
        ldt, are, aim = sp1[:, 0], sp1[:, 1], sp1[:, 2]
        bre = sp2[:, 0].rearrange("p (g h) -> p g h", g=32)
        bim = sp2[:, 1].rearrange("p (g h) -> p g h", g=32)
        cre = sp2[:, 2].rearrange("p (g h) -> p g h", g=32)
        cim = sp2[:, 3].rearrange("p (g h) -> p g h", g=32)
        dt_ = T([64, 32]); mag = T([64, 32]); ang = T([64, 32]); t1 = T([64, 32]); t2 = T([64, 32])
        ti = T([64, 32], I32); cosv = T([64, 32]); sinv = T([64, 32])
        a_act(dt_, ldt, AF.Exp)
        v_tt(t1, dt_, are, ALU.mult)
        a_act(mag, t1, AF.Exp)
        v_tt(ang, dt_, aim, ALU.mult)
        TWO_PI = 2.0 * np.pi

        def sin_of(out, shift):
            v_ts(t1, ang, 1.0 / TWO_PI, float(shift), ALU.mult, ALU.add)
            v_cp(ti, t1)
            v_cp(t2, ti)
            v_tt(t1, t1, t2, ALU.subtract)
            v_ts(t2, t1, 0.5, None, ALU.is_gt)
            v_tt(t1, t1, t2, ALU.subtract)
            v_ts(t2, t1, -0.5, None, ALU.is_lt)
            v_tt(t1, t1, t2, ALU.add)
            a_act(out, t1, AF.Sin, scale=TWO_PI)

        dump("dt", dt_, [P]); dump("mag", mag, [P]); dump("ang", ang, [P])
        sin_of(sinv, 0.0)
        dump("frac_s", t1, [P]); dump("sinv", sinv, [P])
        sin_of(cosv, 0.25)
        dump("cosv", cosv, [P])
        pwr = T([64, 9, 32]); pwi = T([64, 9, 32])
        S.op("dve", lambda e: e.memset(pwr[:, 0], 1.0), reads=[P], writes=[P])
        S.op("dve", lambda e: e.memset(pwi[:, 0], 0.0), reads=[P], writes=[P])
        v_tt(pwr[:, 1], mag, cosv, ALU.mult)
        v_tt(pwi[:, 1], mag, sinv, ALU.mult)
        for k in range(1, 8):
            v_tt(t1, pwr[:, k], pwr[:, 1], ALU.mult)
            v_tt(t2, pwi[:, k], pwi[:, 1], ALU.mult)
            v_tt(pwr[:, k + 1], t1, t2, ALU.subtract)
            v_tt(t1, pwr[:, k], pwi[:, 1], ALU.mult)
            v_tt(t2, pwi[:, k], pwr[:, 1], ALU.mult)
            v_tt(pwi[:, k + 1], t1, t2, ALU.add)
        dump("pwr", pwr.rearrange("p k g -> p (k g)"), [P]); dump("pwi", pwi.rearrange("p k g -> p (k g)"), [P])
        ipr = T([64, 9, 32]); ipi = T([64, 9, 32]); n2 = T([64, 9, 32]); n3 = T([64, 9, 32])
        v_tt(n2, pwr, pwr, ALU.mult)
        v_tt(n3, pwi, pwi, ALU.mult)
        v_tt(n2, n2, n3, ALU.add)
        S.op("dve", lambda e: e.reciprocal(out=n2, in_=n2), reads=[P], writes=[P])
        v_tt(ipr, pwr, n2, ALU.mult)
        v_tt(ipi, pwi, n2, ALU.mult)
        v_ts(ipi, ipi, -1.0, None, ALU.mult)
        den = T([64, 32]); nr = T([64, 32]); qre = T([64, 32]); qim = T([64, 32])
        v_tt(den, are, are, ALU.mult)
        v_tt(t1, aim, aim, ALU.mult)
        v_tt(den, den, t1, ALU.add)
        S.op("dve", lambda e: e.reciprocal(out=den, in_=den), reads=[P], writes=[P])
        v_ts(nr, pwr[:, 1], -1.0, None, ALU.add)
        v_tt(t1, nr, are, ALU.mult)
        v_tt(t2, pwi[:, 1], aim, ALU.mult)
        v_tt(t1, t1, t2, ALU.add)
        v_tt(qre, t1, den, ALU.mult)
        v_tt(t1, pwi[:, 1], are, ALU.mult)
        v_tt(t2, nr, aim, ALU.mult)
        v_tt(t1, t1, t2, ALU.subtract)
        v_tt(qim, t1, den, ALU.mult)
        Bre = T([64, 32, 16]); Bim = T([64, 32, 16]); w1_ = T([64, 32, 16]); w2_ = T([64, 32, 16])
        qre_b = qre.unsqueeze(2).to_broadcast([64, 32, 16])
        qim_b = qim.unsqueeze(2).to_broadcast([64, 32, 16])
        v_tt(w1_, bre, qre_b, ALU.mult); v_tt(w2_, bim, qim_b, ALU.mult); v_tt(Bre, w1_, w2_, ALU.subtract)
        v_tt(w1_, bim, qre_b, ALU.mult); v_tt(w2_, bre, qim_b, ALU.mult); v_tt(Bim, w1_, w2_, ALU.add)
        big_off = cnt["o"]
        big1 = T([64, 32, 8, 16]); big2 = T([64, 32, 8, 16])
        Cmr = T([64, 32, 8, 16]); Cmi = T([64, 32, 8, 16])
        q0 = cnt["o"]
        Bsr = T([64, 32, 8, 16], BF16); Bsi = T([64, 32, 8, 16], BF16)
        cmr_bf = T([64, 32, 8, 16], BF16); cmi_bf = T([64, 32, 8, 16], BF16)
        Bmr = A.alloc([64, 32, 8, 16], F32, at=RB + q0); Bmi = A.alloc([64, 32, 8, 16], F32, at=RB + q0 + 16 * KB)

        def bc_x(x):
            return x.unsqueeze(2).to_broadcast([64, 32, 8, 16])

        def bc_p(p, lo, hi, rev=False):
            sl = p[:, lo:hi, :].rearrange("p k g -> p g k")
            return sl.unsqueeze(3).to_broadcast([64, 32, 8, 16])

        def cmul_big(o_re, o_im, xr, xi, pr, pi, neg_im=False):
            v_tt(big1, bc_x(xr), pr, ALU.mult); v_tt(big2, bc_x(xi), pi, ALU.mult)
            v_tt(o_re, big1, big2, ALU.subtract)
            v_tt(big1, bc_x(xr), pi, ALU.mult); v_tt(big2, bc_x(xi), pr, ALU.mult)
            if neg_im:
                v_tt(o_im, big1, big2, ALU.add)
                v_ts(o_im, o_im, -1.0, None, ALU.mult)
            else:
                v_tt(o_im, big1, big2, ALU.add)

        cmul_big(Cmr, Cmi, cre, cim, bc_p(pwr, 1, 9), bc_p(pwi, 1, 9), neg_im=True)
        v_cp(cmr_bf, Cmr); v_cp(cmi_bf, Cmi)
        S.dma("sp", scr_cm[:, 0:4096], cmr_bf.rearrange("p g j h -> p (g j h)"), reads=[P])
        S.dma("sp", scr_cm[:, 4096:8192], cmi_bf.rearrange("p g j h -> p (g j h)"), reads=[P])
        pwr_rev = T([64, 8, 32]); pwi_rev = T([64, 8, 32])
        for i in range(8):
            v_cp(pwr_rev[:, i], pwr[:, 7 - i]); v_cp(pwi_rev[:, i], pwi[:, 7 - i])
        prr = pwr_rev.rearrange("p k g -> p g k").unsqueeze(3).to_broadcast([64, 32, 8, 16])
        pri = pwi_rev.rearrange("p k g -> p g k").unsqueeze(3).to_broadcast([64, 32, 8, 16])
        cmul_big(Bsr, Bsi, Bre, Bim, prr, pri)
        wb_bf = T([128, 2, 32, 64], BF16)
        for o, src in ((0, Bsr), (1, Bsi)):
            for g8 in range(4):
                ps = PS[7]
                for gg in range(8):
                    g = g8 * 8 + gg
                    S.op("pe", lambda e, g=g, gg=gg, src=src: e.matmul(ps[:, gg * 64:(gg + 1) * 64], src[:, g].rearrange("p i h -> p (i h)"),
                                                                       identS[0:64, 0:64], start=True, stop=True),
                         reads=[P], writes=[PSB[7]], signal=(gg == 7))
                S.op("dve", lambda e, o=o, g8=g8: e.tensor_copy(out=wb_bf[:, o, g8 * 8:(g8 + 1) * 8, :],
                                                               in_=ps[:].rearrange("p (a b) -> p a b", a=8)), reads=[PSB[7], P], writes=[P])
        S.dma("sp", scr_wb, wb_bf.rearrange("p o g n -> p (o g n)"), reads=[P])
        at_ = T([64, 2, 2, 32])
        v_cp(at_[:, 0, 0], pwr[:, 8]); v_cp(at_[:, 1, 1], pwr[:, 8]); v_cp(at_[:, 1, 0], pwi[:, 8])
        v_ts(at_[:, 0, 1], pwi[:, 8], -1.0, None, ALU.mult)
        S.dma("sp", scr_at, at_.rearrange("p o k g -> p (o k g)"), reads=[P])
        cmul_big(Bmr, Bmi, Bre, Bim, bc_p(ipr, 1, 9), bc_p(ipi, 1, 9))
        toep_bf = A.alloc([128, 32, 128], BF16, at=RB + big_off)
        for g4 in range(8):
            ps = PS[7]
            for gg in range(4):
                g = g4 * 4 + gg
                S.op("pe", lambda e, g=g, gg=gg: e.matmul(ps[:, gg * 128:(gg + 1) * 128], Bmr[:, g].rearrange("p i h -> p (i h)"),
                                                          Cmr[:, g].rearrange("p j h -> p (j h)"), start=True, stop=False),
                     reads=[P], writes=[PSB[7]], signal=False)
                S.op("pe", lambda e, g=g, gg=gg: e.matmul(ps[:, gg * 128:(gg + 1) * 128], Bmi[:, g].rearrange("p i h -> p (i h)"),
                                                          Cmi[:, g].rearrange("p j h -> p (j h)"), start=False, stop=True),
                     reads=[P], writes=[PSB[7]], signal=(gg == 3))
            S.op("dve", lambda e, g4=g4: e.tensor_tensor(
                out=toep_bf[:, g4 * 4:(g4 + 1) * 4, :], in0=ps[:].rearrange("p (a b) -> p a b", a=4),
                in1=bmaskS.unsqueeze(1).to_broadcast([128, 4, 128]), op=ALU.mult), reads=[PSB[7], P], writes=[P])
        S.dma("sp", scr_toep, toep_bf.rearrange("p g m -> p (g m)"), reads=[P])
        S.barrier()

    def mixer0(s):
        S.barrier()
        for i in range(3):
            S.dma("pool", winS[:, :, i * 512:(i + 1) * 512],
                  win_d.rearrange("(kc p) f -> p kc f", p=128)[:, :, i * 512:(i + 1) * 512], writes=[WINB[i]])
        S.dma("pool", identM, ident_d, writes=[IDMB])
        diag_jobs = [(ct, k) for ct in (0, 1, 3) for k in range(31)]

        def diag_some(n):
            for _ in range(min(n, len(diag_jobs))):
                ct, k = diag_jobs.pop(0)
                S.op("dve", lambda e, ct=ct, k=k: e.tensor_scalar(out=diag[ct][:, k, :], in0=identM, scalar1=cw[:, ct, k:k + 1], scalar2=None,
                                                              op0=ALU.mult), reads=[IDMB, M0CB], writes=[DIAGB[ct]])
        rmsnorm_tile(4, 0, 4, hT2, H2B)
        S.op("pool", lambda e: e.memset(vpad[:, :, 0:30], 0.0), writes=VB)
        pc = 0
        for ct in range(4):
            for tb in range(4):
                pa, pg = (pc % 2) * 2, (pc % 2) * 2 + 1
                pc += 1
                for pi, oc in ((pa, ct), (pg, ct + 4)):
                    for kc in range(KC):
                        S.op("pe", lambda e, pi=pi, oc=oc, kc=kc, tb=tb: e.matmul(
                            PS[pi][:], winS[:, kc, oc * 128:(oc + 1) * 128], hT2[:, kc, tb * TT:(tb + 1) * TT],
                            start=(kc == 0), stop=(kc == KC - 1)),
                            reads=[WINB[oc // 4], H2B[kc][tb]], writes=[PSB[pi]], signal=(kc == KC - 1))
                sg = sil[pc % 2]
                S.op("act", lambda e, sg=sg, pg=pg: e.activation(out=sg, in_=PS[pg][:], func=AF.Sigmoid),
                     reads=[PSB[pg]], writes=[SILB[pc % 2]])
                S.op("dve", lambda e, sg=sg, pa=pa, ct=ct, tb=tb: e.tensor_tensor(
                    out=vpad[:, ct, 30 + tb * TT:30 + (tb + 1) * TT], in0=sg, in1=PS[pa][:], op=ALU.mult),
                    reads=[SILB[pc % 2], PSB[pa]], writes=[VB[ct]])
                diag_some(6)
        for ct in range(4):
            for tb in range(4):
                pi = 4 + (pc % 2)
                pc += 1
                oc = 8 + ct
                for kc in range(KC):
                    S.op("pe", lambda e, pi=pi, oc=oc, kc=kc, tb=tb: e.matmul(
                        PS[pi][:], winS[:, kc, oc * 128:(oc + 1) * 128], hT2[:, kc, tb * TT:(tb + 1) * TT],
                        start=(kc == 0), stop=(kc == KC - 1)),
                        reads=[WINB[2], H2B[kc][tb]], writes=[PSB[pi]], signal=(kc == KC - 1))
                S.op("act", lambda e, pi=pi, ct=ct, tb=tb: e.activation(out=uT4[:, ct, :, tb * 64:(tb + 1) * 64],
                                                                       in_=PS[pi][:].rearrange("p (c i) -> p i c", i=8), func=AF.Copy),
                     reads=[PSB[pi]], writes=[UB[ct][tb]])
        diag_some(len(diag_jobs))
        S.barrier()
        S.dma("pool", selin.rearrange("p a b -> p (a b)"), selin_d, writes=[SELIB])
        for ct in range(2, 3):
            for k in range(31):
                S.op("dve", lambda e, ct=ct, k=k: e.tensor_scalar(out=diag[ct][:, k, :], in0=identM, scalar1=cw[:, ct, k:k + 1], scalar2=None,
                                                              op0=ALU.mult), reads=[IDMB, M0CB], writes=[DIAGB[ct]])
        lnb = [R(96 + 2 * i, [128, TT], F32) for i in range(7)]
        ybf = [R(110 + i, [128, TT], BF16) for i in range(2)]
        ysq = [R(112 + i, [128, TT], BF16) for i in range(2)]
        LNB = [Buf("lnb%d" % i) for i in range(7)]
        YBB = [Buf("ybf%d" % i) for i in range(2)]
        YSB = [Buf("ysq%d" % i) for i in range(2)]
        YC2 = [[Buf("yc%d_%d" % (c, t)) for t in range(4)] for c in range(4)]
        tiles = [(tb, ct) for tb in range(4) for ct in range(4)]

        def conv_tile(n):
            tb, ct = tiles[n]
            pi = 2 + (n % 2)
            j = n % 2
            for k in range(31):
                S.op("pe", lambda e, pi=pi, ct=ct, k=k, tb=tb: e.matmul(
                    PS[pi][:], diag[ct][:, k, :], vpad[:, ct, k + tb * TT:k + (tb + 1) * TT], start=(k == 0), stop=(k == 30)),
                    reads=[DIAGB[ct], VB[ct]], writes=[PSB[pi]], signal=(k == 30))
            bia = chv[:, 0, ct:ct + 1]
            S.op("act", lambda e, pi=pi, ct=ct, tb=tb: e.activation(out=ycv[:, ct, tb * TT:(tb + 1) * TT], in_=PS[pi][:], func=AF.Identity, bias=bia),
                 reads=[PSB[pi], M0CB], writes=[YC2[ct][tb]])
            S.op("act", lambda e, pi=pi, j=j: e.activation(out=ybf[j], in_=PS[pi][:], func=AF.Identity, bias=bia),
                 reads=[PSB[pi], M0CB], writes=[YBB[j]])
            S.op("act", lambda e, pi=pi, j=j: e.activation(out=ysq[j], in_=PS[pi][:], func=AF.Square, bias=bia),
                 reads=[PSB[pi], M0CB], writes=[YSB[j]])

        def stat_mm(n):
            tb, ct = tiles[n]
            j = n % 2
            sb_, qb_ = ((0, 1), (4, 5))[tb % 2]
            S.op("pe", lambda e: e.matmul(PS[sb_][:], ones_bf, ybf[j], start=(ct == 0), stop=(ct == 3)),
                 reads=[YBB[j], CONSTB], writes=[PSB[sb_]], signal=True)
            S.op("pe", lambda e: e.matmul(PS[qb_][:], ones_bf, ysq[j], start=(ct == 0), stop=(ct == 3)),
                 reads=[YSB[j], CONSTB], writes=[PSB[qb_]], signal=True)

        def ln_stages(tb):
            sb_, qb_ = ((0, 1), (4, 5))[tb % 2]
            mean, var, msq = lnb[2 + tb % 2], lnb[4 + tb % 2], lnb[6]
            MB_, VB_, QB_ = LNB[2 + tb % 2], LNB[4 + tb % 2], LNB[6]
            sl = slice(tb * TT, (tb + 1) * TT)

            def st_a():
                S.op("act", lambda e: e.activation(out=mean, in_=PS[sb_][:], func=AF.Copy, scale=1.0 / 512), reads=[PSB[sb_]], writes=[MB_])
                S.op("act", lambda e: e.activation(out=msq, in_=PS[sb_][:], func=AF.Square, scale=1.0 / 512), reads=[PSB[sb_]], writes=[QB_])
                S.op("dve", lambda e: e.scalar_tensor_tensor(out=var, in0=PS[qb_][:], scalar=1.0 / 512, in1=msq, op0=ALU.mult, op1=ALU.subtract),
                     reads=[PSB[qb_], QB_], writes=[VB_])

            def st_b():
                S.op("act", lambda e: e.activation(out=var, in_=var, func=AF.Sqrt, bias=lnepsc), reads=[VB_, M0CB], writes=[VB_])
                S.op("dve", lambda e: e.reciprocal(out=var, in_=var), reads=[VB_], writes=[VB_])

            def t_ops(ct):
                t_ = lnb[ct % 2]
                TB_ = LNB[ct % 2]
                S.op("dve", lambda e: e.tensor_tensor(out=t_, in0=ycv[:, ct, sl], in1=mean, op=ALU.subtract),
                     reads=[YC2[ct][tb], MB_], writes=[TB_])
                S.op("dve", lambda e: e.tensor_tensor(out=t_, in0=t_, in1=var, op=ALU.mult),
                     reads=[TB_, VB_], writes=[TB_])

            def silu(ct):
                t_ = lnb[ct % 2]
                S.op("act", lambda e: e.activation(out=mcat[:, ct, sl], in_=t_, func=AF.Silu,
                                                   scale=chv[:, 1, ct:ct + 1], bias=chv[:, 2, ct:ct + 1]),
                     reads=[LNB[ct % 2], M0CB], writes=[MCB[ct][tb]])

            return [st_a, st_b, lambda: (t_ops(0), t_ops(1)), lambda: (silu(0), silu(1), t_ops(2), t_ops(3)), lambda: (silu(2), silu(3))]

        pending = []
        conv_tile(0)
        for n in range(len(tiles)):
            if n + 1 < len(tiles):
                conv_tile(n + 1)
            stat_mm(n)
            for st in pending:
                if st:
                    st.pop(0)()
            if tiles[n][1] == 3:
                pending.append(ln_stages(tiles[n][0]))
        while any(pending):
            for st in pending:
                if st:
                    st.pop(0)()
        dump("mcA", mcat[:, 0:4, :].rearrange("p a b -> p (a b)"), [b for r in MCB[0:4] for b in r])
        dump("u", uT.rearrange("p a b -> p (a b)"), [b for r in UB for b in r])
        S.barrier()
        S.dma("sp", wbS.rearrange("p o g n -> p (o g n)"), scr_wb, writes=[WBB])
        S.dma("sp", toepS.rearrange("p g m -> p (g m)"), scr_toep, writes=[TOEPB])
        S.dma("pool", selout.rearrange("p a b -> p (a b)"), selout_d, writes=[PPB])
        S.dma("pool", gluwS, gluw_d.rearrange("(kc p) f -> p kc f", p=128), writes=[PPB])
        scr_cm4 = scr_cm.rearrange("p (o g m) -> p o g m", o=2, g=32)
        scr_at4 = scr_at.rearrange("p (o k g) -> p o k g", o=2, k=2)
        for gh in range(2):
            for o in range(2):
                S.dma("sp", cmS[64 * gh:64 * gh + 64, o, :, :], scr_cm4[:, o, gh * 16:(gh + 1) * 16, :], writes=[CMB[gh * 2 + o]])
                for k in range(2):
                    S.dma("sp", atS[64 * gh:64 * gh + 64, o, k, :], scr_at4[:, o, k, gh * 16:(gh + 1) * 16], writes=[ATB[gh * 4 + o * 2 + k]])
        for g2 in range(16):
            pi = g2 % 2
            for gg in range(2):
                g = g2 * 2 + gg
                ct, g8 = g // 8, g % 8
                for i in range(8):
                    S.op("pe", lambda e, pi=pi, gg=gg, ct=ct, g8=g8, i=i: e.matmul(
                        PS[pi][:, gg * 256:(gg + 1) * 256], selin[:, g8, 112 - 16 * i:240 - 16 * i], uT4[:, ct, i, :],
                        start=(i == 0), stop=(i == 7)),
                        reads=[SELIB] + UB[ct], writes=[PSB[pi]], signal=(gg == 1 and i == 7))
            S.op("act", lambda e, pi=pi, g2=g2: e.activation(out=Ut[:, g2 * 2:g2 * 2 + 2, :],
                                                            in_=PS[pi][:].rearrange("p (a b) -> p a b", a=2), func=AF.Copy),
                 reads=[PSB[pi]], writes=[UTB[g2 * 2], UTB[g2 * 2 + 1]])
        S.op("dve", lambda e: e.memset(Xh[:, 31], 0.0), writes=[XHB[31]])
        for cb in range(16):
            pi = 2 + (cb % 2)
            psv = PS[pi][:].rearrange("p (c o g) -> p c o g", c=16, o=2)
            for g in range(32):
                gh, gl = g // 16, g % 16
                for o in range(2):
                    S.op("pe", lambda e, psv=psv, g=g, gh=gh, gl=gl, o=o, cb=cb: e.matmul(
                        psv[64 * gh:64 * gh + 64, :, o, gl], wbS[:, o, g, :], Ut[:, g, cb * 16:(cb + 1) * 16], start=True, stop=True),
                        reads=[WBB, UTB[g]], writes=[PSB[pi]], signal=(g == 31 and o == 1))
            for cc in range(16):
                c = cb * 16 + cc
                prev = Xh[:, (c - 1) % 32]
                S.op("dve", lambda e, prev=prev: e.tensor_tensor(
                    out=pq, in0=prev.unsqueeze(1).to_broadcast([128, 2, 2, 16]), in1=atS, op=ALU.mult),
                    reads=[XHB[(c - 1) % 32]] + ATB, writes=[PQB])
                S.op("dve", lambda e: e.tensor_tensor(out=rtmp, in0=pq[:, :, 0, :], in1=pq[:, :, 1, :], op=ALU.add),
                     reads=[PQB], writes=[RTB])
                S.op("dve", lambda e, psv=psv, cc=cc, c=c: e.tensor_tensor(out=Xh[:, c % 32], in0=rtmp, in1=psv[:, cc], op=ALU.add),
                     reads=[RTB, PSB[pi]], writes=[XHB[c % 32]])
            half = (cb % 2) * 16
            S.op("act", lambda e, cb=cb, half=half: e.activation(
                out=Xbf[:, :, :, cb * 16:(cb + 1) * 16], in_=Xh[:, half:half + 16].rearrange("p c o g -> p o g c"), func=AF.Copy),
                reads=XHB[half:half + 16], writes=[XBFB])
        for g2 in range(16):
            pi = 4 + (g2 % 2)
            for gg in range(2):
                g = g2 * 2 + gg
                S.op("pe", lambda e, pi=pi, gg=gg, g=g: e.matmul(PS[pi][:, gg * 256:(gg + 1) * 256], toepS[:, g, :], Ut[:, g, :],
                                                                start=True, stop=False),
                     reads=[TOEPB, UTB[g]], writes=[PSB[pi]], signal=False)
                hs = slice(64 * (g // 16), 64 * (g // 16) + 64)
                gl = g % 16
                S.op("pe", lambda e, pi=pi, gg=gg, gl=gl, hs=hs: e.matmul(PS[pi][:, gg * 256 + 1:(gg + 1) * 256], cmS[hs, 0, gl, :], Xbf[hs, 0, gl, 0:255],
                                                                start=False, stop=False),
                     reads=CMB + [XBFB], writes=[PSB[pi]], signal=False)
                S.op("pe", lambda e, pi=pi, gg=gg, gl=gl, hs=hs: e.matmul(PS[pi][:, gg * 256 + 1:(gg + 1) * 256], cmS[hs, 1, gl, :], Xbf[hs, 1, gl, 0:255],
                                                                start=False, stop=True),
                     reads=CMB + [XBFB], writes=[PSB[pi]], signal=(gg == 1))
            S.op("act", lambda e, pi=pi, g2=g2: e.activation(out=Ut[:, g2 * 2:g2 * 2 + 2, :],
                                                            in_=PS[pi][:].rearrange("p (a b) -> p a b", a=2), func=AF.Copy),
                 reads=[PSB[pi]], writes=[UTB[g2 * 2], UTB[g2 * 2 + 1]])
        dump("toep", toepS.rearrange("p g m -> p (g m)"), [TOEPB])
        dump("wb", wbS.rearrange("p o g n -> p (o g n)"), [WBB])
        dump("Y", Ut.rearrange("p g c -> p (g c)"), UTB)
        S.barrier()
        S.dma("pool", woutS, wout_d.rearrange("(kc p) f -> p kc f", p=128), writes=[WOUTB])
        pc = 0
        for ct in range(4):
            for tb in range(4):
                pi = pc % 2
                pc += 1
                for j in range(8):
                    for g8 in range(8):
                        S.op("pe", lambda e, pi=pi, j=j, g8=g8, ct=ct, tb=tb: e.matmul(
                            PS[pi][:, j * 64:(j + 1) * 64], selout[:, j, 112 - 16 * g8:240 - 16 * g8],
                            Ut[:, ct * 8 + g8, tb * 64:(tb + 1) * 64], start=(g8 == 0), stop=(g8 == 7)),
                            reads=[PPB, UTB[ct * 8 + g8]], writes=[PSB[pi]], signal=(j == 7 and g8 == 7))
                gb = 3 * (pc % 2)
                ys, x2, x3 = gt[gb], gt[gb + 1], gt[gb + 2]
                sg_ = x2
                GY, G2, G3 = GTB[gb], GTB[gb + 1], GTB[gb + 2]
                sl = slice(tb * TT, (tb + 1) * TT)
                S.op("dve", lambda e, pi=pi, ct=ct, sl=sl: e.scalar_tensor_tensor(
                    out=ys.rearrange("p (c j) -> p c j", j=8), in0=uT4[:, ct, :, tb * 64:(tb + 1) * 64].rearrange("p j c -> p c j"),
                    scalar=chv[:, 3, ct:ct + 1], in1=PS[pi][:].rearrange("p (j c) -> p c j", j=8), op0=ALU.mult, op1=ALU.add),
                    reads=[PSB[pi], UB[ct][tb], M0CB], writes=[GY])
                S.op("act", lambda e: e.activation(out=x2, in_=ys, func=AF.Square), reads=[GY], writes=[G2])
                S.op("dve", lambda e: e.tensor_scalar(out=x2, in0=x2, scalar1=0.044715, scalar2=1.0, op0=ALU.mult, op1=ALU.add),
                     reads=[G2], writes=[G2])
                S.op("dve", lambda e: e.tensor_tensor(out=x3, in0=x2, in1=ys, op=ALU.mult), reads=[G2, GY], writes=[G3])
                S.op("act", lambda e: e.activation(out=sg_, in_=x3, func=AF.Sigmoid, scale=1.5957691216057308), reads=[G3], writes=[G2])
                S.op("dve", lambda e, ct=ct, sl=sl: e.tensor_tensor(out=y1bS[:, ct, sl], in0=ys, in1=sg_, op=ALU.mult),
                     reads=[GY, G2], writes=[Y1B[ct][tb]])
        for ot in range(4):
            for tb in range(4):
                pi = 2 + (pc % 2)
                pc += 1
                sl = slice(tb * TT, (tb + 1) * TT)
                for ct in range(4):
                    S.op("pe", lambda e, pi=pi, ct=ct, ot=ot, sl=sl: e.matmul(
                        PS[pi][:], gluwS[:, ct, ot * 128:(ot + 1) * 128], y1bS[:, ct, sl], start=(ct == 0), stop=(ct == 3)),
                        reads=[PPB, Y1B[ct][tb]], writes=[PSB[pi]], signal=(ct == 3))
                sg = sgt[pc % 2]
                S.op("act", lambda e, pi=pi, sg=sg, ot=ot: e.activation(out=sg, in_=PS[pi][:], func=AF.Sigmoid, bias=chv[:, 4, ot:ot + 1]),
                     reads=[PSB[pi], M0CB], writes=[SGTB[pc % 2]])
                S.op("dve", lambda e, sg=sg, ot=ot, sl=sl: e.tensor_tensor(out=mcat[:, 4 + ot, sl], in0=y1bS[:, ot, sl], in1=sg, op=ALU.mult),
                     reads=[SGTB[pc % 2], Y1B[ot][tb]], writes=[MCB[4 + ot][tb]])
        dump("mcB", mcat[:, 4:8, :].rearrange("p a b -> p (a b)"), [b for r in MCB[4:8] for b in r])
        dump("y1", y1bS.rearrange("p a b -> p (a b)"), [b for r in Y1B for b in r])
        for dc in range(KC):
            for tb in range(4):
                pi = 4 + (pc % 2)
                pc += 1
                sl = slice(tb * TT, (tb + 1) * TT)
                for mc in range(8):
                    S.op("pe", lambda e, pi=pi, mc=mc, dc=dc, sl=sl: e.matmul(
                        PS[pi][:], woutS[:, mc, dc * 128:(dc + 1) * 128], mcat[:, mc, sl], start=(mc == 0), stop=(mc == 7)),
                        reads=[WOUTB, MCB[mc][tb]], writes=[PSB[pi]], signal=(mc == 7))
                xs = xT[:, dc, sl]
                S.op("dve", lambda e, pi=pi, xs=xs: e.tensor_tensor(out=xs, in0=PS[pi][:], in1=xs, op=ALU.add),
                     reads=[PSB[pi], XB[dc][tb]], writes=[XB[dc][tb]])
        S.barrier()

    wqkv_d = din("w_qkv", [D, 3 * D])
    wo_d = din("w_o", [D, D])
    cbias_d = din("cbias", [128, 2 * 256])
    en_d = din("en", [8, 8 * 128])
    negm_d = din("negm", [128, 4 * 64])
    NEG = -30000.0
    ahT = R(0, [128, KC, L], BF16)
    qring = [R(32 + 8 * i, [128, KC, 512], BF16) for i in range(2)]
    qT = R(48, [128, 4, L], BF16)
    kT = R(64, [128, 4, L], BF16)
    Vh = R(80, [128, 16, 512], BF16)
    oT = R(96, [128, 4, L], BF16)
    woS = R(112, [128, 4, D], BF16)
    pT = [R(120 + i, [128, 2, 256], BF16) for i in range(2)]
    cbiasS = R(122, [128, 2, 256], BF16)
    EnS = R(123, [8, 8, 128], BF16)
    identA = R(125, [128, 128], BF16)
    negmS = R(125.5, [128, 4, 64], F32)
    rden = R(126.5, [128, 256], F32)
    mbT = [R(127.5 + 2 * i, [8, 4, 256], BF16) for i in range(2)]
    kmf = R(131.5, [128, 4, 8], F32)
    kmT = R(131.75, [128, 4, 8], BF16)
    gsb = R(132, [128, 32], F32)
    top8 = R(132.25, [128, 8], F32)
    mball = R(132.5, [128, 8, 32], BF16)
    rden2 = R(133, [128, 256], F32)
    rdens = [rden, rden2]
    RDBS = [Buf("rden0"), Buf("rden1")]
    mbTall = R(32, [8, 4, 4, 256], BF16)
    assert RB + int(134 * KB) <= A.nbytes

    AHB = [[Buf("ah%d_%d" % (k, t)) for t in range(4)] for k in range(KC)]
    QRB = [Buf("qring%d" % i) for i in range(2)]
    QTB = [Buf("qT%d" % h) for h in range(4)]
    KTB = [Buf("kT%d" % h) for h in range(4)]
    VHB = [Buf("vh%d" % k) for k in range(16)]
    OTB = [[Buf("oT%d_%d" % (h, q)) for q in range(8)] for h in range(4)]
    WOB = Buf("woS")
    PTB = [Buf("pT%d" % i) for i in range(2)]
    ACB = Buf("attnconst")
    RDB = Buf("rden")
    MBTB = [Buf("mbT%d" % i) for i in range(2)]
    KMB = Buf("km")
    GSB = Buf("gsb")
    T8B = Buf("top8")
    MBB = Buf("mb")
    SCALE = 128.0 ** -0.5

    def mixer1(s):
        S.barrier()
        S.dma("pool", cbiasS.rearrange("p a b -> p (a b)"), cbias_d, writes=[ACB])
        S.dma("pool", EnS.rearrange("p a b -> p (a b)"), en_d, writes=[ACB])
        S.dma("pool", identA, ident_d, writes=[ACB])
        S.dma("sp", negmS.rearrange("p a b -> p (a b)"), negm_d, writes=[ACB])
        rmsnorm_tile(5, 0, 4, ahT, AHB)
        nring = [0]
        pcs = [0]

        def load_cols(c0):
            i = nring[0] % 2
            nring[0] += 1
            S.dma("pool", qring[i], wqkv_d.rearrange("(kc p) f -> p kc f", p=128)[:, :, c0:c0 + 512], writes=[QRB[i]])
            return qring[i], QRB[i]

        for half in range(2):
            for which, dstT, dstB in ((0, qT, QTB), (1, kT, KTB)):
                if which == 0 and half == 1:
                    wsl, wb_ = pre_q
                else:
                    wsl, wb_ = load_cols(which * D + half * 512)
                for hl in range(4):
                    for tb in range(4):
                        pi = pcs[0] % 2
                        pcs[0] += 1
                        for kc in range(KC):
                            S.op("pe", lambda e, pi=pi, wsl=wsl, kc=kc, hl=hl, tb=tb: e.matmul(
                                PS[pi][:], wsl[:, kc, hl * 128:(hl + 1) * 128], ahT[:, kc, tb * TT:(tb + 1) * TT],
                                start=(kc == 0), stop=(kc == KC - 1)),
                                reads=[wb_, AHB[kc][tb]], writes=[PSB[pi]], signal=(kc == KC - 1))
                        S.op("act", lambda e, pi=pi, dstT=dstT, hl=hl, tb=tb: e.activation(
                            out=dstT[:, hl, tb * TT:(tb + 1) * TT], in_=PS[pi][:], func=AF.Copy),
                            reads=[PSB[pi]], writes=[dstB[hl]])
            wsl, wb_ = load_cols(2 * D + half * 512)
            for kt in range(16):
                pi = pcs[0] % 2
                pcs[0] += 1
                tb = kt // 4
                for kc in range(KC):
                    S.op("pe", lambda e, pi=pi, wsl=wsl, kc=kc, kt=kt: e.matmul(
                        PS[pi][:], ahT[:, kc, kt * 128:(kt + 1) * 128], wsl[:, kc, :], start=(kc == 0), stop=(kc == KC - 1)),
                        reads=[wb_, AHB[kc][tb]], writes=[PSB[pi]], signal=(kc == KC - 1))
                S.op("act", lambda e, pi=pi, kt=kt: e.activation(out=Vh[:, kt, :], in_=PS[pi][:], func=AF.Copy),
                     reads=[PSB[pi]], writes=[VHB[kt]])
            S.dma("pool", woS, wo_d.rearrange("(kc p) f -> p kc f", p=128)[:, half * 4:(half + 1) * 4, :], writes=[WOB])
            if half == 0:
                pre_q = load_cols(0 * D + 1 * 512)
            for hl in range(4):
                S.op("dve", lambda e, hl=hl: e.tensor_reduce(out=kmf[:, hl, :], in_=kT[:, hl, :].rearrange("p (n k) -> p n k", n=8),
                                                            axis=AX.X, op=ALU.add), reads=[KTB[hl]], writes=[KMB])
            S.op("dve", lambda e: e.tensor_copy(out=kmT, in_=kmf), reads=[KMB], writes=[KMB])
            for qb in range(4, 8):
                for q2 in range(2):
                    idx = (qb - 4) * 2 + q2
                    qsl = slice(qb * 256 + q2 * 128, qb * 256 + (q2 + 1) * 128)
                    for hl in range(4):
                        S.op("pe", lambda e, hl=hl, qsl=qsl, idx=idx: e.matmul(PS[6][:, idx * 32 + hl * 8:idx * 32 + (hl + 1) * 8], qT[:, hl, qsl],
                                                                              kmT[:, hl, :], start=True, stop=True),
                             reads=[QTB[hl], KMB], writes=[PSB[6]], signal=(hl == 3))
            def mask_job(qb, q2):
                idx = (qb - 4) * 2 + q2
                S.op("dve", lambda e: e.tensor_tensor(out=gsb, in0=PS[6][:, idx * 32:(idx + 1) * 32], in1=negmS[:, qb - 4, 0:32], op=ALU.add),
                     reads=[PSB[6], ACB], writes=[GSB])
                for hl in range(4):
                    S.op("dve", lambda e, hl=hl: e.max(out=top8, in_=gsb[:, hl * 8:(hl + 1) * 8]), reads=[GSB], writes=[T8B])
                    S.op("dve", lambda e, hl=hl: e.tensor_scalar(out=mball[:, idx, hl * 8:(hl + 1) * 8], in0=gsb[:, hl * 8:(hl + 1) * 8],
                                                                scalar1=top8[:, 2:3], scalar2=NEG, op0=ALU.is_lt, op1=ALU.mult),
                         reads=[GSB, T8B], writes=[MBB])

            mask_jobs = [(qb, q2) for qb in range(4, 8) for q2 in range(2)]

            def mask_transposes():
                for qb in range(4, 8):
                    for q2 in range(2):
                        idx = (qb - 4) * 2 + q2
                        tbk = 6 + (idx % 2)
                        for hl in range(4):
                            S.op("pe", lambda e, hl=hl, idx=idx, tbk=tbk: e.matmul(PS[tbk][0:8, hl * 128:(hl + 1) * 128], mball[:, idx, hl * 8:(hl + 1) * 8], identA,
                                                                         start=True, stop=True),
                                 reads=[MBB, ACB], writes=[PSB[tbk]], signal=(hl == 3))
                        S.op("act", lambda e, qb=qb, q2=q2, tbk=tbk: e.activation(out=mbTall[:, qb - 4, :, q2 * 128:(q2 + 1) * 128],
                                                                        in_=PS[tbk][0:8, :].rearrange("p (h q) -> p h q", h=4), func=AF.Copy),
                             reads=[PSB[tbk]], writes=[MBTB[0]])

            items = []
            for qb in range(8):
                for hl in range(4):
                    npair = qb + 1
                    for kp in range(npair):
                        items.append((qb, hl, kp, kp == 0, kp == npair - 1))

            def emit_S(i):
                qb, hl, kp, _, _ = items[i]
                qs = slice(qb * 256, (qb + 1) * 256)
                gated = qb >= 4
                pi = 2 + (i % 2)
                for k2 in range(2):
                    kt = kp * 2 + k2
                    n = kp
                    osl = PS[pi][:, k2 * 256:(k2 + 1) * 256]
                    extra = (n == qb) or gated
                    S.op("pe", lambda e, osl=osl, hl=hl, kt=kt, extra=extra, qs=qs: e.matmul(
                        osl, kT[:, hl, kt * 128:(kt + 1) * 128], qT[:, hl, qs], start=True, stop=(not extra)),
                        reads=[KTB[hl], QTB[hl]], writes=[PSB[pi]], signal=((not extra) and k2 == 1))
                    if n == qb:
                        S.op("pe", lambda e, osl=osl, k2=k2: e.matmul(osl, identA, cbiasS[:, k2, :], start=False, stop=True),
                             reads=[ACB], writes=[PSB[pi]], signal=(k2 == 1))
                    elif gated:
                        S.op("pe", lambda e, osl=osl, n=n, hl=hl, qb=qb: e.matmul(osl, EnS[:, n, :], mbTall[:, qb - 4, hl, :], start=False, stop=True),
                             reads=[ACB, MBTB[0]], writes=[PSB[pi]], signal=(k2 == 1))

            def emit_rest(i, gi):
                qb, hl, kp, first, last = items[i]
                qs = slice(qb * 256, (qb + 1) * 256)
                pi = 2 + (i % 2)
                ti = i % 2
                po, pd = ((4, 5), (0, 1))[gi % 2]
                S.op("act", lambda e, pi=pi, ti=ti: e.activation(out=pT[ti].rearrange("p a b -> p (a b)"), in_=PS[pi][:],
                                                                func=AF.Exp, scale=SCALE),
                     reads=[PSB[pi]], writes=[PTB[ti]])
                for k2 in range(2):
                    kt = kp * 2 + k2
                    f_ = first and k2 == 0
                    l_ = last and k2 == 1
                    S.op("pe", lambda e, po=po, kt=kt, hl=hl, ti=ti, k2=k2, f_=f_, l_=l_: e.matmul(
                        PS[po][:, 0:256], Vh[:, kt, hl * 128:(hl + 1) * 128], pT[ti][:, k2, :], start=f_, stop=l_),
                        reads=[VHB[kt], PTB[ti]], writes=[PSB[po]], signal=False)
                    S.op("pe", lambda e, pd=pd, ti=ti, k2=k2, f_=f_, l_=l_: e.matmul(
                        PS[pd][:, 0:256], ones_bf, pT[ti][:, k2, :], start=f_, stop=l_),
                        reads=[CONSTB, PTB[ti]], writes=[PSB[pd]], signal=(k2 == 1))
                if last:
                    rd = rdens[gi % 2]
                    S.op("dve", lambda e, pd=pd, rd=rd: e.reciprocal(out=rd, in_=PS[pd][:, 0:256]), reads=[PSB[pd]], writes=[RDBS[gi % 2]])
                    S.op("dve", lambda e, po=po, hl=hl, rd=rd, qs=qs: e.tensor_tensor(out=oT[:, hl, qs], in0=PS[po][:, 0:256], in1=rd, op=ALU.mult),
                         reads=[PSB[po], RDBS[gi % 2]], writes=[OTB[hl][qb]])

            n_items = len(items)
            first_gated = next(i for i, it in enumerate(items) if it[0] >= 4)
            gi = 0
            emit_S(0)
            for i in range(n_items):
                if i + 1 < n_items:
                    if i + 1 == first_gated:
                        while mask_jobs:
                            mask_job(*mask_jobs.pop(0))
                        mask_transposes()
                    emit_S(i + 1)
                emit_rest(i, gi)
                if items[i][4]:
                    gi += 1
                    if mask_jobs:
                        mask_job(*mask_jobs.pop(0))
            if half == 0:
                dump("oT0", oT.rearrange("p a b -> p (a b)"), [b for r_ in OTB for b in r_])
                dump("qT0", qT.rearrange("p a b -> p (a b)"), QTB)
                dump("kT0", kT.rearrange("p a b -> p (a b)"), KTB)
                dump("Vh0", Vh.rearrange("p a b -> p (a b)"), VHB)
            for dc in range(KC):
                for tb in range(4):
                    pi = pcs[0] % 2
                    pcs[0] += 1
                    sl = slice(tb * TT, (tb + 1) * TT)
                    for hl in range(4):
                        S.op("pe", lambda e, pi=pi, hl=hl, dc=dc, sl=sl: e.matmul(
                            PS[pi][:], woS[:, hl, dc * 128:(dc + 1) * 128], oT[:, hl, sl], start=(hl == 0), stop=(hl == 3)),
                            reads=[WOB, OTB[hl][2 * tb], OTB[hl][2 * tb + 1]], writes=[PSB[pi]], signal=(hl == 3))
                    xs = xT[:, dc, sl]
                    S.op("dve", lambda e, pi=pi, xs=xs: e.tensor_tensor(out=xs, in0=PS[pi][:], in1=xs, op=ALU.add),
                         reads=[PSB[pi], XB[dc][tb]], writes=[XB[dc][tb]])
            S.barrier()

    OUTB = [Buf("o%d" % i) for i in range(2)]
    for s in range(nseq):
        def load_blocks(sq, blks):
            for blk in blks:
                for kc in range(KC):
                    S.dma("sp", xT[:, kc, blk * TT:(blk + 1) * TT], xT_d[sq, kc * 128:(kc + 1) * 128, blk * TT:(blk + 1) * TT],
                          writes=[XB[kc][blk]])

        if s == 0 or "final" not in stages:
            load_blocks(s, range(L // TT))
        if s == 0 and "mix0" in stages:
            ssm_param_prep()
        if "ffn00" in stages:
            ffn_chain([(0, 0, 0), (0, 0, 1024)])
        if "mix0" in stages:
            mixer0(s)
        if "ffn01" in stages and "ffn10" in stages:
            ffn_chain([(1, 1, 0), (1, 1, 1024), (2, 2, 0), (2, 2, 1024)])
        else:
            if "ffn01" in stages:
                ffn_chain([(1, 1, 0), (1, 1, 1024)])
            if "ffn10" in stages:
                ffn_chain([(2, 2, 0), (2, 2, 1024)])
        if "mix1" in stages:
            mixer1(s)
        def final_tile(t0, s=s):
            rmsnorm_tile(6, t0, 2, hT, HB, inplace=True)
            for blk in (t0 // TT, t0 // TT + 1):
                for kc in range(KC):
                    S.dma("sp", outT_d[s, kc * 128:(kc + 1) * 128, blk * TT:(blk + 1) * TT], xT[:, kc, blk * TT:(blk + 1) * TT],
                          reads=[XB[kc][blk]])
            if s + 1 < nseq:
                load_blocks(s + 1, (t0 // TT, t0 // TT + 1))

        fused_tail = ("ffn11" in stages) and ("final" in stages)
        if "ffn11" in stages:
            ffn_chain([(3, 3, 0), (3, 3, 1024)], last_hook=(lambda: final_tile(0)) if fused_tail else None)
        if "final" in stages:
            for t0 in ((1024,) if fused_tail else (0, 1024)):
                final_tile(t0)
        if "final" not in stages:
            for blk in range(L // TT):
                for kc in range(KC):
                    S.dma("sp", outT_d[s, kc * 128:(kc + 1) * 128, blk * TT:(blk + 1) * TT], xT[:, kc, blk * TT:(blk + 1) * TT],
                          reads=[XB[kc][blk]])
    S.wait_all("sp", [b for row in XB for b in row])
    nc._sched_ninst = S.ninst
    nc._dbg_names = dbg_names
    return nc


def _prep_inputs(inputs):
    f = np.float32
    x = np.asarray(inputs["x"], f)
    ffn_norm = np.asarray(inputs["ffn_norm"], f)
    vecs = [ffn_norm[0, 0], ffn_norm[0, 1], ffn_norm[1, 0], ffn_norm[1, 1],
            np.asarray(inputs["mix_norm"], f)[0], np.asarray(inputs["mix_norm"], f)[1],
            np.asarray(inputs["final_norm"], f)]
    gains = np.stack([v.reshape(KC, 128).T for v in vecs], axis=1)
    gains = np.ascontiguousarray(gains.reshape(128, 7 * KC))
    shared = {
        "gains": gains,
        "w1": np.ascontiguousarray(np.asarray(inputs["ffn_w1"], f).reshape(4, D, FF)),
        "w3": np.ascontiguousarray(np.asarray(inputs["ffn_w3"], f).reshape(4, D, FF)),
        "w2": np.ascontiguousarray(np.asarray(inputs["ffn_w2"], f).reshape(4, FF, D)),
    }
    shared["w_in"] = np.ascontiguousarray(np.asarray(inputs["ab_w_in"], f)[0])
    shared["w_out"] = np.ascontiguousarray(np.asarray(inputs["ab_w_out"], f)[0])
    shared["glu_w"] = np.ascontiguousarray(np.asarray(inputs["ssm_glu_w"], f)[0])
    cwm = np.asarray(inputs["conv_w"], f)[0]
    shared["cw"] = np.ascontiguousarray(cwm.T.reshape(4, 128, 31).transpose(1, 0, 2).reshape(128, 4 * 31))
    chv = [np.asarray(inputs[k], f)[0] for k in ("conv_b", "conv_ln_g", "conv_ln_b", "ssm_d", "ssm_glu_b")]
    shared["chv"] = np.ascontiguousarray(np.stack([v.reshape(4, 128).T for v in chv], axis=1).reshape(128, 20))
    ldt = np.broadcast_to(np.asarray(inputs["ssm_log_dt"], f)[0][None, :], (64, 32))
    are = np.asarray(inputs["ssm_a_re"], f)[0].T
    aim = np.asarray(inputs["ssm_a_im"], f)[0].T
    shared["sp1"] = np.ascontiguousarray(np.stack([ldt, are, aim], axis=1).reshape(64, 96))
    bre = np.asarray(inputs["ssm_b_re"], f)[0].transpose(1, 0, 2).reshape(64, 512)
    bim = np.asarray(inputs["ssm_b_im"], f)[0].transpose(1, 0, 2).reshape(64, 512)
    cre = np.asarray(inputs["ssm_c_re"], f)[0].transpose(2, 0, 1).reshape(64, 512)
    cim = np.asarray(inputs["ssm_c_im"], f)[0].transpose(2, 0, 1).reshape(64, 512)
    shared["sp2"] = np.ascontiguousarray(np.stack([bre, bim, cre, cim], axis=1).reshape(64, 2048))
    shared["w_qkv"] = np.ascontiguousarray(np.asarray(inputs["attn_w_qkv"], f)[0])
    shared["w_o"] = np.ascontiguousarray(np.asarray(inputs["attn_w_o"], f)[0])
    shared.update(_const_tables())
    in_maps = []
    for c in range(NCORES):
        m = dict(shared)
        m["xT"] = np.ascontiguousarray(x[2 * c:2 * c + 2].transpose(0, 2, 1))
        in_maps.append(m)
    return in_maps


def _const_tables():
    f = np.float32
    ident = np.eye(128, dtype=f)
    p = np.arange(128)
    bmask = (p[None, :] // 16 >= p[:, None] // 16).astype(f)
    sel = np.zeros((128, 8, 240), f)
    for g8 in range(8):
        for h in range(16):
            sel[16 * g8 + h, g8, 112 + h] = 1.0
    cb = np.zeros((128, 2, 256), f)
    for par in range(2):
        cb[:, par, :] = np.where((par * 128 + p[:, None]) <= np.arange(256)[None, :], 0.0, -30000.0)
    en = np.zeros((8, 8, 128), f)
    for n in range(8):
        en[n, n, :] = 1.0
    negm = np.zeros((128, 4, 64), f)
    for qb in range(4, 8):
        for h in range(8):
            negm[:, qb - 4, h * 8 + qb:h * 8 + 8] = -1e30
    return {"cbias": np.ascontiguousarray(cb.reshape(128, 512)), "en": np.ascontiguousarray(en.reshape(8, 1024)),
            "negm": np.ascontiguousarray(negm.reshape(128, 256)),
            "ident": ident, "bmask": bmask, "selin": np.ascontiguousarray(sel.reshape(128, 1920)),
            "selout": np.ascontiguousarray(sel.reshape(128, 1920))}


_NC_CACHE = {}


def kernel(**inputs):
    in_maps = _prep_inputs(inputs)
    if "nc" not in _NC_CACHE:
        _NC_CACHE["nc"] = build_program()
    nc = _NC_CACHE["nc"]
    res = run_bass_kernel_spmd(nc, in_maps, core_ids=list(range(NCORES)))
    out = np.empty((2 * NCORES, L, D), np.float32)
    for c in range(NCORES):
        out[2 * c:2 * c + 2] = np.asarray(res.results[c]["outT"]).transpose(0, 2, 1)
    return out
```
